# Optimizing a Trainium2 kernel written in Bass

```python
import math
import jax
import jax.numpy as jnp
from jax import lax
import numpy as np


D_MODEL = 2048
BATCH = 4
SEQ = 2048
DEPTH = 2
DEC_BATCH = 32
DEC_SEQ = 8
PAST_LEN = 8192
PAGE_SIZE = 128

N_EVEN = (DEPTH + 1) // 2
N_ODD = DEPTH // 2
D_RNN = D_MODEL // 2
RG_HEADS = 8
RG_HEAD_DIM = D_RNN // RG_HEADS
RG_CONV = 4
RG_C = 8.0
N_HEADS = 16
HEAD_DIM = 64
N_KV = 4
HPG = N_HEADS // N_KV
D_ATTN = N_HEADS * HEAD_DIM
D_KV = N_KV * HEAD_DIM
BLOCK = 64
TOPK = 16
WINDOW = 512
N_BRANCH = 3
Q_BLOCK = 128
SCALE = HEAD_DIM ** -0.5
FORCE = 1e4
NEG = -1e30
N_BUCKETS = 32
MAX_DIST = 128
D_CONV = D_MODEL
SC_WIDTH = 3
D_FF = 4 * D_MODEL
EPS = 1e-6
D_IN_E = 2 * D_RNN + D_ATTN + 6 * D_KV + N_BRANCH * N_HEADS
D_OUT_E = D_RNN + D_ATTN
F32 = jnp.float32

kernel_name = 'nsa_rglru_shortconv_hybrid_step'


def _rmsnorm(x, g):
    xf = x.astype(F32)
    y = xf * lax.rsqrt(jnp.mean(xf * xf, axis=-1, keepdims=True) + EPS)
    return (y * g.astype(F32)).astype(x.dtype)


def _t5_bucket(dist):
    n = jnp.maximum(dist, 0)
    max_exact = N_BUCKETS // 2
    nf = jnp.maximum(n, 1).astype(F32)
    large = max_exact + (jnp.log(nf / max_exact) / math.log(MAX_DIST / max_exact) * (N_BUCKETS - max_exact)).astype(jnp.int32)
    large = jnp.minimum(large, N_BUCKETS - 1)
    return jnp.where(n < max_exact, n, large)


def _causal_dwconv(x, buf, w):
    width = w.shape[0]
    t = x.shape[1]
    xx = jnp.concatenate([buf.astype(x.dtype), x], axis=1)
    y = xx[:, 0:t] * w[0]
    for j in range(1, width):
        y = y + xx[:, j:j + t] * w[j]
    return y, xx[:, t:]


def _rglru(x, h0, W):
    b_, t, _ = x.shape
    xf = x.astype(F32)
    xh = xf.reshape(b_, t, RG_HEADS, RG_HEAD_DIM)
    r = jax.nn.sigmoid(jnp.einsum('bthi,hij->bthj', xh, W['rg_wa'].astype(F32)).reshape(b_, t, D_RNN) + W['rg_ba'].astype(F32))
    i = jax.nn.sigmoid(jnp.einsum('bthi,hij->bthj', xh, W['rg_wx'].astype(F32)).reshape(b_, t, D_RNN) + W['rg_bx'].astype(F32))
    log_a = -RG_C * r * jax.nn.softplus(-W['rg_lambda'].astype(F32))
    a = jnp.exp(log_a)
    u = jnp.sqrt(-jnp.expm1(2.0 * log_a)) * (i * xf)

    def step(hc, au):
        hc = au[0] * hc + au[1]
        return hc, hc

    h_last, hs = lax.scan(step, h0.astype(F32), (jnp.swapaxes(a, 0, 1), jnp.swapaxes(u, 0, 1)))
    return jnp.swapaxes(hs, 0, 1).astype(x.dtype), h_last.astype(x.dtype)


def _split_even(z):
    sizes = [D_RNN, D_RNN, D_ATTN] + [D_KV] * 6 + [N_BRANCH * N_HEADS]
    cuts = [int(c) for c in np.cumsum(sizes)[:-1]]
    return jnp.split(z, cuts, axis=-1)


def _even_project(h, W):
    b_, t = h.shape[:2]
    z = jnp.einsum('btd,de->bte', h, W['w_in'])
    rx, rg, q, kc, vc, ks, vs, kw, vw, gl = _split_even(z)
    q = _rmsnorm(q.reshape(b_, t, N_HEADS, HEAD_DIM), W['q_norm']).reshape(b_, t, N_KV, HPG, HEAD_DIM)
    kc = kc.reshape(b_, t, N_KV, HEAD_DIM)
    vc = vc.reshape(b_, t, N_KV, HEAD_DIM)
    ks = _rmsnorm(ks.reshape(b_, t, N_KV, HEAD_DIM), W['k_norm'][1])
    vs = vs.reshape(b_, t, N_KV, HEAD_DIM)
    kw = _rmsnorm(kw.reshape(b_, t, N_KV, HEAD_DIM), W['k_norm'][2])
    vw = vw.reshape(b_, t, N_KV, HEAD_DIM)
    gates = jax.nn.sigmoid(gl.astype(F32)).reshape(b_, t, N_BRANCH, N_KV, HPG, 1).astype(h.dtype)
    return rx, rg, q, kc, vc, ks, vs, kw, vw, gates


def _rg_branch(rx, rg, conv_buf, h0, W):
    xc, new_buf = _causal_dwconv(rx, conv_buf, W['rg_conv_w'])
    y, h_last = _rglru(xc + W['rg_conv_b'], h0, W)
    return jax.nn.gelu(rg) * y, new_buf, h_last


def _compress(k, pe, w1, w2):
    b_, t = k.shape[:2]
    kb = k.reshape(b_, t // BLOCK, BLOCK, N_KV, HEAD_DIM) + pe
    hid = jax.nn.gelu(jnp.einsum('bnjgd,jde->bnge', kb, w1))
    return jnp.einsum('bnge,ef->bngf', hid, w2)


def _dense_core(q, k, v, dist, valid, tab):
    t, m = dist.shape
    s = jnp.einsum('btgnd,bmgd->btgnm', q, k).astype(F32) * SCALE
    bias = tab[_t5_bucket(dist)].astype(F32).reshape(t, m, N_KV, HPG).transpose(0, 2, 3, 1)
    msk = valid[:, None, None, :]
    s = jnp.where(msk, s + bias, NEG)
    p = jnp.where(msk, jax.nn.softmax(s, axis=-1), 0.0)
    o = jnp.einsum('btgnm,bmgd->btgnd', p.astype(v.dtype), v)
    return o, p


def _select_blocks(p, qpos, n_sel):
    nb = p.shape[-1]
    imp = p.sum(axis=3)
    cur = (qpos // BLOCK)[:, None]
    blk = jnp.arange(nb)[None, :]
    forced = (blk == cur) | (blk == cur - 1) | (blk == 0)
    imp = imp + jnp.where(forced, FORCE, 0.0)[None, :, None, :]
    imp = jnp.where((blk > cur)[None, :, None, :], NEG, imp)
    return lax.top_k(imp, n_sel)[1]


def _sel_core(q, kg, vg, tok, qpos, tab):
    s = jnp.einsum('tgnd,tgld->tgnl', q, kg).astype(F32) * SCALE
    dist = qpos[:, None, None] - tok
    tab_g = tab.reshape(N_BUCKETS, N_KV, HPG)
    bias = tab_g[_t5_bucket(dist), jnp.arange(N_KV)[None, :, None]].astype(F32).transpose(0, 1, 3, 2)
    msk = (dist >= 0)[:, :, None, :]
    s = jnp.where(msk, s + bias, NEG)
    p = jnp.where(msk, jax.nn.softmax(s, axis=-1), 0.0)
    return jnp.einsum('tgnl,tgld->tgnd', p.astype(vg.dtype), vg)


def _block_tokens(ii):
    t = ii.shape[0]
    return (ii[..., None] * BLOCK + jnp.arange(BLOCK)).reshape(t, N_KV, -1)


def _sel_prompt(q, k, v, idx, tab):
    b_, s_ = q.shape[:2]
    nqb = s_ // Q_BLOCK
    n_sel = idx.shape[-1]
    qb = q.reshape(b_ * nqb, Q_BLOCK, N_KV, HPG, HEAD_DIM)
    ib = idx.reshape(b_ * nqb, Q_BLOCK, N_KV, n_sel)
    bid = jnp.repeat(jnp.arange(b_), nqb)
    start = jnp.tile(jnp.arange(nqb) * Q_BLOCK, b_)
    garr = jnp.arange(N_KV)[None, :, None]

    def one(args):
        qi, ii, b, s0 = args
        tok = _block_tokens(ii)
        return _sel_core(qi, k[b, tok, garr], v[b, tok, garr], tok, s0 + jnp.arange(Q_BLOCK), tab)

    o = lax.map(one, (qb, ib, bid, start))
    return o.reshape(b_, s_, N_KV, HPG, HEAD_DIM)


def _sel_sample(q, pool_k, pool_v, new_k, new_v, page_table, idx, qpos, tab):
    t = q.shape[1]
    past = page_table.shape[1] * PAGE_SIZE
    garr = jnp.arange(N_KV)[None, :, None]

    def one(args):
        qi, ii, pt, nk, nv = args
        tok = _block_tokens(ii)
        tp = jnp.minimum(tok, past - 1)
        phys = pt[tp // PAGE_SIZE]
        slot = tp % PAGE_SIZE
        tn = jnp.clip(tok - past, 0, t - 1)
        in_past = (tok < past)[..., None]
        kg = jnp.where(in_past, pool_k[phys, slot, garr], nk[tn, garr])
        vg = jnp.where(in_past, pool_v[phys, slot, garr], nv[tn, garr])
        return _sel_core(qi, kg, vg, tok, qpos, tab)

    return lax.map(one, (q, idx, page_table, new_k, new_v))


def _win_prompt(q, k, v, tab):
    b_, s_ = q.shape[:2]
    nqb = s_ // Q_BLOCK
    span = WINDOW + Q_BLOCK
    pad = ((0, 0), (WINDOW, 0), (0, 0), (0, 0))
    kp = jnp.pad(k, pad)
    vp = jnp.pad(v, pad)
    qb = jnp.swapaxes(q.reshape(b_, nqb, Q_BLOCK, N_KV, HPG, HEAD_DIM), 0, 1)

    def one(args):
        qi, i = args
        s0 = i * Q_BLOCK
        kb = lax.dynamic_slice_in_dim(kp, s0, span, axis=1)
        vb = lax.dynamic_slice_in_dim(vp, s0, span, axis=1)
        qpos = s0 + jnp.arange(Q_BLOCK)
        kpos = s0 - WINDOW + jnp.arange(span)
        dist = qpos[:, None] - kpos[None, :]
        valid = (dist >= 0) & (dist <= WINDOW) & (kpos[None, :] >= 0)
        return _dense_core(qi, kb, vb, dist, valid, tab)[0]

    o = lax.map(one, (qb, jnp.arange(nqb)))
    return jnp.swapaxes(o, 0, 1).reshape(b_, s_, N_KV, HPG, HEAD_DIM)


def _even_out(y_rnn, o_c, o_s, o_w, gates, W):
    b_, t = y_rnn.shape[:2]
    o = gates[:, :, 0] * o_c + gates[:, :, 1] * o_s + gates[:, :, 2] * o_w
    cat = jnp.concatenate([y_rnn, o.reshape(b_, t, D_ATTN)], axis=-1)
    return jnp.einsum('bte,ed->btd', cat, W['w_out'])


def _even_prompt(h, W, tab):
    b_, s_ = h.shape[:2]
    rx, rg, q, kc, vc, ks, vs, kw, vw, gates = _even_project(h, W)
    y_rnn, cbuf, h_last = _rg_branch(rx, rg, jnp.zeros((b_, RG_CONV - 1, D_RNN), h.dtype), jnp.zeros((b_, D_RNN), F32), W)
    qpos = jnp.arange(s_)
    kcb = _rmsnorm(_compress(kc, W['pe_k'], W['w1_k'], W['w2_k']), W['k_norm'][0])
    vcb = _compress(vc, W['pe_v'], W['w1_v'], W['w2_v'])
    nb = kcb.shape[1]
    dist = qpos[:, None] - (jnp.arange(nb) * BLOCK + (BLOCK - 1))[None, :]
    o_c, p_c = _dense_core(q, kcb, vcb, dist, dist >= 0, tab)
    idx = _select_blocks(p_c, qpos, min(TOPK, nb))
    o_s = _sel_prompt(q, ks, vs, idx, tab)
    o_w = _win_prompt(q, kw, vw, tab)
    out = _even_out(y_rnn, o_c, o_s, o_w, gates, W)
    keep = min(WINDOW, s_)
    return out, (kc, vc, ks, vs, kw[:, s_ - keep:], vw[:, s_ - keep:], h_last, cbuf)


def _even_sample(h, W, tab, pool_ck, pool_cv, pool_sk, pool_sv, win_k, win_v, h_state, conv_state, page_table):
    b_, t = h.shape[:2]
    past = page_table.shape[1] * PAGE_SIZE
    rx, rg, q, kc, vc, ks, vs, kw, vw, gates = _even_project(h, W)
    y_rnn, cbuf, h_last = _rg_branch(rx, rg, conv_state, h_state, W)
    qpos = past + jnp.arange(t)
    nb = -(-(past + t) // BLOCK)
    padw = ((0, 0), (0, nb * BLOCK - past - t), (0, 0), (0, 0))
    kc_all = jnp.pad(jnp.concatenate([pool_ck[page_table].reshape(b_, past, N_KV, HEAD_DIM), kc], axis=1), padw)
    vc_all = jnp.pad(jnp.concatenate([pool_cv[page_table].reshape(b_, past, N_KV, HEAD_DIM), vc], axis=1), padw)
    kcb = _rmsnorm(_compress(kc_all, W['pe_k'], W['w1_k'], W['w2_k']), W['k_norm'][0])
    vcb = _compress(vc_all, W['pe_v'], W['w1_v'], W['w2_v'])
    dist = qpos[:, None] - (jnp.arange(nb) * BLOCK + (BLOCK - 1))[None, :]
    o_c, p_c = _dense_core(q, kcb, vcb, dist, dist >= 0, tab)
    idx = _select_blocks(p_c, qpos, min(TOPK, nb))
    o_s = _sel_sample(q, pool_sk, pool_sv, ks, vs, page_table, idx, qpos, tab)
    wb = win_k.shape[1]
    wk = jnp.concatenate([win_k, kw], axis=1)
    wv = jnp.concatenate([win_v, vw], axis=1)
    kpos = past - wb + jnp.arange(wb + t)
    dist_w = qpos[:, None] - kpos[None, :]
    valid_w = (dist_w >= 0) & (dist_w <= WINDOW) & (kpos[None, :] >= 0)
    o_w = _dense_core(q, wk, wv, dist_w, valid_w, tab)[0]
    out = _even_out(y_rnn, o_c, o_s, o_w, gates, W)
    keep = min(WINDOW, past + t)
    return out, (kc, vc, ks, vs, wk[:, wb + t - keep:], wv[:, wb + t - keep:], h_last, cbuf)


def _shortconv_mixer(h, buf, w_in, w_conv, w_out):
    z = jnp.einsum('btd,de->bte', h, w_in)
    bg, cg, u = jnp.split(z, 3, axis=-1)
    y, new_buf = _causal_dwconv(cg * u, buf, w_conv)
    return jnp.einsum('btc,cd->btd', bg * y, w_out), new_buf


def _ffn(h, w1, w2):
    a = jax.nn.relu(jnp.einsum('btd,df->btf', h, w1))
    return jnp.einsum('btf,fd->btd', a * a, w2)


def setup_inputs(seed: int = 0) -> dict:
    key = jax.random.key(seed)
    keys = iter(jax.random.split(key, 48))

    def nrm(shape, scale):
        return jax.random.normal(next(keys), shape, F32) * scale

    n_pages = PAST_LEN // PAGE_SIZE
    n_used = DEC_BATCH * n_pages
    n_pool = n_used + max(1, n_used // 4)
    wb = min(WINDOW, PAST_LEN)
    pool_shape = (N_EVEN, n_pool, PAGE_SIZE, N_KV, HEAD_DIM)
    win_shape = (N_EVEN, DEC_BATCH, wb, N_KV, HEAD_DIM)
    x_prompt = nrm((BATCH, SEQ, D_MODEL), 1.0)
    x_sample = nrm((DEC_BATCH, DEC_SEQ, D_MODEL), 1.0)
    cache_cmp_k = nrm(pool_shape, 1.0)
    cache_cmp_v = nrm(pool_shape, 1.0)
    cache_sel_k = nrm(pool_shape, 1.0)
    cache_sel_v = nrm(pool_shape, 1.0)
    cache_win_k = nrm(win_shape, 1.0)
    cache_win_v = nrm(win_shape, 1.0)
    state_rglru_h = nrm((N_EVEN, DEC_BATCH, D_RNN), 0.5)
    state_rglru_conv = nrm((N_EVEN, DEC_BATCH, RG_CONV - 1, D_RNN), 1.0)
    state_sconv = nrm((N_ODD, DEC_BATCH, SC_WIDTH - 1, D_CONV), 1.0)
    page_table = jax.random.permutation(next(keys), n_pool)[:n_used].reshape(DEC_BATCH, n_pages).astype(jnp.int32)
    a_c = jax.random.uniform(next(keys), (N_EVEN, D_RNN), F32, minval=0.9, maxval=0.999)
    s_l = a_c ** (1.0 / RG_C)
    rg_lambda = jnp.log(s_l) - jnp.log1p(-s_l)
    return {
        'x_prompt': x_prompt,
        'x_sample': x_sample,
        'cache_cmp_k': cache_cmp_k,
        'cache_cmp_v': cache_cmp_v,
        'cache_sel_k': cache_sel_k,
        'cache_sel_v': cache_sel_v,
        'cache_win_k': cache_win_k,
        'cache_win_v': cache_win_v,
        'state_rglru_h': state_rglru_h,
        'state_rglru_conv': state_rglru_conv,
        'state_sconv': state_sconv,
        'page_table': page_table,
        'rel_bias': nrm((N_BUCKETS, N_HEADS), 0.5),
        'norm_mix': 1.0 + nrm((DEPTH, D_MODEL), 0.02),
        'norm_ffn': 1.0 + nrm((DEPTH, D_MODEL), 0.02),
        'w_ff1': nrm((DEPTH, D_MODEL, D_FF), D_MODEL ** -0.5),
        'w_ff2': nrm((DEPTH, D_FF, D_MODEL), D_FF ** -0.5),
        'w_in_e': nrm((N_EVEN, D_MODEL, D_IN_E), D_MODEL ** -0.5),
        'w_out_e': nrm((N_EVEN, D_OUT_E, D_MODEL), D_OUT_E ** -0.5),
        'rg_conv_w': nrm((N_EVEN, RG_CONV, D_RNN), 0.5),
        'rg_conv_b': nrm((N_EVEN, D_RNN), 0.05),
        'rg_wa': nrm((N_EVEN, RG_HEADS, RG_HEAD_DIM, RG_HEAD_DIM), RG_HEAD_DIM ** -0.5),
        'rg_ba': nrm((N_EVEN, D_RNN), 0.1),
        'rg_wx': nrm((N_EVEN, RG_HEADS, RG_HEAD_DIM, RG_HEAD_DIM), RG_HEAD_DIM ** -0.5),
        'rg_bx': nrm((N_EVEN, D_RNN), 0.1),
        'rg_lambda': rg_lambda,
        'q_norm': 1.0 + nrm((N_EVEN, HEAD_DIM), 0.02),
        'k_norm': 1.0 + nrm((N_EVEN, N_BRANCH, HEAD_DIM), 0.02),
        'cmp_pe_k': nrm((N_EVEN, BLOCK, N_KV, HEAD_DIM), 0.5),
        'cmp_pe_v': nrm((N_EVEN, BLOCK, N_KV, HEAD_DIM), 0.5),
        'cmp_w1_k': nrm((N_EVEN, BLOCK, HEAD_DIM, HEAD_DIM), (BLOCK * HEAD_DIM) ** -0.5),
        'cmp_w2_k': nrm((N_EVEN, HEAD_DIM, HEAD_DIM), HEAD_DIM ** -0.5),
        'cmp_w1_v': nrm((N_EVEN, BLOCK, HEAD_DIM, HEAD_DIM), (BLOCK * HEAD_DIM) ** -0.5),
        'cmp_w2_v': nrm((N_EVEN, HEAD_DIM, HEAD_DIM), HEAD_DIM ** -0.5),
        'w_in_o': nrm((N_ODD, D_MODEL, 3 * D_CONV), D_MODEL ** -0.5),
        'sc_w': nrm((N_ODD, SC_WIDTH, D_CONV), SC_WIDTH ** -0.5),
        'w_out_o': nrm((N_ODD, D_CONV, D_MODEL), D_CONV ** -0.5),
    }


def reference(x_prompt, x_sample, cache_cmp_k, cache_cmp_v, cache_sel_k, cache_sel_v, cache_win_k, cache_win_v,
              state_rglru_h, state_rglru_conv, state_sconv, page_table, rel_bias, norm_mix, norm_ffn, w_ff1, w_ff2,
              w_in_e, w_out_e, rg_conv_w, rg_conv_b, rg_wa, rg_ba, rg_wx, rg_bx, rg_lambda, q_norm, k_norm,
              cmp_pe_k, cmp_pe_v, cmp_w1_k, cmp_w2_k, cmp_w1_v, cmp_w2_v, w_in_o, sc_w, w_out_o):
    y_p = x_prompt
    y_s = x_sample
    ev_p, ev_s, od_p, od_s = [], [], [], []
    for li in range(DEPTH):
        j = li // 2
        h_p = _rmsnorm(y_p, norm_mix[li])
        h_s = _rmsnorm(y_s, norm_mix[li])
        if li % 2 == 0:
            W = {'w_in': w_in_e[j], 'w_out': w_out_e[j], 'rg_conv_w': rg_conv_w[j], 'rg_conv_b': rg_conv_b[j],
                 'rg_wa': rg_wa[j], 'rg_ba': rg_ba[j], 'rg_wx': rg_wx[j], 'rg_bx': rg_bx[j], 'rg_lambda': rg_lambda[j],
                 'q_norm': q_norm[j], 'k_norm': k_norm[j], 'pe_k': cmp_pe_k[j], 'pe_v': cmp_pe_v[j],
                 'w1_k': cmp_w1_k[j], 'w2_k': cmp_w2_k[j], 'w1_v': cmp_w1_v[j], 'w2_v': cmp_w2_v[j]}
            m_p, st_p = _even_prompt(h_p, W, rel_bias)
            m_s, st_s = _even_sample(h_s, W, rel_bias, cache_cmp_k[j], cache_cmp_v[j], cache_sel_k[j], cache_sel_v[j],
                                     cache_win_k[j], cache_win_v[j], state_rglru_h[j], state_rglru_conv[j], page_table)
            ev_p.append(st_p)
            ev_s.append(st_s)
        else:
            m_p, b_p = _shortconv_mixer(h_p, jnp.zeros((h_p.shape[0], SC_WIDTH - 1, D_CONV), h_p.dtype), w_in_o[j], sc_w[j], w_out_o[j])
            m_s, b_s = _shortconv_mixer(h_s, state_sconv[j], w_in_o[j], sc_w[j], w_out_o[j])
            od_p.append(b_p)
            od_s.append(b_s)
        y_p = y_p + m_p
        y_s = y_s + m_s
        y_p = y_p + _ffn(_rmsnorm(y_p, norm_ffn[li]), w_ff1[li], w_ff2[li])
        y_s = y_s + _ffn(_rmsnorm(y_s, norm_ffn[li]), w_ff1[li], w_ff2[li])

    def _stk(lst, i):
        return jnp.stack([st[i] for st in lst])

    cmp_k_p = _stk(ev_p, 0)
    cmp_k_s = _stk(ev_s, 0)
    cmp_v_p = _stk(ev_p, 1)
    cmp_v_s = _stk(ev_s, 1)
    sel_k_p = _stk(ev_p, 2)
    sel_k_s = _stk(ev_s, 2)
    sel_v_p = _stk(ev_p, 3)
    sel_v_s = _stk(ev_s, 3)
    win_k_p = _stk(ev_p, 4)
    win_k_s = _stk(ev_s, 4)
    win_v_p = _stk(ev_p, 5)
    win_v_s = _stk(ev_s, 5)
    rglru_h_p = _stk(ev_p, 6)
    rglru_h_s = _stk(ev_s, 6)
    rglru_conv_p = _stk(ev_p, 7)
    rglru_conv_s = _stk(ev_s, 7)
    sconv_p = jnp.stack(od_p)
    sconv_s = jnp.stack(od_s)
    return (y_p, y_s, cmp_k_p, cmp_k_s, cmp_v_p, cmp_v_s, sel_k_p, sel_k_s, sel_v_p, sel_v_s,
            win_k_p, win_k_s, win_v_p, win_v_s, rglru_h_p, rglru_h_s, rglru_conv_p, rglru_conv_s, sconv_p, sconv_s)
```

```python
import os
import numpy as np
from contextlib import ExitStack
import concourse.bass as bass
import concourse.mybir as mybir
from concourse.bass_utils import run_bass_kernel_spmd

F32 = mybir.dt.float32
BF16 = mybir.dt.bfloat16
I32 = mybir.dt.int32
ALU = mybir.AluOpType
AF = mybir.ActivationFunctionType

SAME_ENGINE_SYNC = True
EPOCH = 30000
NDMASEM = 48

NT = 1088
NPRE = 1024
TT = [(0, 512), (512, 512), (1024, 64)]
TTP = [(0, 512), (512, 512)]
EPS = 1e-6
MASKNEG = -30000.0
SCALE = 0.125
DOFF = 2304
DL = 10752
DEBUG_STAGE = 3
NCORES = 8


class Prog:
    ENGS = ('pe', 'dve', 'act', 'pool', 'sp')

    def __init__(self, nc, stack):
        self.nc = nc
        self.stack = stack
        self.nsem = 0
        self.ops = {e: [] for e in self.ENGS}
        self.csem = {e: self._newsem('c_' + e) for e in self.ENGS}
        self.ccnt = {e: 0 for e in self.ENGS}
        self.seen = {e: {} for e in self.ENGS}
        self.lastw = {}
        self.readers = {}
        self.dsem = [self._newsem('d%d' % i) for i in range(NDMASEM)]
        self.dval = [0] * NDMASEM
        self.di = 0
        self.dq = {'sp': (0, 32), 'pool': (32, 16), 'act': (0, 32)}
        self.dqi = {'sp': 0, 'pool': 0, 'act': 0}
        self.nwaits = 0
        self.allsems = []

    def _newsem(self, name):
        self.nsem += 1
        s = self.stack.enter_context(self.nc.semaphore(name + '_%d' % self.nsem))
        return s

    def _deps(self, reads, writes):
        deps = []
        for k in list(reads) + list(writes):
            t = self.lastw.get(k)
            if t is not None:
                deps.append(t)
        for k in writes:
            deps.extend(self.readers.get(k, ()))
        for k in reads:
            if isinstance(k, tuple) and k and k[0] == 'ps':
                deps.extend(self.readers.get(k, ()))
        return deps

    def _needed(self, eng, deps):
        out = {}
        own = self.csem[eng]
        for sem, val in deps:
            if sem is own:
                if eng == 'pe' or not SAME_ENGINE_SYNC:
                    continue
            key = id(sem)
            if self.seen[eng].get(key, 0) >= val:
                continue
            if key not in out or out[key][1] < val:
                out[key] = (sem, val)
        for key, (sem, val) in out.items():
            self.seen[eng][key] = val
        self.nwaits += len(out)
        return list(out.values())

    def _commit(self, tok, reads, writes):
        for k in writes:
            self.lastw[k] = tok
            self.readers[k] = []
        for k in reads:
            if k in writes:
                continue
            self.readers.setdefault(k, []).append(tok)

    def op(self, eng, fn, reads=(), writes=(), inc=True, mode=None):
        if eng == 'pe':
            m_ = mode or 'full'
            if m_ != getattr(self, 'pemode', 'full'):
                self.ops['pe'].append(([], lambda e: e.drain(), None))
            self.pemode = m_
        deps = self._deps(reads, writes)
        waits = self._needed(eng, deps)
        if inc:
            self.ccnt[eng] += 1
            tok = (self.csem[eng], self.ccnt[eng])
            incinfo = (self.csem[eng], 1)
        else:
            tok = (self.csem[eng], self.ccnt[eng] + 1)
            incinfo = None
        self.ops[eng].append((waits, fn, incinfo))
        self._commit(tok, reads, writes)
        if inc and self.ccnt[eng] >= EPOCH:
            self.csem[eng] = self._newsem('c_' + eng)
            self.ccnt[eng] = 0
        return tok

    def dma(self, q, out, in_, reads=(), writes=(), **kw):
        return self.custom_dma(q, lambda e: e.dma_start(out=out, in_=in_, **kw), reads, writes)

    def custom_dma(self, q, fn, reads=(), writes=()):
        base, cnt = self.dq[q]
        i = base + self.dqi[q] % cnt
        self.dqi[q] += 1
        self.di += 1
        sem = self.dsem[i]
        prev = self.dval[i]
        val = prev + 16
        self.dval[i] = val
        deps = self._deps(reads, writes)
        if prev > 0:
            deps.append((sem, prev))
        waits = self._needed(q, deps)
        tok = (sem, val)
        self.ops[q].append((waits, fn, (sem, 16)))
        self._commit(tok, reads, writes)
        return tok

    def barrier(self):
        for eng in self.ENGS:
            deps = [(self.dsem[i], self.dval[i]) for i in range(NDMASEM) if self.dval[i] > 0]
            for e in self.ENGS:
                if e != eng and self.ccnt[e] > 0:
                    deps.append((self.csem[e], self.ccnt[e]))
            waits = self._needed(eng, deps)
            if waits:
                self.ops[eng].append((waits, None, None))
        self.lastw.clear()
        self.readers.clear()

    def finish(self):
        deps = [(self.dsem[i], self.dval[i]) for i in range(NDMASEM) if self.dval[i] > 0]
        for e in self.ENGS:
            if e != 'sp' and self.ccnt[e] > 0:
                deps.append((self.csem[e], self.ccnt[e]))
        waits = self._needed('sp', deps)
        self.ops['sp'].append((waits, None, None))

    def emit(self):
        nc = self.nc
        engmap = {'pe': 'tensor', 'dve': 'vector', 'act': 'scalar', 'pool': 'gpsimd', 'sp': 'sync'}
        with nc.Block() as block:
            for e in self.ENGS:
                lst = self.ops[e]

                def body(eng, lst=lst):
                    for waits, fn, incinfo in lst:
                        for sem, val in waits:
                            eng.wait_ge(sem, val)
                        if fn is None:
                            continue
                        ins = fn(eng)
                        if incinfo is not None:
                            ins.then_inc(incinfo[0], incinfo[1])
                getattr(block, engmap[e])(body)


class Builder:
    def __init__(self):
        self.nc = bass.Bass("TRN2", target_bir_lowering=False)
        self.st = ExitStack()
        self.P = Prog(self.nc, self.st)
        self.D = {}
        self.jobs = []
        self.wi = 0
        self.psi = {}
        self.in_names = []
        self.out_names = []

    def din(self, name, shape, dt=F32):
        self.D[name] = self.nc.dram_tensor(name, list(shape), dt, kind="ExternalInput").ap()
        self.in_names.append(name)
        return self.D[name]

    def dout(self, name, shape, dt=F32):
        self.D[name] = self.nc.dram_tensor(name, list(shape), dt, kind="ExternalOutput").ap()
        self.out_names.append(name)
        return self.D[name]

    def sb(self, name, shape, dt, st=None):
        return (st or self.st).enter_context(self.nc.sbuf_tensor("s_" + name, list(shape), dt))

    def psrot(self, banks):
        i = self.psi.get(banks, 0)
        self.psi[banks] = i + 1
        return banks[i % len(banks)]

    def job(self, wsrc, fn):
        self.jobs.append((wsrc, fn))

    def run_jobs(self):
        P = self.P
        jobs = self.jobs
        if os.environ.get('K_STOP') is not None and not getattr(self, '_stopped', False):
            jobs = jobs[:int(os.environ['K_STOP'])]
            self._stopped = True
        widx = [i for i, j in enumerate(jobs) if j[0] is not None]
        loaded = {}
        nl = 0
        for i, (wsrc, fn) in enumerate(jobs):
            ahead = [k for k in widx if k >= i][:2]
            if ahead:
                while nl < len(widx) and widx[nl] <= ahead[-1]:
                    k = widx[nl]
                    src = jobs[k][0]
                    bi = self.wi % 2
                    self.wi += 1
                    t = self.wt[bi]
                    kc, n = src.shape[1], src.shape[2]
                    if getattr(self, 'wstage', None) is not None:
                        stg_ = self.wstage
                        P.dma('sp', stg_[:, 0:kc, 0:n], src, writes=['wstage'])
                        P.op('pool', lambda e, t=t, stg_=stg_, kc=kc, n=n: e.tensor_copy(t[:, 0:kc, 0:n], stg_[:, 0:kc, 0:n]), reads=['wstage'], writes=[('wt', bi)])
                    else:
                        P.dma('pool', t[:, 0:kc, 0:n], src, writes=[('wt', bi)])
                    loaded[k] = (t, ('wt', bi))
                    nl += 1
            if wsrc is None:
                fn()
            else:
                t, key = loaded.pop(i)
                fn(t, key)
        self.jobs = []

    def wview(self, W, k0, kc, c0, n):
        return W[k0:k0 + kc * 128, c0:c0 + n].rearrange("(kc p) n -> p kc n", p=128)

    def dense_fm(self, wsrc, kc, nm, xparts, epi, banks=(0, 1, 2, 3)):
        P = self.P

        def fn(wt, wkey):
            for m in range(nm):
                for (xt, xkeyf, t0, n, tag) in xparts:
                    b = self.psrot(banks)
                    pst = self.ps[b]
                    pkey = ('ps', b)
                    for k in range(kc):
                        P.op('pe', lambda e, pst=pst, wt=wt, xt=xt, k=k, m=m, t0=t0, n=n: e.matmul(
                            pst[:, 0:n], wt[:, k, m * 128:(m + 1) * 128], xt[:, k, t0:t0 + n],
                            start=(k == 0), stop=(k == kc - 1)),
                            reads=[wkey, xkeyf(k)], writes=[pkey], inc=(k == kc - 1))
                    epi(m, tag, t0, n, pst[:, 0:n], pkey)
        self.job(wsrc, fn)

    def dense_tm(self, wsrc, kc, ncols, xparts, epi, banks=(0, 1, 2, 3)):
        P = self.P

        def fn(wt, wkey):
            for (xt, xkeyf, t0, n, tag) in xparts:
                b = self.psrot(banks)
                pst = self.ps[b]
                pkey = ('ps', b)
                for k in range(kc):
                    P.op('pe', lambda e, pst=pst, wt=wt, xt=xt, k=k, t0=t0, n=n: e.matmul(
                        pst[:, 0:ncols], xt[:, k, t0:t0 + 128], wt[:, k, 0:ncols],
                        start=(k == 0), stop=(k == kc - 1)),
                        reads=[wkey, xkeyf(k)], writes=[pkey], inc=(k == kc - 1))
                epi(tag, t0, n, pst[0:n, 0:ncols], pkey)
        self.job(wsrc, fn)

    def rmsnorm(self, srcf, skey, gain, dstf, dkey, tiles, fetch=None):
        P = self.P

        def getsrc(c, t0, n):
            if fetch is None:
                return srcf(c, t0, n), skey(c)
            i = self.xsi % 2
            self.xsi += 1
            P.dma('sp', self.xs[i][:, 0:n], fetch(c, t0, n), writes=[('xs', i)])
            return self.xs[i][:, 0:n], ('xs', i)

        def fn():
            for (t0, n) in tiles:
                b = self.psrot((4, 5))
                pst = self.ps[b]
                for c in range(16):
                    src, sk = getsrc(c, t0, n)
                    P.op('act', lambda e, c=c, n=n, src=src: e.activation(self.sq[:, c % 4, 0:n], src, AF.Square),
                         reads=[sk], writes=[('sq', c % 4)])
                    P.op('pe', lambda e, c=c, n=n, pst=pst: e.matmul(pst[:, 0:n], self.ones_bf[:, :], self.sq[:, c % 4, 0:n],
                                                                     start=(c == 0), stop=(c == 15)),
                         reads=[('sq', c % 4), 'ones'], writes=[('ps', b)], inc=True)
                P.op('act', lambda e, n=n, pst=pst: e.activation(self.rstd[:, 0:n], pst[:, 0:n], AF.Sqrt, bias=self.epst[:, 0:1], scale=1.0 / 2048.0),
                     reads=[('ps', b), 'epst'], writes=['rstd'])
                P.op('dve', lambda e, n=n: e.reciprocal(self.rstd[:, 0:n], self.rstd[:, 0:n]), reads=['rstd'], writes=['rstd'])
                for c in range(16):
                    src, sk = getsrc(c, t0, n)
                    P.op('dve', lambda e, c=c, t0=t0, n=n, src=src: e.scalar_tensor_tensor(
                        dstf(c, t0, n), src, gain[:, c:c + 1], self.rstd[:, 0:n], op0=ALU.mult, op1=ALU.mult),
                        reads=[sk, 'rstd', 'gains'], writes=[dkey(c)])
        self.job(None, fn)

    def ffn(self, li):
        P = self.P
        W1 = self.D['w_ff1'][li]
        W2 = self.D['w_ff2'][li]
        xparts = [(self.hT, lambda k: ('hT', k), t0, n, None) for (t0, n) in TT]
        aparts = [(self.aT, lambda k: ('aT', k), t0, n, None) for (t0, n) in TT]
        for q in range(4):
            for cg in range(8):
                def epi1(m, tag, t0, n, ps, pkey, cg=cg):
                    c = cg * 2 + m
                    P.op('act', lambda e, ps=ps, n=n: e.activation(self.tmpf[:, 0:n], ps, AF.Square), reads=[pkey], writes=['tmpf'])
                    P.op('dve', lambda e, ps=ps, c=c, t0=t0, n=n: e.scalar_tensor_tensor(
                        self.aT[:, c, t0:t0 + n], ps, 0.0, self.tmpf[:, 0:n], op0=ALU.is_gt, op1=ALU.mult),
                        reads=[pkey, 'tmpf'], writes=[('aT', c)])
                self.dense_fm(self.wview(W1, 0, 16, q * 2048 + cg * 256, 256), 16, 2, xparts, epi1)
            for cg in range(8):
                def epi2(m, tag, t0, n, ps, pkey, cg=cg):
                    c = cg * 2 + m
                    P.op('dve', lambda e, ps=ps, c=c, t0=t0, n=n: e.tensor_tensor(
                        self.yT[:, c, t0:t0 + n], ps, self.yT[:, c, t0:t0 + n], op=ALU.add),
                        reads=[pkey], writes=[('yT', c)])
                self.dense_fm(self.wview(W2, q * 2048, 16, cg * 256, 256), 16, 2, aparts, epi2)

    def mixer1(self):
        P = self.P
        W = self.D['w_in_o3']
        xparts = [(self.hT, lambda k: ('hT', k), t0, n, None) for (t0, n) in TT]
        for c in range(16):
            for m_ in range(3):
                def epi(m, tag, t0, n, ps, pkey, c=c, m_=m_):
                    if m_ == 0:
                        P.op('act', lambda e, ps=ps, t0=t0, n=n: e.activation(self.tb[:, t0:t0 + n], ps, AF.Copy), reads=[pkey], writes=['tb'])
                    elif m_ == 1:
                        P.op('act', lambda e, ps=ps, t0=t0, n=n: e.activation(self.tc[:, t0:t0 + n], ps, AF.Copy), reads=[pkey], writes=['tc'])
                    else:
                        P.op('dve', lambda e, ps=ps, t0=t0, n=n: e.tensor_tensor(self.tv[:, t0:t0 + n], ps, self.tc[:, t0:t0 + n], op=ALU.mult),
                             reads=[pkey, 'tc'], writes=['tv'])
                        if t0 + n == NT:
                            self.conv1(c)
                self.dense_fm(self.wview(W, 0, 16, c * 384 + m_ * 128, 128), 16, 1, xparts, epi)
        Wo = self.D['w_out_o']
        gparts = [(self.gT, lambda k: ('aT', k), t0, n, None) for (t0, n) in TT]
        for cg in range(8):
            def epi2(m, tag, t0, n, ps, pkey, cg=cg):
                c = cg * 2 + m
                P.op('dve', lambda e, ps=ps, c=c, t0=t0, n=n: e.tensor_tensor(
                    self.yT[:, c, t0:t0 + n], ps, self.yT[:, c, t0:t0 + n], op=ALU.add),
                    reads=[pkey], writes=[('yT', c)])
            self.dense_fm(self.wview(Wo, 0, 16, cg * 256, 256), 16, 2, gparts, epi2)

    def conv1(self, c):
        P = self.P
        w = self.scw
        P.op('dve', lambda e: e.tensor_scalar(self.tv[:, 0:32], self.tv[:, 0:32], self.flag[:, 0:1], None, op0=ALU.mult),
             reads=['tv', 'flag'], writes=['tv'])
        P.op('dve', lambda e: e.tensor_scalar(self.ty[:, 32:1056], self.tv[:, 30:1054], w[:, 0, c:c + 1], None, op0=ALU.mult),
             reads=['tv', 'scw'], writes=['ty'])
        P.op('dve', lambda e: e.scalar_tensor_tensor(self.ty[:, 32:1056], self.tv[:, 31:1055], w[:, 1, c:c + 1], self.ty[:, 32:1056], op0=ALU.mult, op1=ALU.add),
             reads=['tv', 'scw'], writes=['ty'])
        P.op('dve', lambda e: e.scalar_tensor_tensor(self.ty[:, 32:1056], self.tv[:, 32:1056], w[:, 2, c:c + 1], self.ty[:, 32:1056], op0=ALU.mult, op1=ALU.add),
             reads=['tv', 'scw'], writes=['ty'])
        P.op('dve', lambda e: e.tensor_copy(self.tvs[:, :, 0:2], self.stsc[:, c, :, :]), reads=['stsc'], writes=['tvs'])
        P.op('dve', lambda e: e.tensor_copy(self.tvs[:, :, 2:10], self.tv[:, 1056:1088].rearrange("p (s t) -> p s t", s=4)), reads=['tv'], writes=['tvs'])
        ys = self.ty[:, 1056:1088].rearrange("p (s t) -> p s t", s=4)
        P.op('dve', lambda e: e.tensor_scalar(ys, self.tvs[:, :, 0:8], w[:, 0, c:c + 1], None, op0=ALU.mult), reads=['tvs', 'scw'], writes=['ty'])
        P.op('dve', lambda e: e.scalar_tensor_tensor(ys, self.tvs[:, :, 1:9], w[:, 1, c:c + 1], ys, op0=ALU.mult, op1=ALU.add), reads=['tvs', 'scw'], writes=['ty'])
        P.op('dve', lambda e: e.scalar_tensor_tensor(ys, self.tvs[:, :, 2:10], w[:, 2, c:c + 1], ys, op0=ALU.mult, op1=ALU.add), reads=['tvs', 'scw'], writes=['ty'])
        P.op('pool', lambda e: e.memset(self.ty[:, 0:32], 0.0), writes=['ty'])
        P.op('dve', lambda e: e.tensor_tensor(self.gT[:, c, :], self.tb[:, :], self.ty[:, :], op=ALU.mult), reads=['tb', 'ty'], writes=[('aT', c)])
        P.op('act', lambda e: e.activation(self.scp[:, c, :], self.tv[:, 1054:1056], AF.Copy), reads=['tv'], writes=['scp'])
        P.op('act', lambda e: e.activation(self.scs[:, c, :, :], self.tvs[:, :, 8:10], AF.Copy), reads=['tvs'], writes=['scs'])

    def stage(self):
        i = self.stgi % 2
        self.stgi += 1
        return self.stg[i], ('stg', i)

    def headnorm(self, src, skey, n, gain, outs):
        P = self.P
        P.op('act', lambda e: e.activation(self.sqh[:, 0:n], src, AF.Square), reads=[skey], writes=['sqh'])
        b = self.psrot((4, 5))
        pst = self.ps[b]
        P.op('pe', lambda e: e.matmul(pst[:, 0:n], self.bones[:, :], self.sqh[:, 0:n], start=True, stop=True),
             reads=['sqh', 'bones'], writes=[('ps', b)])
        P.op('act', lambda e: e.activation(self.rstd2[:, 0:n], pst[:, 0:n], AF.Sqrt, bias=self.epst[:, 0:1], scale=1.0 / 64.0),
             reads=[('ps', b), 'epst'], writes=['rstd'])
        P.op('dve', lambda e: e.reciprocal(self.rstd2[:, 0:n], self.rstd2[:, 0:n]), reads=['rstd'], writes=['rstd'])
        for dst, dkeys in outs:
            P.op('dve', lambda e, dst=dst: e.scalar_tensor_tensor(dst, src, gain, self.rstd2[:, 0:n], op0=ALU.mult, op1=ALU.mult),
                 reads=[skey, 'rstd', 'cst'], writes=dkeys)

    def gelu_mul(self, ps, pkey, n, dst, dkey):
        P = self.P
        x = self.gx[:, 0:n]
        t = self.gt[:, 0:n]
        P.op('act', lambda e: e.activation(x, ps, AF.Copy), reads=[pkey], writes=[('stg', 0)])
        P.op('dve', lambda e: e.tensor_tensor(t, x, x, op=ALU.mult), reads=[('stg', 0)], writes=[('stg', 1)])
        P.op('dve', lambda e: e.tensor_scalar(t, t, 0.044715, 1.0, op0=ALU.mult, op1=ALU.add), reads=[('stg', 1)], writes=[('stg', 1)])
        P.op('dve', lambda e: e.tensor_tensor(t, t, x, op=ALU.mult), reads=[('stg', 1), ('stg', 0)], writes=[('stg', 1)])
        P.op('act', lambda e: e.activation(t, t, AF.Sigmoid, scale=1.5957691216057308), reads=[('stg', 1)], writes=[('stg', 1)])
        P.op('dve', lambda e: e.tensor_tensor(x, x, t, op=ALU.mult), reads=[('stg', 0), ('stg', 1)], writes=[('stg', 0)])
        P.op('dve', lambda e: e.tensor_tensor(dst, x, dst, op=ALU.mult), reads=[('stg', 0)], writes=[dkey])

    def rg_head(self, r, W, segs, mask_flag):
        P = self.P
        prm = self.rgp
        xx = self.rxb
        xc = self.rxc
        for sg in segs:
            c0, L = sg['c0'], sg['L']
            hb = self.rhist0 if sg['hslot'] == 0 else self.rhists[:, sg['hslot'] - 1, :]
            P.op('dve', lambda e, sg=sg, hb=hb: e.tensor_copy(hb[:, 0:3], sg['hist']), reads=sg['hist_keys'], writes=[('rhist', sg['hslot'])])
            P.op('act', lambda e, hb=hb, c0=c0, L=L: e.activation(hb[:, 3:3 + L], xx[:, c0:c0 + L], AF.Copy), reads=['rxb'], writes=[('rhist', sg['hslot'])])
            P.op('dve', lambda e, hb=hb, c0=c0, L=L: e.tensor_scalar(xc[:, c0:c0 + L], hb[:, 0:L], prm[:, r, 0:1], prm[:, r, 4:5], op0=ALU.mult, op1=ALU.add),
                 reads=[('rhist', sg['hslot']), 'cst'], writes=['rxc'])
            for j in range(1, 4):
                P.op('dve', lambda e, hb=hb, c0=c0, L=L, j=j: e.scalar_tensor_tensor(xc[:, c0:c0 + L], hb[:, j:j + L], prm[:, r, j:j + 1], xc[:, c0:c0 + L], op0=ALU.mult, op1=ALU.add),
                     reads=[('rhist', sg['hslot']), 'cst'], writes=['rxc'])
            if sg.get('hist_out') is not None:
                P.op('act', lambda e, sg=sg, hb=hb, L=L: e.activation(sg['hist_out'], hb[:, L:L + 3], AF.Copy), reads=[('rhist', sg['hslot'])], writes=sg['hist_out_keys'])
        lo = min(sg['c0'] for sg in segs)
        hi = max(sg['c0'] + sg['L'] for sg in segs)
        P.op('dve', lambda e: e.tensor_copy(self.rxcb[:, lo:hi], xc[:, lo:hi]), reads=['rxc'], writes=['rxcb'])
        ga = self.rga
        gi = self.rgi
        for (t0, n) in [(t, min(512, hi - t)) for t in range(lo, hi, 512)]:
            for which, wmat, bcol, dstt in ((0, self.rgwa, 5, ga), (1, self.rgwx, 6, gi)):
                b = self.psrot((4, 5))
                pst = self.ps[b]
                P.op('pe', lambda e, pst=pst, wmat=wmat, t0=t0, n=n: e.matmul(pst[:, 0:n], wmat[:, r, :], self.rxcb[:, t0:t0 + n], start=True, stop=True),
                     reads=['rxcb', 'cst'], writes=[('ps', b)])
                P.op('act', lambda e, pst=pst, dstt=dstt, bcol=bcol, t0=t0, n=n: e.activation(dstt[:, t0:t0 + n], pst[:, 0:n], AF.Sigmoid, bias=prm[:, r, bcol:bcol + 1], scale=1.0),
                     reads=[('ps', b), 'cst'], writes=['rga' if which == 0 else 'rgi'])
        P.op('act', lambda e: e.activation(ga[:, lo:hi], ga[:, lo:hi], AF.Exp, scale=self.clam[:, r:r + 1]), reads=['rga', 'clam'], writes=['rga'])
        tm = self.rtm
        P.op('dve', lambda e: e.tensor_tensor(tm[:, lo:hi], ga[:, lo:hi], ga[:, lo:hi], op=ALU.mult), reads=['rga'], writes=['rtm'])
        P.op('dve', lambda e: e.tensor_scalar(tm[:, lo:hi], tm[:, lo:hi], -1.0, 1.0, op0=ALU.mult, op1=ALU.add), reads=['rtm'], writes=['rtm'])
        P.op('dve', lambda e: e.tensor_scalar(tm[:, lo:hi], tm[:, lo:hi], 0.0, None, op0=ALU.max), reads=['rtm'], writes=['rtm'])
        P.op('act', lambda e: e.activation(tm[:, lo:hi], tm[:, lo:hi], AF.Sqrt), reads=['rtm'], writes=['rtm'])
        P.op('dve', lambda e: e.tensor_tensor(gi[:, lo:hi], gi[:, lo:hi], xc[:, lo:hi], op=ALU.mult), reads=['rgi', 'rxc'], writes=['rgi'])
        P.op('dve', lambda e: e.tensor_tensor(gi[:, lo:hi], gi[:, lo:hi], tm[:, lo:hi], op=ALU.mult), reads=['rgi', 'rtm'], writes=['rgi'])
        if mask_flag:
            P.op('dve', lambda e: e.tensor_scalar(gi[:, lo:hi], gi[:, lo:hi], self.flag[:, 0:1], None, op0=ALU.mult), reads=['rgi', 'flag'], writes=['rgi'])
        for sg in segs:
            c0, L = sg['c0'], sg['L']
            P.op('dve', lambda e, sg=sg, c0=c0, L=L: e.tensor_tensor_scan(tm[:, c0:c0 + L], ga[:, c0:c0 + L], gi[:, c0:c0 + L], sg['init'], ALU.mult, ALU.add),
                 reads=['rga', 'rgi'] + sg['init_keys'], writes=['rtm'])
            if sg.get('state_out') is not None:
                P.op('act', lambda e, sg=sg, c0=c0, L=L: e.activation(sg['state_out'], tm[:, c0 + L - 1:c0 + L], AF.Copy), reads=['rtm'], writes=sg['state_out_keys'])
            for (dst, dkeys, s0, sl) in sg.get('h_out', []):
                P.op('act', lambda e, dst=dst, c0=c0, s0=s0, sl=sl: e.activation(dst, tm[:, c0 + s0:c0 + s0 + sl], AF.Copy), reads=['rtm'], writes=dkeys)

    def l0_inproj(self):
        P = self.P
        W = self.D['w_in_p']
        D = self.D
        NPP = 4
        for pas in range(NPP + 1):
            ismain = (pas == NPP)
            if not ismain:
                p0 = pas * 256
                self.rmsnorm(None, None, self.gains[:, 0, :], lambda c, t0, n: self.hTp[:, c, 0:n], lambda c: ('hTp', c), [(0, 256)],
                             fetch=lambda c, t0, n, p0=p0: D['xT_pre'][:, c, p0:p0 + n])
                xt, xkf = self.hTp, (lambda k: ('hTp', k))
                fm_tiles = [(0, 256)]
                tm_tiles = [(0, 128, ('ctx', pas * 2)), (128, 128, ('ctx', pas * 2 + 1))]
                W_ = 256
            else:
                self.rmsnorm(None, None, self.gains[:, 0, :], lambda c, t0, n: self.hT[:, c, t0:t0 + n], lambda c: ('hT', c), TT,
                             fetch=lambda c, t0, n: D['xT_main'][:, c, t0:t0 + n])
                xt, xkf = self.hT, (lambda k: ('hT', k))
                fm_tiles = list(TT)
                tm_tiles = [(32 + 128 * j, 128, ('ctx', 8 + j)) for j in range(8)] + ([] if os.environ.get('K_NOSMP') else [(1056 + 8 * s, 8, ('smp', s)) for s in range(4)])
                W_ = NT
            fmparts = [(xt, xkf, t0, n, None) for (t0, n) in fm_tiles]
            kcol0 = 0 if ismain else 0
            kbase = 1024 if ismain else pas * 256

            for r in range(8):
                def epi_rx(m, tag, t0, n, ps, pkey, r=r, ismain=ismain, pas=pas, W_=W_):
                    P.op('act', lambda e: e.activation(self.rxb[:, t0:t0 + n], ps, AF.Copy), reads=[pkey], writes=['rxb'])
                    if t0 + n == W_:
                        if not ismain:
                            sg = dict(c0=0, L=256, hslot=0, hist=self.convh[:, r, :], hist_keys=[('convh', r)],
                                      hist_out=self.convh[:, r, :], hist_out_keys=[('convh', r)],
                                      init=self.hst[:, r:r + 1], init_keys=[('hst', r)],
                                      state_out=self.hst[:, r:r + 1], state_out_keys=[('hst', r)])
                            if pas == NPP - 1:
                                sg['h_out'] = [(self.mixT[:, r, 0:32], [('aT', r)], 224, 32)]
                            if os.environ.get('K_NORG') is None:
                                self.rg_head(r, 256, [sg], True)
                        else:
                            segs = [dict(c0=32, L=1024, hslot=0, hist=self.convh[:, r, :], hist_keys=[('convh', r)],
                                         hist_out=self.rgo_conv[:, r, 0, :], hist_out_keys=['rgo_conv'],
                                         init=self.hst[:, r:r + 1], init_keys=[('hst', r)],
                                         state_out=self.rgo_h[:, r, 0:1], state_out_keys=['rgo_h'],
                                         h_out=[(self.mixT[:, r, 32:1056], [('aT', r)], 0, 1024)])]
                            for s_ in range(4):
                                segs.append(dict(c0=1056 + 8 * s_, L=8, hslot=1 + s_, hist=self.stconv[:, r, s_, :], hist_keys=['cst'],
                                                 hist_out=self.rgo_conv[:, r, 1 + s_, :], hist_out_keys=['rgo_conv'],
                                                 init=self.sth[:, r, s_:s_ + 1], init_keys=['cst'],
                                                 state_out=self.rgo_h[:, r, 1 + s_:2 + s_], state_out_keys=['rgo_h'],
                                                 h_out=[(self.mixT[:, r, 1056 + 8 * s_:1064 + 8 * s_], [('aT', r)], 0, 8)]))
                            if os.environ.get('K_NORG') is None:
                                self.rg_head(r, NT, segs, False)
                self.dense_fm(self.wview(W, 0, 16, r * 128, 128), 16, 1, fmparts, epi_rx)

            for which, col0, KT, gcol, oname in ((0, 3072, self.KTs, 1, 'ksT_o'), (1, 3328, self.KTw, 2, 'kwT_o')):
                def epi_k(m, tag, t0, n, ps, pkey, KT=KT, gcol=gcol, oname=oname, ismain=ismain, kbase=kbase, which=which):
                    tq = self.tmpq[:, 0:n]
                    P.op('act', lambda e: e.activation(tq, ps, AF.Copy), reads=[pkey], writes=['tmpf'])
                    outs = [(KT[:, m, kbase + t0:kbase + t0 + n], [('KT', which, m)])]
                    if ismain:
                        stg, skey = self.stage()
                        outs.append((stg[:, 0:n], [skey]))
                    self.headnorm(tq, 'tmpf', n, self.kg[:, gcol:gcol + 1], outs)
                    if ismain:
                        P.dma('sp', D[oname][:, m, t0:t0 + n], stg[:, 0:n], reads=[skey], writes=[(oname, m, t0)])
                self.dense_fm(self.wview(W, 0, 16, col0, 256), 16, 2, fmparts, epi_k)

            tmparts = [(xt, xkf, t0, n, tag) for (t0, n, tag) in tm_tiles]
            for half_ in range(2):
                def epi_kcvc(tag, t0, n, ps, pkey, half_=half_, ismain=ismain):
                    kind, idx = tag
                    if kind == 'ctx':
                        P.op('act', lambda e: e.activation(self.kcpg[:, idx % 8, half_, :], ps, AF.Copy), reads=[pkey], writes=[('kcpg', idx % 8, half_)])
                    if ismain:
                        stg, skey = self.stage()
                        P.op('dve', lambda e: e.tensor_copy(stg[0:n, 0:256], ps), reads=[pkey], writes=[skey])
                        nm = ('kc_o', 'vc_o')[half_] if kind == 'ctx' else ('kc_s_o', 'vc_s_o')[half_]
                        r0 = (idx - 8) * 128 if kind == 'ctx' else idx * 8
                        if os.environ.get('K_NODMA') is None:
                            P.dma('sp', D[nm][r0:r0 + n, :], stg[0:n, 0:256], reads=[skey], writes=[(nm, r0)])
                self.dense_tm(self.wview(W, 0, 16, 3584 + 256 * half_, 256), 16, 256, tmparts, epi_kcvc)
            if pas >= NPP - 1:
                kk = [('kcpg', i_, hf_) for i_ in range(8) for hf_ in range(2)]
                self.job(None, lambda kk=kk, pg0=(8 if ismain else 0): self.compress1(self.kcpg, 8, kk, 'data', pg0=pg0))
            for half_ in range(2):
                def epi_v(tag, t0, n, ps, pkey, half_=half_, ismain=ismain):
                    kind, idx = tag
                    Vt = (self.Vs, self.Vw)[half_]
                    if kind == 'ctx':
                        dst = Vt[0:n, idx, :, 0:64]
                        dk = [('V', half_, idx)]
                    else:
                        dst = (self.Vs_s, self.Vw_s)[half_][0:n, idx, :, 0:64]
                        dk = [('Vsm', half_, idx)]
                    P.op('act', lambda e: e.activation(dst, ps.rearrange("p (g d) -> p g d", g=4), AF.Copy), reads=[pkey], writes=dk)
                    if ismain:
                        stg, skey = self.stage()
                        P.op('dve', lambda e: e.tensor_copy(stg[0:n, 0:256], ps), reads=[pkey], writes=[skey])
                        nm = ('vs_o', 'vw_o')[half_] if kind == 'ctx' else ('vs_s_o', 'vw_s_o')[half_]
                        r0 = (idx - 8) * 128 if kind == 'ctx' else idx * 8
                        P.dma('sp', D[nm][r0:r0 + n, :], stg[0:n, 0:256], reads=[skey], writes=[(nm, r0)])
                self.dense_tm(self.wview(W, 0, 16, 4096 + 256 * half_, 256), 16, 256, tmparts, epi_v)

            if ismain:
                for r in range(8):
                    def epi_rg(m, tag, t0, n, ps, pkey, r=r):
                        self.gelu_mul(ps, pkey, n, self.mixT[:, r, t0:t0 + n], ('aT', r))
                    self.dense_fm(self.wview(W, 0, 16, 1024 + r * 128, 128), 16, 1, fmparts, epi_rg)
                for i in range(8):
                    def epi_q(m, tag, t0, n, ps, pkey, i=i):
                        tq = self.tmpq[:, 0:n]
                        P.op('act', lambda e: e.activation(tq, ps, AF.Copy), reads=[pkey], writes=['tmpf'])
                        self.headnorm(tq, 'tmpf', n, self.kg[:, 3:4], [(self.qT[:, i, t0:t0 + n], [('qT', i), 'rga', 'rgi', 'rtm', 'rxc'])])
                    self.dense_fm(self.wview(W, 0, 16, 2048 + i * 128, 128), 16, 1, fmparts, epi_q)
                gl_tiles = [(0, 32, 0)] + [(32 + 128 * j, 128, 1 + j) for j in range(8)] + [(1056 + 8 * s, 8, 9 + s) for s in range(4)]
                glparts = [(xt, xkf, t0, n, gi_) for (t0, n, gi_) in gl_tiles]

                def epi_gl(tag, t0, n, ps, pkey):
                    P.op('act', lambda e: e.activation(self.gates[0:n, tag, :], ps, AF.Sigmoid), reads=[pkey], writes=[('gates', tag)])
                self.dense_tm(self.wview(W, 0, 16, 4608, 48), 16, 48, glparts, epi_gl)

    def compress1(self, X, npg, xkeys, kind, pg0=0, kvs=(0, 1), load_w=True):
        P = self.P
        cx = self.cx
        cw1b_, hx_, htt_, hid_ = cx['cw1b'], cx['hx'], cx['htt'], cx['hid']
        for kv in kvs:
            w1 = cw1b_
            if load_w:
                P.dma('pool', w1[:], self.D['cw1k' if kv == 0 else 'cw1v'], writes=['cw1'])
            for half in range(2):
                if kind == 'pe' and half == 1:
                    continue
                b = self.psrot((4, 5))
                pst = self.ps[b]
                lo, hi = half * 64, half * 64 + 64
                Xv = X[lo:hi, 0:npg, kv, :].rearrange("p n (g d) -> p n g d", g=4)
                for d in range(64):
                    P.op('pe', lambda e, pst=pst, w1=w1, Xv=Xv, d=d, lo=lo, hi=hi: e.matmul(
                        pst[0:64, 0:npg * 4], w1[lo:hi, d, :], Xv[:, :, :, d], start=(d == 0), stop=(d == 63)),
                        reads=list(xkeys) + ['cw1'], writes=[('ps', b)], inc=(d == 63), mode='k64m64')
                psv = pst[0:64, 0:npg * 4]
                if kind == 'pe':
                    P.op('act', lambda e, psv=psv, kv=kv: e.activation(self.cpe[:, kv, :], psv, AF.Copy), reads=[('ps', b)], writes=['cpe'])
                    continue
                hx = hx_[:, 0:npg * 4]
                hxv = hx.rearrange("p (n g) -> p n g", g=4)
                ht = htt_[:, 0:npg * 4]
                P.op('act', lambda e, psv=psv, hx=hx: e.activation(hx, psv, AF.Copy), reads=[('ps', b)], writes=['hx'])
                for g in range(4):
                    P.op('dve', lambda e, hxv=hxv, g=g, kv=kv: e.tensor_scalar(hxv[:, :, g], hxv[:, :, g], self.cpe[:, kv, g:g + 1], None, op0=ALU.add),
                         reads=['hx', 'cpe'], writes=['hx'])
                P.op('dve', lambda e, hx=hx, ht=ht: e.tensor_tensor(ht, hx, hx, op=ALU.mult), reads=['hx'], writes=['htt'])
                P.op('dve', lambda e, ht=ht: e.tensor_scalar(ht, ht, 0.044715, 1.0, op0=ALU.mult, op1=ALU.add), reads=['htt'], writes=['htt'])
                P.op('dve', lambda e, hx=hx, ht=ht: e.tensor_tensor(ht, ht, hx, op=ALU.mult), reads=['htt', 'hx'], writes=['htt'])
                P.op('act', lambda e, ht=ht: e.activation(ht, ht, AF.Sigmoid, scale=1.5957691216057308), reads=['htt'], writes=['htt'])
                dst = hid_[:, kv, 2 * pg0 + half:2 * (pg0 + npg):2, :]
                P.op('dve', lambda e, hx=hx, ht=ht, dst=dst: e.tensor_tensor(dst, hx.rearrange("p (n g) -> p n g", g=4), ht.rearrange("p (n g) -> p n g", g=4), op=ALU.mult),
                     reads=['hx', 'htt'], writes=['hidT'])

    def compress2(self, nblk, kcbT, vcb):
        P = self.P
        cx = self.cx
        cw2k_, cw2v_, vtf_, identf_, hid_ = cx['cw2k'], cx['cw2v'], cx['vtf'], cx['identf'], cx['hid']
        for gp in range(2):
            b = self.psrot((4, 5))
            pst = self.ps[b]
            for lh in range(2):
                P.op('pe', lambda e, pst=pst, lh=lh, gp=gp: e.matmul(pst[:, 0:nblk], cw2k_[:, lh, :], hid_[:, 0, 0:nblk, 2 * gp + lh],
                                                                     start=(lh == 0), stop=(lh == 1)),
                     reads=['hidT', 'cw2'], writes=[('ps', b)], inc=(lh == 1), mode='k64m128')
            tq = self.tmpq[:, 0:nblk]
            P.op('act', lambda e, pst=pst, tq=tq: e.activation(tq, pst[:, 0:nblk], AF.Copy), reads=[('ps', b)], writes=['tmpf'])
            self.headnorm(tq, 'tmpf', nblk, self.kg[:, 0:1], [(kcbT[:, gp, 0:nblk], ['kcbT'])])
        for g in range(4):
            b = self.psrot((4, 5))
            pst = self.ps[b]
            P.op('pe', lambda e, pst=pst, g=g: e.matmul(pst[0:64, 0:nblk], cw2v_[:, :], hid_[:, 1, 0:nblk, g], start=True, stop=True),
                 reads=['hidT', 'cw2'], writes=[('ps', b)], mode='k64m64')
            vt = vtf_[:, 0:nblk]
            P.op('act', lambda e, pst=pst, vt=vt: e.activation(vt, pst[0:64, 0:nblk], AF.Copy), reads=[('ps', b)], writes=['vtf'])
            b2 = self.psrot((4, 5))
            pst2 = self.ps[b2]
            P.op('pe', lambda e, pst2=pst2, vt=vt: e.transpose(pst2[0:nblk, 0:64], vt, identf_[0:64, 0:64]), reads=['vtf', 'identf'], writes=[('ps', b2)], mode='k64m64t')
            P.op('act', lambda e, pst2=pst2, g=g: e.activation(vcb[0:nblk, g, 0:64], pst2[0:nblk, 0:64], AF.Copy), reads=[('ps', b2)], writes=['vcb'])

    def load_qm(self, h, qc0, N):
        P = self.P
        g = h // 4
        hp = g % 2
        qi, _ = self.qchunk(h)
        lo, hi = hp * 64, hp * 64 + 64
        k = hp * 2 + (self.qmi[hp] % 2)
        self.qmi[hp] += 1
        qm = self.qm[k]
        P.op('act', lambda e: e.activation(qm[lo:hi, 0:N], self.qT[lo:hi, qi, qc0:qc0 + N], AF.Copy), reads=[('qT', qi)], writes=[('qm', k)])
        return qm, ('qm', k)

    def qchunk(self, h):
        if (h % 8) < 4:
            return (h % 4) + 4 * (h // 8), 0
        return ((h - 4) % 4) + 4 * ((h - 4) // 8), 1

    def softmax_unit(self, pst, b, nk, N, h, x0, branch):
        P = self.P
        i = self.pti % 2
        self.pti += 1
        pT = self.pT[i]
        pk = ('pT', i)
        if branch == 'win' and x0 >= 384:
            tt = self.ptmp[i]
            P.op('dve', lambda e: e.scalar_tensor_tensor(tt[0:nk, 0:N], pst[0:nk, 0:N], SCALE, self.mw[0:nk, x0 - 384:x0 - 384 + N], op0=ALU.mult, op1=ALU.add),
                 reads=[('ps', b), 'mw'], writes=[('ptmp', i)])
            P.op('act', lambda e: e.activation(pT[0:nk, 0:N], tt[0:nk, 0:N], AF.Exp, bias=self.chd[0:nk, h:h + 1], scale=1.0),
                 reads=[('ptmp', i), 'chd'], writes=[pk])
        elif x0 >= 256:
            P.op('act', lambda e: e.activation(pT[0:nk, 0:N], pst[0:nk, 0:N], AF.Exp, bias=self.chd[0:nk, h:h + 1], scale=SCALE),
                 reads=[('ps', b), 'chd'], writes=[pk])
        else:
            tt = self.ptmp[i]
            P.op('dve', lambda e: e.scalar_tensor_tensor(tt[0:nk, 0:N], pst[0:nk, 0:N], SCALE, self.G[0:nk, h, x0:x0 + N], op0=ALU.mult, op1=ALU.add),
                 reads=[('ps', b), 'G'], writes=[('ptmp', i)])
            P.op('act', lambda e: e.activation(pT[0:nk, 0:N], tt[0:nk, 0:N], AF.Exp), reads=[('ptmp', i)], writes=[pk])
        return pT, pk

    def branch_epilogue(self, po, pob, N, h, gate_ap, first):
        P = self.P
        rd = self.rden[0:N, 0:1]
        P.op('dve', lambda e: e.tensor_scalar(rd, po[0:N, 64:65], 1e-30, None, op0=ALU.max), reads=[('ps', pob)], writes=['rden'])
        P.op('dve', lambda e: e.reciprocal(rd, rd), reads=['rden'], writes=['rden'])
        sg = self.rden[0:N, 1:2]
        P.op('dve', lambda e: e.tensor_tensor(sg, rd, gate_ap, op=ALU.mult), reads=['rden', 'gates'], writes=['rden2'])
        om = self.omix[0:N, h, :]
        if first:
            P.op('dve', lambda e: e.tensor_scalar(om, po[0:N, 0:64], sg, None, op0=ALU.mult), reads=[('ps', pob), 'rden2'], writes=[('omix', h)])
        else:
            P.op('dve', lambda e: e.scalar_tensor_tensor(om, po[0:N, 0:64], sg, om, op0=ALU.mult, op1=ALU.add), reads=[('ps', pob), 'rden2'], writes=[('omix', h)])
        return rd

    def attn_prompt_tile(self, ti, qc0, N, qpos0):
        P = self.P
        D = self.D
        P.dma('sp', self.selmul[0:N, :], D['selmul'][qc0:qc0 + N, :], writes=['selmul'])
        P.dma('sp', self.seladd[0:N, :], D['seladd'][qc0:qc0 + N, :], writes=['seladd'])
        for hq in range(4):
            for hh in range(4):
                h = hq * 4 + hh
                src = bass.AP(self.vd.tensor, h * DL + DOFF + qpos0 - 63 - 64 * 31, [[64, 32], [1, N]])
                P.dma('sp', self.bcrev[0:32, hh, 0:N], src, reads=['vd'], writes=[('bcrev', hh)])
            b = self.psrot((4, 5))
            pst = self.ps[b]
            for hh in range(4):
                P.op('pe', lambda e, pst=pst, hh=hh: e.matmul(pst[:, hh * 128:hh * 128 + N], self.j32p[:, :], self.bcrev[:, hh, 0:N], start=True, stop=True),
                     reads=[('bcrev', hh), 'jmat'], writes=[('ps', b)], inc=True)
            for hh in range(4):
                P.op('act', lambda e, pst=pst, hh=hh: e.activation(self.bc[:, hh, 0:N], pst[0:32, hh * 128:hh * 128 + N], AF.Copy),
                     reads=[('ps', b)], writes=['bc'])
            for hh in range(4):
                h = hq * 4 + hh
                g = h // 4
                gp, hp = g // 2, g % 2
                qi, qh = self.qchunk(h)
                assert qh == hp
                lo, hi = hp * 64, hp * 64 + 64
                b = self.psrot((0, 1, 2, 3))
                pst = self.ps[b]
                qm, qmk = self.load_qm(h, qc0, N)
                P.op('pe', lambda e, pst=pst, gp=gp, qm=qm: e.matmul(pst[:, 0:N], self.kcbT[:, gp, 0:128], qm[:, 0:N], start=True, stop=True),
                     reads=['kcbT', qmk], writes=[('ps', b)])
                i = self.pti % 2
                self.pti += 1
                tt, pT = self.ptmp[i], self.pT[i]
                P.op('dve', lambda e, pst=pst, tt=tt, hh=hh: e.scalar_tensor_tensor(tt[0:32, 0:N], pst[0:32, 0:N], SCALE, self.bc[:, hh, 0:N], op0=ALU.mult, op1=ALU.add),
                     reads=[('ps', b), 'bc'], writes=[('ptmp', i)])
                P.op('act', lambda e, tt=tt, pT=pT: e.activation(pT[0:32, 0:N], tt[0:32, 0:N], AF.Exp), reads=[('ptmp', i)], writes=[('pT', i)])
                pob = 6 + (h % 2)
                po = self.ps[pob]
                P.op('pe', lambda e, po=po, pT=pT, g=g: e.matmul(po[:, 0:98], pT[:, 0:128], self.vcb[:, g, 0:98], start=True, stop=True),
                     reads=[('pT', i), 'vcb'], writes=[('ps', pob)])
                rd = self.branch_epilogue(po, pob, N, h, self.gates[0:N, ti, h:h + 1], True)
                im = self.imp[0:N, g, :]
                if h % 4 == 0:
                    P.op('dve', lambda e, po=po, rd=rd, im=im: e.tensor_scalar(im, po[0:N, 66:98], rd, None, op0=ALU.mult), reads=[('ps', pob), 'rden'], writes=[('imp', g)])
                else:
                    P.op('dve', lambda e, po=po, rd=rd, im=im: e.scalar_tensor_tensor(im, po[0:N, 66:98], rd, im, op0=ALU.mult, op1=ALU.add), reads=[('ps', pob), 'rden'], writes=[('imp', g)])
        for g in range(4):
            im = self.imp[0:N, g, :]
            P.op('dve', lambda e, im=im: e.tensor_tensor(im, im, self.selmul[0:N, :], op=ALU.mult), reads=['selmul'], writes=[('imp', g)])
            P.op('dve', lambda e, im=im: e.tensor_tensor(im, im, self.seladd[0:N, :], op=ALU.add), reads=['seladd'], writes=[('imp', g)])
            P.op('dve', lambda e, im=im: e.max(self.mx8[0:N, :], im), reads=[('imp', g)], writes=['mx8'])
            P.op('dve', lambda e, im=im: e.match_replace(self.impw[0:N, :], self.mx8[0:N, :], im, -3.0e38), reads=[('imp', g), 'mx8'], writes=['impw'])
            P.op('dve', lambda e: e.max(self.mx8[0:N, :], self.impw[0:N, :]), reads=['impw'], writes=['mx8'])
            P.op('dve', lambda e, im=im: e.tensor_scalar(self.impw[0:N, :], im, self.mx8[0:N, 7:8], None, op0=ALU.is_ge), reads=[('imp', g), 'mx8'], writes=['impw'])
            P.op('dve', lambda e, g=g: e.tensor_scalar(self.negM[0:N, g, :], self.impw[0:N, :], -1.0, -MASKNEG, op0=ALU.add, op1=ALU.mult), reads=['impw'], writes=[('negM', g)])
        for br, KT, Vt, gofs in (('sel', self.KTs, self.Vs, 16), ('win', self.KTw, self.Vw, 32)):
            for h in range(16):
                g = h // 4
                gp, hp = g // 2, g % 2
                qi, _ = self.qchunk(h)
                lo, hi = hp * 64, hp * 64 + 64
                ktmax = (qpos0 + N - 1) // 128
                kts = []
                for kt in range(ktmax + 1):
                    x0 = qpos0 - 128 * kt
                    if br == 'win' and x0 >= 640:
                        continue
                    kts.append((kt, x0))
                pob = 6 + (h % 2)
                po = self.ps[pob]
                qm, qmk = self.load_qm(h, qc0, N)
                for idx, (kt, x0) in enumerate(kts):
                    kcol0 = 128 * kt if kt < 8 else 1024 + 32 + 128 * (kt - 8)
                    b = self.psrot((0, 1, 2, 3))
                    pst = self.ps[b]
                    P.op('pe', lambda e, pst=pst, gp=gp, kcol0=kcol0, KT=KT, br=br, qm=qm: e.matmul(
                        pst[:, 0:N], KT[:, gp, kcol0:kcol0 + 128], qm[:, 0:N], start=True, stop=(br != 'sel')),
                        reads=[('KT', 0 if br == 'sel' else 1, gp), qmk], writes=[('ps', b)], inc=(br != 'sel'))
                    if br == 'sel':
                        if idx == 0 and h % 4 == 0:
                            P.op('dve', lambda e, g=g: e.tensor_copy(self.mexp[0:N, :, :], self.negM[0:N, g, :].unsqueeze(2).to_broadcast([N, 32, 64])),
                                 reads=[('negM', g)], writes=['mexp'])
                        mk = self.mexp[:, 2 * kt:2 * kt + 2, :].rearrange("p a d -> p (a d)")
                        P.op('pe', lambda e, pst=pst, mk=mk: e.matmul(pst[:, 0:N], mk, self.identb[:, 0:N], start=False, stop=True),
                             reads=['mexp', 'identb'], writes=[('ps', b)])
                    pT, pk = self.softmax_unit(pst, b, 128, N, h, x0, br)
                    P.op('pe', lambda e, po=po, pT=pT, kt=kt, g=g, Vt=Vt, idx=idx, nk=len(kts): e.matmul(
                        po[:, 0:66], pT[:, 0:128], Vt[:, kt, g, 0:66], start=(idx == 0), stop=(idx == nk - 1)),
                        reads=[pk, ('V', 0 if br == 'sel' else 1, kt)], writes=[('ps', pob)], inc=(idx == len(kts) - 1))
                self.branch_epilogue(po, pob, N, h, self.gates[0:N, ti, gofs + h:gofs + h + 1], False)
        for c in range(8):
            b = self.psrot((4, 5))
            pst = self.ps[b]
            src = self.omix[:, 2 * c:2 * c + 2, :].rearrange("p a d -> p (a d)")
            P.op('pe', lambda e, pst=pst, src=src: e.transpose(pst[:, 0:128], src, self.identf[:, :]), reads=[('omix', 2 * c), ('omix', 2 * c + 1), 'identf'], writes=[('ps', b)])
            P.op('act', lambda e, pst=pst, c=c: e.activation(self.mixT[:, 8 + c, qc0:qc0 + N], pst[:, 0:N], AF.Copy), reads=[('ps', b)], writes=[('aT', 8 + c)])

    def attn_setup(self, A):
        P, D = self.P, self.D
        self.identf = A("identf", [128, 128], F32)
        self.identb = A("identb", [128, 128], BF16)
        self.jmat = A("jmat", [128, 128], F32)
        self.G = A("G", [128, 16, 256], F32)
        self.chd = A("chd", [128, 16], F32)
        self.mw = A("mw", [128, 256], F32)
        self.tab = A("tab", [33, 16], F32)
        self.e1 = [A("e1_%d" % i, [33, 512], F32) for i in range(1)]
        self.vds = [A("vds%d" % i, [16, 512], F32) for i in range(1)]
        self.grev = [A("grev%d" % i, [128, 256], F32) for i in range(1)]
        self.bcrev = A("bcrev", [128, 4, 128], F32)
        self.j32p = A("j32p", [128, 128], F32)
        self.qm = [A("qm%d" % i, [128, 128], BF16) for i in range(4)]
        self.qmi = [0, 0]
        self.bc = A("bc", [32, 4, 128], F32)
        self.mexp = A("mexp", [128, 32, 64], BF16)
        self.selmul = A("selmul", [128, 32], F32)
        self.seladd = A("seladd", [128, 32], F32)
        self.pT = [A("pT%d" % i, [128, 128], BF16) for i in range(2)]
        self.ptmp = [A("ptmp%d" % i, [128, 128], F32) for i in range(2)]
        self.pti = 0
        self.rden = A("rden", [128, 2], F32)
        self.omix = A("omix", [128, 16, 64], F32)
        self.imp = A("imp", [128, 4, 32], F32)
        self.impw = A("impw", [128, 32], F32)
        self.mx8 = A("mx8", [128, 8], F32)
        self.negM = A("negM", [128, 4, 32], BF16)
        P.dma('sp', self.identf[:], D['ident'], writes=['identf'])
        P.dma('pool', self.identb[:], D['ident'], writes=['identb'])
        P.dma('sp', self.jmat[:], D['jmat'], writes=['jmat'])
        P.dma('sp', self.j32p[:], D['j32p'], writes=['jmat'])
        P.op('pool', lambda e: e.memset(self.bcrev[:], 0.0), writes=[('bcrev', i_) for i_ in range(4)])
        for i_ in range(4):
            P.op('pool', lambda e, i_=i_: e.memset(self.qm[i_][:], 0.0), writes=[('qm', i_)])
        for i_ in range(2):
            P.op('pool', lambda e, i_=i_: e.memset(self.pT[i_][:], 0.0), writes=[('pT', i_)])
        P.op('pool', lambda e: e.memset(self.mexp[:], 0.0), writes=['mexp'])
        P.op('pool', lambda e: e.memset(self.omix[:], 0.0), writes=[('omix', i_) for i_ in range(16)])
        P.dma('sp', self.mw[:], D['mw'], writes=['mw'])
        P.dma('sp', self.tab[:], D['tabaug'], writes=['tab'])
        for ch in range(DL // 512):
            i = 0
            P.dma('sp', self.e1[i][:], D['e1h'][:, ch * 512:(ch + 1) * 512], writes=[('e1', i)])
            b = self.psrot((4, 5))
            pst = self.ps[b]
            P.op('pe', lambda e, pst=pst, i=i: e.matmul(pst[0:16, 0:512], self.tab[:, :], self.e1[i][:, :], start=True, stop=True),
                 reads=['tab', ('e1', i)], writes=[('ps', b)], mode='k33m16')
            P.op('act', lambda e, pst=pst, i=i: e.activation(self.vds[i][:, :], pst[0:16, 0:512], AF.Copy), reads=[('ps', b)], writes=[('vds', i)])
            P.dma('sp', self.vd[:, ch * 512:(ch + 1) * 512], self.vds[i][:, :], reads=[('vds', i)], writes=['vd'])
        for h in range(16):
            i = 0
            src = bass.AP(self.vd.tensor, h * DL + DOFF - 127, [[1, 128], [1, 256]])
            P.dma('sp', self.grev[i][:, :], src, reads=['vd'], writes=[('grev', i)])
            b = self.psrot((4, 5))
            pst = self.ps[b]
            P.op('pe', lambda e, pst=pst, i=i: e.matmul(pst[:, 0:256], self.jmat[:, :], self.grev[i][:, :], start=True, stop=True),
                 reads=['jmat', ('grev', i)], writes=[('ps', b)])
            P.op('act', lambda e, pst=pst, h=h: e.activation(self.G[:, h, :], pst[:, 0:256], AF.Copy), reads=[('ps', b)], writes=['G'])
        P.op('dve', lambda e: e.tensor_copy(self.chd[:, :], self.G[:, :, 255]), reads=['G'], writes=['chd'])

    def save_sample_bias(self):
        P = self.P
        t0 = self.ptmp[0][:, :].rearrange("p (h t) -> p h t", h=16)
        t1 = self.ptmp[1][:, :].rearrange("p (h t) -> p h t", h=16)
        sbd = self.sbd

        def put(k):
            P.dma('sp', sbd[k], self.ptmp[0][:, :], reads=[('ptmp', 0)], writes=[('sbd', k)])
        P.op('dve', lambda e: e.tensor_copy(t0, self.G[:, :, 128:136]), reads=['G'], writes=[('ptmp', 0)])
        put(0)
        P.op('pool', lambda e: e.memset(self.ptmp[0][:, :], MASKNEG), writes=[('ptmp', 0)])
        P.op('dve', lambda e: e.tensor_copy(t0[0:8], self.G[0:8, :, 0:8]), reads=['G'], writes=[('ptmp', 0)])
        put(1)
        for k, c0 in ((2, 128), (3, 0)):
            P.op('dve', lambda e, c0=c0: e.tensor_copy(t0, self.mw[:, c0:c0 + 8].unsqueeze(1).to_broadcast([128, 16, 8])), reads=['mw'], writes=[('ptmp', 0)])
            P.op('dve', lambda e: e.tensor_tensor(t0, t0, self.chd[:, :].unsqueeze(2).to_broadcast([128, 16, 8]), op=ALU.add), reads=['chd'], writes=[('ptmp', 0)])
            put(k)
        P.op('dve', lambda e: e.tensor_copy(t0, self.chd[:, :].unsqueeze(2).to_broadcast([128, 16, 8])), reads=['chd'], writes=[('ptmp', 0)])
        put(4)
        src = bass.AP(self.vd.tensor, DOFF + 8192 - 63 - 64 * 127, [[64, 128], [DL, 16], [1, 8]])
        P.dma('sp', t1, src, reads=['vd'], writes=[('ptmp', 1)])
        b = self.psrot((4, 5))
        pst = self.ps[b]
        P.op('pe', lambda e: e.matmul(pst[:, 0:128], self.jmat[:, :], self.ptmp[1][:, :], start=True, stop=True), reads=['jmat', ('ptmp', 1)], writes=[('ps', b)])
        P.op('act', lambda e: e.activation(self.ptmp[0][:, :], pst[:, 0:128], AF.Copy), reads=[('ps', b)], writes=[('ptmp', 0)])
        put(5)

    def attn_sample(self, S, n_pool):
        P, D = self.P, self.D
        NPGS = 4
        identf2 = S("identf2", [128, 128], F32)
        identb2 = S("identb2", [128, 128], BF16)
        sbias = S("sbias", [128, 6, 128], F32)
        rep = S("rep", [128, 128], F32)
        nsel = S("nsel", [128, 4], F32)
        seln = S("seln", [128, 4, 8], BF16)
        selsum = S("selsum", [128, 128], BF16)
        t0f = S("t0f", [128, 128], F32)
        selas = S("selas", [128, 128], F32)
        cw1b2 = S("cw1b2", [128, 64, 64], BF16)
        cw2k2 = S("cw2k2", [64, 2, 128], BF16)
        cw2v2 = S("cw2v2", [64, 64], BF16)
        hx2 = S("hx2", [64, 32], F32)
        htt2 = S("htt2", [64, 32], F32)
        vtf2 = S("vtf2", [64, 128], F32)
        Xs = S("Xs", [128, NPGS, 2, 256], BF16)
        hidS = S("hidS", [64, 2, 128, 4], BF16)
        self.cx = dict(cw1b=cw1b2, hx=hx2, htt=htt2, hid=hidS, cw2k=cw2k2, cw2v=cw2v2, vtf=vtf2, identf=identf2)
        vcbS = S("vcbS", [128, 4, 196], BF16)
        KTp = [S("KTp%d" % i, [128, 4, 128], BF16) for i in range(2)]
        Vp = [S("Vp%d" % i, [128, 4, 66], BF16) for i in range(2)]
        Vraw = [S("Vraw%d" % i, [128, 256], BF16) for i in range(2)]
        wkT = S("wkT", [128, 4, 512], BF16)
        wv = S("wv", [128, 4, 4, 66], BF16)
        indpg = [S("indpg%d" % i, [128, 128], BF16) for i in range(2)]
        negMs = S("negMs", [128, 4, 128], F32)
        negMT = S("negMT", [128, 4, 4, 8], BF16)
        qms = [S("qms%d" % i, [128, 32], BF16) for i in range(4)]
        ktn = [S("ktn%d" % i, [128, 128], BF16) for i in range(2)]
        pTs = [S("pTs%d" % i, [128, 128], BF16) for i in range(2)]
        ptm = [S("ptm%d" % i, [128, 32], F32) for i in range(2)]
        osum = S("osum", [128, 4, 64], F32)
        opad = S("opad", [128, 2, 128], BF16)
        grow = S("grow", [128, 12], F32)
        gtmp = S("gtmp", [128, 3, 4, 4], F32)
        gts = S("gts", [128, 48], F32)
        rdn = S("rdn", [128, 2], F32)
        imps = S("imps", [128, 128], F32)
        impw = S("impw2", [128, 128], F32)
        mx8 = S("mx8b", [128, 8], F32)
        pnb = S("pnb", [128, 128], BF16)
        ptb = S("ptb", [128, 64], I32)
        ptf = S("ptf", [128, 64], F32)
        iop = S("iop", [128, 2], F32)
        idxn = S("idxn", [128, 64], I32)
        idxt = S("idxt", [128, 64], I32)

        P.dma('sp', identf2[:], D['ident'], writes=['identf'])
        P.dma('pool', identb2[:], D['ident'], writes=['identb'])
        P.dma('sp', sbias[:], self.sbd.rearrange("k p c -> p k c"), reads=[('sbd', k) for k in range(6)], writes=['sbias'])
        P.dma('sp', rep[:], D['rep'], writes=['rep'])
        P.dma('sp', nsel[:], D['nsel'], writes=['nsel'])
        P.dma('pool', seln[:], D['seln'], writes=['seln'])
        P.dma('pool', selsum[:], D['selsum'], writes=['selsum'])
        P.dma('sp', t0f[:], D['t0f'], writes=['t0f'])
        P.op('pool', lambda e: e.memset(selas[:], 0.0), writes=['selas'])
        P.dma('sp', selas[0:8, :], D['seladd_s'], writes=['selas'])
        P.dma('pool', cw2k2[:], D['cw2k'], writes=['cw2'])
        P.dma('pool', cw2v2[:], D['cw2v'], writes=['cw2'])
        for t_ in (Vp[0], Vp[1], wv):
            P.op('pool', lambda e, t_=t_: e.memset(t_[:], 1.0), writes=['vinit'])
        for i_, t_ in enumerate(qms):
            P.op('pool', lambda e, t_=t_: e.memset(t_[:], 0.0), writes=[('qms', i_)])
        for i_, t_ in enumerate(ktn):
            P.op('pool', lambda e, t_=t_: e.memset(t_[:], 0.0), writes=[('ktn', i_)])
        for i_, t_ in enumerate(pTs):
            P.op('pool', lambda e, t_=t_: e.memset(t_[:], 0.0), writes=[('pTs', i_)])
        P.op('pool', lambda e: e.memset(opad[:], 0.0), writes=['opad'])
        P.op('pool', lambda e: e.memset(pnb[:], 0.0), writes=['pnb'])
        P.op('pool', lambda e: e.memset(negMs[:], 0.0), writes=['negMs'])
        P.op('pool', lambda e: e.memset(osum[:], 0.0), writes=[('osum', g_) for g_ in range(4)])
        P.op('pool', lambda e: e.memset(gts[:], 0.0), writes=['gts'])
        P.op('pool', lambda e: e.memset(imps[:], 0.0), writes=['imps'])
        P.op('pool', lambda e: e.memset(vcbS[:], 0.0), writes=['vcbS'])
        P.op('pool', lambda e: e.memset(vcbS[:, :, 64:65], 1.0), writes=['vcbS'])
        for g in range(4):
            P.op('dve', lambda e, g=g: e.tensor_copy(vcbS[:, g, 66:194], identb2[:, :]), reads=['identb'], writes=['vcbS'])
        P.dma('sp', iop[:], D['iopc'], writes=['iop'])
        pck = D['pck']
        pcv = D['pcv']
        pskT = D['pskT']
        psv = D['psv'].rearrange("n (g d) -> n g d", g=4)
        sb_ = lambda k, g: sbias[:, k, :].rearrange("p (h t) -> p h t", h=16)[:, 4 * g:4 * g + 4, :].rearrange("p h t -> p (h t)")
        pti = [0]
        qmi = [0]

        def unit(pst, b, g, bk, lhsV, vkeys, po, pob, first, last, ncols=66):
            i = pti[0] % 2
            pti[0] += 1
            P.op('dve', lambda e: e.scalar_tensor_tensor(ptm[i][:, :], pst[:, 0:32], SCALE, sb_(bk, g), op0=ALU.mult, op1=ALU.add),
                 reads=[('ps', b), 'sbias'], writes=[('ptm', i)])
            P.op('act', lambda e: e.activation(pTs[i][:, 0:32], ptm[i][:, :], AF.Exp), reads=[('ptm', i)], writes=[('pTs', i)])
            P.op('pe', lambda e: e.matmul(po[:, 0:ncols], pTs[i][:, 0:128], lhsV, start=first, stop=last),
                 reads=[('pTs', i)] + vkeys, writes=[('ps', pob)], inc=last)

        def epilogue(po, pob, g, br, first):
            rd = rdn[0:32, 0:1]
            sg = rdn[0:32, 1:2]
            P.op('dve', lambda e: e.tensor_scalar(rd, po[0:32, 64:65], 1e-30, None, op0=ALU.max), reads=[('ps', pob)], writes=['rdn'])
            P.op('dve', lambda e: e.reciprocal(rd, rd), reads=['rdn'], writes=['rdn'])
            P.op('dve', lambda e: e.tensor_tensor(sg, rd, grow[0:32, br * 4 + g:br * 4 + g + 1], op=ALU.mult), reads=['rdn', 'grow'], writes=['rdn2'])
            om = osum[0:32, g, :]
            if first:
                P.op('dve', lambda e: e.tensor_scalar(om, po[0:32, 0:64], sg, None, op0=ALU.mult), reads=[('ps', pob), 'rdn2'], writes=[('osum', g)])
            else:
                P.op('dve', lambda e: e.scalar_tensor_tensor(om, po[0:32, 0:64], sg, om, op0=ALU.mult, op1=ALU.add), reads=[('ps', pob), 'rdn2'], writes=[('osum', g)])
            return rd

        def load_q(s, g):
            hp = g % 2
            lo, hi = hp * 64, hp * 64 + 64
            k = hp * 2 + g // 2
            c0 = 4 * (g // 2)
            P.op('act', lambda e: e.activation(qms[k][lo:hi, :].rearrange("p (n t) -> p n t", n=4), self.qT[lo:hi, c0:c0 + 4, 1056 + 8 * s:1064 + 8 * s], AF.Copy),
                 reads=[('qT', c0 + n_) for n_ in range(4)], writes=[('qms', k)])
            return qms[k], ('qms', k)

        for s in range(4):
            P.dma('sp', ptb[:], D['pt'][:, 64 * s:64 * s + 64].partition_broadcast(128), reads=['idxn', 'idxt'], writes=['ptb'])
            P.op('dve', lambda e: e.tensor_copy(ptf[:], ptb[:]), reads=['ptb'], writes=['ptf'])
            P.op('dve', lambda e: e.tensor_scalar(ptf[:], ptf[:], 128.0, iop[:, 0:1], op0=ALU.mult, op1=ALU.add), reads=['ptf', 'iop'], writes=['ptf'])
            P.op('dve', lambda e: e.tensor_copy(idxn[:], ptf[:]), reads=['ptf'], writes=['idxn'])
            P.op('dve', lambda e: e.tensor_copy(ptf[:], ptb[:]), reads=['ptb', 'idxn'], writes=['ptf'])
            P.op('dve', lambda e: e.tensor_scalar(ptf[:], ptf[:], 64.0, iop[:, 1:2], op0=ALU.mult, op1=ALU.add), reads=['ptf', 'iop'], writes=['ptf'])
            P.op('dve', lambda e: e.tensor_copy(idxt[:], ptf[:]), reads=['ptf'], writes=['idxt'])
            P.op('act', lambda e, s=s: e.activation(gts[0:8, :], self.gates[0:8, 9 + s, :], AF.Copy), reads=[('gates', 9 + s)], writes=['gts'])
            b = self.psrot((0, 1, 2, 3))
            pst = self.ps[b]
            P.op('pe', lambda e, pst=pst: e.matmul(pst[:, 0:48], rep[:, :], gts[:, :], start=True, stop=True), reads=['rep', 'gts'], writes=[('ps', b)])
            P.op('dve', lambda e, pst=pst: e.tensor_tensor(gtmp[:, :, :, :], pst[:, 0:48].rearrange("p (b g n) -> p b g n", b=3, g=4),
                                                           nsel[:, :].unsqueeze(1).unsqueeze(1).to_broadcast([128, 3, 4, 4]), op=ALU.mult),
                 reads=[('ps', b), 'nsel'], writes=['gtmp'])
            P.op('dve', lambda e: e.tensor_reduce(grow[:, :], gtmp[:, :, :, :].rearrange("p b g n -> p (b g) n"), mybir.AxisListType.X, ALU.add),
                 reads=['gtmp'], writes=['grow'])
            for kv, src in ((0, pck), (1, pcv)):
                P.dma('pool', cw1b2[:], D['cw1k' if kv == 0 else 'cw1v'], writes=['cw1'])
                for st_ in range(64 // NPGS):
                    for j in range(NPGS):
                        col = st_ * NPGS + j
                        P.custom_dma('pool', lambda e, j=j, kv=kv, src=src, col=col: e.indirect_dma_start(
                            out=Xs[:, j, kv, :], out_offset=None, in_=src, in_offset=bass.IndirectOffsetOnAxis(ap=idxn[:, col:col + 1], axis=0)),
                            reads=['idxn'], writes=[('Xs', j, kv)])
                    self.compress1(Xs, NPGS, [('Xs', j, kv) for j in range(NPGS)], 'data', pg0=st_ * NPGS, kvs=(kv,), load_w=False)
            self.compress2(128, self.kcbT, vcbS)
            for g in range(4):
                gp = g // 2
                qm, qmk = load_q(s, g)
                b = self.psrot((0, 1, 2, 3))
                pst = self.ps[b]
                P.op('pe', lambda e, pst=pst, gp=gp, qm=qm: e.matmul(pst[:, 0:32], self.kcbT[:, gp, 0:128], qm[:, 0:32], start=True, stop=True),
                     reads=['kcbT', qmk], writes=[('ps', b)])
                pob = 4 + g
                po = self.ps[pob]
                unit(pst, b, g, 5, vcbS[:, g, 0:194], ['vcbS'], po, pob, True, True, ncols=194)
                rd = epilogue(po, pob, g, 0, True)
                P.op('dve', lambda e, po=po, rd=rd: e.tensor_scalar(pnb[0:32, :], po[0:32, 66:194], rd, None, op0=ALU.mult), reads=[('ps', pob), 'rdn'], writes=['pnb'])
                b2 = self.psrot((0, 1, 2, 3))
                ps2 = self.ps[b2]
                P.op('pe', lambda e, ps2=ps2: e.matmul(ps2[:, 0:128], selsum[:, :], pnb[:, :], start=True, stop=True), reads=['selsum', 'pnb'], writes=[('ps', b2)])
                P.op('dve', lambda e, ps2=ps2: e.tensor_tensor(imps[0:8, :], ps2[0:8, 0:128], selas[0:8, :], op=ALU.add), reads=[('ps', b2), 'selas'], writes=['imps'])
                P.op('dve', lambda e: e.max(mx8[0:8, :], imps[0:8, :]), reads=['imps'], writes=['mx8'])
                P.op('dve', lambda e: e.match_replace(impw[0:8, :], mx8[0:8, :], imps[0:8, :], -3.0e38), reads=['imps', 'mx8'], writes=['impw'])
                P.op('dve', lambda e: e.max(mx8[0:8, :], impw[0:8, :]), reads=['impw'], writes=['mx8'])
                P.op('dve', lambda e: e.tensor_scalar(impw[0:8, :], imps[0:8, :], mx8[0:8, 6:7], None, op0=ALU.is_ge), reads=['imps', 'mx8'], writes=['impw'])
                P.op('dve', lambda e, g=g: e.tensor_scalar(negMs[0:8, g, :], impw[0:8, :], -1.0, -MASKNEG, op0=ALU.add, op1=ALU.mult), reads=['impw'], writes=['negMs'])
                b3 = self.psrot((0, 1, 2, 3))
                ps3 = self.ps[b3]
                P.op('pe', lambda e, ps3=ps3, g=g: e.transpose(ps3[:, 0:128], negMs[:, g, :], identf2[:, :]), reads=['negMs', 'identf'], writes=[('ps', b3)])
                P.op('act', lambda e, ps3=ps3: e.activation(ptm[0][:, 0:8], ps3[:, 0:8], AF.Copy), reads=[('ps', b3)], writes=[('ptm', 0)])
                P.op('dve', lambda e, g=g: e.tensor_copy(negMT[:, g, :, :], ptm[0][:, 0:8].unsqueeze(1).to_broadcast([128, 4, 8])), reads=[('ptm', 0)], writes=[('negMT', g)])
            qq = [load_q(s, g) for g in range(4)]
            for i_ in range(64):
                col = i_
                kb = i_ % 2
                P.custom_dma('pool', lambda e, kb=kb, col=col: e.indirect_dma_start(
                    out=KTp[kb][:, :, :].rearrange("p g k -> p (g k)"), out_offset=None, in_=pskT, in_offset=bass.IndirectOffsetOnAxis(ap=idxt[:, col:col + 1], axis=0)),
                    reads=['idxt'], writes=[('KTp', kb)])
                P.custom_dma('pool', lambda e, kb=kb, col=col: e.indirect_dma_start(
                    out=Vraw[kb][:, :], out_offset=None, in_=D['psv'], in_offset=bass.IndirectOffsetOnAxis(ap=idxn[:, col:col + 1], axis=0)),
                    reads=['idxn'], writes=[('Vraw', kb)])
                P.op('act', lambda e, kb=kb: e.activation(Vp[kb][:, :, 0:64], Vraw[kb][:, :].rearrange("p (g d) -> p g d", g=4), AF.Copy),
                     reads=[('Vraw', kb), 'vinit'], writes=[('Vp', kb)])
                P.op('dve', lambda e, kb=kb, i_=i_: e.tensor_scalar(indpg[kb][:, :], t0f[:, :], float(2 * i_), None, op0=ALU.is_equal), reads=['t0f'], writes=[('indpg', kb)])
                for g in range(4):
                    qm, qmk = qq[g]
                    b = self.psrot((0, 1, 2, 3))
                    pst = self.ps[b]
                    P.op('pe', lambda e, pst=pst, kb=kb, g=g, qm=qm: e.matmul(pst[:, 0:32], KTp[kb][:, g, :], qm[:, 0:32], start=True, stop=False),
                         reads=[('KTp', kb), qmk], writes=[('ps', b)], inc=False)
                    P.op('pe', lambda e, pst=pst, kb=kb, g=g: e.matmul(pst[:, 0:32], indpg[kb][:, :], negMT[:, g, :, :].rearrange("p n t -> p (n t)"), start=False, stop=True),
                         reads=[('indpg', kb), ('negMT', g)], writes=[('ps', b)])
                    pob = 4 + g
                    unit(pst, b, g, 0 if i_ == 63 else 4, Vp[kb][:, g, 0:66], [('Vp', kb)], self.ps[pob], pob, i_ == 0, False)
            for g in range(4):
                gp = g // 2
                qm, qmk = qq[g]
                kk = g % 2
                P.op('act', lambda e, kk=kk, gp=gp, s=s: e.activation(ktn[kk][:, 0:8], self.KTs[:, gp, 2080 + 8 * s:2088 + 8 * s], AF.Copy),
                     reads=[('KT', 0, gp)], writes=[('ktn', kk)])
                b = self.psrot((0, 1, 2, 3))
                pst = self.ps[b]
                P.op('pe', lambda e, pst=pst, kk=kk, qm=qm: e.matmul(pst[:, 0:32], ktn[kk][:, :], qm[:, 0:32], start=True, stop=True),
                     reads=[('ktn', kk), qmk], writes=[('ps', b)])
                pob = 4 + g
                unit(pst, b, g, 1, self.Vs_s[:, s, g, 0:66], [('Vsm', 0, s)], self.ps[pob], pob, False, True)
                epilogue(self.ps[pob], pob, g, 1, False)
            for hh in range(2):
                P.dma('pool', wkT[hh * 64:(hh + 1) * 64, :, :], D['winkT'][s], writes=['wkT'])
            for j_ in range(4):
                P.dma('pool', wv[:, j_, :, 0:64], D['winv'][s][128 * j_:128 * j_ + 128, :].rearrange("p (g d) -> p g d", g=4), reads=['vinit'], writes=['wv'])
            for g in range(4):
                gp = g // 2
                qm, qmk = qq[g]
                pob = 4 + g
                for j in range(4):
                    b = self.psrot((0, 1, 2, 3))
                    pst = self.ps[b]
                    P.op('pe', lambda e, pst=pst, g=g, j=j, qm=qm: e.matmul(pst[:, 0:32], wkT[:, g, 128 * j:128 * j + 128], qm[:, 0:32], start=True, stop=True),
                         reads=['wkT', qmk], writes=[('ps', b)])
                    unit(pst, b, g, (2, 3, 4, 0)[j], wv[:, j, g, 0:66], ['wv'], self.ps[pob], pob, j == 0, False)
                kk = g % 2
                P.op('act', lambda e, kk=kk, gp=gp, s=s: e.activation(ktn[kk][:, 0:8], self.KTw[:, gp, 2080 + 8 * s:2088 + 8 * s], AF.Copy),
                     reads=[('KT', 1, gp)], writes=[('ktn', kk)])
                b = self.psrot((0, 1, 2, 3))
                pst = self.ps[b]
                P.op('pe', lambda e, pst=pst, kk=kk, qm=qm: e.matmul(pst[:, 0:32], ktn[kk][:, :], qm[:, 0:32], start=True, stop=True),
                     reads=[('ktn', kk), qmk], writes=[('ps', b)])
                unit(pst, b, g, 1, self.Vw_s[:, s, g, 0:66], [('Vsm', 1, s)], self.ps[pob], pob, False, True)
                epilogue(self.ps[pob], pob, g, 2, False)
            for g in range(4):
                P.op('act', lambda e, g=g: e.activation(opad[0:32, 0, 0:64], osum[0:32, g, :], AF.Copy), reads=[('osum', g)], writes=['opad'])
                P.op('act', lambda e, g=g: e.activation(opad[0:32, 1, 64:128], osum[0:32, g, :], AF.Copy), reads=[('osum', g)], writes=['opad'])
                for pr in range(2):
                    b = self.psrot((0, 1, 2, 3))
                    pst = self.ps[b]
                    for hh in range(2):
                        P.op('pe', lambda e, pst=pst, hh=hh, pr=pr: e.matmul(pst[:, 0:8], opad[:, hh, :], seln[:, 2 * pr + hh, :], start=(hh == 0), stop=(hh == 1)),
                             reads=['opad', 'seln'], writes=[('ps', b)], inc=(hh == 1))
                    c = 8 + 2 * g + pr
                    P.op('act', lambda e, pst=pst, c=c, s=s: e.activation(self.mixT[:, c, 1056 + 8 * s:1064 + 8 * s], pst[:, 0:8], AF.Copy), reads=[('ps', b)], writes=[('aT', c)])

    def build(self, n_pool):
        nc, P = self.nc, self.P
        D = self.D
        st = self.st
        self.din('xT_main', [128, 16, NT])
        self.din('xT_pre', [128, 16, NPRE])
        self.din('gains', [128, 4, 16])
        self.din('flag', [128, 1])
        self.din('w_ff1', [2, 2048, 8192])
        self.din('w_ff2', [2, 8192, 2048])
        self.din('w_in_o3', [2048, 6144])
        self.din('w_out_o', [2048, 2048])
        self.din('scw', [128, 3, 16])
        self.din('stsc', [128, 16, 4, 2])
        self.din('w_in_p', [2048, 4656])
        self.din('rgp', [128, 8, 8])
        self.din('rg_wa', [128, 8, 128])
        self.din('rg_wx', [128, 8, 128])
        self.din('kg', [128, 4])
        self.din('sth', [128, 8, 4])
        self.din('stconv', [128, 8, 4, 3])
        self.din('wink', [4, 512, 256])
        for nm, shp in (('ident', [128, 128]), ('jmat', [128, 128]), ('mw', [128, 256]), ('tabaug', [33, 16]), ('e1h', [33, DL]),
                        ('selmul', [1056, 32]), ('seladd', [1056, 32]), ('bvalid', [32, 1]), ('j32p', [128, 128]), ('cw1k', [128, 64, 64]), ('cw1v', [128, 64, 64]),
                        ('cw2k', [64, 2, 128]), ('cw2v', [64, 64]), ('cpe_in', [128, 1, 2, 256]), ('w_out_e', [2048, 2048]),
                        ('pck', [n_pool * 128, 256]), ('pcv', [n_pool * 128, 256]), ('psv', [n_pool * 128, 256]), ('pskT', [n_pool * 64, 512]),
                        ('winkT', [4, 64, 4, 512]), ('rep', [128, 128]), ('nsel', [128, 4]), ('seln', [128, 4, 8]), ('selsum', [128, 128]),
                        ('t0f', [128, 128]), ('iopc', [128, 2]), ('selmul_s', [8, 128]), ('seladd_s', [8, 128])):
            self.din(nm, shp)
        self.din('pt', [1, 256], I32)
        self.sbd = nc.dram_tensor('sbd', [6, 128, 128], F32, kind='Internal').ap()
        self.vd = nc.dram_tensor('vd', [16, DL], F32, kind='Internal').ap()
        self.din('winv', [4, 512, 256])
        self.dout('yT_out', [128, 16, NT])
        self.dout('scp_out', [128, 16, 2])
        self.dout('scs_out', [128, 16, 4, 2])
        for nm in ('kc_o', 'vc_o', 'vs_o', 'vw_o'):
            self.dout(nm, [1024, 256])
        for nm in ('kc_s_o', 'vc_s_o', 'vs_s_o', 'vw_s_o'):
            self.dout(nm, [32, 256])
        self.dout('ksT_o', [128, 2, NT])
        self.dout('kwT_o', [128, 2, NT])
        self.dout('rgo_h_o', [128, 8, 5])
        self.dout('wk_copy_o', [4, 504, 256])
        self.dout('wv_copy_o', [4, 504, 256])
        self.dout('rgo_conv_o', [128, 8, 5, 3])
        self.ps = [st.enter_context(nc.psum_tensor("ps%d" % i, [128, 512], F32)) for i in range(8)]
        self.wt = [self.sb("wt%d" % i, [128, 16, 256], BF16) for i in range(2)]
        self.hT = self.sb("hT", [128, 16, NT + 128], BF16)
        self.aT = self.sb("aT", [128, 16, NT], BF16)
        self.gT = self.aT
        self.mixT = self.aT
        self.sq = self.sb("sq", [128, 4, 512], BF16)
        self.rstd = self.sb("rstd", [128, 512], F32)
        self.tmpf = self.sb("tmpf", [128, 512], F32)
        self.ones_bf = self.sb("ones_bf", [128, 128], BF16)
        self.bones = self.sb("bones", [128, 128], BF16)
        self.epst = self.sb("epst", [128, 1], F32)
        self.onec = self.sb("onec", [128, 1], F32)
        self.gains = self.sb("gains", [128, 4, 16], F32)
        self.flag = self.sb("flag", [128, 1], F32)
        self.scw = self.sb("scw", [128, 3, 16], F32)
        self.stsc = self.sb("stsc", [128, 16, 4, 2], F32)
        self.xs = [self.sb("xs%d" % i, [128, 512], F32) for i in range(2)]
        self.xsi = 0
        self.stgi = 0

        P.op('pool', lambda e: e.memset(self.ones_bf[:], 1.0), writes=['ones'])
        P.op('pool', lambda e: e.memset(self.hT[:, :, NT:NT + 128], 0.0), writes=['hTpad'])
        P.op('pool', lambda e: e.memset(self.bones[:], 0.0), writes=['bones'])
        P.op('pool', lambda e: e.memset(self.bones[0:64, 0:64], 1.0), writes=['bones'])
        P.op('pool', lambda e: e.memset(self.bones[64:128, 64:128], 1.0), writes=['bones'])
        P.op('pool', lambda e: e.memset(self.epst[:], EPS), writes=['epst'])
        P.op('pool', lambda e: e.memset(self.onec[:], 1.0), writes=['onec'])
        P.dma('sp', self.gains[:], D['gains'], writes=['gains'])
        P.dma('sp', self.flag[:], D['flag'], writes=['flag'])
        P.dma('sp', self.scw[:], D['scw'], writes=['scw'])
        P.dma('sp', self.stsc[:], D['stsc'], writes=['stsc'])

        st0 = ExitStack()
        L0 = lambda name, shape, dt: self.sb(name, shape, dt, st0)
        self.KTs = L0("KTs", [128, 2, 1024 + NT], BF16)
        self.KTw = L0("KTw", [128, 2, 1024 + NT], BF16)
        self.Vs = L0("Vs", [128, 16, 4, 66], BF16)
        self.Vw = L0("Vw", [128, 16, 4, 66], BF16)
        self.Vs_s = L0("Vs_s", [128, 4, 4, 66], BF16)
        self.Vw_s = L0("Vw_s", [128, 4, 4, 66], BF16)
        self.rgscr = L0("rgscr", [128, 4, NT], F32)
        self.qT = self.rgscr[:].rearrange("p a t -> p (a t)").bitcast(BF16).rearrange("p (c t) -> p c t", c=8)
        self.gates = L0("gates", [128, 13, 48], F32)
        self.sqh = L0("sqh", [128, 512], BF16)
        self.rstd2 = self.rstd
        self.tmpq = self.tmpf
        self.rgp = L0("rgp", [128, 8, 8], F32)
        self.rgwa = L0("rgwa", [128, 8, 128], BF16)
        self.rgwx = L0("rgwx", [128, 8, 128], BF16)
        self.kg = L0("kg", [128, 4], F32)
        self.sth = L0("sth", [128, 8, 4], F32)
        self.stconv = L0("stconv", [128, 8, 4, 3], F32)
        self.clam = L0("clam", [128, 8], F32)
        self.convh = L0("convh", [128, 8, 3], F32)
        self.hst = L0("hst", [128, 8], F32)
        self.rgo_h = L0("rgo_h", [128, 8, 5], F32)
        self.rgo_conv = L0("rgo_conv", [128, 8, 5, 3], F32)
        self.rxc = self.rgscr[:, 0, :]
        self.rga = self.rgscr[:, 1, :]
        self.rgi = self.rgscr[:, 2, :]
        self.rtm = self.rgscr[:, 3, :]
        self.kcbT = L0("kcbT", [128, 2, 128], BF16)
        self.vcb = L0("vcb", [128, 4, 100], BF16)
        self.hidT = L0("hidT", [64, 2, 32, 4], BF16)
        self.cpe = L0("cpe", [64, 2, 4], F32)
        self.bvalid = L0("bvalid", [32, 1], F32)
        st0a = ExitStack()
        LA = lambda name, shape, dt: self.sb(name, shape, dt, st0a)
        self.hTp = LA("hTp", [128, 16, 256], BF16)
        self.kcpg = LA("kcpg", [128, 8, 2, 256], BF16)
        self.stg = [LA("stg%d" % i, [128, 512], F32) for i in range(2)]
        self.gx = self.stg[0]
        self.gt = self.stg[1]
        self.rxb = LA("rxb", [128, NT], F32)
        self.rxcb = LA("rxcb", [128, NT], BF16)
        self.rhist0 = LA("rhist0", [128, 3 + 1024], F32)
        self.rhists = LA("rhists", [128, 4, 11], F32)
        self.cw1b = LA("cw1b", [128, 64, 64], BF16)
        self.pe_t = LA("pe_t", [128, 1, 2, 256], BF16)
        self.hx = LA("hx", [64, 32], F32)
        self.htt = LA("htt", [64, 32], F32)
        P.dma('pool', self.pe_t[:], D['cpe_in'], writes=['pe_t'])
        P.dma('sp', self.bvalid[:], D['bvalid'], writes=['bvalid'])
        self.cx = dict(cw1b=self.cw1b, hx=self.hx, htt=self.htt, hid=self.hidT)
        self.job(None, lambda: self.compress1(self.pe_t, 1, ['pe_t'], 'pe'))

        for nm, t in (('rgp', self.rgp), ('kg', self.kg), ('sth', self.sth), ('stconv', self.stconv)):
            P.dma('sp', t[:], D[nm], writes=['cst'])
        P.dma('pool', self.rgwa[:], D['rg_wa'], writes=['cst'])
        P.dma('pool', self.rgwx[:], D['rg_wx'], writes=['cst'])
        P.op('pool', lambda e: e.memset(self.convh[:], 0.0), writes=[('convh', r) for r in range(8)])
        P.op('pool', lambda e: e.memset(self.hst[:], 0.0), writes=[('hst', r) for r in range(8)])
        P.op('pool', lambda e: e.memset(self.Vs[:], 1.0), writes=[('V', 0, i) for i in range(16)])
        P.op('pool', lambda e: e.memset(self.Vw[:], 1.0), writes=[('V', 1, i) for i in range(16)])
        for Vt, hh in ((self.Vs, 0), (self.Vw, 1)):
            P.op('dve', lambda e, Vt=Vt: e.tensor_scalar(Vt[:, 0:8, :, 64:65], Vt[:, 0:8, :, 64:65], self.flag[:, 0:1], None, op0=ALU.mult), reads=['flag'], writes=[('V', hh, i) for i in range(16)])
        P.op('pool', lambda e: e.memset(self.Vs_s[:], 1.0), writes=[('Vsm', 0, i) for i in range(4)])
        P.op('pool', lambda e: e.memset(self.Vw_s[:], 1.0), writes=[('Vsm', 1, i) for i in range(4)])
        P.op('act', lambda e: e.activation(self.clam[:], self.rgp[:, :, 7], AF.Exp, scale=-1.0), reads=['cst'], writes=['clam'])
        P.op('act', lambda e: e.activation(self.clam[:], self.clam[:], AF.Ln, bias=self.onec[:, 0:1], scale=1.0), reads=['clam', 'onec'], writes=['clam'])
        P.op('dve', lambda e: e.tensor_scalar(self.clam[:], self.clam[:], -8.0, None, op0=ALU.mult), reads=['clam'], writes=['clam'])

        if os.environ.get('K_NOCOPY') is None:
            P.dma('sp', D['wk_copy_o'], D['wink'][:, 8:512, :], writes=['wkc'])
            P.dma('sp', D['wv_copy_o'], D['winv'][:, 8:512, :], writes=['wvc'])
        self.l0_inproj()
        self.run_jobs()
        P.dma('sp', D['rgo_h_o'], self.rgo_h[:], reads=['rgo_h'], writes=['rgo_h_o'])
        P.dma('sp', D['rgo_conv_o'], self.rgo_conv[:], reads=['rgo_conv'], writes=['rgo_conv_o'])
        P.barrier()
        st0a.close()
        st0b = ExitStack()
        LB = lambda name, shape, dt: self.sb(name, shape, dt, st0b)
        self.cw2k = LB("cw2k", [64, 2, 128], BF16)
        self.cw2v = LB("cw2v", [64, 64], BF16)
        self.vtf = LB("vtf", [64, 32], F32)
        P.dma('pool', self.cw2k[:], D['cw2k'], writes=['cw2'])
        P.dma('pool', self.cw2v[:], D['cw2v'], writes=['cw2'])
        self.attn_setup(LB)
        P.op('pool', lambda e: e.memset(self.vcb[:], 0.0), writes=['vcb'])
        P.op('pool', lambda e: e.memset(self.kcbT[:], 0.0), writes=['kcbT'])
        self.cx = dict(cw2k=self.cw2k, cw2v=self.cw2v, vtf=self.vtf, identf=self.identf, hid=self.hidT)
        self.compress2(32, self.kcbT, self.vcb)
        for g in range(4):
            P.op('dve', lambda e, g=g: e.tensor_scalar(self.vcb[0:32, g, 0:64], self.vcb[0:32, g, 0:64], self.bvalid[:, 0:1], None, op0=ALU.mult), reads=['bvalid'], writes=['vcb'])
            P.op('dve', lambda e, g=g: e.tensor_copy(self.vcb[0:32, g, 64:65], self.bvalid[:, 0:1]), reads=['bvalid'], writes=['vcb'])
            P.op('dve', lambda e, g=g: e.tensor_copy(self.vcb[0:32, g, 66:98], self.identb[0:32, 0:32]), reads=['identb'], writes=['vcb'])
        P.op('pool', lambda e: e.memset(self.mixT[:, 8:16, 1056:1088], 0.0), writes=[('aT', c_) for c_ in range(8, 16)])
        if os.environ.get('K_NOATT') is None:
            self.attn_prompt_tile(0, 0, 32, 992)
            for j in range(8):
                self.attn_prompt_tile(1 + j, 32 + 128 * j, 128, 1024 + 128 * j)
        self.save_sample_bias()
        P.barrier()
        st0b.close()
        st0c = ExitStack()
        LC = lambda name, shape, dt: self.sb(name, shape, dt, st0c)
        if os.environ.get('K_NOSAMPLE') is None:
            self.attn_sample(LC, n_pool)
        P.barrier()
        st0c.close()
        st0.close()

        self.yT = self.sb("yT", [128, 16, NT], F32)
        self.wstage = self.sb("wstage", [128, 16, 256], F32)
        self.tb = self.sb("tb", [128, NT], F32)
        self.tc = self.sb("tc", [128, NT], F32)
        self.tv = self.sb("tv", [128, NT], F32)
        self.ty = self.sb("ty", [128, NT], F32)
        self.tvs = self.sb("tvs", [128, 4, 10], F32)
        self.scp = self.sb("scp", [128, 16, 2], F32)
        self.scs = self.sb("scs", [128, 16, 4, 2], F32)
        for c in range(16):
            P.dma('sp', self.yT[:, c, :], D['xT_main'][:, c, :], writes=[('yT', c)])
        ysrc = lambda c, t0, n: self.yT[:, c, t0:t0 + n]
        hdst = lambda c, t0, n: self.hT[:, c, t0:t0 + n]
        ykey = lambda c: ('yT', c)
        hkey = lambda c: ('hT', c)
        Woe = D['w_out_e']
        mparts = [(self.mixT, lambda k: ('aT', k), t0, n, None) for (t0, n) in TT]
        for cg in range(8):
            def epi_o(m, tag, t0, n, ps, pkey, cg=cg):
                c = cg * 2 + m
                P.op('dve', lambda e, ps=ps, c=c, t0=t0, n=n: e.tensor_tensor(
                    self.yT[:, c, t0:t0 + n], ps, self.yT[:, c, t0:t0 + n], op=ALU.add),
                    reads=[pkey], writes=[('yT', c)])
            self.dense_fm(self.wview(Woe, 0, 16, cg * 256, 256), 16, 2, mparts, epi_o)
        self.rmsnorm(ysrc, ykey, self.gains[:, 2, :], hdst, hkey, TT)
        self.ffn(0)
        self.rmsnorm(ysrc, ykey, self.gains[:, 1, :], hdst, hkey, TT)
        self.mixer1()
        self.rmsnorm(ysrc, ykey, self.gains[:, 3, :], hdst, hkey, TT)
        self.ffn(1)
        self.run_jobs()
        for c in range(16):
            P.dma('sp', D['yT_out'][:, c, :], self.yT[:, c, :], reads=[('yT', c)], writes=[('yo', c)])
        P.dma('sp', D['scp_out'], self.scp[:], reads=['scp'], writes=['scpo'])
        P.dma('sp', D['scs_out'], self.scs[:], reads=['scs'], writes=['scso'])
        P.finish()
        P.emit()
        self.st.close()
        return nc


_CACHE = {}


def _fm(a, nchunk):
    T = a.shape[0]
    return np.ascontiguousarray(a.reshape(T, nchunk, 128).transpose(2, 1, 0))


def _vec_pm(v, nchunk):
    return np.ascontiguousarray(v.reshape(nchunk, 128).T)


def _tm(a):
    return np.ascontiguousarray(a.transpose(2, 1, 0).reshape(a.shape[2], -1))


def _q_perm():
    cols = []
    for i in range(8):
        hA = (i % 4) + 8 * (i // 4)
        hB = hA + 4
        cols += list(range(2048 + hA * 64, 2048 + hA * 64 + 64)) + list(range(2048 + hB * 64, 2048 + hB * 64 + 64))
    return cols


def prepare_inputs(inp):
    f = lambda k: np.asarray(inp[k])
    x_prompt = f('x_prompt')
    x_sample = f('x_sample')
    sh = {}
    sh['gains'] = np.ascontiguousarray(np.stack([_vec_pm(f('norm_mix')[0], 16), _vec_pm(f('norm_mix')[1], 16),
                                                 _vec_pm(f('norm_ffn')[0], 16), _vec_pm(f('norm_ffn')[1], 16)], axis=1))
    sh['w_ff1'] = f('w_ff1')
    sh['w_ff2'] = f('w_ff2')
    wio = f('w_in_o')[0]
    sh['w_in_o3'] = np.ascontiguousarray(wio.reshape(2048, 3, 16, 128).transpose(0, 2, 1, 3).reshape(2048, 6144))
    sh['w_out_o'] = f('w_out_o')[0]
    sh['scw'] = np.ascontiguousarray(f('sc_w')[0].reshape(3, 16, 128).transpose(2, 0, 1))
    wie = f('w_in_e')[0]
    R = lambda a, b: list(range(a, b))
    perm = R(0, 2048) + _q_perm() + R(3584, 3840) + R(4096, 4352) + R(3072, 3584) + R(3840, 4096) + R(4352, 4608) + R(4608, 4656)
    sh['w_in_p'] = np.ascontiguousarray(wie[:, perm])
    rows = [f('rg_conv_w')[0][j] for j in range(4)] + [f('rg_conv_b')[0], f('rg_ba')[0], f('rg_bx')[0], f('rg_lambda')[0]]
    sh['rgp'] = np.ascontiguousarray(np.stack([_vec_pm(r_, 8) for r_ in rows], axis=2))
    sh['rg_wa'] = np.ascontiguousarray(f('rg_wa')[0].transpose(1, 0, 2))
    sh['rg_wx'] = np.ascontiguousarray(f('rg_wx')[0].transpose(1, 0, 2))
    kn = f('k_norm')[0]
    qn = f('q_norm')[0]
    sh['kg'] = np.ascontiguousarray(np.stack([np.tile(kn[0], 2), np.tile(kn[1], 2), np.tile(kn[2], 2), np.tile(qn, 2)], axis=1))
    sh['w_out_e'] = f('w_out_e')[0]
    sh['ident'] = np.eye(128, dtype=np.float32)
    sh['jmat'] = np.ascontiguousarray(np.eye(128, dtype=np.float32)[::-1])
    j32p = np.zeros((128, 128), np.float32)
    j32p[0:32, 0:32] = np.eye(32, dtype=np.float32)[::-1]
    sh['j32p'] = j32p
    pp = np.arange(128)[:, None]
    xx = np.arange(384, 640)[None, :]
    sh['mw'] = np.where(xx - pp > 512, np.float32(MASKNEG), np.float32(0.0)).astype(np.float32)
    sh['tabaug'] = np.concatenate([f('rel_bias'), np.full((1, 16), MASKNEG, np.float32)], axis=0).astype(np.float32)
    dist = np.arange(DL) - DOFF
    nn = np.maximum(dist, 0)
    nf = np.maximum(nn, 1).astype(np.float32)
    large = 16 + (np.log(nf / np.float32(16.0)).astype(np.float32) / np.float32(np.log(8.0)) * np.float32(16.0)).astype(np.int32)
    large = np.minimum(large, 31)
    buck = np.where(nn < 16, nn, large)
    e1h = np.zeros((33, DL), np.float32)
    e1h[buck[dist >= 0], np.nonzero(dist >= 0)[0]] = 1.0
    e1h[32, dist < 0] = 1.0
    sh['e1h'] = e1h
    w1k = f('cmp_w1_k')[0]; w1v = f('cmp_w1_v')[0]
    sh['cw1k'] = np.ascontiguousarray(np.concatenate([w1k, w1k], axis=0))
    sh['cw1v'] = np.ascontiguousarray(np.concatenate([w1v, w1v], axis=0))
    w2k = f('cmp_w2_k')[0]
    c2 = np.zeros((64, 2, 128), np.float32)
    c2[:, 0, 0:64] = w2k
    c2[:, 1, 64:128] = w2k
    sh['cw2k'] = c2
    sh['cw2v'] = f('cmp_w2_v')[0]
    pek = f('cmp_pe_k')[0].reshape(64, 256); pev = f('cmp_pe_v')[0].reshape(64, 256)
    pe2 = np.stack([pek, pev], axis=1)
    sh['cpe_in'] = np.ascontiguousarray(np.concatenate([pe2, pe2], axis=0)[:, None])
    npool = f('cache_cmp_k').shape[1]
    sh['pck'] = f('cache_cmp_k')[0].reshape(npool * 128, 256)
    sh['pcv'] = f('cache_cmp_v')[0].reshape(npool * 128, 256)
    sh['psv'] = f('cache_sel_v')[0].reshape(npool * 128, 256)
    sh['pskT'] = np.ascontiguousarray(f('cache_sel_k')[0].transpose(0, 3, 2, 1)).reshape(npool * 64, 512)
    rr = np.arange(128)
    rep = np.zeros((128, 128), np.float32)
    for c_ in range(32):
        rep[c_ % 8, c_] = 1.0
    sh['rep'] = rep
    sh['nsel'] = ((rr[:, None] // 8 == np.arange(4)[None, :]) & (rr[:, None] < 32)).astype(np.float32)
    sh['seln'] = ((rr[:, None, None] < 32) & (rr[:, None, None] // 8 == np.arange(4)[None, :, None]) & (rr[:, None, None] % 8 == np.arange(8)[None, None, :])).astype(np.float32)
    ssum = np.zeros((128, 128), np.float32)
    for r_ in range(32):
        ssum[r_, r_ % 8] = 1.0
    sh['selsum'] = ssum
    sh['iopc'] = np.stack([rr, rr % 64], axis=1).astype(np.float32)
    sh['t0f'] = (rr[:, None] - (rr[None, :] // 64)).astype(np.float32)
    sh['selmul_s'] = np.ones((8, 128), np.float32)
    sa = np.zeros((8, 128), np.float32)
    sa[:, 0] = 1.0e4
    sa[:, 127] = 1.0e4
    sh['seladd_s'] = sa
    maps = []
    for c in range(8):
        b, half = c // 2, c % 2
        p0 = 1024 * half
        m = dict(sh)
        b0 = 0 if half == 1 else 16
        m['pt'] = np.ascontiguousarray(f('page_table')[4 * c:4 * c + 4].reshape(1, 256).astype(np.int32))
        m['winkT'] = np.ascontiguousarray(f('cache_win_k')[0, 4 * c:4 * c + 4].transpose(0, 3, 2, 1))
        qp = np.concatenate([np.arange(992, 1024), np.arange(1024, 2048)])
        cur = (qp // 64)[:, None]
        blk = np.arange(32)[None, :]
        bad = (blk > cur) | (blk < b0)
        forced = ((blk == cur) | (blk == cur - 1) | (blk == b0)) & ~bad
        m['selmul'] = np.where(bad, 0.0, 1.0).astype(np.float32)
        m['seladd'] = np.where(bad, -1.0e30, np.where(forced, 1.0e4, 0.0)).astype(np.float32)
        m['bvalid'] = (np.arange(32) >= b0).astype(np.float32)[:, None]
        own = x_prompt[b, p0:p0 + 1024]
        pre = x_prompt[b, 0:1024] if half == 1 else np.zeros((1024, 2048), np.float32)
        xs = x_sample[4 * c:4 * c + 4].reshape(32, 2048)
        main = np.concatenate([pre[992:1024], own, xs], axis=0)
        m['xT_main'] = _fm(main, 16)
        m['xT_pre'] = _fm(pre, 16)
        m['flag'] = np.full((128, 1), float(half), np.float32)
        st = f('state_sconv')[0, 4 * c:4 * c + 4]
        m['stsc'] = np.ascontiguousarray(st.reshape(4, 2, 16, 128).transpose(3, 2, 0, 1))
        m['sth'] = np.ascontiguousarray(f('state_rglru_h')[0, 4 * c:4 * c + 4].reshape(4, 8, 128).transpose(2, 1, 0))
        m['stconv'] = np.ascontiguousarray(f('state_rglru_conv')[0, 4 * c:4 * c + 4].reshape(4, 3, 8, 128).transpose(3, 2, 0, 1))
        m['wink'] = np.ascontiguousarray(f('cache_win_k')[0, 4 * c:4 * c + 4].reshape(4, 512, 256))
        m['winv'] = np.ascontiguousarray(f('cache_win_v')[0, 4 * c:4 * c + 4].reshape(4, 512, 256))
        maps.append(m)
    return maps


def _kT_to_rows(a):
    T = a.shape[2]
    return np.ascontiguousarray(a.reshape(2, 64, 2, T).transpose(3, 2, 0, 1).reshape(T, 4, 64))


def kernel(**inputs):
    n_pool = int(np.asarray(inputs['cache_cmp_k']).shape[1])
    if 'nc' not in _CACHE:
        b = Builder()
        _CACHE['nc'] = b.build(n_pool)
        _CACHE['in_names'] = b.in_names
    nc = _CACHE['nc']
    maps = prepare_inputs(inputs)
    maps = [{k: m[k] for k in _CACHE['in_names']} for m in maps]
    res = run_bass_kernel_spmd(nc, maps[:NCORES], core_ids=list(range(NCORES)))
    R = res.results
    z = lambda *s: np.zeros(s, np.float32)
    y_p = z(4, 2048, 2048); y_s = z(32, 8, 2048)
    ck_p = z(1, 4, 2048, 4, 64); cv_p = z(1, 4, 2048, 4, 64); sk_p = z(1, 4, 2048, 4, 64); sv_p = z(1, 4, 2048, 4, 64)
    ck_s = z(1, 32, 8, 4, 64); cv_s = z(1, 32, 8, 4, 64); sk_s = z(1, 32, 8, 4, 64); sv_s = z(1, 32, 8, 4, 64)
    wk_p = z(1, 4, 512, 4, 64); wv_p = z(1, 4, 512, 4, 64); wk_s = z(1, 32, 512, 4, 64); wv_s = z(1, 32, 512, 4, 64)
    h_p = z(1, 4, 1024); h_s = z(1, 32, 1024); cv1_p = z(1, 4, 3, 1024); cv1_s = z(1, 32, 3, 1024)
    sconv_p = z(1, 4, 2, 2048); sconv_s = z(1, 32, 2, 2048)
    for c in range(NCORES):
        b, half = c // 2, c % 2
        p0 = 1024 * half
        r = R[c]
        yt = _tm(r['yT_out'])
        y_p[b, p0:p0 + 1024] = yt[32:1056]
        y_s[4 * c:4 * c + 4] = yt[1056:1088].reshape(4, 8, 2048)
        ck_p[0, b, p0:p0 + 1024] = r['kc_o'].reshape(1024, 4, 64)
        cv_p[0, b, p0:p0 + 1024] = r['vc_o'].reshape(1024, 4, 64)
        sv_p[0, b, p0:p0 + 1024] = r['vs_o'].reshape(1024, 4, 64)
        ks_rows = _kT_to_rows(r['ksT_o'])
        kw_rows = _kT_to_rows(r['kwT_o'])
        sk_p[0, b, p0:p0 + 1024] = ks_rows[32:1056]
        ck_s[0, 4 * c:4 * c + 4] = r['kc_s_o'].reshape(4, 8, 4, 64)
        cv_s[0, 4 * c:4 * c + 4] = r['vc_s_o'].reshape(4, 8, 4, 64)
        sv_s[0, 4 * c:4 * c + 4] = r['vs_s_o'].reshape(4, 8, 4, 64)
        sk_s[0, 4 * c:4 * c + 4] = ks_rows[1056:1088].reshape(4, 8, 4, 64)
        wk_s[0, 4 * c:4 * c + 4, 0:504] = r['wk_copy_o'].reshape(4, 504, 4, 64)
        wv_s[0, 4 * c:4 * c + 4, 0:504] = r['wv_copy_o'].reshape(4, 504, 4, 64)
        wk_s[0, 4 * c:4 * c + 4, 504:512] = kw_rows[1056:1088].reshape(4, 8, 4, 64)
        wv_s[0, 4 * c:4 * c + 4, 504:512] = r['vw_s_o'].reshape(4, 8, 4, 64)
        rh = r['rgo_h_o']
        rc = r['rgo_conv_o']
        h_s[0, 4 * c:4 * c + 4] = rh[:, :, 1:5].transpose(2, 1, 0).reshape(4, 1024)
        cv1_s[0, 4 * c:4 * c + 4] = rc[:, :, 1:5, :].transpose(2, 3, 1, 0).reshape(4, 3, 1024)
        sconv_s[0, 4 * c:4 * c + 4] = r['scs_out'].transpose(2, 3, 1, 0).reshape(4, 2, 2048)
        if half == 1:
            wk_p[0, b] = kw_rows[32 + 512:32 + 1024]
            wv_p[0, b] = r['vw_o'][512:1024].reshape(512, 4, 64)
            h_p[0, b] = rh[:, :, 0].T.reshape(1024)
            cv1_p[0, b] = rc[:, :, 0, :].transpose(2, 1, 0).reshape(3, 1024)
            sconv_p[0, b] = _tm(r['scp_out'])
    return (y_p, y_s, ck_p, ck_s, cv_p, cv_s, sk_p, sk_s, sv_p, sv_s, wk_p, wk_s, wv_p, wv_s,
            h_p, h_s, cv1_p, cv1_s, sconv_p, sconv_s)
```

```python
import os
import numpy as np
from contextlib import ExitStack
import concourse.bass as bass
import concourse.mybir as mybir
from concourse.bass_utils import run_bass_kernel_spmd

F32 = mybir.dt.float32
BF16 = mybir.dt.bfloat16
I32 = mybir.dt.int32
ALU = mybir.AluOpType
AF = mybir.ActivationFunctionType

SAME_ENGINE_SYNC = True
EPOCH = 30000
NDMASEM = 48

NT = 1088
NPRE = 1024
TT = [(0, 512), (512, 512), (1024, 64)]
TTP = [(0, 512), (512, 512)]
EPS = 1e-6
MASKNEG = -30000.0
SCALE = 0.125
DOFF = 2304
DL = 10752
DEBUG_STAGE = 3
NCORES = 8


class Prog:
    ENGS = ('pe', 'dve', 'act', 'pool', 'sp')

    def __init__(self, nc, stack):
        self.nc = nc
        self.stack = stack
        self.nsem = 0
        self.ops = {e: [] for e in self.ENGS}
        self.csem = {e: self._newsem('c_' + e) for e in self.ENGS}
        self.ccnt = {e: 0 for e in self.ENGS}
        self.seen = {e: {} for e in self.ENGS}
        self.lastw = {}
        self.readers = {}
        self.dsem = [self._newsem('d%d' % i) for i in range(NDMASEM)]
        self.dval = [0] * NDMASEM
        self.di = 0
        self.dq = {'sp': (0, 32), 'pool': (32, 16), 'act': (0, 32)}
        self.dqi = {'sp': 0, 'pool': 0, 'act': 0}
        self.nwaits = 0
        self.allsems = []

    def _newsem(self, name):
        self.nsem += 1
        s = self.stack.enter_context(self.nc.semaphore(name + '_%d' % self.nsem))
        return s

    def _deps(self, reads, writes):
        deps = []
        for k in list(reads) + list(writes):
            t = self.lastw.get(k)
            if t is not None:
                deps.append(t)
        for k in writes:
            deps.extend(self.readers.get(k, ()))
        for k in reads:
            if isinstance(k, tuple) and k and k[0] == 'ps':
                deps.extend(self.readers.get(k, ()))
        return deps

    def _needed(self, eng, deps):
        out = {}
        own = self.csem[eng]
        for sem, val in deps:
            if sem is own:
                if eng == 'pe' or not SAME_ENGINE_SYNC:
                    continue
            key = id(sem)
            if self.seen[eng].get(key, 0) >= val:
                continue
            if key not in out or out[key][1] < val:
                out[key] = (sem, val)
        for key, (sem, val) in out.items():
            self.seen[eng][key] = val
        self.nwaits += len(out)
        return list(out.values())

    def _commit(self, tok, reads, writes):
        for k in writes:
            self.lastw[k] = tok
            self.readers[k] = []
        for k in reads:
            if k in writes:
                continue
            self.readers.setdefault(k, []).append(tok)

    def op(self, eng, fn, reads=(), writes=(), inc=True, mode=None):
        if eng == 'pe':
            m_ = mode or 'full'
            if m_ != getattr(self, 'pemode', 'full'):
                self.ops['pe'].append(([], lambda e: e.drain(), None))
            self.pemode = m_
        deps = self._deps(reads, writes)
        waits = self._needed(eng, deps)
        if inc:
            self.ccnt[eng] += 1
            tok = (self.csem[eng], self.ccnt[eng])
            incinfo = (self.csem[eng], 1)
        else:
            tok = (self.csem[eng], self.ccnt[eng] + 1)
            incinfo = None
        self.ops[eng].append((waits, fn, incinfo))
        self._commit(tok, reads, writes)
        if inc and self.ccnt[eng] >= EPOCH:
            self.csem[eng] = self._newsem('c_' + eng)
            self.ccnt[eng] = 0
        return tok

    def dma(self, q, out, in_, reads=(), writes=(), **kw):
        return self.custom_dma(q, lambda e: e.dma_start(out=out, in_=in_, **kw), reads, writes)

    def custom_dma(self, q, fn, reads=(), writes=()):
        base, cnt = self.dq[q]
        i = base + self.dqi[q] % cnt
        self.dqi[q] += 1
        self.di += 1
        sem = self.dsem[i]
        prev = self.dval[i]
        val = prev + 16
        self.dval[i] = val
        deps = self._deps(reads, writes)
        if prev > 0:
            deps.append((sem, prev))
        waits = self._needed(q, deps)
        tok = (sem, val)
        self.ops[q].append((waits, fn, (sem, 16)))
        self._commit(tok, reads, writes)
        return tok

    def barrier(self):
        for eng in self.ENGS:
            deps = [(self.dsem[i], self.dval[i]) for i in range(NDMASEM) if self.dval[i] > 0]
            for e in self.ENGS:
                if e != eng and self.ccnt[e] > 0:
                    deps.append((self.csem[e], self.ccnt[e]))
            waits = self._needed(eng, deps)
            if waits:
                self.ops[eng].append((waits, None, None))
        self.lastw.clear()
        self.readers.clear()

    def finish(self):
        deps = [(self.dsem[i], self.dval[i]) for i in range(NDMASEM) if self.dval[i] > 0]
        for e in self.ENGS:
            if e != 'sp' and self.ccnt[e] > 0:
                deps.append((self.csem[e], self.ccnt[e]))
        waits = self._needed('sp', deps)
        self.ops['sp'].append((waits, None, None))

    def emit(self):
        nc = self.nc
        engmap = {'pe': 'tensor', 'dve': 'vector', 'act': 'scalar', 'pool': 'gpsimd', 'sp': 'sync'}
        with nc.Block() as block:
            for e in self.ENGS:
                lst = self.ops[e]

                def body(eng, lst=lst):
                    for waits, fn, incinfo in lst:
                        for sem, val in waits:
                            eng.wait_ge(sem, val)
                        if fn is None:
                            continue
                        ins = fn(eng)
                        if incinfo is not None:
                            ins.then_inc(incinfo[0], incinfo[1])
                getattr(block, engmap[e])(body)


class Builder:
    def __init__(self):
        self.nc = bass.Bass("TRN2", target_bir_lowering=False)
        self.st = ExitStack()
        self.P = Prog(self.nc, self.st)
        self.D = {}
        self.jobs = []
        self.wi = 0
        self.psi = {}
        self.in_names = []
        self.out_names = []

    def din(self, name, shape, dt=F32):
        self.D[name] = self.nc.dram_tensor(name, list(shape), dt, kind="ExternalInput").ap()
        self.in_names.append(name)
        return self.D[name]

    def dout(self, name, shape, dt=F32):
        self.D[name] = self.nc.dram_tensor(name, list(shape), dt, kind="ExternalOutput").ap()
        self.out_names.append(name)
        return self.D[name]

    def sb(self, name, shape, dt, st=None):
        return (st or self.st).enter_context(self.nc.sbuf_tensor("s_" + name, list(shape), dt))

    def psrot(self, banks):
        i = self.psi.get(banks, 0)
        self.psi[banks] = i + 1
        return banks[i % len(banks)]

    def job(self, wsrc, fn):
        self.jobs.append((wsrc, fn))

    def run_jobs(self):
        P = self.P
        jobs = self.jobs
        if os.environ.get('K_STOP') is not None and not getattr(self, '_stopped', False):
            jobs = jobs[:int(os.environ['K_STOP'])]
            self._stopped = True
        widx = [i for i, j in enumerate(jobs) if j[0] is not None]
        loaded = {}
        nl = 0
        for i, (wsrc, fn) in enumerate(jobs):
            ahead = [k for k in widx if k >= i][:2]
            if ahead:
                while nl < len(widx) and widx[nl] <= ahead[-1]:
                    k = widx[nl]
                    src = jobs[k][0]
                    bi = self.wi % 2
                    self.wi += 1
                    t = self.wt[bi]
                    kc, n = src.shape[1], src.shape[2]
                    if getattr(self, 'wstage', None) is not None:
                        stg_ = self.wstage
                        P.dma('sp', stg_[:, 0:kc, 0:n], src, writes=['wstage'])
                        P.op('pool', lambda e, t=t, stg_=stg_, kc=kc, n=n: e.tensor_copy(t[:, 0:kc, 0:n], stg_[:, 0:kc, 0:n]), reads=['wstage'], writes=[('wt', bi)])
                    else:
                        P.dma('pool', t[:, 0:kc, 0:n], src, writes=[('wt', bi)])
                    loaded[k] = (t, ('wt', bi))
                    nl += 1
            if wsrc is None:
                fn()
            else:
                t, key = loaded.pop(i)
                fn(t, key)
        self.jobs = []

    def wview(self, W, k0, kc, c0, n):
        return W[k0:k0 + kc * 128, c0:c0 + n].rearrange("(kc p) n -> p kc n", p=128)

    def dense_fm(self, wsrc, kc, nm, xparts, epi, banks=(0, 1, 2, 3)):
        P = self.P

        def fn(wt, wkey):
            for m in range(nm):
                for (xt, xkeyf, t0, n, tag) in xparts:
                    b = self.psrot(banks)
                    pst = self.ps[b]
                    pkey = ('ps', b)
                    for k in range(kc):
                        P.op('pe', lambda e, pst=pst, wt=wt, xt=xt, k=k, m=m, t0=t0, n=n: e.matmul(
                            pst[:, 0:n], wt[:, k, m * 128:(m + 1) * 128], xt[:, k, t0:t0 + n],
                            start=(k == 0), stop=(k == kc - 1)),
                            reads=[wkey, xkeyf(k)], writes=[pkey], inc=(k == kc - 1))
                    epi(m, tag, t0, n, pst[:, 0:n], pkey)
        self.job(wsrc, fn)

    def dense_tm(self, wsrc, kc, ncols, xparts, epi, banks=(0, 1, 2, 3)):
        P = self.P

        def fn(wt, wkey):
            for (xt, xkeyf, t0, n, tag) in xparts:
                b = self.psrot(banks)
                pst = self.ps[b]
                pkey = ('ps', b)
                for k in range(kc):
                    P.op('pe', lambda e, pst=pst, wt=wt, xt=xt, k=k, t0=t0, n=n: e.matmul(
                        pst[:, 0:ncols], xt[:, k, t0:t0 + 128], wt[:, k, 0:ncols],
                        start=(k == 0), stop=(k == kc - 1)),
                        reads=[wkey, xkeyf(k)], writes=[pkey], inc=(k == kc - 1))
                epi(tag, t0, n, pst[0:n, 0:ncols], pkey)
        self.job(wsrc, fn)

    def rmsnorm(self, srcf, skey, gain, dstf, dkey, tiles, fetch=None):
        P = self.P

        def getsrc(c, t0, n):
            if fetch is None:
                return srcf(c, t0, n), skey(c)
            i = self.xsi % 2
            self.xsi += 1
            P.dma('sp', self.xs[i][:, 0:n], fetch(c, t0, n), writes=[('xs', i)])
            return self.xs[i][:, 0:n], ('xs', i)

        def fn():
            for (t0, n) in tiles:
                b = self.psrot((4, 5))
                pst = self.ps[b]
                for c in range(16):
                    src, sk = getsrc(c, t0, n)
                    P.op('act', lambda e, c=c, n=n, src=src: e.activation(self.sq[:, c % 4, 0:n], src, AF.Square),
                         reads=[sk], writes=[('sq', c % 4)])
                    P.op('pe', lambda e, c=c, n=n, pst=pst: e.matmul(pst[:, 0:n], self.ones_bf[:, :], self.sq[:, c % 4, 0:n],
                                                                     start=(c == 0), stop=(c == 15)),
                         reads=[('sq', c % 4), 'ones'], writes=[('ps', b)], inc=True)
                P.op('act', lambda e, n=n, pst=pst: e.activation(self.rstd[:, 0:n], pst[:, 0:n], AF.Sqrt, bias=self.epst[:, 0:1], scale=1.0 / 2048.0),
                     reads=[('ps', b), 'epst'], writes=['rstd'])
                P.op('dve', lambda e, n=n: e.reciprocal(self.rstd[:, 0:n], self.rstd[:, 0:n]), reads=['rstd'], writes=['rstd'])
                for c in range(16):
                    src, sk = getsrc(c, t0, n)
                    P.op('dve', lambda e, c=c, t0=t0, n=n, src=src: e.scalar_tensor_tensor(
                        dstf(c, t0, n), src, gain[:, c:c + 1], self.rstd[:, 0:n], op0=ALU.mult, op1=ALU.mult),
                        reads=[sk, 'rstd', 'gains'], writes=[dkey(c)])
        self.job(None, fn)

    def ffn(self, li):
        P = self.P
        W1 = self.D['w_ff1'][li]
        W2 = self.D['w_ff2'][li]
        xparts = [(self.hT, lambda k: ('hT', k), t0, n, None) for (t0, n) in TT]
        aparts = [(self.aT, lambda k: ('aT', k), t0, n, None) for (t0, n) in TT]
        for q in range(4):
            for cg in range(8):
                def epi1(m, tag, t0, n, ps, pkey, cg=cg):
                    c = cg * 2 + m
                    P.op('act', lambda e, ps=ps, n=n: e.activation(self.tmpf[:, 0:n], ps, AF.Square), reads=[pkey], writes=['tmpf'])
                    P.op('dve', lambda e, ps=ps, c=c, t0=t0, n=n: e.scalar_tensor_tensor(
                        self.aT[:, c, t0:t0 + n], ps, 0.0, self.tmpf[:, 0:n], op0=ALU.is_gt, op1=ALU.mult),
                        reads=[pkey, 'tmpf'], writes=[('aT', c)])
                self.dense_fm(self.wview(W1, 0, 16, q * 2048 + cg * 256, 256), 16, 2, xparts, epi1)
            for cg in range(8):
                def epi2(m, tag, t0, n, ps, pkey, cg=cg):
                    c = cg * 2 + m
                    P.op('dve', lambda e, ps=ps, c=c, t0=t0, n=n: e.tensor_tensor(
                        self.yT[:, c, t0:t0 + n], ps, self.yT[:, c, t0:t0 + n], op=ALU.add),
                        reads=[pkey], writes=[('yT', c)])
                self.dense_fm(self.wview(W2, q * 2048, 16, cg * 256, 256), 16, 2, aparts, epi2)

    def mixer1(self):
        P = self.P
        W = self.D['w_in_o3']
        xparts = [(self.hT, lambda k: ('hT', k), t0, n, None) for (t0, n) in TT]
        for c in range(16):
            for m_ in range(3):
                def epi(m, tag, t0, n, ps, pkey, c=c, m_=m_):
                    if m_ == 0:
                        P.op('act', lambda e, ps=ps, t0=t0, n=n: e.activation(self.tb[:, t0:t0 + n], ps, AF.Copy), reads=[pkey], writes=['tb'])
                    elif m_ == 1:
                        P.op('act', lambda e, ps=ps, t0=t0, n=n: e.activation(self.tc[:, t0:t0 + n], ps, AF.Copy), reads=[pkey], writes=['tc'])
                    else:
                        P.op('dve', lambda e, ps=ps, t0=t0, n=n: e.tensor_tensor(self.tv[:, t0:t0 + n], ps, self.tc[:, t0:t0 + n], op=ALU.mult),
                             reads=[pkey, 'tc'], writes=['tv'])
                        if t0 + n == NT:
                            self.conv1(c)
                self.dense_fm(self.wview(W, 0, 16, c * 384 + m_ * 128, 128), 16, 1, xparts, epi)
        Wo = self.D['w_out_o']
        gparts = [(self.gT, lambda k: ('aT', k), t0, n, None) for (t0, n) in TT]
        for cg in range(8):
            def epi2(m, tag, t0, n, ps, pkey, cg=cg):
                c = cg * 2 + m
                P.op('dve', lambda e, ps=ps, c=c, t0=t0, n=n: e.tensor_tensor(
                    self.yT[:, c, t0:t0 + n], ps, self.yT[:, c, t0:t0 + n], op=ALU.add),
                    reads=[pkey], writes=[('yT', c)])
            self.dense_fm(self.wview(Wo, 0, 16, cg * 256, 256), 16, 2, gparts, epi2)

    def conv1(self, c):
        P = self.P
        w = self.scw
        P.op('dve', lambda e: e.tensor_scalar(self.tv[:, 0:32], self.tv[:, 0:32], self.flag[:, 0:1], None, op0=ALU.mult),
             reads=['tv', 'flag'], writes=['tv'])
        P.op('dve', lambda e: e.tensor_scalar(self.ty[:, 32:1056], self.tv[:, 30:1054], w[:, 0, c:c + 1], None, op0=ALU.mult),
             reads=['tv', 'scw'], writes=['ty'])
        P.op('dve', lambda e: e.scalar_tensor_tensor(self.ty[:, 32:1056], self.tv[:, 31:1055], w[:, 1, c:c + 1], self.ty[:, 32:1056], op0=ALU.mult, op1=ALU.add),
             reads=['tv', 'scw'], writes=['ty'])
        P.op('dve', lambda e: e.scalar_tensor_tensor(self.ty[:, 32:1056], self.tv[:, 32:1056], w[:, 2, c:c + 1], self.ty[:, 32:1056], op0=ALU.mult, op1=ALU.add),
             reads=['tv', 'scw'], writes=['ty'])
        P.op('dve', lambda e: e.tensor_copy(self.tvs[:, :, 0:2], self.stsc[:, c, :, :]), reads=['stsc'], writes=['tvs'])
        P.op('dve', lambda e: e.tensor_copy(self.tvs[:, :, 2:10], self.tv[:, 1056:1088].rearrange("p (s t) -> p s t", s=4)), reads=['tv'], writes=['tvs'])
        ys = self.ty[:, 1056:1088].rearrange("p (s t) -> p s t", s=4)
        P.op('dve', lambda e: e.tensor_scalar(ys, self.tvs[:, :, 0:8], w[:, 0, c:c + 1], None, op0=ALU.mult), reads=['tvs', 'scw'], writes=['ty'])
        P.op('dve', lambda e: e.scalar_tensor_tensor(ys, self.tvs[:, :, 1:9], w[:, 1, c:c + 1], ys, op0=ALU.mult, op1=ALU.add), reads=['tvs', 'scw'], writes=['ty'])
        P.op('dve', lambda e: e.scalar_tensor_tensor(ys, self.tvs[:, :, 2:10], w[:, 2, c:c + 1], ys, op0=ALU.mult, op1=ALU.add), reads=['tvs', 'scw'], writes=['ty'])
        P.op('pool', lambda e: e.memset(self.ty[:, 0:32], 0.0), writes=['ty'])
        P.op('dve', lambda e: e.tensor_tensor(self.gT[:, c, :], self.tb[:, :], self.ty[:, :], op=ALU.mult), reads=['tb', 'ty'], writes=[('aT', c)])
        P.op('act', lambda e: e.activation(self.scp[:, c, :], self.tv[:, 1054:1056], AF.Copy), reads=['tv'], writes=['scp'])
        P.op('act', lambda e: e.activation(self.scs[:, c, :, :], self.tvs[:, :, 8:10], AF.Copy), reads=['tvs'], writes=['scs'])

    def stage(self):
        i = self.stgi % 2
        self.stgi += 1
        return self.stg[i], ('stg', i)

    def headnorm(self, src, skey, n, gain, outs):
        P = self.P
        P.op('act', lambda e: e.activation(self.sqh[:, 0:n], src, AF.Square), reads=[skey], writes=['sqh'])
        b = self.psrot((4, 5))
        pst = self.ps[b]
        P.op('pe', lambda e: e.matmul(pst[:, 0:n], self.bones[:, :], self.sqh[:, 0:n], start=True, stop=True),
             reads=['sqh', 'bones'], writes=[('ps', b)])
        P.op('act', lambda e: e.activation(self.rstd2[:, 0:n], pst[:, 0:n], AF.Sqrt, bias=self.epst[:, 0:1], scale=1.0 / 64.0),
             reads=[('ps', b), 'epst'], writes=['rstd'])
        P.op('dve', lambda e: e.reciprocal(self.rstd2[:, 0:n], self.rstd2[:, 0:n]), reads=['rstd'], writes=['rstd'])
        for dst, dkeys in outs:
            P.op('dve', lambda e, dst=dst: e.scalar_tensor_tensor(dst, src, gain, self.rstd2[:, 0:n], op0=ALU.mult, op1=ALU.mult),
                 reads=[skey, 'rstd', 'cst'], writes=dkeys)

    def gelu_mul(self, ps, pkey, n, dst, dkey):
        P = self.P
        x = self.gx[:, 0:n]
        t = self.gt[:, 0:n]
        P.op('act', lambda e: e.activation(x, ps, AF.Copy), reads=[pkey], writes=[('stg', 0)])
        P.op('dve', lambda e: e.tensor_tensor(t, x, x, op=ALU.mult), reads=[('stg', 0)], writes=[('stg', 1)])
        P.op('dve', lambda e: e.tensor_scalar(t, t, 0.044715, 1.0, op0=ALU.mult, op1=ALU.add), reads=[('stg', 1)], writes=[('stg', 1)])
        P.op('dve', lambda e: e.tensor_tensor(t, t, x, op=ALU.mult), reads=[('stg', 1), ('stg', 0)], writes=[('stg', 1)])
        P.op('act', lambda e: e.activation(t, t, AF.Sigmoid, scale=1.5957691216057308), reads=[('stg', 1)], writes=[('stg', 1)])
        P.op('dve', lambda e: e.tensor_tensor(x, x, t, op=ALU.mult), reads=[('stg', 0), ('stg', 1)], writes=[('stg', 0)])
        P.op('dve', lambda e: e.tensor_tensor(dst, x, dst, op=ALU.mult), reads=[('stg', 0)], writes=[dkey])

    def rg_head(self, r, W, segs, mask_flag):
        P = self.P
        prm = self.rgp
        xx = self.rxb
        xc = self.rxc
        for sg in segs:
            c0, L = sg['c0'], sg['L']
            hb = self.rhist0 if sg['hslot'] == 0 else self.rhists[:, sg['hslot'] - 1, :]
            P.op('dve', lambda e, sg=sg, hb=hb: e.tensor_copy(hb[:, 0:3], sg['hist']), reads=sg['hist_keys'], writes=[('rhist', sg['hslot'])])
            P.op('act', lambda e, hb=hb, c0=c0, L=L: e.activation(hb[:, 3:3 + L], xx[:, c0:c0 + L], AF.Copy), reads=['rxb'], writes=[('rhist', sg['hslot'])])
            P.op('dve', lambda e, hb=hb, c0=c0, L=L: e.tensor_scalar(xc[:, c0:c0 + L], hb[:, 0:L], prm[:, r, 0:1], prm[:, r, 4:5], op0=ALU.mult, op1=ALU.add),
                 reads=[('rhist', sg['hslot']), 'cst'], writes=['rxc'])
            for j in range(1, 4):
                P.op('dve', lambda e, hb=hb, c0=c0, L=L, j=j: e.scalar_tensor_tensor(xc[:, c0:c0 + L], hb[:, j:j + L], prm[:, r, j:j + 1], xc[:, c0:c0 + L], op0=ALU.mult, op1=ALU.add),
                     reads=[('rhist', sg['hslot']), 'cst'], writes=['rxc'])
            if sg.get('hist_out') is not None:
                P.op('act', lambda e, sg=sg, hb=hb, L=L: e.activation(sg['hist_out'], hb[:, L:L + 3], AF.Copy), reads=[('rhist', sg['hslot'])], writes=sg['hist_out_keys'])
        lo = min(sg['c0'] for sg in segs)
        hi = max(sg['c0'] + sg['L'] for sg in segs)
        P.op('dve', lambda e: e.tensor_copy(self.rxcb[:, lo:hi], xc[:, lo:hi]), reads=['rxc'], writes=['rxcb'])
        ga = self.rga
        gi = self.rgi
        for (t0, n) in [(t, min(512, hi - t)) for t in range(lo, hi, 512)]:
            for which, wmat, bcol, dstt in ((0, self.rgwa, 5, ga), (1, self.rgwx, 6, gi)):
                b = self.psrot((4, 5))
                pst = self.ps[b]
                P.op('pe', lambda e, pst=pst, wmat=wmat, t0=t0, n=n: e.matmul(pst[:, 0:n], wmat[:, r, :], self.rxcb[:, t0:t0 + n], start=True, stop=True),
                     reads=['rxcb', 'cst'], writes=[('ps', b)])
                P.op('act', lambda e, pst=pst, dstt=dstt, bcol=bcol, t0=t0, n=n: e.activation(dstt[:, t0:t0 + n], pst[:, 0:n], AF.Sigmoid, bias=prm[:, r, bcol:bcol + 1], scale=1.0),
                     reads=[('ps', b), 'cst'], writes=['rga' if which == 0 else 'rgi'])
        P.op('act', lambda e: e.activation(ga[:, lo:hi], ga[:, lo:hi], AF.Exp, scale=self.clam[:, r:r + 1]), reads=['rga', 'clam'], writes=['rga'])
        tm = self.rtm
        P.op('dve', lambda e: e.tensor_tensor(tm[:, lo:hi], ga[:, lo:hi], ga[:, lo:hi], op=ALU.mult), reads=['rga'], writes=['rtm'])
        P.op('dve', lambda e: e.tensor_scalar(tm[:, lo:hi], tm[:, lo:hi], -1.0, 1.0, op0=ALU.mult, op1=ALU.add), reads=['rtm'], writes=['rtm'])
        P.op('dve', lambda e: e.tensor_scalar(tm[:, lo:hi], tm[:, lo:hi], 0.0, None, op0=ALU.max), reads=['rtm'], writes=['rtm'])
        P.op('act', lambda e: e.activation(tm[:, lo:hi], tm[:, lo:hi], AF.Sqrt), reads=['rtm'], writes=['rtm'])
        P.op('dve', lambda e: e.tensor_tensor(gi[:, lo:hi], gi[:, lo:hi], xc[:, lo:hi], op=ALU.mult), reads=['rgi', 'rxc'], writes=['rgi'])
        P.op('dve', lambda e: e.tensor_tensor(gi[:, lo:hi], gi[:, lo:hi], tm[:, lo:hi], op=ALU.mult), reads=['rgi', 'rtm'], writes=['rgi'])
        if mask_flag:
            P.op('dve', lambda e: e.tensor_scalar(gi[:, lo:hi], gi[:, lo:hi], self.flag[:, 0:1], None, op0=ALU.mult), reads=['rgi', 'flag'], writes=['rgi'])
        for sg in segs:
            c0, L = sg['c0'], sg['L']
            P.op('dve', lambda e, sg=sg, c0=c0, L=L: e.tensor_tensor_scan(tm[:, c0:c0 + L], ga[:, c0:c0 + L], gi[:, c0:c0 + L], sg['init'], ALU.mult, ALU.add),
                 reads=['rga', 'rgi'] + sg['init_keys'], writes=['rtm'])
            if sg.get('state_out') is not None:
                P.op('act', lambda e, sg=sg, c0=c0, L=L: e.activation(sg['state_out'], tm[:, c0 + L - 1:c0 + L], AF.Copy), reads=['rtm'], writes=sg['state_out_keys'])
            for (dst, dkeys, s0, sl) in sg.get('h_out', []):
                P.op('act', lambda e, dst=dst, c0=c0, s0=s0, sl=sl: e.activation(dst, tm[:, c0 + s0:c0 + s0 + sl], AF.Copy), reads=['rtm'], writes=dkeys)

    def l0_inproj(self):
        P = self.P
        W = self.D['w_in_p']
        D = self.D
        NPP = 4
        for pas in range(NPP + 1):
            ismain = (pas == NPP)
            if not ismain:
                p0 = pas * 256
                self.rmsnorm(None, None, self.gains[:, 0, :], lambda c, t0, n: self.hTp[:, c, 0:n], lambda c: ('hTp', c), [(0, 256)],
                             fetch=lambda c, t0, n, p0=p0: D['xT_pre'][:, c, p0:p0 + n])
                xt, xkf = self.hTp, (lambda k: ('hTp', k))
                fm_tiles = [(0, 256)]
                tm_tiles = [(0, 128, ('ctx', pas * 2)), (128, 128, ('ctx', pas * 2 + 1))]
                W_ = 256
            else:
                self.rmsnorm(None, None, self.gains[:, 0, :], lambda c, t0, n: self.hT[:, c, t0:t0 + n], lambda c: ('hT', c), TT,
                             fetch=lambda c, t0, n: D['xT_main'][:, c, t0:t0 + n])
                xt, xkf = self.hT, (lambda k: ('hT', k))
                fm_tiles = list(TT)
                tm_tiles = [(32 + 128 * j, 128, ('ctx', 8 + j)) for j in range(8)] + ([] if os.environ.get('K_NOSMP') else [(1056 + 8 * s, 8, ('smp', s)) for s in range(4)])
                W_ = NT
            fmparts = [(xt, xkf, t0, n, None) for (t0, n) in fm_tiles]
            kcol0 = 0 if ismain else 0
            kbase = 1024 if ismain else pas * 256

            for r in range(8):
                def epi_rx(m, tag, t0, n, ps, pkey, r=r, ismain=ismain, pas=pas, W_=W_):
                    P.op('act', lambda e: e.activation(self.rxb[:, t0:t0 + n], ps, AF.Copy), reads=[pkey], writes=['rxb'])
                    if t0 + n == W_:
                        if not ismain:
                            sg = dict(c0=0, L=256, hslot=0, hist=self.convh[:, r, :], hist_keys=[('convh', r)],
                                      hist_out=self.convh[:, r, :], hist_out_keys=[('convh', r)],
                                      init=self.hst[:, r:r + 1], init_keys=[('hst', r)],
                                      state_out=self.hst[:, r:r + 1], state_out_keys=[('hst', r)])
                            if pas == NPP - 1:
                                sg['h_out'] = [(self.mixT[:, r, 0:32], [('aT', r)], 224, 32)]
                            if os.environ.get('K_NORG') is None:
                                self.rg_head(r, 256, [sg], True)
                        else:
                            segs = [dict(c0=32, L=1024, hslot=0, hist=self.convh[:, r, :], hist_keys=[('convh', r)],
                                         hist_out=self.rgo_conv[:, r, 0, :], hist_out_keys=['rgo_conv'],
                                         init=self.hst[:, r:r + 1], init_keys=[('hst', r)],
                                         state_out=self.rgo_h[:, r, 0:1], state_out_keys=['rgo_h'],
                                         h_out=[(self.mixT[:, r, 32:1056], [('aT', r)], 0, 1024)])]
                            for s_ in range(4):
                                segs.append(dict(c0=1056 + 8 * s_, L=8, hslot=1 + s_, hist=self.stconv[:, r, s_, :], hist_keys=['cst'],
                                                 hist_out=self.rgo_conv[:, r, 1 + s_, :], hist_out_keys=['rgo_conv'],
                                                 init=self.sth[:, r, s_:s_ + 1], init_keys=['cst'],
                                                 state_out=self.rgo_h[:, r, 1 + s_:2 + s_], state_out_keys=['rgo_h'],
                                                 h_out=[(self.mixT[:, r, 1056 + 8 * s_:1064 + 8 * s_], [('aT', r)], 0, 8)]))
                            if os.environ.get('K_NORG') is None:
                                self.rg_head(r, NT, segs, False)
                self.dense_fm(self.wview(W, 0, 16, r * 128, 128), 16, 1, fmparts, epi_rx)

            for which, col0, KT, gcol, oname in ((0, 3072, self.KTs, 1, 'ksT_o'), (1, 3328, self.KTw, 2, 'kwT_o')):
                def epi_k(m, tag, t0, n, ps, pkey, KT=KT, gcol=gcol, oname=oname, ismain=ismain, kbase=kbase, which=which):
                    tq = self.tmpq[:, 0:n]
                    P.op('act', lambda e: e.activation(tq, ps, AF.Copy), reads=[pkey], writes=['tmpf'])
                    outs = [(KT[:, m, kbase + t0:kbase + t0 + n], [('KT', which, m)])]
                    if ismain:
                        stg, skey = self.stage()
                        outs.append((stg[:, 0:n], [skey]))
                    self.headnorm(tq, 'tmpf', n, self.kg[:, gcol:gcol + 1], outs)
                    if ismain:
                        P.dma('sp', D[oname][:, m, t0:t0 + n], stg[:, 0:n], reads=[skey], writes=[(oname, m, t0)])
                self.dense_fm(self.wview(W, 0, 16, col0, 256), 16, 2, fmparts, epi_k)

            tmparts = [(xt, xkf, t0, n, tag) for (t0, n, tag) in tm_tiles]
            for half_ in range(2):
                def epi_kcvc(tag, t0, n, ps, pkey, half_=half_, ismain=ismain):
                    kind, idx = tag
                    if kind == 'ctx':
                        P.op('act', lambda e: e.activation(self.kcpg[:, idx % 8, half_, :], ps, AF.Copy), reads=[pkey], writes=[('kcpg', idx % 8, half_)])
                    if ismain:
                        stg, skey = self.stage()
                        P.op('dve', lambda e: e.tensor_copy(stg[0:n, 0:256], ps), reads=[pkey], writes=[skey])
                        nm = ('kc_o', 'vc_o')[half_] if kind == 'ctx' else ('kc_s_o', 'vc_s_o')[half_]
                        r0 = (idx - 8) * 128 if kind == 'ctx' else idx * 8
                        if os.environ.get('K_NODMA') is None:
                            P.dma('sp', D[nm][r0:r0 + n, :], stg[0:n, 0:256], reads=[skey], writes=[(nm, r0)])
                self.dense_tm(self.wview(W, 0, 16, 3584 + 256 * half_, 256), 16, 256, tmparts, epi_kcvc)
            if pas >= NPP - 1:
                kk = [('kcpg', i_, hf_) for i_ in range(8) for hf_ in range(2)]
                self.job(None, lambda kk=kk, pg0=(8 if ismain else 0): self.compress1(self.kcpg, 8, kk, 'data', pg0=pg0))
            for half_ in range(2):
                def epi_v(tag, t0, n, ps, pkey, half_=half_, ismain=ismain):
                    kind, idx = tag
                    Vt = (self.Vs, self.Vw)[half_]
                    if kind == 'ctx':
                        dst = Vt[0:n, idx, :, 0:64]
                        dk = [('V', half_, idx)]
                    else:
                        dst = (self.Vs_s, self.Vw_s)[half_][0:n, idx, :, 0:64]
                        dk = [('Vsm', half_, idx)]
                    P.op('act', lambda e: e.activation(dst, ps.rearrange("p (g d) -> p g d", g=4), AF.Copy), reads=[pkey], writes=dk)
                    if ismain:
                        stg, skey = self.stage()
                        P.op('dve', lambda e: e.tensor_copy(stg[0:n, 0:256], ps), reads=[pkey], writes=[skey])
                        nm = ('vs_o', 'vw_o')[half_] if kind == 'ctx' else ('vs_s_o', 'vw_s_o')[half_]
                        r0 = (idx - 8) * 128 if kind == 'ctx' else idx * 8
                        P.dma('sp', D[nm][r0:r0 + n, :], stg[0:n, 0:256], reads=[skey], writes=[(nm, r0)])
                self.dense_tm(self.wview(W, 0, 16, 4096 + 256 * half_, 256), 16, 256, tmparts, epi_v)

            if ismain:
                for r in range(8):
                    def epi_rg(m, tag, t0, n, ps, pkey, r=r):
                        self.gelu_mul(ps, pkey, n, self.mixT[:, r, t0:t0 + n], ('aT', r))
                    self.dense_fm(self.wview(W, 0, 16, 1024 + r * 128, 128), 16, 1, fmparts, epi_rg)
                for i in range(8):
                    def epi_q(m, tag, t0, n, ps, pkey, i=i):
                        tq = self.tmpq[:, 0:n]
                        P.op('act', lambda e: e.activation(tq, ps, AF.Copy), reads=[pkey], writes=['tmpf'])
                        self.headnorm(tq, 'tmpf', n, self.kg[:, 3:4], [(self.qT[:, i, t0:t0 + n], [('qT', i), 'rga', 'rgi', 'rtm', 'rxc'])])
                    self.dense_fm(self.wview(W, 0, 16, 2048 + i * 128, 128), 16, 1, fmparts, epi_q)
                gl_tiles = [(0, 32, 0)] + [(32 + 128 * j, 128, 1 + j) for j in range(8)] + [(1056 + 8 * s, 8, 9 + s) for s in range(4)]
                glparts = [(xt, xkf, t0, n, gi_) for (t0, n, gi_) in gl_tiles]

                def epi_gl(tag, t0, n, ps, pkey):
                    P.op('act', lambda e: e.activation(self.gates[0:n, tag, :], ps, AF.Sigmoid), reads=[pkey], writes=[('gates', tag)])
                self.dense_tm(self.wview(W, 0, 16, 4608, 48), 16, 48, glparts, epi_gl)

    def compress1(self, X, npg, xkeys, kind, pg0=0, kvs=(0, 1), load_w=True):
        P = self.P
        cx = self.cx
        cw1b_, hx_, htt_, hid_ = cx['cw1b'], cx['hx'], cx['htt'], cx['hid']
        for kv in kvs:
            w1 = cw1b_
            if load_w:
                P.dma('pool', w1[:], self.D['cw1k' if kv == 0 else 'cw1v'], writes=['cw1'])
            for half in range(2):
                if kind == 'pe' and half == 1:
                    continue
                b = self.psrot((4, 5))
                pst = self.ps[b]
                lo, hi = half * 64, half * 64 + 64
                Xv = X[lo:hi, 0:npg, kv, :].rearrange("p n (g d) -> p n g d", g=4)
                for d in range(64):
                    P.op('pe', lambda e, pst=pst, w1=w1, Xv=Xv, d=d, lo=lo, hi=hi: e.matmul(
                        pst[0:64, 0:npg * 4], w1[lo:hi, d, :], Xv[:, :, :, d], start=(d == 0), stop=(d == 63)),
                        reads=list(xkeys) + ['cw1'], writes=[('ps', b)], inc=(d == 63), mode='k64m64')
                psv = pst[0:64, 0:npg * 4]
                if kind == 'pe':
                    P.op('act', lambda e, psv=psv, kv=kv: e.activation(self.cpe[:, kv, :], psv, AF.Copy), reads=[('ps', b)], writes=['cpe'])
                    continue
                hx = hx_[:, 0:npg * 4]
                hxv = hx.rearrange("p (n g) -> p n g", g=4)
                ht = htt_[:, 0:npg * 4]
                P.op('act', lambda e, psv=psv, hx=hx: e.activation(hx, psv, AF.Copy), reads=[('ps', b)], writes=['hx'])
                for g in range(4):
                    P.op('dve', lambda e, hxv=hxv, g=g, kv=kv: e.tensor_scalar(hxv[:, :, g], hxv[:, :, g], self.cpe[:, kv, g:g + 1], None, op0=ALU.add),
                         reads=['hx', 'cpe'], writes=['hx'])
                P.op('dve', lambda e, hx=hx, ht=ht: e.tensor_tensor(ht, hx, hx, op=ALU.mult), reads=['hx'], writes=['htt'])
                P.op('dve', lambda e, ht=ht: e.tensor_scalar(ht, ht, 0.044715, 1.0, op0=ALU.mult, op1=ALU.add), reads=['htt'], writes=['htt'])
                P.op('dve', lambda e, hx=hx, ht=ht: e.tensor_tensor(ht, ht, hx, op=ALU.mult), reads=['htt', 'hx'], writes=['htt'])
                P.op('act', lambda e, ht=ht: e.activation(ht, ht, AF.Sigmoid, scale=1.5957691216057308), reads=['htt'], writes=['htt'])
                dst = hid_[:, kv, 2 * pg0 + half:2 * (pg0 + npg):2, :]
                P.op('dve', lambda e, hx=hx, ht=ht, dst=dst: e.tensor_tensor(dst, hx.rearrange("p (n g) -> p n g", g=4), ht.rearrange("p (n g) -> p n g", g=4), op=ALU.mult),
                     reads=['hx', 'htt'], writes=['hidT'])

    def compress2(self, nblk, kcbT, vcb):
        P = self.P
        cx = self.cx
        cw2k_, cw2v_, vtf_, identf_, hid_ = cx['cw2k'], cx['cw2v'], cx['vtf'], cx['identf'], cx['hid']
        for gp in range(2):
            b = self.psrot((4, 5))
            pst = self.ps[b]
            for lh in range(2):
                P.op('pe', lambda e, pst=pst, lh=lh, gp=gp: e.matmul(pst[:, 0:nblk], cw2k_[:, lh, :], hid_[:, 0, 0:nblk, 2 * gp + lh],
                                                                     start=(lh == 0), stop=(lh == 1)),
                     reads=['hidT', 'cw2'], writes=[('ps', b)], inc=(lh == 1), mode='k64m128')
            tq = self.tmpq[:, 0:nblk]
            P.op('act', lambda e, pst=pst, tq=tq: e.activation(tq, pst[:, 0:nblk], AF.Copy), reads=[('ps', b)], writes=['tmpf'])
            self.headnorm(tq, 'tmpf', nblk, self.kg[:, 0:1], [(kcbT[:, gp, 0:nblk], ['kcbT'])])
        for g in range(4):
            b = self.psrot((4, 5))
            pst = self.ps[b]
            P.op('pe', lambda e, pst=pst, g=g: e.matmul(pst[0:64, 0:nblk], cw2v_[:, :], hid_[:, 1, 0:nblk, g], start=True, stop=True),
                 reads=['hidT', 'cw2'], writes=[('ps', b)], mode='k64m64')
            vt = vtf_[:, 0:nblk]
            P.op('act', lambda e, pst=pst, vt=vt: e.activation(vt, pst[0:64, 0:nblk], AF.Copy), reads=[('ps', b)], writes=['vtf'])
            b2 = self.psrot((4, 5))
            pst2 = self.ps[b2]
            P.op('pe', lambda e, pst2=pst2, vt=vt: e.transpose(pst2[0:nblk, 0:64], vt, identf_[0:64, 0:64]), reads=['vtf', 'identf'], writes=[('ps', b2)], mode='k64m64t')
            P.op('act', lambda e, pst2=pst2, g=g: e.activation(vcb[0:nblk, g, 0:64], pst2[0:nblk, 0:64], AF.Copy), reads=[('ps', b2)], writes=['vcb'])

    def load_qm(self, h, qc0, N):
        P = self.P
        g = h // 4
        hp = g % 2
        qi, _ = self.qchunk(h)
        lo, hi = hp * 64, hp * 64 + 64
        k = hp * 2 + (self.qmi[hp] % 2)
        self.qmi[hp] += 1
        qm = self.qm[k]
        P.op('act', lambda e: e.activation(qm[lo:hi, 0:N], self.qT[lo:hi, qi, qc0:qc0 + N], AF.Copy), reads=[('qT', qi)], writes=[('qm', k)])
        return qm, ('qm', k)

    def qchunk(self, h):
        if (h % 8) < 4:
            return (h % 4) + 4 * (h // 8), 0
        return ((h - 4) % 4) + 4 * ((h - 4) // 8), 1

    def softmax_unit(self, pst, b, nk, N, h, x0, branch):
        P = self.P
        i = self.pti % 2
        self.pti += 1
        pT = self.pT[i]
        pk = ('pT', i)
        if branch == 'win' and x0 >= 384:
            tt = self.ptmp[i]
            P.op('dve', lambda e: e.scalar_tensor_tensor(tt[0:nk, 0:N], pst[0:nk, 0:N], SCALE, self.mw[0:nk, x0 - 384:x0 - 384 + N], op0=ALU.mult, op1=ALU.add),
                 reads=[('ps', b), 'mw'], writes=[('ptmp', i)])
            P.op('act', lambda e: e.activation(pT[0:nk, 0:N], tt[0:nk, 0:N], AF.Exp, bias=self.chd[0:nk, h:h + 1], scale=1.0),
                 reads=[('ptmp', i), 'chd'], writes=[pk])
        elif x0 >= 256:
            P.op('act', lambda e: e.activation(pT[0:nk, 0:N], pst[0:nk, 0:N], AF.Exp, bias=self.chd[0:nk, h:h + 1], scale=SCALE),
                 reads=[('ps', b), 'chd'], writes=[pk])
        else:
            tt = self.ptmp[i]
            P.op('dve', lambda e: e.scalar_tensor_tensor(tt[0:nk, 0:N], pst[0:nk, 0:N], SCALE, self.G[0:nk, h, x0:x0 + N], op0=ALU.mult, op1=ALU.add),
                 reads=[('ps', b), 'G'], writes=[('ptmp', i)])
            P.op('act', lambda e: e.activation(pT[0:nk, 0:N], tt[0:nk, 0:N], AF.Exp), reads=[('ptmp', i)], writes=[pk])
        return pT, pk

    def branch_epilogue(self, po, pob, N, h, gate_ap, first):
        P = self.P
        rd = self.rden[0:N, 0:1]
        P.op('dve', lambda e: e.tensor_scalar(rd, po[0:N, 64:65], 1e-30, None, op0=ALU.max), reads=[('ps', pob)], writes=['rden'])
        P.op('dve', lambda e: e.reciprocal(rd, rd), reads=['rden'], writes=['rden'])
        sg = self.rden[0:N, 1:2]
        P.op('dve', lambda e: e.tensor_tensor(sg, rd, gate_ap, op=ALU.mult), reads=['rden', 'gates'], writes=['rden2'])
        om = self.omix[0:N, h, :]
        if first:
            P.op('dve', lambda e: e.tensor_scalar(om, po[0:N, 0:64], sg, None, op0=ALU.mult), reads=[('ps', pob), 'rden2'], writes=[('omix', h)])
        else:
            P.op('dve', lambda e: e.scalar_tensor_tensor(om, po[0:N, 0:64], sg, om, op0=ALU.mult, op1=ALU.add), reads=[('ps', pob), 'rden2'], writes=[('omix', h)])
        return rd

    def attn_prompt_tile(self, ti, qc0, N, qpos0):
        P = self.P
        D = self.D
        P.dma('sp', self.selmul[0:N, :], D['selmul'][qc0:qc0 + N, :], writes=['selmul'])
        P.dma('sp', self.seladd[0:N, :], D['seladd'][qc0:qc0 + N, :], writes=['seladd'])
        for hq in range(4):
            for hh in range(4):
                h = hq * 4 + hh
                src = bass.AP(self.vd.tensor, h * DL + DOFF + qpos0 - 63 - 64 * 31, [[64, 32], [1, N]])
                P.dma('sp', self.bcrev[0:32, hh, 0:N], src, reads=['vd'], writes=[('bcrev', hh)])
            b = self.psrot((4, 5))
            pst = self.ps[b]
            for hh in range(4):
                P.op('pe', lambda e, pst=pst, hh=hh: e.matmul(pst[:, hh * 128:hh * 128 + N], self.j32p[:, :], self.bcrev[:, hh, 0:N], start=True, stop=True),
                     reads=[('bcrev', hh), 'jmat'], writes=[('ps', b)], inc=True)
            for hh in range(4):
                P.op('act', lambda e, pst=pst, hh=hh: e.activation(self.bc[:, hh, 0:N], pst[0:32, hh * 128:hh * 128 + N], AF.Copy),
                     reads=[('ps', b)], writes=['bc'])
            for hh in range(4):
                h = hq * 4 + hh
                g = h // 4
                gp, hp = g // 2, g % 2
                qi, qh = self.qchunk(h)
                assert qh == hp
                lo, hi = hp * 64, hp * 64 + 64
                b = self.psrot((0, 1, 2, 3))
                pst = self.ps[b]
                qm, qmk = self.load_qm(h, qc0, N)
                P.op('pe', lambda e, pst=pst, gp=gp, qm=qm: e.matmul(pst[:, 0:N], self.kcbT[:, gp, 0:128], qm[:, 0:N], start=True, stop=True),
                     reads=['kcbT', qmk], writes=[('ps', b)])
                i = self.pti % 2
                self.pti += 1
                tt, pT = self.ptmp[i], self.pT[i]
                P.op('dve', lambda e, pst=pst, tt=tt, hh=hh: e.scalar_tensor_tensor(tt[0:32, 0:N], pst[0:32, 0:N], SCALE, self.bc[:, hh, 0:N], op0=ALU.mult, op1=ALU.add),
                     reads=[('ps', b), 'bc'], writes=[('ptmp', i)])
                P.op('act', lambda e, tt=tt, pT=pT: e.activation(pT[0:32, 0:N], tt[0:32, 0:N], AF.Exp), reads=[('ptmp', i)], writes=[('pT', i)])
                pob = 6 + (h % 2)
                po = self.ps[pob]
                P.op('pe', lambda e, po=po, pT=pT, g=g: e.matmul(po[:, 0:98], pT[:, 0:128], self.vcb[:, g, 0:98], start=True, stop=True),
                     reads=[('pT', i), 'vcb'], writes=[('ps', pob)])
                rd = self.branch_epilogue(po, pob, N, h, self.gates[0:N, ti, h:h + 1], True)
                im = self.imp[0:N, g, :]
                if h % 4 == 0:
                    P.op('dve', lambda e, po=po, rd=rd, im=im: e.tensor_scalar(im, po[0:N, 66:98], rd, None, op0=ALU.mult), reads=[('ps', pob), 'rden'], writes=[('imp', g)])
                else:
                    P.op('dve', lambda e, po=po, rd=rd, im=im: e.scalar_tensor_tensor(im, po[0:N, 66:98], rd, im, op0=ALU.mult, op1=ALU.add), reads=[('ps', pob), 'rden'], writes=[('imp', g)])
        for g in range(4):
            im = self.imp[0:N, g, :]
            P.op('dve', lambda e, im=im: e.tensor_tensor(im, im, self.selmul[0:N, :], op=ALU.mult), reads=['selmul'], writes=[('imp', g)])
            P.op('dve', lambda e, im=im: e.tensor_tensor(im, im, self.seladd[0:N, :], op=ALU.add), reads=['seladd'], writes=[('imp', g)])
            P.op('dve', lambda e, im=im: e.max(self.mx8[0:N, :], im), reads=[('imp', g)], writes=['mx8'])
            P.op('dve', lambda e, im=im: e.match_replace(self.impw[0:N, :], self.mx8[0:N, :], im, -3.0e38), reads=[('imp', g), 'mx8'], writes=['impw'])
            P.op('dve', lambda e: e.max(self.mx8[0:N, :], self.impw[0:N, :]), reads=['impw'], writes=['mx8'])
            P.op('dve', lambda e, im=im: e.tensor_scalar(self.impw[0:N, :], im, self.mx8[0:N, 7:8], None, op0=ALU.is_ge), reads=[('imp', g), 'mx8'], writes=['impw'])
            P.op('dve', lambda e, g=g: e.tensor_scalar(self.negM[0:N, g, :], self.impw[0:N, :], -1.0, -MASKNEG, op0=ALU.add, op1=ALU.mult), reads=['impw'], writes=[('negM', g)])
        for br, KT, Vt, gofs in (('sel', self.KTs, self.Vs, 16), ('win', self.KTw, self.Vw, 32)):
            for h in range(16):
                g = h // 4
                gp, hp = g // 2, g % 2
                qi, _ = self.qchunk(h)
                lo, hi = hp * 64, hp * 64 + 64
                ktmax = (qpos0 + N - 1) // 128
                kts = []
                for kt in range(ktmax + 1):
                    x0 = qpos0 - 128 * kt
                    if br == 'win' and x0 >= 640:
                        continue
                    kts.append((kt, x0))
                pob = 6 + (h % 2)
                po = self.ps[pob]
                qm, qmk = self.load_qm(h, qc0, N)
                pend = None
                for idx, (kt, x0) in enumerate(kts):
                    kcol0 = 128 * kt if kt < 8 else 1024 + 32 + 128 * (kt - 8)
                    b = self.psrot((0, 1, 2, 3))
                    pst = self.ps[b]
                    P.op('pe', lambda e, pst=pst, gp=gp, kcol0=kcol0, KT=KT, br=br, qm=qm: e.matmul(
                        pst[:, 0:N], KT[:, gp, kcol0:kcol0 + 128], qm[:, 0:N], start=True, stop=(br != 'sel')),
                        reads=[('KT', 0 if br == 'sel' else 1, gp), qmk], writes=[('ps', b)], inc=(br != 'sel'))
                    if br == 'sel':
                        if idx == 0 and h % 4 == 0:
                            P.op('dve', lambda e, g=g: e.tensor_copy(self.mexp[0:N, :, :], self.negM[0:N, g, :].unsqueeze(2).to_broadcast([N, 32, 64])),
                                 reads=[('negM', g)], writes=['mexp'])
                        mk = self.mexp[:, 2 * kt:2 * kt + 2, :].rearrange("p a d -> p (a d)")
                        P.op('pe', lambda e, pst=pst, mk=mk: e.matmul(pst[:, 0:N], mk, self.identb[:, 0:N], start=False, stop=True),
                             reads=['mexp', 'identb'], writes=[('ps', b)])
                    pT, pk = self.softmax_unit(pst, b, 128, N, h, x0, br)
                    cur = (idx, kt, pT, pk)
                    if pend is not None:
                        i2, kt2, pT2, pk2 = pend
                        P.op('pe', lambda e, po=po, pT2=pT2, kt2=kt2, g=g, Vt=Vt, i2=i2, nk=len(kts): e.matmul(
                            po[:, 0:66], pT2[:, 0:128], Vt[:, kt2, g, 0:66], start=(i2 == 0), stop=(i2 == nk - 1)),
                            reads=[pk2, ('V', 0 if br == 'sel' else 1, kt2)], writes=[('ps', pob)], inc=(i2 == len(kts) - 1))
                    pend = cur
                i2, kt2, pT2, pk2 = pend
                P.op('pe', lambda e, po=po, pT2=pT2, kt2=kt2, g=g, Vt=Vt, i2=i2, nk=len(kts): e.matmul(
                    po[:, 0:66], pT2[:, 0:128], Vt[:, kt2, g, 0:66], start=(i2 == 0), stop=(i2 == nk - 1)),
                    reads=[pk2, ('V', 0 if br == 'sel' else 1, kt2)], writes=[('ps', pob)], inc=True)
                self.branch_epilogue(po, pob, N, h, self.gates[0:N, ti, gofs + h:gofs + h + 1], False)
        for c in range(8):
            b = self.psrot((4, 5))
            pst = self.ps[b]
            src = self.omix[:, 2 * c:2 * c + 2, :].rearrange("p a d -> p (a d)")
            P.op('pe', lambda e, pst=pst, src=src: e.transpose(pst[:, 0:128], src, self.identf[:, :]), reads=[('omix', 2 * c), ('omix', 2 * c + 1), 'identf'], writes=[('ps', b)])
            P.op('act', lambda e, pst=pst, c=c: e.activation(self.mixT[:, 8 + c, qc0:qc0 + N], pst[:, 0:N], AF.Copy), reads=[('ps', b)], writes=[('aT', 8 + c)])

    def attn_setup(self, A):
        P, D = self.P, self.D
        self.identf = A("identf", [128, 128], F32)
        self.identb = A("identb", [128, 128], BF16)
        self.jmat = A("jmat", [128, 128], F32)
        self.G = A("G", [128, 16, 256], F32)
        self.chd = A("chd", [128, 16], F32)
        self.mw = A("mw", [128, 256], F32)
        self.tab = A("tab", [33, 16], F32)
        self.e1 = [A("e1_%d" % i, [33, 512], F32) for i in range(1)]
        self.vds = [A("vds%d" % i, [16, 512], F32) for i in range(1)]
        self.grev = [A("grev%d" % i, [128, 256], F32) for i in range(1)]
        self.bcrev = A("bcrev", [128, 4, 128], F32)
        self.j32p = A("j32p", [128, 128], F32)
        self.qm = [A("qm%d" % i, [128, 128], BF16) for i in range(4)]
        self.qmi = [0, 0]
        self.bc = A("bc", [32, 4, 128], F32)
        self.mexp = A("mexp", [128, 32, 64], BF16)
        self.selmul = A("selmul", [128, 32], F32)
        self.seladd = A("seladd", [128, 32], F32)
        self.pT = [A("pT%d" % i, [128, 128], BF16) for i in range(2)]
        self.ptmp = [A("ptmp%d" % i, [128, 128], F32) for i in range(2)]
        self.pti = 0
        self.rden = A("rden", [128, 2], F32)
        self.omix = A("omix", [128, 16, 64], F32)
        self.imp = A("imp", [128, 4, 32], F32)
        self.impw = A("impw", [128, 32], F32)
        self.mx8 = A("mx8", [128, 8], F32)
        self.negM = A("negM", [128, 4, 32], BF16)
        P.dma('sp', self.identf[:], D['ident'], writes=['identf'])
        P.dma('pool', self.identb[:], D['ident'], writes=['identb'])
        P.dma('sp', self.jmat[:], D['jmat'], writes=['jmat'])
        P.dma('sp', self.j32p[:], D['j32p'], writes=['jmat'])
        P.op('pool', lambda e: e.memset(self.bcrev[:], 0.0), writes=[('bcrev', i_) for i_ in range(4)])
        for i_ in range(4):
            P.op('pool', lambda e, i_=i_: e.memset(self.qm[i_][:], 0.0), writes=[('qm', i_)])
        for i_ in range(2):
            P.op('pool', lambda e, i_=i_: e.memset(self.pT[i_][:], 0.0), writes=[('pT', i_)])
        P.op('pool', lambda e: e.memset(self.mexp[:], 0.0), writes=['mexp'])
        P.op('pool', lambda e: e.memset(self.omix[:], 0.0), writes=[('omix', i_) for i_ in range(16)])
        P.dma('sp', self.mw[:], D['mw'], writes=['mw'])
        P.dma('sp', self.tab[:], D['tabaug'], writes=['tab'])
        for ch in range(DL // 512):
            i = 0
            P.dma('sp', self.e1[i][:], D['e1h'][:, ch * 512:(ch + 1) * 512], writes=[('e1', i)])
            b = self.psrot((4, 5))
            pst = self.ps[b]
            P.op('pe', lambda e, pst=pst, i=i: e.matmul(pst[0:16, 0:512], self.tab[:, :], self.e1[i][:, :], start=True, stop=True),
                 reads=['tab', ('e1', i)], writes=[('ps', b)], mode='k33m16')
            P.op('act', lambda e, pst=pst, i=i: e.activation(self.vds[i][:, :], pst[0:16, 0:512], AF.Copy), reads=[('ps', b)], writes=[('vds', i)])
            P.dma('sp', self.vd[:, ch * 512:(ch + 1) * 512], self.vds[i][:, :], reads=[('vds', i)], writes=['vd'])
        for h in range(16):
            i = 0
            src = bass.AP(self.vd.tensor, h * DL + DOFF - 127, [[1, 128], [1, 256]])
            P.dma('sp', self.grev[i][:, :], src, reads=['vd'], writes=[('grev', i)])
            b = self.psrot((4, 5))
            pst = self.ps[b]
            P.op('pe', lambda e, pst=pst, i=i: e.matmul(pst[:, 0:256], self.jmat[:, :], self.grev[i][:, :], start=True, stop=True),
                 reads=['jmat', ('grev', i)], writes=[('ps', b)])
            P.op('act', lambda e, pst=pst, h=h: e.activation(self.G[:, h, :], pst[:, 0:256], AF.Copy), reads=[('ps', b)], writes=['G'])
        P.op('dve', lambda e: e.tensor_copy(self.chd[:, :], self.G[:, :, 255]), reads=['G'], writes=['chd'])

    def save_sample_bias(self):
        P = self.P
        t0 = self.ptmp[0][:, :].rearrange("p (h t) -> p h t", h=16)
        t1 = self.ptmp[1][:, :].rearrange("p (h t) -> p h t", h=16)
        sbd = self.sbd

        def put(k):
            P.dma('sp', sbd[k], self.ptmp[0][:, :], reads=[('ptmp', 0)], writes=[('sbd', k)])
        P.op('dve', lambda e: e.tensor_copy(t0, self.G[:, :, 128:136]), reads=['G'], writes=[('ptmp', 0)])
        put(0)
        P.op('pool', lambda e: e.memset(self.ptmp[0][:, :], MASKNEG), writes=[('ptmp', 0)])
        P.op('dve', lambda e: e.tensor_copy(t0[0:8], self.G[0:8, :, 0:8]), reads=['G'], writes=[('ptmp', 0)])
        put(1)
        for k, c0 in ((2, 128), (3, 0)):
            P.op('dve', lambda e, c0=c0: e.tensor_copy(t0, self.mw[:, c0:c0 + 8].unsqueeze(1).to_broadcast([128, 16, 8])), reads=['mw'], writes=[('ptmp', 0)])
            P.op('dve', lambda e: e.tensor_tensor(t0, t0, self.chd[:, :].unsqueeze(2).to_broadcast([128, 16, 8]), op=ALU.add), reads=['chd'], writes=[('ptmp', 0)])
            put(k)
        P.op('dve', lambda e: e.tensor_copy(t0, self.chd[:, :].unsqueeze(2).to_broadcast([128, 16, 8])), reads=['chd'], writes=[('ptmp', 0)])
        put(4)
        src = bass.AP(self.vd.tensor, DOFF + 8192 - 63 - 64 * 127, [[64, 128], [DL, 16], [1, 8]])
        P.dma('sp', t1, src, reads=['vd'], writes=[('ptmp', 1)])
        b = self.psrot((4, 5))
        pst = self.ps[b]
        P.op('pe', lambda e: e.matmul(pst[:, 0:128], self.jmat[:, :], self.ptmp[1][:, :], start=True, stop=True), reads=['jmat', ('ptmp', 1)], writes=[('ps', b)])
        P.op('act', lambda e: e.activation(self.ptmp[0][:, :], pst[:, 0:128], AF.Copy), reads=[('ps', b)], writes=[('ptmp', 0)])
        put(5)

    def attn_sample(self, S, n_pool):
        P, D = self.P, self.D
        NPGS = 4
        identf2 = S("identf2", [128, 128], F32)
        identb2 = S("identb2", [128, 128], BF16)
        sbias = S("sbias", [128, 6, 128], F32)
        rep = S("rep", [128, 128], F32)
        nsel = S("nsel", [128, 4], F32)
        seln = S("seln", [128, 4, 8], BF16)
        selsum = S("selsum", [128, 128], BF16)
        t0f = S("t0f", [128, 128], F32)
        selas = S("selas", [128, 128], F32)
        cw1b2 = S("cw1b2", [128, 64, 64], BF16)
        cw2k2 = S("cw2k2", [64, 2, 128], BF16)
        cw2v2 = S("cw2v2", [64, 64], BF16)
        hx2 = S("hx2", [64, 32], F32)
        htt2 = S("htt2", [64, 32], F32)
        vtf2 = S("vtf2", [64, 128], F32)
        Xs = S("Xs", [128, NPGS, 2, 256], BF16)
        hidS = S("hidS", [64, 2, 128, 4], BF16)
        self.cx = dict(cw1b=cw1b2, hx=hx2, htt=htt2, hid=hidS, cw2k=cw2k2, cw2v=cw2v2, vtf=vtf2, identf=identf2)
        vcbS = S("vcbS", [128, 4, 196], BF16)
        KTp = [S("KTp%d" % i, [128, 4, 128], BF16) for i in range(2)]
        Vp = [S("Vp%d" % i, [128, 4, 66], BF16) for i in range(2)]
        Vraw = [S("Vraw%d" % i, [128, 256], BF16) for i in range(2)]
        wkT = S("wkT", [128, 4, 512], BF16)
        wv = S("wv", [128, 4, 4, 66], BF16)
        indpg = [S("indpg%d" % i, [128, 128], BF16) for i in range(2)]
        negMs = S("negMs", [128, 4, 128], F32)
        negMT = S("negMT", [128, 4, 4, 8], BF16)
        qms = [S("qms%d" % i, [128, 32], BF16) for i in range(4)]
        ktn = [S("ktn%d" % i, [128, 128], BF16) for i in range(2)]
        pTs = [S("pTs%d" % i, [128, 128], BF16) for i in range(2)]
        ptm = [S("ptm%d" % i, [128, 32], F32) for i in range(2)]
        osum = S("osum", [128, 4, 64], F32)
        opad = S("opad", [128, 2, 128], BF16)
        grow = S("grow", [128, 12], F32)
        gtmp = S("gtmp", [128, 3, 4, 4], F32)
        gts = S("gts", [128, 48], F32)
        rdn = S("rdn", [128, 2], F32)
        imps = S("imps", [128, 128], F32)
        impw = S("impw2", [128, 128], F32)
        mx8 = S("mx8b", [128, 8], F32)
        pnb = S("pnb", [128, 128], BF16)
        ptb = S("ptb", [128, 64], I32)
        ptf = S("ptf", [128, 64], F32)
        iop = S("iop", [128, 2], F32)
        idxn = S("idxn", [128, 64], I32)
        idxt = S("idxt", [128, 64], I32)

        P.dma('sp', identf2[:], D['ident'], writes=['identf'])
        P.dma('pool', identb2[:], D['ident'], writes=['identb'])
        P.dma('sp', sbias[:], self.sbd.rearrange("k p c -> p k c"), reads=[('sbd', k) for k in range(6)], writes=['sbias'])
        P.dma('sp', rep[:], D['rep'], writes=['rep'])
        P.dma('sp', nsel[:], D['nsel'], writes=['nsel'])
        P.dma('pool', seln[:], D['seln'], writes=['seln'])
        P.dma('pool', selsum[:], D['selsum'], writes=['selsum'])
        P.dma('sp', t0f[:], D['t0f'], writes=['t0f'])
        P.op('pool', lambda e: e.memset(selas[:], 0.0), writes=['selas'])
        P.dma('sp', selas[0:8, :], D['seladd_s'], writes=['selas'])
        P.dma('pool', cw2k2[:], D['cw2k'], writes=['cw2'])
        P.dma('pool', cw2v2[:], D['cw2v'], writes=['cw2'])
        for t_ in (Vp[0], Vp[1], wv):
            P.op('pool', lambda e, t_=t_: e.memset(t_[:], 1.0), writes=['vinit'])
        for i_, t_ in enumerate(qms):
            P.op('pool', lambda e, t_=t_: e.memset(t_[:], 0.0), writes=[('qms', i_)])
        for i_, t_ in enumerate(ktn):
            P.op('pool', lambda e, t_=t_: e.memset(t_[:], 0.0), writes=[('ktn', i_)])
        for i_, t_ in enumerate(pTs):
            P.op('pool', lambda e, t_=t_: e.memset(t_[:], 0.0), writes=[('pTs', i_)])
        P.op('pool', lambda e: e.memset(opad[:], 0.0), writes=['opad'])
        P.op('pool', lambda e: e.memset(pnb[:], 0.0), writes=['pnb'])
        P.op('pool', lambda e: e.memset(negMs[:], 0.0), writes=['negMs'])
        P.op('pool', lambda e: e.memset(osum[:], 0.0), writes=[('osum', g_) for g_ in range(4)])
        P.op('pool', lambda e: e.memset(gts[:], 0.0), writes=['gts'])
        P.op('pool', lambda e: e.memset(imps[:], 0.0), writes=['imps'])
        P.op('pool', lambda e: e.memset(vcbS[:], 0.0), writes=['vcbS'])
        P.op('pool', lambda e: e.memset(vcbS[:, :, 64:65], 1.0), writes=['vcbS'])
        for g in range(4):
            P.op('dve', lambda e, g=g: e.tensor_copy(vcbS[:, g, 66:194], identb2[:, :]), reads=['identb'], writes=['vcbS'])
        P.dma('sp', iop[:], D['iopc'], writes=['iop'])
        pck = D['pck']
        pcv = D['pcv']
        pskT = D['pskT']
        psv = D['psv'].rearrange("n (g d) -> n g d", g=4)
        sb_ = lambda k, g: sbias[:, k, :].rearrange("p (h t) -> p h t", h=16)[:, 4 * g:4 * g + 4, :].rearrange("p h t -> p (h t)")
        pti = [0]
        qmi = [0]

        def unit(pst, b, g, bk, lhsV, vkeys, po, pob, first, last, ncols=66):
            i = pti[0] % 2
            pti[0] += 1
            P.op('dve', lambda e: e.scalar_tensor_tensor(ptm[i][:, :], pst[:, 0:32], SCALE, sb_(bk, g), op0=ALU.mult, op1=ALU.add),
                 reads=[('ps', b), 'sbias'], writes=[('ptm', i)])
            P.op('act', lambda e: e.activation(pTs[i][:, 0:32], ptm[i][:, :], AF.Exp), reads=[('ptm', i)], writes=[('pTs', i)])
            P.op('pe', lambda e: e.matmul(po[:, 0:ncols], pTs[i][:, 0:128], lhsV, start=first, stop=last),
                 reads=[('pTs', i)] + vkeys, writes=[('ps', pob)], inc=last)

        def epilogue(po, pob, g, br, first):
            rd = rdn[0:32, 0:1]
            sg = rdn[0:32, 1:2]
            P.op('dve', lambda e: e.tensor_scalar(rd, po[0:32, 64:65], 1e-30, None, op0=ALU.max), reads=[('ps', pob)], writes=['rdn'])
            P.op('dve', lambda e: e.reciprocal(rd, rd), reads=['rdn'], writes=['rdn'])
            P.op('dve', lambda e: e.tensor_tensor(sg, rd, grow[0:32, br * 4 + g:br * 4 + g + 1], op=ALU.mult), reads=['rdn', 'grow'], writes=['rdn2'])
            om = osum[0:32, g, :]
            if first:
                P.op('dve', lambda e: e.tensor_scalar(om, po[0:32, 0:64], sg, None, op0=ALU.mult), reads=[('ps', pob), 'rdn2'], writes=[('osum', g)])
            else:
                P.op('dve', lambda e: e.scalar_tensor_tensor(om, po[0:32, 0:64], sg, om, op0=ALU.mult, op1=ALU.add), reads=[('ps', pob), 'rdn2'], writes=[('osum', g)])
            return rd

        def load_q(s, g):
            hp = g % 2
            lo, hi = hp * 64, hp * 64 + 64
            k = hp * 2 + g // 2
            c0 = 4 * (g // 2)
            P.op('act', lambda e: e.activation(qms[k][lo:hi, :].rearrange("p (n t) -> p n t", n=4), self.qT[lo:hi, c0:c0 + 4, 1056 + 8 * s:1064 + 8 * s], AF.Copy),
                 reads=[('qT', c0 + n_) for n_ in range(4)], writes=[('qms', k)])
            return qms[k], ('qms', k)

        for s in range(4):
            P.dma('sp', ptb[:], D['pt'][:, 64 * s:64 * s + 64].partition_broadcast(128), reads=['idxn', 'idxt'], writes=['ptb'])
            P.op('dve', lambda e: e.tensor_copy(ptf[:], ptb[:]), reads=['ptb'], writes=['ptf'])
            P.op('dve', lambda e: e.tensor_scalar(ptf[:], ptf[:], 128.0, iop[:, 0:1], op0=ALU.mult, op1=ALU.add), reads=['ptf', 'iop'], writes=['ptf'])
            P.op('dve', lambda e: e.tensor_copy(idxn[:], ptf[:]), reads=['ptf'], writes=['idxn'])
            P.op('dve', lambda e: e.tensor_copy(ptf[:], ptb[:]), reads=['ptb', 'idxn'], writes=['ptf'])
            P.op('dve', lambda e: e.tensor_scalar(ptf[:], ptf[:], 64.0, iop[:, 1:2], op0=ALU.mult, op1=ALU.add), reads=['ptf', 'iop'], writes=['ptf'])
            P.op('dve', lambda e: e.tensor_copy(idxt[:], ptf[:]), reads=['ptf'], writes=['idxt'])
            P.op('act', lambda e, s=s: e.activation(gts[0:8, :], self.gates[0:8, 9 + s, :], AF.Copy), reads=[('gates', 9 + s)], writes=['gts'])
            b = self.psrot((0, 1, 2, 3))
            pst = self.ps[b]
            P.op('pe', lambda e, pst=pst: e.matmul(pst[:, 0:48], rep[:, :], gts[:, :], start=True, stop=True), reads=['rep', 'gts'], writes=[('ps', b)])
            P.op('dve', lambda e, pst=pst: e.tensor_tensor(gtmp[:, :, :, :], pst[:, 0:48].rearrange("p (b g n) -> p b g n", b=3, g=4),
                                                           nsel[:, :].unsqueeze(1).unsqueeze(1).to_broadcast([128, 3, 4, 4]), op=ALU.mult),
                 reads=[('ps', b), 'nsel'], writes=['gtmp'])
            P.op('dve', lambda e: e.tensor_reduce(grow[:, :], gtmp[:, :, :, :].rearrange("p b g n -> p (b g) n"), mybir.AxisListType.X, ALU.add),
                 reads=['gtmp'], writes=['grow'])
            for kv, src in ((0, pck), (1, pcv)):
                P.dma('pool', cw1b2[:], D['cw1k' if kv == 0 else 'cw1v'], writes=['cw1'])
                for st_ in range(64 // NPGS):
                    for j in range(NPGS):
                        col = st_ * NPGS + j
                        P.custom_dma('pool', lambda e, j=j, kv=kv, src=src, col=col: e.indirect_dma_start(
                            out=Xs[:, j, kv, :], out_offset=None, in_=src, in_offset=bass.IndirectOffsetOnAxis(ap=idxn[:, col:col + 1], axis=0)),
                            reads=['idxn'], writes=[('Xs', j, kv)])
                    self.compress1(Xs, NPGS, [('Xs', j, kv) for j in range(NPGS)], 'data', pg0=st_ * NPGS, kvs=(kv,), load_w=False)
            self.compress2(128, self.kcbT, vcbS)
            for g in range(4):
                gp = g // 2
                qm, qmk = load_q(s, g)
                b = self.psrot((0, 1, 2, 3))
                pst = self.ps[b]
                P.op('pe', lambda e, pst=pst, gp=gp, qm=qm: e.matmul(pst[:, 0:32], self.kcbT[:, gp, 0:128], qm[:, 0:32], start=True, stop=True),
                     reads=['kcbT', qmk], writes=[('ps', b)])
                pob = 4 + g
                po = self.ps[pob]
                unit(pst, b, g, 5, vcbS[:, g, 0:194], ['vcbS'], po, pob, True, True, ncols=194)
                rd = epilogue(po, pob, g, 0, True)
                P.op('dve', lambda e, po=po, rd=rd: e.tensor_scalar(pnb[0:32, :], po[0:32, 66:194], rd, None, op0=ALU.mult), reads=[('ps', pob), 'rdn'], writes=['pnb'])
                b2 = self.psrot((0, 1, 2, 3))
                ps2 = self.ps[b2]
                P.op('pe', lambda e, ps2=ps2: e.matmul(ps2[:, 0:128], selsum[:, :], pnb[:, :], start=True, stop=True), reads=['selsum', 'pnb'], writes=[('ps', b2)])
                P.op('dve', lambda e, ps2=ps2: e.tensor_tensor(imps[0:8, :], ps2[0:8, 0:128], selas[0:8, :], op=ALU.add), reads=[('ps', b2), 'selas'], writes=['imps'])
                P.op('dve', lambda e: e.max(mx8[0:8, :], imps[0:8, :]), reads=['imps'], writes=['mx8'])
                P.op('dve', lambda e: e.match_replace(impw[0:8, :], mx8[0:8, :], imps[0:8, :], -3.0e38), reads=['imps', 'mx8'], writes=['impw'])
                P.op('dve', lambda e: e.max(mx8[0:8, :], impw[0:8, :]), reads=['impw'], writes=['mx8'])
                P.op('dve', lambda e: e.tensor_scalar(impw[0:8, :], imps[0:8, :], mx8[0:8, 6:7], None, op0=ALU.is_ge), reads=['imps', 'mx8'], writes=['impw'])
                P.op('dve', lambda e, g=g: e.tensor_scalar(negMs[0:8, g, :], impw[0:8, :], -1.0, -MASKNEG, op0=ALU.add, op1=ALU.mult), reads=['impw'], writes=['negMs'])
                b3 = self.psrot((0, 1, 2, 3))
                ps3 = self.ps[b3]
                P.op('pe', lambda e, ps3=ps3, g=g: e.transpose(ps3[:, 0:128], negMs[:, g, :], identf2[:, :]), reads=['negMs', 'identf'], writes=[('ps', b3)])
                P.op('act', lambda e, ps3=ps3: e.activation(ptm[0][:, 0:8], ps3[:, 0:8], AF.Copy), reads=[('ps', b3)], writes=[('ptm', 0)])
                P.op('dve', lambda e, g=g: e.tensor_copy(negMT[:, g, :, :], ptm[0][:, 0:8].unsqueeze(1).to_broadcast([128, 4, 8])), reads=[('ptm', 0)], writes=[('negMT', g)])
            qq = [load_q(s, g) for g in range(4)]
            for i_ in range(64):
                col = i_
                kb = i_ % 2
                P.custom_dma('pool', lambda e, kb=kb, col=col: e.indirect_dma_start(
                    out=KTp[kb][:, :, :].rearrange("p g k -> p (g k)"), out_offset=None, in_=pskT, in_offset=bass.IndirectOffsetOnAxis(ap=idxt[:, col:col + 1], axis=0)),
                    reads=['idxt'], writes=[('KTp', kb)])
                P.custom_dma('pool', lambda e, kb=kb, col=col: e.indirect_dma_start(
                    out=Vraw[kb][:, :], out_offset=None, in_=D['psv'], in_offset=bass.IndirectOffsetOnAxis(ap=idxn[:, col:col + 1], axis=0)),
                    reads=['idxn'], writes=[('Vraw', kb)])
                P.op('act', lambda e, kb=kb: e.activation(Vp[kb][:, :, 0:64], Vraw[kb][:, :].rearrange("p (g d) -> p g d", g=4), AF.Copy),
                     reads=[('Vraw', kb), 'vinit'], writes=[('Vp', kb)])
                P.op('dve', lambda e, kb=kb, i_=i_: e.tensor_scalar(indpg[kb][:, :], t0f[:, :], float(2 * i_), None, op0=ALU.is_equal), reads=['t0f'], writes=[('indpg', kb)])
                for g in range(4):
                    qm, qmk = qq[g]
                    b = self.psrot((0, 1, 2, 3))
                    pst = self.ps[b]
                    P.op('pe', lambda e, pst=pst, kb=kb, g=g, qm=qm: e.matmul(pst[:, 0:32], KTp[kb][:, g, :], qm[:, 0:32], start=True, stop=False),
                         reads=[('KTp', kb), qmk], writes=[('ps', b)], inc=False)
                    P.op('pe', lambda e, pst=pst, kb=kb, g=g: e.matmul(pst[:, 0:32], indpg[kb][:, :], negMT[:, g, :, :].rearrange("p n t -> p (n t)"), start=False, stop=True),
                         reads=[('indpg', kb), ('negMT', g)], writes=[('ps', b)])
                    pob = 4 + g
                    unit(pst, b, g, 0 if i_ == 63 else 4, Vp[kb][:, g, 0:66], [('Vp', kb)], self.ps[pob], pob, i_ == 0, False)
            for g in range(4):
                gp = g // 2
                qm, qmk = qq[g]
                kk = g % 2
                P.op('act', lambda e, kk=kk, gp=gp, s=s: e.activation(ktn[kk][:, 0:8], self.KTs[:, gp, 2080 + 8 * s:2088 + 8 * s], AF.Copy),
                     reads=[('KT', 0, gp)], writes=[('ktn', kk)])
                b = self.psrot((0, 1, 2, 3))
                pst = self.ps[b]
                P.op('pe', lambda e, pst=pst, kk=kk, qm=qm: e.matmul(pst[:, 0:32], ktn[kk][:, :], qm[:, 0:32], start=True, stop=True),
                     reads=[('ktn', kk), qmk], writes=[('ps', b)])
                pob = 4 + g
                unit(pst, b, g, 1, self.Vs_s[:, s, g, 0:66], [('Vsm', 0, s)], self.ps[pob], pob, False, True)
                epilogue(self.ps[pob], pob, g, 1, False)
            for hh in range(2):
                P.dma('pool', wkT[hh * 64:(hh + 1) * 64, :, :], D['winkT'][s], writes=['wkT'])
            for j_ in range(4):
                P.dma('pool', wv[:, j_, :, 0:64], D['winv'][s][128 * j_:128 * j_ + 128, :].rearrange("p (g d) -> p g d", g=4), reads=['vinit'], writes=['wv'])
            for g in range(4):
                gp = g // 2
                qm, qmk = qq[g]
                pob = 4 + g
                for j in range(4):
                    b = self.psrot((0, 1, 2, 3))
                    pst = self.ps[b]
                    P.op('pe', lambda e, pst=pst, g=g, j=j, qm=qm: e.matmul(pst[:, 0:32], wkT[:, g, 128 * j:128 * j + 128], qm[:, 0:32], start=True, stop=True),
                         reads=['wkT', qmk], writes=[('ps', b)])
                    unit(pst, b, g, (2, 3, 4, 0)[j], wv[:, j, g, 0:66], ['wv'], self.ps[pob], pob, j == 0, False)
                kk = g % 2
                P.op('act', lambda e, kk=kk, gp=gp, s=s: e.activation(ktn[kk][:, 0:8], self.KTw[:, gp, 2080 + 8 * s:2088 + 8 * s], AF.Copy),
                     reads=[('KT', 1, gp)], writes=[('ktn', kk)])
                b = self.psrot((0, 1, 2, 3))
                pst = self.ps[b]
                P.op('pe', lambda e, pst=pst, kk=kk, qm=qm: e.matmul(pst[:, 0:32], ktn[kk][:, :], qm[:, 0:32], start=True, stop=True),
                     reads=[('ktn', kk), qmk], writes=[('ps', b)])
                unit(pst, b, g, 1, self.Vw_s[:, s, g, 0:66], [('Vsm', 1, s)], self.ps[pob], pob, False, True)
                epilogue(self.ps[pob], pob, g, 2, False)
            for g in range(4):
                P.op('act', lambda e, g=g: e.activation(opad[0:32, 0, 0:64], osum[0:32, g, :], AF.Copy), reads=[('osum', g)], writes=['opad'])
                P.op('act', lambda e, g=g: e.activation(opad[0:32, 1, 64:128], osum[0:32, g, :], AF.Copy), reads=[('osum', g)], writes=['opad'])
                for pr in range(2):
                    b = self.psrot((0, 1, 2, 3))
                    pst = self.ps[b]
                    for hh in range(2):
                        P.op('pe', lambda e, pst=pst, hh=hh, pr=pr: e.matmul(pst[:, 0:8], opad[:, hh, :], seln[:, 2 * pr + hh, :], start=(hh == 0), stop=(hh == 1)),
                             reads=['opad', 'seln'], writes=[('ps', b)], inc=(hh == 1))
                    c = 8 + 2 * g + pr
                    P.op('act', lambda e, pst=pst, c=c, s=s: e.activation(self.mixT[:, c, 1056 + 8 * s:1064 + 8 * s], pst[:, 0:8], AF.Copy), reads=[('ps', b)], writes=[('aT', c)])

    def build(self, n_pool):
        nc, P = self.nc, self.P
        D = self.D
        st = self.st
        self.din('xT_main', [128, 16, NT])
        self.din('xT_pre', [128, 16, NPRE])
        self.din('gains', [128, 4, 16])
        self.din('flag', [128, 1])
        self.din('w_ff1', [2, 2048, 8192])
        self.din('w_ff2', [2, 8192, 2048])
        self.din('w_in_o3', [2048, 6144])
        self.din('w_out_o', [2048, 2048])
        self.din('scw', [128, 3, 16])
        self.din('stsc', [128, 16, 4, 2])
        self.din('w_in_p', [2048, 4656])
        self.din('rgp', [128, 8, 8])
        self.din('rg_wa', [128, 8, 128])
        self.din('rg_wx', [128, 8, 128])
        self.din('kg', [128, 4])
        self.din('sth', [128, 8, 4])
        self.din('stconv', [128, 8, 4, 3])
        self.din('wink', [4, 512, 256])
        for nm, shp in (('ident', [128, 128]), ('jmat', [128, 128]), ('mw', [128, 256]), ('tabaug', [33, 16]), ('e1h', [33, DL]),
                        ('selmul', [1056, 32]), ('seladd', [1056, 32]), ('bvalid', [32, 1]), ('j32p', [128, 128]), ('cw1k', [128, 64, 64]), ('cw1v', [128, 64, 64]),
                        ('cw2k', [64, 2, 128]), ('cw2v', [64, 64]), ('cpe_in', [128, 1, 2, 256]), ('w_out_e', [2048, 2048]),
                        ('pck', [n_pool * 128, 256]), ('pcv', [n_pool * 128, 256]), ('psv', [n_pool * 128, 256]), ('pskT', [n_pool * 64, 512]),
                        ('winkT', [4, 64, 4, 512]), ('rep', [128, 128]), ('nsel', [128, 4]), ('seln', [128, 4, 8]), ('selsum', [128, 128]),
                        ('t0f', [128, 128]), ('iopc', [128, 2]), ('selmul_s', [8, 128]), ('seladd_s', [8, 128])):
            self.din(nm, shp)
        self.din('pt', [1, 256], I32)
        self.sbd = nc.dram_tensor('sbd', [6, 128, 128], F32, kind='Internal').ap()
        self.vd = nc.dram_tensor('vd', [16, DL], F32, kind='Internal').ap()
        self.din('winv', [4, 512, 256])
        self.dout('yT_out', [128, 16, NT])
        self.dout('scp_out', [128, 16, 2])
        self.dout('scs_out', [128, 16, 4, 2])
        for nm in ('kc_o', 'vc_o', 'vs_o', 'vw_o'):
            self.dout(nm, [1024, 256])
        for nm in ('kc_s_o', 'vc_s_o', 'vs_s_o', 'vw_s_o'):
            self.dout(nm, [32, 256])
        self.dout('ksT_o', [128, 2, NT])
        self.dout('kwT_o', [128, 2, NT])
        self.dout('rgo_h_o', [128, 8, 5])
        self.dout('wk_copy_o', [4, 504, 256])
        self.dout('wv_copy_o', [4, 504, 256])
        self.dout('rgo_conv_o', [128, 8, 5, 3])
        self.ps = [st.enter_context(nc.psum_tensor("ps%d" % i, [128, 512], F32)) for i in range(8)]
        self.wt = [self.sb("wt%d" % i, [128, 16, 256], BF16) for i in range(2)]
        self.hT = self.sb("hT", [128, 16, NT + 128], BF16)
        self.aT = self.sb("aT", [128, 16, NT], BF16)
        self.gT = self.aT
        self.mixT = self.aT
        self.sq = self.sb("sq", [128, 4, 512], BF16)
        self.rstd = self.sb("rstd", [128, 512], F32)
        self.tmpf = self.sb("tmpf", [128, 512], F32)
        self.ones_bf = self.sb("ones_bf", [128, 128], BF16)
        self.bones = self.sb("bones", [128, 128], BF16)
        self.epst = self.sb("epst", [128, 1], F32)
        self.onec = self.sb("onec", [128, 1], F32)
        self.gains = self.sb("gains", [128, 4, 16], F32)
        self.flag = self.sb("flag", [128, 1], F32)
        self.scw = self.sb("scw", [128, 3, 16], F32)
        self.stsc = self.sb("stsc", [128, 16, 4, 2], F32)
        self.xs = [self.sb("xs%d" % i, [128, 512], F32) for i in range(2)]
        self.xsi = 0
        self.stgi = 0

        P.op('pool', lambda e: e.memset(self.ones_bf[:], 1.0), writes=['ones'])
        P.op('pool', lambda e: e.memset(self.hT[:, :, NT:NT + 128], 0.0), writes=['hTpad'])
        P.op('pool', lambda e: e.memset(self.bones[:], 0.0), writes=['bones'])
        P.op('pool', lambda e: e.memset(self.bones[0:64, 0:64], 1.0), writes=['bones'])
        P.op('pool', lambda e: e.memset(self.bones[64:128, 64:128], 1.0), writes=['bones'])
        P.op('pool', lambda e: e.memset(self.epst[:], EPS), writes=['epst'])
        P.op('pool', lambda e: e.memset(self.onec[:], 1.0), writes=['onec'])
        P.dma('sp', self.gains[:], D['gains'], writes=['gains'])
        P.dma('sp', self.flag[:], D['flag'], writes=['flag'])
        P.dma('sp', self.scw[:], D['scw'], writes=['scw'])
        P.dma('sp', self.stsc[:], D['stsc'], writes=['stsc'])

        st0 = ExitStack()
        L0 = lambda name, shape, dt: self.sb(name, shape, dt, st0)
        self.KTs = L0("KTs", [128, 2, 1024 + NT], BF16)
        self.KTw = L0("KTw", [128, 2, 1024 + NT], BF16)
        self.Vs = L0("Vs", [128, 16, 4, 66], BF16)
        self.Vw = L0("Vw", [128, 16, 4, 66], BF16)
        self.Vs_s = L0("Vs_s", [128, 4, 4, 66], BF16)
        self.Vw_s = L0("Vw_s", [128, 4, 4, 66], BF16)
        self.rgscr = L0("rgscr", [128, 4, NT], F32)
        self.qT = self.rgscr[:].rearrange("p a t -> p (a t)").bitcast(BF16).rearrange("p (c t) -> p c t", c=8)
        self.gates = L0("gates", [128, 13, 48], F32)
        self.sqh = L0("sqh", [128, 512], BF16)
        self.rstd2 = self.rstd
        self.tmpq = self.tmpf
        self.rgp = L0("rgp", [128, 8, 8], F32)
        self.rgwa = L0("rgwa", [128, 8, 128], BF16)
        self.rgwx = L0("rgwx", [128, 8, 128], BF16)
        self.kg = L0("kg", [128, 4], F32)
        self.sth = L0("sth", [128, 8, 4], F32)
        self.stconv = L0("stconv", [128, 8, 4, 3], F32)
        self.clam = L0("clam", [128, 8], F32)
        self.convh = L0("convh", [128, 8, 3], F32)
        self.hst = L0("hst", [128, 8], F32)
        self.rgo_h = L0("rgo_h", [128, 8, 5], F32)
        self.rgo_conv = L0("rgo_conv", [128, 8, 5, 3], F32)
        self.rxc = self.rgscr[:, 0, :]
        self.rga = self.rgscr[:, 1, :]
        self.rgi = self.rgscr[:, 2, :]
        self.rtm = self.rgscr[:, 3, :]
        self.kcbT = L0("kcbT", [128, 2, 128], BF16)
        self.vcb = L0("vcb", [128, 4, 100], BF16)
        self.hidT = L0("hidT", [64, 2, 32, 4], BF16)
        self.cpe = L0("cpe", [64, 2, 4], F32)
        self.bvalid = L0("bvalid", [32, 1], F32)
        st0a = ExitStack()
        LA = lambda name, shape, dt: self.sb(name, shape, dt, st0a)
        self.hTp = LA("hTp", [128, 16, 256], BF16)
        self.kcpg = LA("kcpg", [128, 8, 2, 256], BF16)
        self.stg = [LA("stg%d" % i, [128, 512], F32) for i in range(2)]
        self.gx = self.stg[0]
        self.gt = self.stg[1]
        self.rxb = LA("rxb", [128, NT], F32)
        self.rxcb = LA("rxcb", [128, NT], BF16)
        self.rhist0 = LA("rhist0", [128, 3 + 1024], F32)
        self.rhists = LA("rhists", [128, 4, 11], F32)
        self.cw1b = LA("cw1b", [128, 64, 64], BF16)
        self.pe_t = LA("pe_t", [128, 1, 2, 256], BF16)
        self.hx = LA("hx", [64, 32], F32)
        self.htt = LA("htt", [64, 32], F32)
        P.dma('pool', self.pe_t[:], D['cpe_in'], writes=['pe_t'])
        P.dma('sp', self.bvalid[:], D['bvalid'], writes=['bvalid'])
        self.cx = dict(cw1b=self.cw1b, hx=self.hx, htt=self.htt, hid=self.hidT)
        self.job(None, lambda: self.compress1(self.pe_t, 1, ['pe_t'], 'pe'))

        for nm, t in (('rgp', self.rgp), ('kg', self.kg), ('sth', self.sth), ('stconv', self.stconv)):
            P.dma('sp', t[:], D[nm], writes=['cst'])
        P.dma('pool', self.rgwa[:], D['rg_wa'], writes=['cst'])
        P.dma('pool', self.rgwx[:], D['rg_wx'], writes=['cst'])
        P.op('pool', lambda e: e.memset(self.convh[:], 0.0), writes=[('convh', r) for r in range(8)])
        P.op('pool', lambda e: e.memset(self.hst[:], 0.0), writes=[('hst', r) for r in range(8)])
        P.op('pool', lambda e: e.memset(self.Vs[:], 1.0), writes=[('V', 0, i) for i in range(16)])
        P.op('pool', lambda e: e.memset(self.Vw[:], 1.0), writes=[('V', 1, i) for i in range(16)])
        for Vt, hh in ((self.Vs, 0), (self.Vw, 1)):
            P.op('dve', lambda e, Vt=Vt: e.tensor_scalar(Vt[:, 0:8, :, 64:65], Vt[:, 0:8, :, 64:65], self.flag[:, 0:1], None, op0=ALU.mult), reads=['flag'], writes=[('V', hh, i) for i in range(16)])
        P.op('pool', lambda e: e.memset(self.Vs_s[:], 1.0), writes=[('Vsm', 0, i) for i in range(4)])
        P.op('pool', lambda e: e.memset(self.Vw_s[:], 1.0), writes=[('Vsm', 1, i) for i in range(4)])
        P.op('act', lambda e: e.activation(self.clam[:], self.rgp[:, :, 7], AF.Exp, scale=-1.0), reads=['cst'], writes=['clam'])
        P.op('act', lambda e: e.activation(self.clam[:], self.clam[:], AF.Ln, bias=self.onec[:, 0:1], scale=1.0), reads=['clam', 'onec'], writes=['clam'])
        P.op('dve', lambda e: e.tensor_scalar(self.clam[:], self.clam[:], -8.0, None, op0=ALU.mult), reads=['clam'], writes=['clam'])

        if os.environ.get('K_NOCOPY') is None:
            P.dma('sp', D['wk_copy_o'], D['wink'][:, 8:512, :], writes=['wkc'])
            P.dma('sp', D['wv_copy_o'], D['winv'][:, 8:512, :], writes=['wvc'])
        self.l0_inproj()
        self.run_jobs()
        P.dma('sp', D['rgo_h_o'], self.rgo_h[:], reads=['rgo_h'], writes=['rgo_h_o'])
        P.dma('sp', D['rgo_conv_o'], self.rgo_conv[:], reads=['rgo_conv'], writes=['rgo_conv_o'])
        P.barrier()
        st0a.close()
        st0b = ExitStack()
        LB = lambda name, shape, dt: self.sb(name, shape, dt, st0b)
        self.cw2k = LB("cw2k", [64, 2, 128], BF16)
        self.cw2v = LB("cw2v", [64, 64], BF16)
        self.vtf = LB("vtf", [64, 32], F32)
        P.dma('pool', self.cw2k[:], D['cw2k'], writes=['cw2'])
        P.dma('pool', self.cw2v[:], D['cw2v'], writes=['cw2'])
        self.attn_setup(LB)
        P.op('pool', lambda e: e.memset(self.vcb[:], 0.0), writes=['vcb'])
        P.op('pool', lambda e: e.memset(self.kcbT[:], 0.0), writes=['kcbT'])
        self.cx = dict(cw2k=self.cw2k, cw2v=self.cw2v, vtf=self.vtf, identf=self.identf, hid=self.hidT)
        self.compress2(32, self.kcbT, self.vcb)
        for g in range(4):
            P.op('dve', lambda e, g=g: e.tensor_scalar(self.vcb[0:32, g, 0:64], self.vcb[0:32, g, 0:64], self.bvalid[:, 0:1], None, op0=ALU.mult), reads=['bvalid'], writes=['vcb'])
            P.op('dve', lambda e, g=g: e.tensor_copy(self.vcb[0:32, g, 64:65], self.bvalid[:, 0:1]), reads=['bvalid'], writes=['vcb'])
            P.op('dve', lambda e, g=g: e.tensor_copy(self.vcb[0:32, g, 66:98], self.identb[0:32, 0:32]), reads=['identb'], writes=['vcb'])
        P.op('pool', lambda e: e.memset(self.mixT[:, 8:16, 1056:1088], 0.0), writes=[('aT', c_) for c_ in range(8, 16)])
        if os.environ.get('K_NOATT') is None:
            self.attn_prompt_tile(0, 0, 32, 992)
            for j in range(8):
                self.attn_prompt_tile(1 + j, 32 + 128 * j, 128, 1024 + 128 * j)
        self.save_sample_bias()
        P.barrier()
        st0b.close()
        st0c = ExitStack()
        LC = lambda name, shape, dt: self.sb(name, shape, dt, st0c)
        if os.environ.get('K_NOSAMPLE') is None:
            self.attn_sample(LC, n_pool)
        P.barrier()
        st0c.close()
        st0.close()

        self.yT = self.sb("yT", [128, 16, NT], F32)
        self.tb = self.sb("tb", [128, NT], F32)
        self.tc = self.sb("tc", [128, NT], F32)
        self.tv = self.sb("tv", [128, NT], F32)
        self.ty = self.sb("ty", [128, NT], F32)
        self.tvs = self.sb("tvs", [128, 4, 10], F32)
        self.scp = self.sb("scp", [128, 16, 2], F32)
        self.scs = self.sb("scs", [128, 16, 4, 2], F32)
        for c in range(16):
            P.dma('sp', self.yT[:, c, :], D['xT_main'][:, c, :], writes=[('yT', c)])
        ysrc = lambda c, t0, n: self.yT[:, c, t0:t0 + n]
        hdst = lambda c, t0, n: self.hT[:, c, t0:t0 + n]
        ykey = lambda c: ('yT', c)
        hkey = lambda c: ('hT', c)
        Woe = D['w_out_e']
        mparts = [(self.mixT, lambda k: ('aT', k), t0, n, None) for (t0, n) in TT]
        for cg in range(8):
            def epi_o(m, tag, t0, n, ps, pkey, cg=cg):
                c = cg * 2 + m
                P.op('dve', lambda e, ps=ps, c=c, t0=t0, n=n: e.tensor_tensor(
                    self.yT[:, c, t0:t0 + n], ps, self.yT[:, c, t0:t0 + n], op=ALU.add),
                    reads=[pkey], writes=[('yT', c)])
            self.dense_fm(self.wview(Woe, 0, 16, cg * 256, 256), 16, 2, mparts, epi_o)
        self.rmsnorm(ysrc, ykey, self.gains[:, 2, :], hdst, hkey, TT)
        self.ffn(0)
        self.rmsnorm(ysrc, ykey, self.gains[:, 1, :], hdst, hkey, TT)
        self.mixer1()
        self.rmsnorm(ysrc, ykey, self.gains[:, 3, :], hdst, hkey, TT)
        self.ffn(1)
        self.run_jobs()
        for c in range(16):
            P.dma('sp', D['yT_out'][:, c, :], self.yT[:, c, :], reads=[('yT', c)], writes=[('yo', c)])
        P.dma('sp', D['scp_out'], self.scp[:], reads=['scp'], writes=['scpo'])
        P.dma('sp', D['scs_out'], self.scs[:], reads=['scs'], writes=['scso'])
        P.finish()
        P.emit()
        self.st.close()
        return nc


_CACHE = {}


def _fm(a, nchunk):
    T = a.shape[0]
    return np.ascontiguousarray(a.reshape(T, nchunk, 128).transpose(2, 1, 0))


def _vec_pm(v, nchunk):
    return np.ascontiguousarray(v.reshape(nchunk, 128).T)


def _tm(a):
    return np.ascontiguousarray(a.transpose(2, 1, 0).reshape(a.shape[2], -1))


def _q_perm():
    cols = []
    for i in range(8):
        hA = (i % 4) + 8 * (i // 4)
        hB = hA + 4
        cols += list(range(2048 + hA * 64, 2048 + hA * 64 + 64)) + list(range(2048 + hB * 64, 2048 + hB * 64 + 64))
    return cols


def prepare_inputs(inp):
    f = lambda k: np.asarray(inp[k])
    x_prompt = f('x_prompt')
    x_sample = f('x_sample')
    sh = {}
    sh['gains'] = np.ascontiguousarray(np.stack([_vec_pm(f('norm_mix')[0], 16), _vec_pm(f('norm_mix')[1], 16),
                                                 _vec_pm(f('norm_ffn')[0], 16), _vec_pm(f('norm_ffn')[1], 16)], axis=1))
    sh['w_ff1'] = f('w_ff1')
    sh['w_ff2'] = f('w_ff2')
    wio = f('w_in_o')[0]
    sh['w_in_o3'] = np.ascontiguousarray(wio.reshape(2048, 3, 16, 128).transpose(0, 2, 1, 3).reshape(2048, 6144))
    sh['w_out_o'] = f('w_out_o')[0]
    sh['scw'] = np.ascontiguousarray(f('sc_w')[0].reshape(3, 16, 128).transpose(2, 0, 1))
    wie = f('w_in_e')[0]
    R = lambda a, b: list(range(a, b))
    perm = R(0, 2048) + _q_perm() + R(3584, 3840) + R(4096, 4352) + R(3072, 3584) + R(3840, 4096) + R(4352, 4608) + R(4608, 4656)
    sh['w_in_p'] = np.ascontiguousarray(wie[:, perm])
    rows = [f('rg_conv_w')[0][j] for j in range(4)] + [f('rg_conv_b')[0], f('rg_ba')[0], f('rg_bx')[0], f('rg_lambda')[0]]
    sh['rgp'] = np.ascontiguousarray(np.stack([_vec_pm(r_, 8) for r_ in rows], axis=2))
    sh['rg_wa'] = np.ascontiguousarray(f('rg_wa')[0].transpose(1, 0, 2))
    sh['rg_wx'] = np.ascontiguousarray(f('rg_wx')[0].transpose(1, 0, 2))
    kn = f('k_norm')[0]
    qn = f('q_norm')[0]
    sh['kg'] = np.ascontiguousarray(np.stack([np.tile(kn[0], 2), np.tile(kn[1], 2), np.tile(kn[2], 2), np.tile(qn, 2)], axis=1))
    sh['w_out_e'] = f('w_out_e')[0]
    sh['ident'] = np.eye(128, dtype=np.float32)
    sh['jmat'] = np.ascontiguousarray(np.eye(128, dtype=np.float32)[::-1])
    j32p = np.zeros((128, 128), np.float32)
    j32p[0:32, 0:32] = np.eye(32, dtype=np.float32)[::-1]
    sh['j32p'] = j32p
    pp = np.arange(128)[:, None]
    xx = np.arange(384, 640)[None, :]
    sh['mw'] = np.where(xx - pp > 512, np.float32(MASKNEG), np.float32(0.0)).astype(np.float32)
    sh['tabaug'] = np.concatenate([f('rel_bias'), np.full((1, 16), MASKNEG, np.float32)], axis=0).astype(np.float32)
    dist = np.arange(DL) - DOFF
    nn = np.maximum(dist, 0)
    nf = np.maximum(nn, 1).astype(np.float32)
    large = 16 + (np.log(nf / np.float32(16.0)).astype(np.float32) / np.float32(np.log(8.0)) * np.float32(16.0)).astype(np.int32)
    large = np.minimum(large, 31)
    buck = np.where(nn < 16, nn, large)
    e1h = np.zeros((33, DL), np.float32)
    e1h[buck[dist >= 0], np.nonzero(dist >= 0)[0]] = 1.0
    e1h[32, dist < 0] = 1.0
    sh['e1h'] = e1h
    w1k = f('cmp_w1_k')[0]; w1v = f('cmp_w1_v')[0]
    sh['cw1k'] = np.ascontiguousarray(np.concatenate([w1k, w1k], axis=0))
    sh['cw1v'] = np.ascontiguousarray(np.concatenate([w1v, w1v], axis=0))
    w2k = f('cmp_w2_k')[0]
    c2 = np.zeros((64, 2, 128), np.float32)
    c2[:, 0, 0:64] = w2k
    c2[:, 1, 64:128] = w2k
    sh['cw2k'] = c2
    sh['cw2v'] = f('cmp_w2_v')[0]
    pek = f('cmp_pe_k')[0].reshape(64, 256); pev = f('cmp_pe_v')[0].reshape(64, 256)
    pe2 = np.stack([pek, pev], axis=1)
    sh['cpe_in'] = np.ascontiguousarray(np.concatenate([pe2, pe2], axis=0)[:, None])
    npool = f('cache_cmp_k').shape[1]
    sh['pck'] = f('cache_cmp_k')[0].reshape(npool * 128, 256)
    sh['pcv'] = f('cache_cmp_v')[0].reshape(npool * 128, 256)
    sh['psv'] = f('cache_sel_v')[0].reshape(npool * 128, 256)
    sh['pskT'] = np.ascontiguousarray(f('cache_sel_k')[0].transpose(0, 3, 2, 1)).reshape(npool * 64, 512)
    rr = np.arange(128)
    rep = np.zeros((128, 128), np.float32)
    for c_ in range(32):
        rep[c_ % 8, c_] = 1.0
    sh['rep'] = rep
    sh['nsel'] = ((rr[:, None] // 8 == np.arange(4)[None, :]) & (rr[:, None] < 32)).astype(np.float32)
    sh['seln'] = ((rr[:, None, None] < 32) & (rr[:, None, None] // 8 == np.arange(4)[None, :, None]) & (rr[:, None, None] % 8 == np.arange(8)[None, None, :])).astype(np.float32)
    ssum = np.zeros((128, 128), np.float32)
    for r_ in range(32):
        ssum[r_, r_ % 8] = 1.0
    sh['selsum'] = ssum
    sh['iopc'] = np.stack([rr, rr % 64], axis=1).astype(np.float32)
    sh['t0f'] = (rr[:, None] - (rr[None, :] // 64)).astype(np.float32)
    sh['selmul_s'] = np.ones((8, 128), np.float32)
    sa = np.zeros((8, 128), np.float32)
    sa[:, 0] = 1.0e4
    sa[:, 127] = 1.0e4
    sh['seladd_s'] = sa
    maps = []
    for c in range(8):
        b, half = c // 2, c % 2
        p0 = 1024 * half
        m = dict(sh)
        b0 = 0 if half == 1 else 16
        m['pt'] = np.ascontiguousarray(f('page_table')[4 * c:4 * c + 4].reshape(1, 256).astype(np.int32))
        m['winkT'] = np.ascontiguousarray(f('cache_win_k')[0, 4 * c:4 * c + 4].transpose(0, 3, 2, 1))
        qp = np.concatenate([np.arange(992, 1024), np.arange(1024, 2048)])
        cur = (qp // 64)[:, None]
        blk = np.arange(32)[None, :]
        bad = (blk > cur) | (blk < b0)
        forced = ((blk == cur) | (blk == cur - 1) | (blk == b0)) & ~bad
        m['selmul'] = np.where(bad, 0.0, 1.0).astype(np.float32)
        m['seladd'] = np.where(bad, -1.0e30, np.where(forced, 1.0e4, 0.0)).astype(np.float32)
        m['bvalid'] = (np.arange(32) >= b0).astype(np.float32)[:, None]
        own = x_prompt[b, p0:p0 + 1024]
        pre = x_prompt[b, 0:1024] if half == 1 else np.zeros((1024, 2048), np.float32)
        xs = x_sample[4 * c:4 * c + 4].reshape(32, 2048)
        main = np.concatenate([pre[992:1024], own, xs], axis=0)
        m['xT_main'] = _fm(main, 16)
        m['xT_pre'] = _fm(pre, 16)
        m['flag'] = np.full((128, 1), float(half), np.float32)
        st = f('state_sconv')[0, 4 * c:4 * c + 4]
        m['stsc'] = np.ascontiguousarray(st.reshape(4, 2, 16, 128).transpose(3, 2, 0, 1))
        m['sth'] = np.ascontiguousarray(f('state_rglru_h')[0, 4 * c:4 * c + 4].reshape(4, 8, 128).transpose(2, 1, 0))
        m['stconv'] = np.ascontiguousarray(f('state_rglru_conv')[0, 4 * c:4 * c + 4].reshape(4, 3, 8, 128).transpose(3, 2, 0, 1))
        m['wink'] = np.ascontiguousarray(f('cache_win_k')[0, 4 * c:4 * c + 4].reshape(4, 512, 256))
        m['winv'] = np.ascontiguousarray(f('cache_win_v')[0, 4 * c:4 * c + 4].reshape(4, 512, 256))
        maps.append(m)
    return maps


def _kT_to_rows(a):
    T = a.shape[2]
    return np.ascontiguousarray(a.reshape(2, 64, 2, T).transpose(3, 2, 0, 1).reshape(T, 4, 64))


def kernel(**inputs):
    n_pool = int(np.asarray(inputs['cache_cmp_k']).shape[1])
    if 'nc' not in _CACHE:
        b = Builder()
        _CACHE['nc'] = b.build(n_pool)
        _CACHE['in_names'] = b.in_names
    nc = _CACHE['nc']
    maps = prepare_inputs(inputs)
    maps = [{k: m[k] for k in _CACHE['in_names']} for m in maps]
    res = run_bass_kernel_spmd(nc, maps[:NCORES], core_ids=list(range(NCORES)))
    R = res.results
    z = lambda *s: np.zeros(s, np.float32)
    y_p = z(4, 2048, 2048); y_s = z(32, 8, 2048)
    ck_p = z(1, 4, 2048, 4, 64); cv_p = z(1, 4, 2048, 4, 64); sk_p = z(1, 4, 2048, 4, 64); sv_p = z(1, 4, 2048, 4, 64)
    ck_s = z(1, 32, 8, 4, 64); cv_s = z(1, 32, 8, 4, 64); sk_s = z(1, 32, 8, 4, 64); sv_s = z(1, 32, 8, 4, 64)
    wk_p = z(1, 4, 512, 4, 64); wv_p = z(1, 4, 512, 4, 64); wk_s = z(1, 32, 512, 4, 64); wv_s = z(1, 32, 512, 4, 64)
    h_p = z(1, 4, 1024); h_s = z(1, 32, 1024); cv1_p = z(1, 4, 3, 1024); cv1_s = z(1, 32, 3, 1024)
    sconv_p = z(1, 4, 2, 2048); sconv_s = z(1, 32, 2, 2048)
    for c in range(NCORES):
        b, half = c // 2, c % 2
        p0 = 1024 * half
        r = R[c]
        yt = _tm(r['yT_out'])
        y_p[b, p0:p0 + 1024] = yt[32:1056]
        y_s[4 * c:4 * c + 4] = yt[1056:1088].reshape(4, 8, 2048)
        ck_p[0, b, p0:p0 + 1024] = r['kc_o'].reshape(1024, 4, 64)
        cv_p[0, b, p0:p0 + 1024] = r['vc_o'].reshape(1024, 4, 64)
        sv_p[0, b, p0:p0 + 1024] = r['vs_o'].reshape(1024, 4, 64)
        ks_rows = _kT_to_rows(r['ksT_o'])
        kw_rows = _kT_to_rows(r['kwT_o'])
        sk_p[0, b, p0:p0 + 1024] = ks_rows[32:1056]
        ck_s[0, 4 * c:4 * c + 4] = r['kc_s_o'].reshape(4, 8, 4, 64)
        cv_s[0, 4 * c:4 * c + 4] = r['vc_s_o'].reshape(4, 8, 4, 64)
        sv_s[0, 4 * c:4 * c + 4] = r['vs_s_o'].reshape(4, 8, 4, 64)
        sk_s[0, 4 * c:4 * c + 4] = ks_rows[1056:1088].reshape(4, 8, 4, 64)
        wk_s[0, 4 * c:4 * c + 4, 0:504] = r['wk_copy_o'].reshape(4, 504, 4, 64)
        wv_s[0, 4 * c:4 * c + 4, 0:504] = r['wv_copy_o'].reshape(4, 504, 4, 64)
        wk_s[0, 4 * c:4 * c + 4, 504:512] = kw_rows[1056:1088].reshape(4, 8, 4, 64)
        wv_s[0, 4 * c:4 * c + 4, 504:512] = r['vw_s_o'].reshape(4, 8, 4, 64)
        rh = r['rgo_h_o']
        rc = r['rgo_conv_o']
        h_s[0, 4 * c:4 * c + 4] = rh[:, :, 1:5].transpose(2, 1, 0).reshape(4, 1024)
        cv1_s[0, 4 * c:4 * c + 4] = rc[:, :, 1:5, :].transpose(2, 3, 1, 0).reshape(4, 3, 1024)
        sconv_s[0, 4 * c:4 * c + 4] = r['scs_out'].transpose(2, 3, 1, 0).reshape(4, 2, 2048)
        if half == 1:
            wk_p[0, b] = kw_rows[32 + 512:32 + 1024]
            wv_p[0, b] = r['vw_o'][512:1024].reshape(512, 4, 64)
            h_p[0, b] = rh[:, :, 0].T.reshape(1024)
            cv1_p[0, b] = rc[:, :, 0, :].transpose(2, 1, 0).reshape(3, 1024)
            sconv_p[0, b] = _tm(r['scp_out'])
    return (y_p, y_s, ck_p, ck_s, cv_p, cv_s, sk_p, sk_s, sv_p, sv_s, wk_p, wk_s, wv_p, wv_s,
            h_p, h_s, cv1_p, cv1_s, sconv_p, sconv_s)
```

```python
import os
import numpy as np
from contextlib import ExitStack
import concourse.bass as bass
import concourse.mybir as mybir
from concourse.bass_utils import run_bass_kernel_spmd

F32 = mybir.dt.float32
BF16 = mybir.dt.bfloat16
I32 = mybir.dt.int32
ALU = mybir.AluOpType
AF = mybir.ActivationFunctionType

SAME_ENGINE_SYNC = True
EPOCH = 30000
NDMASEM = 48

NT = 1088
NPRE = 1024
TT = [(0, 512), (512, 512), (1024, 64)]
TTP = [(0, 512), (512, 512)]
EPS = 1e-6
MASKNEG = -30000.0
SCALE = 0.125
DOFF = 2304
DL = 10752
DEBUG_STAGE = 3
NCORES = 8


class Prog:
    ENGS = ('pe', 'dve', 'act', 'pool', 'sp')

    def __init__(self, nc, stack):
        self.nc = nc
        self.stack = stack
        self.nsem = 0
        self.ops = {e: [] for e in self.ENGS}
        self.csem = {e: self._newsem('c_' + e) for e in self.ENGS}
        self.ccnt = {e: 0 for e in self.ENGS}
        self.seen = {e: {} for e in self.ENGS}
        self.lastw = {}
        self.readers = {}
        self.dsem = [self._newsem('d%d' % i) for i in range(NDMASEM)]
        self.dval = [0] * NDMASEM
        self.di = 0
        self.dq = {'sp': (0, 32), 'pool': (32, 16), 'act': (0, 32)}
        self.dqi = {'sp': 0, 'pool': 0, 'act': 0}
        self.nwaits = 0
        self.allsems = []

    def _newsem(self, name):
        self.nsem += 1
        s = self.stack.enter_context(self.nc.semaphore(name + '_%d' % self.nsem))
        return s

    def _deps(self, reads, writes):
        deps = []
        for k in list(reads) + list(writes):
            t = self.lastw.get(k)
            if t is not None:
                deps.append(t)
        for k in writes:
            deps.extend(self.readers.get(k, ()))
        for k in reads:
            if isinstance(k, tuple) and k and k[0] == 'ps':
                deps.extend(self.readers.get(k, ()))
        return deps

    def _needed(self, eng, deps):
        out = {}
        own = self.csem[eng]
        for sem, val in deps:
            if sem is own:
                if eng == 'pe' or not SAME_ENGINE_SYNC:
                    continue
            key = id(sem)
            if self.seen[eng].get(key, 0) >= val:
                continue
            if key not in out or out[key][1] < val:
                out[key] = (sem, val)
        for key, (sem, val) in out.items():
            self.seen[eng][key] = val
        self.nwaits += len(out)
        return list(out.values())

    def _commit(self, tok, reads, writes):
        for k in writes:
            self.lastw[k] = tok
            self.readers[k] = []
        for k in reads:
            if k in writes:
                continue
            self.readers.setdefault(k, []).append(tok)

    def op(self, eng, fn, reads=(), writes=(), inc=True, mode=None):
        if eng == 'pe':
            m_ = mode or 'full'
            if m_ != getattr(self, 'pemode', 'full'):
                self.ops['pe'].append(([], lambda e: e.drain(), None))
            self.pemode = m_
        deps = self._deps(reads, writes)
        waits = self._needed(eng, deps)
        if inc:
            self.ccnt[eng] += 1
            tok = (self.csem[eng], self.ccnt[eng])
            incinfo = (self.csem[eng], 1)
        else:
            tok = (self.csem[eng], self.ccnt[eng] + 1)
            incinfo = None
        self.ops[eng].append((waits, fn, incinfo))
        self._commit(tok, reads, writes)
        if inc and self.ccnt[eng] >= EPOCH:
            self.csem[eng] = self._newsem('c_' + eng)
            self.ccnt[eng] = 0
        return tok

    def dma(self, q, out, in_, reads=(), writes=(), **kw):
        return self.custom_dma(q, lambda e: e.dma_start(out=out, in_=in_, **kw), reads, writes)

    def custom_dma(self, q, fn, reads=(), writes=()):
        base, cnt = self.dq[q]
        i = base + self.dqi[q] % cnt
        self.dqi[q] += 1
        self.di += 1
        sem = self.dsem[i]
        prev = self.dval[i]
        val = prev + 16
        self.dval[i] = val
        deps = self._deps(reads, writes)
        if prev > 0:
            deps.append((sem, prev))
        waits = self._needed(q, deps)
        tok = (sem, val)
        self.ops[q].append((waits, fn, (sem, 16)))
        self._commit(tok, reads, writes)
        return tok

    def barrier(self):
        for eng in self.ENGS:
            deps = [(self.dsem[i], self.dval[i]) for i in range(NDMASEM) if self.dval[i] > 0]
            for e in self.ENGS:
                if e != eng and self.ccnt[e] > 0:
                    deps.append((self.csem[e], self.ccnt[e]))
            waits = self._needed(eng, deps)
            if waits:
                self.ops[eng].append((waits, None, None))
        self.lastw.clear()
        self.readers.clear()

    def finish(self):
        deps = [(self.dsem[i], self.dval[i]) for i in range(NDMASEM) if self.dval[i] > 0]
        for e in self.ENGS:
            if e != 'sp' and self.ccnt[e] > 0:
                deps.append((self.csem[e], self.ccnt[e]))
        waits = self._needed('sp', deps)
        self.ops['sp'].append((waits, None, None))

    def emit(self):
        nc = self.nc
        engmap = {'pe': 'tensor', 'dve': 'vector', 'act': 'scalar', 'pool': 'gpsimd', 'sp': 'sync'}
        with nc.Block() as block:
            for e in self.ENGS:
                lst = self.ops[e]

                def body(eng, lst=lst):
                    for waits, fn, incinfo in lst:
                        for sem, val in waits:
                            eng.wait_ge(sem, val)
                        if fn is None:
                            continue
                        ins = fn(eng)
                        if incinfo is not None:
                            ins.then_inc(incinfo[0], incinfo[1])
                getattr(block, engmap[e])(body)


class Builder:
    def __init__(self):
        self.nc = bass.Bass("TRN2", target_bir_lowering=False)
        self.st = ExitStack()
        self.P = Prog(self.nc, self.st)
        self.D = {}
        self.jobs = []
        self.wi = 0
        self.psi = {}
        self.in_names = []
        self.out_names = []

    def din(self, name, shape, dt=F32):
        self.D[name] = self.nc.dram_tensor(name, list(shape), dt, kind="ExternalInput").ap()
        self.in_names.append(name)
        return self.D[name]

    def dout(self, name, shape, dt=F32):
        self.D[name] = self.nc.dram_tensor(name, list(shape), dt, kind="ExternalOutput").ap()
        self.out_names.append(name)
        return self.D[name]

    def sb(self, name, shape, dt, st=None):
        return (st or self.st).enter_context(self.nc.sbuf_tensor("s_" + name, list(shape), dt))

    def psrot(self, banks):
        i = self.psi.get(banks, 0)
        self.psi[banks] = i + 1
        return banks[i % len(banks)]

    def job(self, wsrc, fn):
        self.jobs.append((wsrc, fn))

    def run_jobs(self):
        P = self.P
        jobs = self.jobs
        if os.environ.get('K_STOP') is not None and not getattr(self, '_stopped', False):
            jobs = jobs[:int(os.environ['K_STOP'])]
            self._stopped = True
        widx = [i for i, j in enumerate(jobs) if j[0] is not None]
        loaded = {}
        nl = 0
        for i, (wsrc, fn) in enumerate(jobs):
            ahead = [k for k in widx if k >= i][:2]
            if ahead:
                while nl < len(widx) and widx[nl] <= ahead[-1]:
                    k = widx[nl]
                    src = jobs[k][0]
                    bi = self.wi % 2
                    self.wi += 1
                    t = self.wt[bi]
                    kc, n = src.shape[1], src.shape[2]
                    if getattr(self, 'wstage', None) is not None:
                        stg_ = self.wstage
                        P.dma('sp', stg_[:, 0:kc, 0:n], src, writes=['wstage'])
                        P.op('pool', lambda e, t=t, stg_=stg_, kc=kc, n=n: e.tensor_copy(t[:, 0:kc, 0:n], stg_[:, 0:kc, 0:n]), reads=['wstage'], writes=[('wt', bi)])
                    else:
                        P.dma('pool', t[:, 0:kc, 0:n], src, writes=[('wt', bi)])
                    loaded[k] = (t, ('wt', bi))
                    nl += 1
            if wsrc is None:
                fn()
            else:
                t, key = loaded.pop(i)
                fn(t, key)
        self.jobs = []

    def wview(self, W, k0, kc, c0, n):
        return W[k0:k0 + kc * 128, c0:c0 + n].rearrange("(kc p) n -> p kc n", p=128)

    def dense_fm(self, wsrc, kc, nm, xparts, epi, banks=(0, 1, 2, 3)):
        P = self.P

        def fn(wt, wkey):
            for m in range(nm):
                for (xt, xkeyf, t0, n, tag) in xparts:
                    b = self.psrot(banks)
                    pst = self.ps[b]
                    pkey = ('ps', b)
                    for k in range(kc):
                        P.op('pe', lambda e, pst=pst, wt=wt, xt=xt, k=k, m=m, t0=t0, n=n: e.matmul(
                            pst[:, 0:n], wt[:, k, m * 128:(m + 1) * 128], xt[:, k, t0:t0 + n],
                            start=(k == 0), stop=(k == kc - 1)),
                            reads=[wkey, xkeyf(k)], writes=[pkey], inc=(k == kc - 1))
                    epi(m, tag, t0, n, pst[:, 0:n], pkey)
        self.job(wsrc, fn)

    def dense_tm(self, wsrc, kc, ncols, xparts, epi, banks=(0, 1, 2, 3)):
        P = self.P

        def fn(wt, wkey):
            for (xt, xkeyf, t0, n, tag) in xparts:
                b = self.psrot(banks)
                pst = self.ps[b]
                pkey = ('ps', b)
                for k in range(kc):
                    P.op('pe', lambda e, pst=pst, wt=wt, xt=xt, k=k, t0=t0, n=n: e.matmul(
                        pst[:, 0:ncols], xt[:, k, t0:t0 + 128], wt[:, k, 0:ncols],
                        start=(k == 0), stop=(k == kc - 1)),
                        reads=[wkey, xkeyf(k)], writes=[pkey], inc=(k == kc - 1))
                epi(tag, t0, n, pst[0:n, 0:ncols], pkey)
        self.job(wsrc, fn)

    def rmsnorm(self, srcf, skey, gain, dstf, dkey, tiles, fetch=None):
        P = self.P

        def getsrc(c, t0, n):
            if fetch is None:
                return srcf(c, t0, n), skey(c)
            i = self.xsi % 2
            self.xsi += 1
            P.dma('sp', self.xs[i][:, 0:n], fetch(c, t0, n), writes=[('xs', i)])
            return self.xs[i][:, 0:n], ('xs', i)

        def fn():
            for (t0, n) in tiles:
                b = self.psrot((4, 5))
                pst = self.ps[b]
                for c in range(16):
                    src, sk = getsrc(c, t0, n)
                    P.op('act', lambda e, c=c, n=n, src=src: e.activation(self.sq[:, c % 4, 0:n], src, AF.Square),
                         reads=[sk], writes=[('sq', c % 4)])
                    P.op('pe', lambda e, c=c, n=n, pst=pst: e.matmul(pst[:, 0:n], self.ones_bf[:, :], self.sq[:, c % 4, 0:n],
                                                                     start=(c == 0), stop=(c == 15)),
                         reads=[('sq', c % 4), 'ones'], writes=[('ps', b)], inc=True)
                P.op('act', lambda e, n=n, pst=pst: e.activation(self.rstd[:, 0:n], pst[:, 0:n], AF.Sqrt, bias=self.epst[:, 0:1], scale=1.0 / 2048.0),
                     reads=[('ps', b), 'epst'], writes=['rstd'])
                P.op('dve', lambda e, n=n: e.reciprocal(self.rstd[:, 0:n], self.rstd[:, 0:n]), reads=['rstd'], writes=['rstd'])
                for c in range(16):
                    src, sk = getsrc(c, t0, n)
                    P.op('dve', lambda e, c=c, t0=t0, n=n, src=src: e.scalar_tensor_tensor(
                        dstf(c, t0, n), src, gain[:, c:c + 1], self.rstd[:, 0:n], op0=ALU.mult, op1=ALU.mult),
                        reads=[sk, 'rstd', 'gains'], writes=[dkey(c)])
        self.job(None, fn)

    def ffn(self, li):
        P = self.P
        W1 = self.D['w_ff1'][li]
        W2 = self.D['w_ff2'][li]
        xparts = [(self.hT, lambda k: ('hT', k), t0, n, None) for (t0, n) in TT]
        aparts = [(self.aT, lambda k: ('aT', k), t0, n, None) for (t0, n) in TT]
        for q in range(4):
            for cg in range(8):
                def epi1(m, tag, t0, n, ps, pkey, cg=cg):
                    c = cg * 2 + m
                    P.op('act', lambda e, ps=ps, n=n: e.activation(self.tmpf[:, 0:n], ps, AF.Square), reads=[pkey], writes=['tmpf'])
                    P.op('dve', lambda e, ps=ps, c=c, t0=t0, n=n: e.scalar_tensor_tensor(
                        self.aT[:, c, t0:t0 + n], ps, 0.0, self.tmpf[:, 0:n], op0=ALU.is_gt, op1=ALU.mult),
                        reads=[pkey, 'tmpf'], writes=[('aT', c)])
                self.dense_fm(self.wview(W1, 0, 16, q * 2048 + cg * 256, 256), 16, 2, xparts, epi1)
            for cg in range(8):
                def epi2(m, tag, t0, n, ps, pkey, cg=cg):
                    c = cg * 2 + m
                    P.op('dve', lambda e, ps=ps, c=c, t0=t0, n=n: e.tensor_tensor(
                        self.yT[:, c, t0:t0 + n], ps, self.yT[:, c, t0:t0 + n], op=ALU.add),
                        reads=[pkey], writes=[('yT', c)])
                self.dense_fm(self.wview(W2, q * 2048, 16, cg * 256, 256), 16, 2, aparts, epi2)

    def mixer1(self):
        P = self.P
        W = self.D['w_in_o3']
        xparts = [(self.hT, lambda k: ('hT', k), t0, n, None) for (t0, n) in TT]
        for c in range(16):
            for m_ in range(3):
                def epi(m, tag, t0, n, ps, pkey, c=c, m_=m_):
                    if m_ == 0:
                        P.op('act', lambda e, ps=ps, t0=t0, n=n: e.activation(self.tb[:, t0:t0 + n], ps, AF.Copy), reads=[pkey], writes=['tb'])
                    elif m_ == 1:
                        P.op('act', lambda e, ps=ps, t0=t0, n=n: e.activation(self.tc[:, t0:t0 + n], ps, AF.Copy), reads=[pkey], writes=['tc'])
                    else:
                        P.op('dve', lambda e, ps=ps, t0=t0, n=n: e.tensor_tensor(self.tv[:, t0:t0 + n], ps, self.tc[:, t0:t0 + n], op=ALU.mult),
                             reads=[pkey, 'tc'], writes=['tv'])
                        if t0 + n == NT:
                            self.conv1(c)
                self.dense_fm(self.wview(W, 0, 16, c * 384 + m_ * 128, 128), 16, 1, xparts, epi)
        Wo = self.D['w_out_o']
        gparts = [(self.gT, lambda k: ('aT', k), t0, n, None) for (t0, n) in TT]
        for cg in range(8):
            def epi2(m, tag, t0, n, ps, pkey, cg=cg):
                c = cg * 2 + m
                P.op('dve', lambda e, ps=ps, c=c, t0=t0, n=n: e.tensor_tensor(
                    self.yT[:, c, t0:t0 + n], ps, self.yT[:, c, t0:t0 + n], op=ALU.add),
                    reads=[pkey], writes=[('yT', c)])
            self.dense_fm(self.wview(Wo, 0, 16, cg * 256, 256), 16, 2, gparts, epi2)

    def conv1(self, c):
        P = self.P
        w = self.scw
        P.op('dve', lambda e: e.tensor_scalar(self.tv[:, 0:32], self.tv[:, 0:32], self.flag[:, 0:1], None, op0=ALU.mult),
             reads=['tv', 'flag'], writes=['tv'])
        P.op('dve', lambda e: e.tensor_scalar(self.ty[:, 32:1056], self.tv[:, 30:1054], w[:, 0, c:c + 1], None, op0=ALU.mult),
             reads=['tv', 'scw'], writes=['ty'])
        P.op('dve', lambda e: e.scalar_tensor_tensor(self.ty[:, 32:1056], self.tv[:, 31:1055], w[:, 1, c:c + 1], self.ty[:, 32:1056], op0=ALU.mult, op1=ALU.add),
             reads=['tv', 'scw'], writes=['ty'])
        P.op('dve', lambda e: e.scalar_tensor_tensor(self.ty[:, 32:1056], self.tv[:, 32:1056], w[:, 2, c:c + 1], self.ty[:, 32:1056], op0=ALU.mult, op1=ALU.add),
             reads=['tv', 'scw'], writes=['ty'])
        P.op('dve', lambda e: e.tensor_copy(self.tvs[:, :, 0:2], self.stsc[:, c, :, :]), reads=['stsc'], writes=['tvs'])
        P.op('dve', lambda e: e.tensor_copy(self.tvs[:, :, 2:10], self.tv[:, 1056:1088].rearrange("p (s t) -> p s t", s=4)), reads=['tv'], writes=['tvs'])
        ys = self.ty[:, 1056:1088].rearrange("p (s t) -> p s t", s=4)
        P.op('dve', lambda e: e.tensor_scalar(ys, self.tvs[:, :, 0:8], w[:, 0, c:c + 1], None, op0=ALU.mult), reads=['tvs', 'scw'], writes=['ty'])
        P.op('dve', lambda e: e.scalar_tensor_tensor(ys, self.tvs[:, :, 1:9], w[:, 1, c:c + 1], ys, op0=ALU.mult, op1=ALU.add), reads=['tvs', 'scw'], writes=['ty'])
        P.op('dve', lambda e: e.scalar_tensor_tensor(ys, self.tvs[:, :, 2:10], w[:, 2, c:c + 1], ys, op0=ALU.mult, op1=ALU.add), reads=['tvs', 'scw'], writes=['ty'])
        P.op('pool', lambda e: e.memset(self.ty[:, 0:32], 0.0), writes=['ty'])
        P.op('dve', lambda e: e.tensor_tensor(self.gT[:, c, :], self.tb[:, :], self.ty[:, :], op=ALU.mult), reads=['tb', 'ty'], writes=[('aT', c)])
        P.op('act', lambda e: e.activation(self.scp[:, c, :], self.tv[:, 1054:1056], AF.Copy), reads=['tv'], writes=['scp'])
        P.op('act', lambda e: e.activation(self.scs[:, c, :, :], self.tvs[:, :, 8:10], AF.Copy), reads=['tvs'], writes=['scs'])

    def stage(self):
        i = self.stgi % 2
        self.stgi += 1
        return self.stg[i], ('stg', i)

    def headnorm(self, src, skey, n, gain, outs):
        P = self.P
        P.op('act', lambda e: e.activation(self.sqh[:, 0:n], src, AF.Square), reads=[skey], writes=['sqh'])
        b = self.psrot((4, 5))
        pst = self.ps[b]
        P.op('pe', lambda e: e.matmul(pst[:, 0:n], self.bones[:, :], self.sqh[:, 0:n], start=True, stop=True),
             reads=['sqh', 'bones'], writes=[('ps', b)])
        P.op('act', lambda e: e.activation(self.rstd2[:, 0:n], pst[:, 0:n], AF.Sqrt, bias=self.epst[:, 0:1], scale=1.0 / 64.0),
             reads=[('ps', b), 'epst'], writes=['rstd'])
        P.op('dve', lambda e: e.reciprocal(self.rstd2[:, 0:n], self.rstd2[:, 0:n]), reads=['rstd'], writes=['rstd'])
        for dst, dkeys in outs:
            P.op('dve', lambda e, dst=dst: e.scalar_tensor_tensor(dst, src, gain, self.rstd2[:, 0:n], op0=ALU.mult, op1=ALU.mult),
                 reads=[skey, 'rstd', 'cst'], writes=dkeys)

    def gelu_mul(self, ps, pkey, n, dst, dkey):
        P = self.P
        x = self.gx[:, 0:n]
        t = self.gt[:, 0:n]
        P.op('act', lambda e: e.activation(x, ps, AF.Copy), reads=[pkey], writes=[('stg', 0)])
        P.op('dve', lambda e: e.tensor_tensor(t, x, x, op=ALU.mult), reads=[('stg', 0)], writes=[('stg', 1)])
        P.op('dve', lambda e: e.tensor_scalar(t, t, 0.044715, 1.0, op0=ALU.mult, op1=ALU.add), reads=[('stg', 1)], writes=[('stg', 1)])
        P.op('dve', lambda e: e.tensor_tensor(t, t, x, op=ALU.mult), reads=[('stg', 1), ('stg', 0)], writes=[('stg', 1)])
        P.op('act', lambda e: e.activation(t, t, AF.Sigmoid, scale=1.5957691216057308), reads=[('stg', 1)], writes=[('stg', 1)])
        P.op('dve', lambda e: e.tensor_tensor(x, x, t, op=ALU.mult), reads=[('stg', 0), ('stg', 1)], writes=[('stg', 0)])
        P.op('dve', lambda e: e.tensor_tensor(dst, x, dst, op=ALU.mult), reads=[('stg', 0)], writes=[dkey])

    def rg_head(self, r, W, segs, mask_flag):
        P = self.P
        prm = self.rgp
        xx = self.rxb
        xc = self.rxc
        for sg in segs:
            c0, L = sg['c0'], sg['L']
            hb = self.rhist0 if sg['hslot'] == 0 else self.rhists[:, sg['hslot'] - 1, :]
            P.op('dve', lambda e, sg=sg, hb=hb: e.tensor_copy(hb[:, 0:3], sg['hist']), reads=sg['hist_keys'], writes=[('rhist', sg['hslot'])])
            P.op('act', lambda e, hb=hb, c0=c0, L=L: e.activation(hb[:, 3:3 + L], xx[:, c0:c0 + L], AF.Copy), reads=['rxb'], writes=[('rhist', sg['hslot'])])
            P.op('dve', lambda e, hb=hb, c0=c0, L=L: e.tensor_scalar(xc[:, c0:c0 + L], hb[:, 0:L], prm[:, r, 0:1], prm[:, r, 4:5], op0=ALU.mult, op1=ALU.add),
                 reads=[('rhist', sg['hslot']), 'cst'], writes=['rxc'])
            for j in range(1, 4):
                P.op('dve', lambda e, hb=hb, c0=c0, L=L, j=j: e.scalar_tensor_tensor(xc[:, c0:c0 + L], hb[:, j:j + L], prm[:, r, j:j + 1], xc[:, c0:c0 + L], op0=ALU.mult, op1=ALU.add),
                     reads=[('rhist', sg['hslot']), 'cst'], writes=['rxc'])
            if sg.get('hist_out') is not None:
                P.op('act', lambda e, sg=sg, hb=hb, L=L: e.activation(sg['hist_out'], hb[:, L:L + 3], AF.Copy), reads=[('rhist', sg['hslot'])], writes=sg['hist_out_keys'])
        lo = min(sg['c0'] for sg in segs)
        hi = max(sg['c0'] + sg['L'] for sg in segs)
        P.op('dve', lambda e: e.tensor_copy(self.rxcb[:, lo:hi], xc[:, lo:hi]), reads=['rxc'], writes=['rxcb'])
        ga = self.rga
        gi = self.rgi
        for (t0, n) in [(t, min(512, hi - t)) for t in range(lo, hi, 512)]:
            for which, wmat, bcol, dstt in ((0, self.rgwa, 5, ga), (1, self.rgwx, 6, gi)):
                b = self.psrot((4, 5))
                pst = self.ps[b]
                P.op('pe', lambda e, pst=pst, wmat=wmat, t0=t0, n=n: e.matmul(pst[:, 0:n], wmat[:, r, :], self.rxcb[:, t0:t0 + n], start=True, stop=True),
                     reads=['rxcb', 'cst'], writes=[('ps', b)])
                P.op('act', lambda e, pst=pst, dstt=dstt, bcol=bcol, t0=t0, n=n: e.activation(dstt[:, t0:t0 + n], pst[:, 0:n], AF.Sigmoid, bias=prm[:, r, bcol:bcol + 1], scale=1.0),
                     reads=[('ps', b), 'cst'], writes=['rga' if which == 0 else 'rgi'])
        P.op('act', lambda e: e.activation(ga[:, lo:hi], ga[:, lo:hi], AF.Exp, scale=self.clam[:, r:r + 1]), reads=['rga', 'clam'], writes=['rga'])
        tm = self.rtm
        P.op('dve', lambda e: e.tensor_tensor(tm[:, lo:hi], ga[:, lo:hi], ga[:, lo:hi], op=ALU.mult), reads=['rga'], writes=['rtm'])
        P.op('dve', lambda e: e.tensor_scalar(tm[:, lo:hi], tm[:, lo:hi], -1.0, 1.0, op0=ALU.mult, op1=ALU.add), reads=['rtm'], writes=['rtm'])
        P.op('dve', lambda e: e.tensor_scalar(tm[:, lo:hi], tm[:, lo:hi], 0.0, None, op0=ALU.max), reads=['rtm'], writes=['rtm'])
        P.op('act', lambda e: e.activation(tm[:, lo:hi], tm[:, lo:hi], AF.Sqrt), reads=['rtm'], writes=['rtm'])
        P.op('dve', lambda e: e.tensor_tensor(gi[:, lo:hi], gi[:, lo:hi], xc[:, lo:hi], op=ALU.mult), reads=['rgi', 'rxc'], writes=['rgi'])
        P.op('dve', lambda e: e.tensor_tensor(gi[:, lo:hi], gi[:, lo:hi], tm[:, lo:hi], op=ALU.mult), reads=['rgi', 'rtm'], writes=['rgi'])
        if mask_flag:
            P.op('dve', lambda e: e.tensor_scalar(gi[:, lo:hi], gi[:, lo:hi], self.flag[:, 0:1], None, op0=ALU.mult), reads=['rgi', 'flag'], writes=['rgi'])
        for sg in segs:
            c0, L = sg['c0'], sg['L']
            P.op('dve', lambda e, sg=sg, c0=c0, L=L: e.tensor_tensor_scan(tm[:, c0:c0 + L], ga[:, c0:c0 + L], gi[:, c0:c0 + L], sg['init'], ALU.mult, ALU.add),
                 reads=['rga', 'rgi'] + sg['init_keys'], writes=['rtm'])
            if sg.get('state_out') is not None:
                P.op('act', lambda e, sg=sg, c0=c0, L=L: e.activation(sg['state_out'], tm[:, c0 + L - 1:c0 + L], AF.Copy), reads=['rtm'], writes=sg['state_out_keys'])
            for (dst, dkeys, s0, sl) in sg.get('h_out', []):
                P.op('act', lambda e, dst=dst, c0=c0, s0=s0, sl=sl: e.activation(dst, tm[:, c0 + s0:c0 + s0 + sl], AF.Copy), reads=['rtm'], writes=dkeys)

    def l0_inproj(self):
        P = self.P
        W = self.D['w_in_p']
        D = self.D
        NPP = 4
        for pas in range(NPP + 1):
            ismain = (pas == NPP)
            if not ismain:
                p0 = pas * 256
                self.rmsnorm(None, None, self.gains[:, 0, :], lambda c, t0, n: self.hTp[:, c, 0:n], lambda c: ('hTp', c), [(0, 256)],
                             fetch=lambda c, t0, n, p0=p0: D['xT_pre'][:, c, p0:p0 + n])
                xt, xkf = self.hTp, (lambda k: ('hTp', k))
                fm_tiles = [(0, 256)]
                tm_tiles = [(0, 128, ('ctx', pas * 2)), (128, 128, ('ctx', pas * 2 + 1))]
                W_ = 256
            else:
                self.rmsnorm(None, None, self.gains[:, 0, :], lambda c, t0, n: self.hT[:, c, t0:t0 + n], lambda c: ('hT', c), TT,
                             fetch=lambda c, t0, n: D['xT_main'][:, c, t0:t0 + n])
                xt, xkf = self.hT, (lambda k: ('hT', k))
                fm_tiles = list(TT)
                tm_tiles = [(32 + 128 * j, 128, ('ctx', 8 + j)) for j in range(8)] + ([] if os.environ.get('K_NOSMP') else [(1056 + 8 * s, 8, ('smp', s)) for s in range(4)])
                W_ = NT
            fmparts = [(xt, xkf, t0, n, None) for (t0, n) in fm_tiles]
            kcol0 = 0 if ismain else 0
            kbase = 1024 if ismain else pas * 256

            for r in range(8):
                def epi_rx(m, tag, t0, n, ps, pkey, r=r, ismain=ismain, pas=pas, W_=W_):
                    P.op('act', lambda e: e.activation(self.rxb[:, t0:t0 + n], ps, AF.Copy), reads=[pkey], writes=['rxb'])
                    if t0 + n == W_:
                        if not ismain:
                            sg = dict(c0=0, L=256, hslot=0, hist=self.convh[:, r, :], hist_keys=[('convh', r)],
                                      hist_out=self.convh[:, r, :], hist_out_keys=[('convh', r)],
                                      init=self.hst[:, r:r + 1], init_keys=[('hst', r)],
                                      state_out=self.hst[:, r:r + 1], state_out_keys=[('hst', r)])
                            if pas == NPP - 1:
                                sg['h_out'] = [(self.mixT[:, r, 0:32], [('aT', r)], 224, 32)]
                            if os.environ.get('K_NORG') is None:
                                self.rg_head(r, 256, [sg], True)
                        else:
                            segs = [dict(c0=32, L=1024, hslot=0, hist=self.convh[:, r, :], hist_keys=[('convh', r)],
                                         hist_out=self.rgo_conv[:, r, 0, :], hist_out_keys=['rgo_conv'],
                                         init=self.hst[:, r:r + 1], init_keys=[('hst', r)],
                                         state_out=self.rgo_h[:, r, 0:1], state_out_keys=['rgo_h'],
                                         h_out=[(self.mixT[:, r, 32:1056], [('aT', r)], 0, 1024)])]
                            for s_ in range(4):
                                segs.append(dict(c0=1056 + 8 * s_, L=8, hslot=1 + s_, hist=self.stconv[:, r, s_, :], hist_keys=['cst'],
                                                 hist_out=self.rgo_conv[:, r, 1 + s_, :], hist_out_keys=['rgo_conv'],
                                                 init=self.sth[:, r, s_:s_ + 1], init_keys=['cst'],
                                                 state_out=self.rgo_h[:, r, 1 + s_:2 + s_], state_out_keys=['rgo_h'],
                                                 h_out=[(self.mixT[:, r, 1056 + 8 * s_:1064 + 8 * s_], [('aT', r)], 0, 8)]))
                            if os.environ.get('K_NORG') is None:
                                self.rg_head(r, NT, segs, False)
                self.dense_fm(self.wview(W, 0, 16, r * 128, 128), 16, 1, fmparts, epi_rx)

            for which, col0, KT, gcol, oname in ((0, 3072, self.KTs, 1, 'ksT_o'), (1, 3328, self.KTw, 2, 'kwT_o')):
                def epi_k(m, tag, t0, n, ps, pkey, KT=KT, gcol=gcol, oname=oname, ismain=ismain, kbase=kbase, which=which):
                    tq = self.tmpq[:, 0:n]
                    P.op('act', lambda e: e.activation(tq, ps, AF.Copy), reads=[pkey], writes=['tmpf'])
                    outs = [(KT[:, m, kbase + t0:kbase + t0 + n], [('KT', which, m)])]
                    if ismain:
                        stg, skey = self.stage()
                        outs.append((stg[:, 0:n], [skey]))
                    self.headnorm(tq, 'tmpf', n, self.kg[:, gcol:gcol + 1], outs)
                    if ismain:
                        P.dma('sp', D[oname][:, m, t0:t0 + n], stg[:, 0:n], reads=[skey], writes=[(oname, m, t0)])
                self.dense_fm(self.wview(W, 0, 16, col0, 256), 16, 2, fmparts, epi_k)

            tmparts = [(xt, xkf, t0, n, tag) for (t0, n, tag) in tm_tiles]
            for half_ in range(2):
                def epi_kcvc(tag, t0, n, ps, pkey, half_=half_, ismain=ismain):
                    kind, idx = tag
                    if kind == 'ctx':
                        P.op('act', lambda e: e.activation(self.kcpg[:, idx % 8, half_, :], ps, AF.Copy), reads=[pkey], writes=[('kcpg', idx % 8, half_)])
                    if ismain:
                        stg, skey = self.stage()
                        P.op('dve', lambda e: e.tensor_copy(stg[0:n, 0:256], ps), reads=[pkey], writes=[skey])
                        nm = ('kc_o', 'vc_o')[half_] if kind == 'ctx' else ('kc_s_o', 'vc_s_o')[half_]
                        r0 = (idx - 8) * 128 if kind == 'ctx' else idx * 8
                        if os.environ.get('K_NODMA') is None:
                            P.dma('sp', D[nm][r0:r0 + n, :], stg[0:n, 0:256], reads=[skey], writes=[(nm, r0)])
                self.dense_tm(self.wview(W, 0, 16, 3584 + 256 * half_, 256), 16, 256, tmparts, epi_kcvc)
            if pas >= NPP - 1:
                kk = [('kcpg', i_, hf_) for i_ in range(8) for hf_ in range(2)]
                self.job(None, lambda kk=kk, pg0=(8 if ismain else 0): self.compress1(self.kcpg, 8, kk, 'data', pg0=pg0))
            for half_ in range(2):
                def epi_v(tag, t0, n, ps, pkey, half_=half_, ismain=ismain):
                    kind, idx = tag
                    Vt = (self.Vs, self.Vw)[half_]
                    if kind == 'ctx':
                        dst = Vt[0:n, idx, :, 0:64]
                        dk = [('V', half_, idx)]
                    else:
                        dst = (self.Vs_s, self.Vw_s)[half_][0:n, idx, :, 0:64]
                        dk = [('Vsm', half_, idx)]
                    P.op('act', lambda e: e.activation(dst, ps.rearrange("p (g d) -> p g d", g=4), AF.Copy), reads=[pkey], writes=dk)
                    if ismain:
                        stg, skey = self.stage()
                        P.op('dve', lambda e: e.tensor_copy(stg[0:n, 0:256], ps), reads=[pkey], writes=[skey])
                        nm = ('vs_o', 'vw_o')[half_] if kind == 'ctx' else ('vs_s_o', 'vw_s_o')[half_]
                        r0 = (idx - 8) * 128 if kind == 'ctx' else idx * 8
                        P.dma('sp', D[nm][r0:r0 + n, :], stg[0:n, 0:256], reads=[skey], writes=[(nm, r0)])
                self.dense_tm(self.wview(W, 0, 16, 4096 + 256 * half_, 256), 16, 256, tmparts, epi_v)

            if ismain:
                for r in range(8):
                    def epi_rg(m, tag, t0, n, ps, pkey, r=r):
                        self.gelu_mul(ps, pkey, n, self.mixT[:, r, t0:t0 + n], ('aT', r))
                    self.dense_fm(self.wview(W, 0, 16, 1024 + r * 128, 128), 16, 1, fmparts, epi_rg)
                for i in range(8):
                    def epi_q(m, tag, t0, n, ps, pkey, i=i):
                        tq = self.tmpq[:, 0:n]
                        P.op('act', lambda e: e.activation(tq, ps, AF.Copy), reads=[pkey], writes=['tmpf'])
                        self.headnorm(tq, 'tmpf', n, self.kg[:, 3:4], [(self.qT[:, i, t0:t0 + n], [('qT', i), 'rga', 'rgi', 'rtm', 'rxc'])])
                    self.dense_fm(self.wview(W, 0, 16, 2048 + i * 128, 128), 16, 1, fmparts, epi_q)
                gl_tiles = [(0, 32, 0)] + [(32 + 128 * j, 128, 1 + j) for j in range(8)] + [(1056 + 8 * s, 8, 9 + s) for s in range(4)]
                glparts = [(xt, xkf, t0, n, gi_) for (t0, n, gi_) in gl_tiles]

                def epi_gl(tag, t0, n, ps, pkey):
                    P.op('act', lambda e: e.activation(self.gates[0:n, tag, :], ps, AF.Sigmoid), reads=[pkey], writes=[('gates', tag)])
                self.dense_tm(self.wview(W, 0, 16, 4608, 48), 16, 48, glparts, epi_gl)

    def compress1(self, X, npg, xkeys, kind, pg0=0, kvs=(0, 1), load_w=True):
        P = self.P
        cx = self.cx
        cw1b_, hx_, htt_, hid_ = cx['cw1b'], cx['hx'], cx['htt'], cx['hid']
        for kv in kvs:
            w1 = cw1b_
            if load_w:
                P.dma('pool', w1[:], self.D['cw1k' if kv == 0 else 'cw1v'], writes=['cw1'])
            for half in range(2):
                if kind == 'pe' and half == 1:
                    continue
                b = self.psrot((4, 5))
                pst = self.ps[b]
                lo, hi = half * 64, half * 64 + 64
                Xv = X[lo:hi, 0:npg, kv, :].rearrange("p n (g d) -> p n g d", g=4)
                for d in range(64):
                    P.op('pe', lambda e, pst=pst, w1=w1, Xv=Xv, d=d, lo=lo, hi=hi: e.matmul(
                        pst[0:64, 0:npg * 4], w1[lo:hi, d, :], Xv[:, :, :, d], start=(d == 0), stop=(d == 63)),
                        reads=list(xkeys) + ['cw1'], writes=[('ps', b)], inc=(d == 63), mode='k64m64')
                psv = pst[0:64, 0:npg * 4]
                if kind == 'pe':
                    P.op('act', lambda e, psv=psv, kv=kv: e.activation(self.cpe[:, kv, :], psv, AF.Copy), reads=[('ps', b)], writes=['cpe'])
                    continue
                hx = hx_[:, 0:npg * 4]
                hxv = hx.rearrange("p (n g) -> p n g", g=4)
                ht = htt_[:, 0:npg * 4]
                P.op('act', lambda e, psv=psv, hx=hx: e.activation(hx, psv, AF.Copy), reads=[('ps', b)], writes=['hx'])
                for g in range(4):
                    P.op('dve', lambda e, hxv=hxv, g=g, kv=kv: e.tensor_scalar(hxv[:, :, g], hxv[:, :, g], self.cpe[:, kv, g:g + 1], None, op0=ALU.add),
                         reads=['hx', 'cpe'], writes=['hx'])
                P.op('dve', lambda e, hx=hx, ht=ht: e.tensor_tensor(ht, hx, hx, op=ALU.mult), reads=['hx'], writes=['htt'])
                P.op('dve', lambda e, ht=ht: e.tensor_scalar(ht, ht, 0.044715, 1.0, op0=ALU.mult, op1=ALU.add), reads=['htt'], writes=['htt'])
                P.op('dve', lambda e, hx=hx, ht=ht: e.tensor_tensor(ht, ht, hx, op=ALU.mult), reads=['htt', 'hx'], writes=['htt'])
                P.op('act', lambda e, ht=ht: e.activation(ht, ht, AF.Sigmoid, scale=1.5957691216057308), reads=['htt'], writes=['htt'])
                dst = hid_[:, kv, 2 * pg0 + half:2 * (pg0 + npg):2, :]
                P.op('dve', lambda e, hx=hx, ht=ht, dst=dst: e.tensor_tensor(dst, hx.rearrange("p (n g) -> p n g", g=4), ht.rearrange("p (n g) -> p n g", g=4), op=ALU.mult),
                     reads=['hx', 'htt'], writes=['hidT'])

    def compress2(self, nblk, kcbT, vcb):
        P = self.P
        cx = self.cx
        cw2k_, cw2v_, vtf_, identf_, hid_ = cx['cw2k'], cx['cw2v'], cx['vtf'], cx['identf'], cx['hid']
        for gp in range(2):
            b = self.psrot((4, 5))
            pst = self.ps[b]
            for lh in range(2):
                P.op('pe', lambda e, pst=pst, lh=lh, gp=gp: e.matmul(pst[:, 0:nblk], cw2k_[:, lh, :], hid_[:, 0, 0:nblk, 2 * gp + lh],
                                                                     start=(lh == 0), stop=(lh == 1)),
                     reads=['hidT', 'cw2'], writes=[('ps', b)], inc=(lh == 1), mode='k64m128')
            tq = self.tmpq[:, 0:nblk]
            P.op('act', lambda e, pst=pst, tq=tq: e.activation(tq, pst[:, 0:nblk], AF.Copy), reads=[('ps', b)], writes=['tmpf'])
            self.headnorm(tq, 'tmpf', nblk, self.kg[:, 0:1], [(kcbT[:, gp, 0:nblk], ['kcbT'])])
        for g in range(4):
            b = self.psrot((4, 5))
            pst = self.ps[b]
            P.op('pe', lambda e, pst=pst, g=g: e.matmul(pst[0:64, 0:nblk], cw2v_[:, :], hid_[:, 1, 0:nblk, g], start=True, stop=True),
                 reads=['hidT', 'cw2'], writes=[('ps', b)], mode='k64m64')
            vt = vtf_[:, 0:nblk]
            P.op('act', lambda e, pst=pst, vt=vt: e.activation(vt, pst[0:64, 0:nblk], AF.Copy), reads=[('ps', b)], writes=['vtf'])
            b2 = self.psrot((4, 5))
            pst2 = self.ps[b2]
            P.op('pe', lambda e, pst2=pst2, vt=vt: e.transpose(pst2[0:nblk, 0:64], vt, identf_[0:64, 0:64]), reads=['vtf', 'identf'], writes=[('ps', b2)], mode='k64m64t')
            P.op('act', lambda e, pst2=pst2, g=g: e.activation(vcb[0:nblk, g, 0:64], pst2[0:nblk, 0:64], AF.Copy), reads=[('ps', b2)], writes=['vcb'])

    def load_qm(self, h, qc0, N):
        P = self.P
        g = h // 4
        hp = g % 2
        qi, _ = self.qchunk(h)
        lo, hi = hp * 64, hp * 64 + 64
        k = hp * 2 + (self.qmi[hp] % 2)
        self.qmi[hp] += 1
        qm = self.qm[k]
        P.op('act', lambda e: e.activation(qm[lo:hi, 0:N], self.qT[lo:hi, qi, qc0:qc0 + N], AF.Copy), reads=[('qT', qi)], writes=[('qm', k)])
        return qm, ('qm', k)

    def qchunk(self, h):
        if (h % 8) < 4:
            return (h % 4) + 4 * (h // 8), 0
        return ((h - 4) % 4) + 4 * ((h - 4) // 8), 1

    def softmax_unit(self, pst, b, nk, N, h, x0, branch):
        P = self.P
        i = self.pti % 2
        self.pti += 1
        pT = self.pT[i]
        pk = ('pT', i)
        if branch == 'win' and x0 >= 384:
            tt = self.ptmp[i]
            P.op('dve', lambda e: e.scalar_tensor_tensor(tt[0:nk, 0:N], pst[0:nk, 0:N], SCALE, self.mw[0:nk, x0 - 384:x0 - 384 + N], op0=ALU.mult, op1=ALU.add),
                 reads=[('ps', b), 'mw'], writes=[('ptmp', i)])
            P.op('act', lambda e: e.activation(pT[0:nk, 0:N], tt[0:nk, 0:N], AF.Exp, bias=self.chd[0:nk, h:h + 1], scale=1.0),
                 reads=[('ptmp', i), 'chd'], writes=[pk])
        elif x0 >= 256:
            P.op('act', lambda e: e.activation(pT[0:nk, 0:N], pst[0:nk, 0:N], AF.Exp, bias=self.chd[0:nk, h:h + 1], scale=SCALE),
                 reads=[('ps', b), 'chd'], writes=[pk])
        else:
            tt = self.ptmp[i]
            P.op('dve', lambda e: e.scalar_tensor_tensor(tt[0:nk, 0:N], pst[0:nk, 0:N], SCALE, self.G[0:nk, h, x0:x0 + N], op0=ALU.mult, op1=ALU.add),
                 reads=[('ps', b), 'G'], writes=[('ptmp', i)])
            P.op('act', lambda e: e.activation(pT[0:nk, 0:N], tt[0:nk, 0:N], AF.Exp), reads=[('ptmp', i)], writes=[pk])
        return pT, pk

    def branch_epilogue(self, po, pob, N, h, gate_ap, first):
        P = self.P
        rd = self.rden[0:N, 0:1]
        P.op('dve', lambda e: e.tensor_scalar(rd, po[0:N, 64:65], 1e-30, None, op0=ALU.max), reads=[('ps', pob)], writes=['rden'])
        P.op('dve', lambda e: e.reciprocal(rd, rd), reads=['rden'], writes=['rden'])
        sg = self.rden[0:N, 1:2]
        P.op('dve', lambda e: e.tensor_tensor(sg, rd, gate_ap, op=ALU.mult), reads=['rden', 'gates'], writes=['rden2'])
        om = self.omix[0:N, h, :]
        if first:
            P.op('dve', lambda e: e.tensor_scalar(om, po[0:N, 0:64], sg, None, op0=ALU.mult), reads=[('ps', pob), 'rden2'], writes=[('omix', h)])
        else:
            P.op('dve', lambda e: e.scalar_tensor_tensor(om, po[0:N, 0:64], sg, om, op0=ALU.mult, op1=ALU.add), reads=[('ps', pob), 'rden2'], writes=[('omix', h)])
        return rd

    def attn_prompt_tile(self, ti, qc0, N, qpos0):
        P = self.P
        D = self.D
        P.dma('sp', self.selmul[0:N, :], D['selmul'][qc0:qc0 + N, :], writes=['selmul'])
        P.dma('sp', self.seladd[0:N, :], D['seladd'][qc0:qc0 + N, :], writes=['seladd'])
        for hq in range(4):
            for hh in range(4):
                h = hq * 4 + hh
                src = bass.AP(self.vd.tensor, h * DL + DOFF + qpos0 - 63 - 64 * 31, [[64, 32], [1, N]])
                P.dma('sp', self.bcrev[0:32, hh, 0:N], src, reads=['vd'], writes=[('bcrev', hh)])
            b = self.psrot((4, 5))
            pst = self.ps[b]
            for hh in range(4):
                P.op('pe', lambda e, pst=pst, hh=hh: e.matmul(pst[:, hh * 128:hh * 128 + N], self.j32p[:, :], self.bcrev[:, hh, 0:N], start=True, stop=True),
                     reads=[('bcrev', hh), 'jmat'], writes=[('ps', b)], inc=True)
            for hh in range(4):
                P.op('act', lambda e, pst=pst, hh=hh: e.activation(self.bc[:, hh, 0:N], pst[0:32, hh * 128:hh * 128 + N], AF.Copy),
                     reads=[('ps', b)], writes=['bc'])
            for hh in range(4):
                h = hq * 4 + hh
                g = h // 4
                gp, hp = g // 2, g % 2
                qi, qh = self.qchunk(h)
                assert qh == hp
                lo, hi = hp * 64, hp * 64 + 64
                b = self.psrot((0, 1, 2, 3))
                pst = self.ps[b]
                qm, qmk = self.load_qm(h, qc0, N)
                P.op('pe', lambda e, pst=pst, gp=gp, qm=qm: e.matmul(pst[:, 0:N], self.kcbT[:, gp, 0:128], qm[:, 0:N], start=True, stop=True),
                     reads=['kcbT', qmk], writes=[('ps', b)])
                i = self.pti % 2
                self.pti += 1
                tt, pT = self.ptmp[i], self.pT[i]
                P.op('dve', lambda e, pst=pst, tt=tt, hh=hh: e.scalar_tensor_tensor(tt[0:32, 0:N], pst[0:32, 0:N], SCALE, self.bc[:, hh, 0:N], op0=ALU.mult, op1=ALU.add),
                     reads=[('ps', b), 'bc'], writes=[('ptmp', i)])
                P.op('act', lambda e, tt=tt, pT=pT: e.activation(pT[0:32, 0:N], tt[0:32, 0:N], AF.Exp), reads=[('ptmp', i)], writes=[('pT', i)])
                pob = 6 + (h % 2)
                po = self.ps[pob]
                P.op('pe', lambda e, po=po, pT=pT, g=g: e.matmul(po[:, 0:98], pT[:, 0:128], self.vcb[:, g, 0:98], start=True, stop=True),
                     reads=[('pT', i), 'vcb'], writes=[('ps', pob)])
                rd = self.branch_epilogue(po, pob, N, h, self.gates[0:N, ti, h:h + 1], True)
                im = self.imp[0:N, g, :]
                if h % 4 == 0:
                    P.op('dve', lambda e, po=po, rd=rd, im=im: e.tensor_scalar(im, po[0:N, 66:98], rd, None, op0=ALU.mult), reads=[('ps', pob), 'rden'], writes=[('imp', g)])
                else:
                    P.op('dve', lambda e, po=po, rd=rd, im=im: e.scalar_tensor_tensor(im, po[0:N, 66:98], rd, im, op0=ALU.mult, op1=ALU.add), reads=[('ps', pob), 'rden'], writes=[('imp', g)])
        for g in range(4):
            im = self.imp[0:N, g, :]
            P.op('dve', lambda e, im=im: e.tensor_tensor(im, im, self.selmul[0:N, :], op=ALU.mult), reads=['selmul'], writes=[('imp', g)])
            P.op('dve', lambda e, im=im: e.tensor_tensor(im, im, self.seladd[0:N, :], op=ALU.add), reads=['seladd'], writes=[('imp', g)])
            P.op('dve', lambda e, im=im: e.max(self.mx8[0:N, :], im), reads=[('imp', g)], writes=['mx8'])
            P.op('dve', lambda e, im=im: e.match_replace(self.impw[0:N, :], self.mx8[0:N, :], im, -3.0e38), reads=[('imp', g), 'mx8'], writes=['impw'])
            P.op('dve', lambda e: e.max(self.mx8[0:N, :], self.impw[0:N, :]), reads=['impw'], writes=['mx8'])
            P.op('dve', lambda e, im=im: e.tensor_scalar(self.impw[0:N, :], im, self.mx8[0:N, 7:8], None, op0=ALU.is_ge), reads=[('imp', g), 'mx8'], writes=['impw'])
            P.op('dve', lambda e, g=g: e.tensor_scalar(self.negM[0:N, g, :], self.impw[0:N, :], -1.0, -MASKNEG, op0=ALU.add, op1=ALU.mult), reads=['impw'], writes=[('negM', g)])
        for br, KT, Vt, gofs in (('sel', self.KTs, self.Vs, 16), ('win', self.KTw, self.Vw, 32)):
            for h in range(16):
                g = h // 4
                gp, hp = g // 2, g % 2
                qi, _ = self.qchunk(h)
                lo, hi = hp * 64, hp * 64 + 64
                ktmax = (qpos0 + N - 1) // 128
                kts = []
                for kt in range(ktmax + 1):
                    x0 = qpos0 - 128 * kt
                    if br == 'win' and x0 >= 640:
                        continue
                    kts.append((kt, x0))
                pob = 6 + (h % 2)
                po = self.ps[pob]
                qm, qmk = self.load_qm(h, qc0, N)
                pend = None
                for idx, (kt, x0) in enumerate(kts):
                    kcol0 = 128 * kt if kt < 8 else 1024 + 32 + 128 * (kt - 8)
                    b = self.psrot((0, 1, 2, 3))
                    pst = self.ps[b]
                    P.op('pe', lambda e, pst=pst, gp=gp, kcol0=kcol0, KT=KT, br=br, qm=qm: e.matmul(
                        pst[:, 0:N], KT[:, gp, kcol0:kcol0 + 128], qm[:, 0:N], start=True, stop=(br != 'sel')),
                        reads=[('KT', 0 if br == 'sel' else 1, gp), qmk], writes=[('ps', b)], inc=(br != 'sel'))
                    if br == 'sel':
                        if idx == 0 and h % 4 == 0:
                            P.op('dve', lambda e, g=g: e.tensor_copy(self.mexp[0:N, :, :], self.negM[0:N, g, :].unsqueeze(2).to_broadcast([N, 32, 64])),
                                 reads=[('negM', g)], writes=['mexp'])
                        mk = self.mexp[:, 2 * kt:2 * kt + 2, :].rearrange("p a d -> p (a d)")
                        P.op('pe', lambda e, pst=pst, mk=mk: e.matmul(pst[:, 0:N], mk, self.identb[:, 0:N], start=False, stop=True),
                             reads=['mexp', 'identb'], writes=[('ps', b)])
                    pT, pk = self.softmax_unit(pst, b, 128, N, h, x0, br)
                    cur = (idx, kt, pT, pk)
                    if pend is not None:
                        i2, kt2, pT2, pk2 = pend
                        P.op('pe', lambda e, po=po, pT2=pT2, kt2=kt2, g=g, Vt=Vt, i2=i2, nk=len(kts): e.matmul(
                            po[:, 0:66], pT2[:, 0:128], Vt[:, kt2, g, 0:66], start=(i2 == 0), stop=(i2 == nk - 1)),
                            reads=[pk2, ('V', 0 if br == 'sel' else 1, kt2)], writes=[('ps', pob)], inc=(i2 == len(kts) - 1))
                    pend = cur
                i2, kt2, pT2, pk2 = pend
                P.op('pe', lambda e, po=po, pT2=pT2, kt2=kt2, g=g, Vt=Vt, i2=i2, nk=len(kts): e.matmul(
                    po[:, 0:66], pT2[:, 0:128], Vt[:, kt2, g, 0:66], start=(i2 == 0), stop=(i2 == nk - 1)),
                    reads=[pk2, ('V', 0 if br == 'sel' else 1, kt2)], writes=[('ps', pob)], inc=True)
                self.branch_epilogue(po, pob, N, h, self.gates[0:N, ti, gofs + h:gofs + h + 1], False)
        for c in range(8):
            b = self.psrot((4, 5))
            pst = self.ps[b]
            src = self.omix[:, 2 * c:2 * c + 2, :].rearrange("p a d -> p (a d)")
            P.op('pe', lambda e, pst=pst, src=src: e.transpose(pst[:, 0:128], src, self.identf[:, :]), reads=[('omix', 2 * c), ('omix', 2 * c + 1), 'identf'], writes=[('ps', b)])
            P.op('act', lambda e, pst=pst, c=c: e.activation(self.mixT[:, 8 + c, qc0:qc0 + N], pst[:, 0:N], AF.Copy), reads=[('ps', b)], writes=[('aT', 8 + c)])

    def attn_setup(self, A):
        P, D = self.P, self.D
        self.identf = A("identf", [128, 128], F32)
        self.identb = A("identb", [128, 128], BF16)
        self.jmat = A("jmat", [128, 128], F32)
        self.G = A("G", [128, 16, 256], F32)
        self.chd = A("chd", [128, 16], F32)
        self.mw = A("mw", [128, 256], F32)
        self.tab = A("tab", [33, 16], F32)
        self.e1 = [A("e1_%d" % i, [33, 512], F32) for i in range(1)]
        self.vds = [A("vds%d" % i, [16, 512], F32) for i in range(1)]
        self.grev = [A("grev%d" % i, [128, 256], F32) for i in range(1)]
        self.bcrev = A("bcrev", [128, 4, 128], F32)
        self.j32p = A("j32p", [128, 128], F32)
        self.qm = [A("qm%d" % i, [128, 128], BF16) for i in range(4)]
        self.qmi = [0, 0]
        self.bc = A("bc", [32, 4, 128], F32)
        self.mexp = A("mexp", [128, 32, 64], BF16)
        self.selmul = A("selmul", [128, 32], F32)
        self.seladd = A("seladd", [128, 32], F32)
        self.pT = [A("pT%d" % i, [128, 128], BF16) for i in range(2)]
        self.ptmp = [A("ptmp%d" % i, [128, 128], F32) for i in range(2)]
        self.pti = 0
        self.rden = A("rden", [128, 2], F32)
        self.omix = A("omix", [128, 16, 64], F32)
        self.imp = A("imp", [128, 4, 32], F32)
        self.impw = A("impw", [128, 32], F32)
        self.mx8 = A("mx8", [128, 8], F32)
        self.negM = A("negM", [128, 4, 32], BF16)
        P.dma('sp', self.identf[:], D['ident'], writes=['identf'])
        P.dma('pool', self.identb[:], D['ident'], writes=['identb'])
        P.dma('sp', self.jmat[:], D['jmat'], writes=['jmat'])
        P.dma('sp', self.j32p[:], D['j32p'], writes=['jmat'])
        P.op('pool', lambda e: e.memset(self.bcrev[:], 0.0), writes=[('bcrev', i_) for i_ in range(4)])
        for i_ in range(4):
            P.op('pool', lambda e, i_=i_: e.memset(self.qm[i_][:], 0.0), writes=[('qm', i_)])
        for i_ in range(2):
            P.op('pool', lambda e, i_=i_: e.memset(self.pT[i_][:], 0.0), writes=[('pT', i_)])
        P.op('pool', lambda e: e.memset(self.mexp[:], 0.0), writes=['mexp'])
        P.op('pool', lambda e: e.memset(self.omix[:], 0.0), writes=[('omix', i_) for i_ in range(16)])
        P.dma('sp', self.mw[:], D['mw'], writes=['mw'])
        P.dma('sp', self.tab[:], D['tabaug'], writes=['tab'])
        for ch in range(DL // 512):
            i = 0
            P.dma('sp', self.e1[i][:], D['e1h'][:, ch * 512:(ch + 1) * 512], writes=[('e1', i)])
            b = self.psrot((4, 5))
            pst = self.ps[b]
            P.op('pe', lambda e, pst=pst, i=i: e.matmul(pst[0:16, 0:512], self.tab[:, :], self.e1[i][:, :], start=True, stop=True),
                 reads=['tab', ('e1', i)], writes=[('ps', b)], mode='k33m16')
            P.op('act', lambda e, pst=pst, i=i: e.activation(self.vds[i][:, :], pst[0:16, 0:512], AF.Copy), reads=[('ps', b)], writes=[('vds', i)])
            P.dma('sp', self.vd[:, ch * 512:(ch + 1) * 512], self.vds[i][:, :], reads=[('vds', i)], writes=['vd'])
        for h in range(16):
            i = 0
            src = bass.AP(self.vd.tensor, h * DL + DOFF - 127, [[1, 128], [1, 256]])
            P.dma('sp', self.grev[i][:, :], src, reads=['vd'], writes=[('grev', i)])
            b = self.psrot((4, 5))
            pst = self.ps[b]
            P.op('pe', lambda e, pst=pst, i=i: e.matmul(pst[:, 0:256], self.jmat[:, :], self.grev[i][:, :], start=True, stop=True),
                 reads=['jmat', ('grev', i)], writes=[('ps', b)])
            P.op('act', lambda e, pst=pst, h=h: e.activation(self.G[:, h, :], pst[:, 0:256], AF.Copy), reads=[('ps', b)], writes=['G'])
        P.op('dve', lambda e: e.tensor_copy(self.chd[:, :], self.G[:, :, 255]), reads=['G'], writes=['chd'])

    def save_sample_bias(self):
        P = self.P
        t0 = self.ptmp[0][:, :].rearrange("p (h t) -> p h t", h=16)
        t1 = self.ptmp[1][:, :].rearrange("p (h t) -> p h t", h=16)
        sbd = self.sbd

        def put(k):
            P.dma('sp', sbd[k], self.ptmp[0][:, :], reads=[('ptmp', 0)], writes=[('sbd', k)])
        P.op('dve', lambda e: e.tensor_copy(t0, self.G[:, :, 128:136]), reads=['G'], writes=[('ptmp', 0)])
        put(0)
        P.op('pool', lambda e: e.memset(self.ptmp[0][:, :], MASKNEG), writes=[('ptmp', 0)])
        P.op('dve', lambda e: e.tensor_copy(t0[0:8], self.G[0:8, :, 0:8]), reads=['G'], writes=[('ptmp', 0)])
        put(1)
        for k, c0 in ((2, 128), (3, 0)):
            P.op('dve', lambda e, c0=c0: e.tensor_copy(t0, self.mw[:, c0:c0 + 8].unsqueeze(1).to_broadcast([128, 16, 8])), reads=['mw'], writes=[('ptmp', 0)])
            P.op('dve', lambda e: e.tensor_tensor(t0, t0, self.chd[:, :].unsqueeze(2).to_broadcast([128, 16, 8]), op=ALU.add), reads=['chd'], writes=[('ptmp', 0)])
            put(k)
        P.op('dve', lambda e: e.tensor_copy(t0, self.chd[:, :].unsqueeze(2).to_broadcast([128, 16, 8])), reads=['chd'], writes=[('ptmp', 0)])
        put(4)
        src = bass.AP(self.vd.tensor, DOFF + 8192 - 63 - 64 * 127, [[64, 128], [DL, 16], [1, 8]])
        P.dma('sp', t1, src, reads=['vd'], writes=[('ptmp', 1)])
        b = self.psrot((4, 5))
        pst = self.ps[b]
        P.op('pe', lambda e: e.matmul(pst[:, 0:128], self.jmat[:, :], self.ptmp[1][:, :], start=True, stop=True), reads=['jmat', ('ptmp', 1)], writes=[('ps', b)])
        P.op('act', lambda e: e.activation(self.ptmp[0][:, :], pst[:, 0:128], AF.Copy), reads=[('ps', b)], writes=[('ptmp', 0)])
        put(5)

    def attn_sample(self, S, n_pool):
        P, D = self.P, self.D
        NPGS = 4
        identf2 = S("identf2", [128, 128], F32)
        identb2 = S("identb2", [128, 128], BF16)
        sbias = S("sbias", [128, 6, 128], F32)
        rep = S("rep", [128, 128], F32)
        nsel = S("nsel", [128, 4], F32)
        seln = S("seln", [128, 4, 8], BF16)
        selsum = S("selsum", [128, 128], BF16)
        t0f = S("t0f", [128, 128], F32)
        selas = S("selas", [128, 128], F32)
        cw1b2 = S("cw1b2", [128, 64, 64], BF16)
        cw2k2 = S("cw2k2", [64, 2, 128], BF16)
        cw2v2 = S("cw2v2", [64, 64], BF16)
        hx2 = S("hx2", [64, 32], F32)
        htt2 = S("htt2", [64, 32], F32)
        vtf2 = S("vtf2", [64, 128], F32)
        Xs = S("Xs", [128, NPGS, 2, 256], BF16)
        hidS = S("hidS", [64, 2, 128, 4], BF16)
        self.cx = dict(cw1b=cw1b2, hx=hx2, htt=htt2, hid=hidS, cw2k=cw2k2, cw2v=cw2v2, vtf=vtf2, identf=identf2)
        vcbS = S("vcbS", [128, 4, 196], BF16)
        KTp = [S("KTp%d" % i, [128, 4, 128], BF16) for i in range(2)]
        Vp = [S("Vp%d" % i, [128, 4, 66], BF16) for i in range(2)]
        Vraw = [S("Vraw%d" % i, [128, 256], BF16) for i in range(2)]
        wkT = S("wkT", [128, 4, 512], BF16)
        wv = S("wv", [128, 4, 4, 66], BF16)
        indpg = [S("indpg%d" % i, [128, 128], BF16) for i in range(2)]
        negMs = S("negMs", [128, 4, 128], F32)
        negMT = S("negMT", [128, 4, 4, 8], BF16)
        qms = [S("qms%d" % i, [128, 32], BF16) for i in range(4)]
        ktn = [S("ktn%d" % i, [128, 128], BF16) for i in range(2)]
        pTs = [S("pTs%d" % i, [128, 128], BF16) for i in range(2)]
        ptm = [S("ptm%d" % i, [128, 32], F32) for i in range(2)]
        osum = S("osum", [128, 4, 64], F32)
        opad = S("opad", [128, 2, 128], BF16)
        grow = S("grow", [128, 12], F32)
        gtmp = S("gtmp", [128, 3, 4, 4], F32)
        gts = S("gts", [128, 48], F32)
        rdn = S("rdn", [128, 2], F32)
        imps = S("imps", [128, 128], F32)
        impw = S("impw2", [128, 128], F32)
        mx8 = S("mx8b", [128, 8], F32)
        pnb = S("pnb", [128, 128], BF16)
        ptb = S("ptb", [128, 64], I32)
        ptf = S("ptf", [128, 64], F32)
        iop = S("iop", [128, 2], F32)
        idxn = S("idxn", [128, 64], I32)
        idxt = S("idxt", [128, 64], I32)

        P.dma('sp', identf2[:], D['ident'], writes=['identf'])
        P.dma('pool', identb2[:], D['ident'], writes=['identb'])
        P.dma('sp', sbias[:], self.sbd.rearrange("k p c -> p k c"), reads=[('sbd', k) for k in range(6)], writes=['sbias'])
        P.dma('sp', rep[:], D['rep'], writes=['rep'])
        P.dma('sp', nsel[:], D['nsel'], writes=['nsel'])
        P.dma('pool', seln[:], D['seln'], writes=['seln'])
        P.dma('pool', selsum[:], D['selsum'], writes=['selsum'])
        P.dma('sp', t0f[:], D['t0f'], writes=['t0f'])
        P.op('pool', lambda e: e.memset(selas[:], 0.0), writes=['selas'])
        P.dma('sp', selas[0:8, :], D['seladd_s'], writes=['selas'])
        P.dma('pool', cw2k2[:], D['cw2k'], writes=['cw2'])
        P.dma('pool', cw2v2[:], D['cw2v'], writes=['cw2'])
        for t_ in (Vp[0], Vp[1], wv):
            P.op('pool', lambda e, t_=t_: e.memset(t_[:], 1.0), writes=['vinit'])
        for i_, t_ in enumerate(qms):
            P.op('pool', lambda e, t_=t_: e.memset(t_[:], 0.0), writes=[('qms', i_)])
        for i_, t_ in enumerate(ktn):
            P.op('pool', lambda e, t_=t_: e.memset(t_[:], 0.0), writes=[('ktn', i_)])
        for i_, t_ in enumerate(pTs):
            P.op('pool', lambda e, t_=t_: e.memset(t_[:], 0.0), writes=[('pTs', i_)])
        P.op('pool', lambda e: e.memset(opad[:], 0.0), writes=['opad'])
        P.op('pool', lambda e: e.memset(pnb[:], 0.0), writes=['pnb'])
        P.op('pool', lambda e: e.memset(negMs[:], 0.0), writes=['negMs'])
        P.op('pool', lambda e: e.memset(osum[:], 0.0), writes=[('osum', g_) for g_ in range(4)])
        P.op('pool', lambda e: e.memset(gts[:], 0.0), writes=['gts'])
        P.op('pool', lambda e: e.memset(imps[:], 0.0), writes=['imps'])
        P.op('pool', lambda e: e.memset(vcbS[:], 0.0), writes=['vcbS'])
        P.op('pool', lambda e: e.memset(vcbS[:, :, 64:65], 1.0), writes=['vcbS'])
        for g in range(4):
            P.op('dve', lambda e, g=g: e.tensor_copy(vcbS[:, g, 66:194], identb2[:, :]), reads=['identb'], writes=['vcbS'])
        P.dma('sp', iop[:], D['iopc'], writes=['iop'])
        pck = D['pck']
        pcv = D['pcv']
        pskT = D['pskT']
        psv = D['psv'].rearrange("n (g d) -> n g d", g=4)
        sb_ = lambda k, g: sbias[:, k, :].rearrange("p (h t) -> p h t", h=16)[:, 4 * g:4 * g + 4, :].rearrange("p h t -> p (h t)")
        pti = [0]
        qmi = [0]

        def unitA(pst, b, g, bk):
            i = pti[0] % 2
            pti[0] += 1
            P.op('dve', lambda e: e.scalar_tensor_tensor(ptm[i][:, :], pst[:, 0:32], SCALE, sb_(bk, g), op0=ALU.mult, op1=ALU.add),
                 reads=[('ps', b), 'sbias'], writes=[('ptm', i)])
            P.op('act', lambda e: e.activation(pTs[i][:, 0:32], ptm[i][:, :], AF.Exp), reads=[('ptm', i)], writes=[('pTs', i)])
            return i

        def unitB(i, lhsV, vkeys, po, pob, first, last, ncols=66):
            P.op('pe', lambda e: e.matmul(po[:, 0:ncols], pTs[i][:, 0:128], lhsV, start=first, stop=last),
                 reads=[('pTs', i)] + vkeys, writes=[('ps', pob)], inc=last)

        def unit(pst, b, g, bk, lhsV, vkeys, po, pob, first, last, ncols=66):
            i = unitA(pst, b, g, bk)
            unitB(i, lhsV, vkeys, po, pob, first, last, ncols)

        def epilogue(po, pob, g, br, first):
            rd = rdn[0:32, 0:1]
            sg = rdn[0:32, 1:2]
            P.op('dve', lambda e: e.tensor_scalar(rd, po[0:32, 64:65], 1e-30, None, op0=ALU.max), reads=[('ps', pob)], writes=['rdn'])
            P.op('dve', lambda e: e.reciprocal(rd, rd), reads=['rdn'], writes=['rdn'])
            P.op('dve', lambda e: e.tensor_tensor(sg, rd, grow[0:32, br * 4 + g:br * 4 + g + 1], op=ALU.mult), reads=['rdn', 'grow'], writes=['rdn2'])
            om = osum[0:32, g, :]
            if first:
                P.op('dve', lambda e: e.tensor_scalar(om, po[0:32, 0:64], sg, None, op0=ALU.mult), reads=[('ps', pob), 'rdn2'], writes=[('osum', g)])
            else:
                P.op('dve', lambda e: e.scalar_tensor_tensor(om, po[0:32, 0:64], sg, om, op0=ALU.mult, op1=ALU.add), reads=[('ps', pob), 'rdn2'], writes=[('osum', g)])
            return rd

        def load_q(s, g):
            hp = g % 2
            lo, hi = hp * 64, hp * 64 + 64
            k = hp * 2 + g // 2
            c0 = 4 * (g // 2)
            P.op('act', lambda e: e.activation(qms[k][lo:hi, :].rearrange("p (n t) -> p n t", n=4), self.qT[lo:hi, c0:c0 + 4, 1056 + 8 * s:1064 + 8 * s], AF.Copy),
                 reads=[('qT', c0 + n_) for n_ in range(4)], writes=[('qms', k)])
            return qms[k], ('qms', k)

        for s in range(4):
            P.dma('sp', ptb[:], D['pt'][:, 64 * s:64 * s + 64].partition_broadcast(128), reads=['idxn', 'idxt'], writes=['ptb'])
            P.op('dve', lambda e: e.tensor_copy(ptf[:], ptb[:]), reads=['ptb'], writes=['ptf'])
            P.op('dve', lambda e: e.tensor_scalar(ptf[:], ptf[:], 128.0, iop[:, 0:1], op0=ALU.mult, op1=ALU.add), reads=['ptf', 'iop'], writes=['ptf'])
            P.op('dve', lambda e: e.tensor_copy(idxn[:], ptf[:]), reads=['ptf'], writes=['idxn'])
            P.op('dve', lambda e: e.tensor_copy(ptf[:], ptb[:]), reads=['ptb', 'idxn'], writes=['ptf'])
            P.op('dve', lambda e: e.tensor_scalar(ptf[:], ptf[:], 64.0, iop[:, 1:2], op0=ALU.mult, op1=ALU.add), reads=['ptf', 'iop'], writes=['ptf'])
            P.op('dve', lambda e: e.tensor_copy(idxt[:], ptf[:]), reads=['ptf'], writes=['idxt'])
            P.op('act', lambda e, s=s: e.activation(gts[0:8, :], self.gates[0:8, 9 + s, :], AF.Copy), reads=[('gates', 9 + s)], writes=['gts'])
            b = self.psrot((0, 1, 2, 3))
            pst = self.ps[b]
            P.op('pe', lambda e, pst=pst: e.matmul(pst[:, 0:48], rep[:, :], gts[:, :], start=True, stop=True), reads=['rep', 'gts'], writes=[('ps', b)])
            P.op('dve', lambda e, pst=pst: e.tensor_tensor(gtmp[:, :, :, :], pst[:, 0:48].rearrange("p (b g n) -> p b g n", b=3, g=4),
                                                           nsel[:, :].unsqueeze(1).unsqueeze(1).to_broadcast([128, 3, 4, 4]), op=ALU.mult),
                 reads=[('ps', b), 'nsel'], writes=['gtmp'])
            P.op('dve', lambda e: e.tensor_reduce(grow[:, :], gtmp[:, :, :, :].rearrange("p b g n -> p (b g) n"), mybir.AxisListType.X, ALU.add),
                 reads=['gtmp'], writes=['grow'])
            for kv, src in ((0, pck), (1, pcv)):
                P.dma('pool', cw1b2[:], D['cw1k' if kv == 0 else 'cw1v'], writes=['cw1'])
                for st_ in range(64 // NPGS):
                    for j in range(NPGS):
                        col = st_ * NPGS + j
                        P.custom_dma('pool', lambda e, j=j, kv=kv, src=src, col=col: e.indirect_dma_start(
                            out=Xs[:, j, kv, :], out_offset=None, in_=src, in_offset=bass.IndirectOffsetOnAxis(ap=idxn[:, col:col + 1], axis=0)),
                            reads=['idxn'], writes=[('Xs', j, kv)])
                    self.compress1(Xs, NPGS, [('Xs', j, kv) for j in range(NPGS)], 'data', pg0=st_ * NPGS, kvs=(kv,), load_w=False)
            self.compress2(128, self.kcbT, vcbS)
            for g in range(4):
                gp = g // 2
                qm, qmk = load_q(s, g)
                b = self.psrot((0, 1, 2, 3))
                pst = self.ps[b]
                P.op('pe', lambda e, pst=pst, gp=gp, qm=qm: e.matmul(pst[:, 0:32], self.kcbT[:, gp, 0:128], qm[:, 0:32], start=True, stop=True),
                     reads=['kcbT', qmk], writes=[('ps', b)])
                pob = 4 + g
                po = self.ps[pob]
                unit(pst, b, g, 5, vcbS[:, g, 0:194], ['vcbS'], po, pob, True, True, ncols=194)
                rd = epilogue(po, pob, g, 0, True)
                P.op('dve', lambda e, po=po, rd=rd: e.tensor_scalar(pnb[0:32, :], po[0:32, 66:194], rd, None, op0=ALU.mult), reads=[('ps', pob), 'rdn'], writes=['pnb'])
                b2 = self.psrot((0, 1, 2, 3))
                ps2 = self.ps[b2]
                P.op('pe', lambda e, ps2=ps2: e.matmul(ps2[:, 0:128], selsum[:, :], pnb[:, :], start=True, stop=True), reads=['selsum', 'pnb'], writes=[('ps', b2)])
                P.op('dve', lambda e, ps2=ps2: e.tensor_tensor(imps[0:8, :], ps2[0:8, 0:128], selas[0:8, :], op=ALU.add), reads=[('ps', b2), 'selas'], writes=['imps'])
                P.op('dve', lambda e: e.max(mx8[0:8, :], imps[0:8, :]), reads=['imps'], writes=['mx8'])
                P.op('dve', lambda e: e.match_replace(impw[0:8, :], mx8[0:8, :], imps[0:8, :], -3.0e38), reads=['imps', 'mx8'], writes=['impw'])
                P.op('dve', lambda e: e.max(mx8[0:8, :], impw[0:8, :]), reads=['impw'], writes=['mx8'])
                P.op('dve', lambda e: e.tensor_scalar(impw[0:8, :], imps[0:8, :], mx8[0:8, 6:7], None, op0=ALU.is_ge), reads=['imps', 'mx8'], writes=['impw'])
                P.op('dve', lambda e, g=g: e.tensor_scalar(negMs[0:8, g, :], impw[0:8, :], -1.0, -MASKNEG, op0=ALU.add, op1=ALU.mult), reads=['impw'], writes=['negMs'])
                b3 = self.psrot((0, 1, 2, 3))
                ps3 = self.ps[b3]
                P.op('pe', lambda e, ps3=ps3, g=g: e.transpose(ps3[:, 0:128], negMs[:, g, :], identf2[:, :]), reads=['negMs', 'identf'], writes=[('ps', b3)])
                P.op('act', lambda e, ps3=ps3: e.activation(ptm[0][:, 0:8], ps3[:, 0:8], AF.Copy), reads=[('ps', b3)], writes=[('ptm', 0)])
                P.op('dve', lambda e, g=g: e.tensor_copy(negMT[:, g, :, :], ptm[0][:, 0:8].unsqueeze(1).to_broadcast([128, 4, 8])), reads=[('ptm', 0)], writes=[('negMT', g)])
            qq = [load_q(s, g) for g in range(4)]
            pend = [None]
            for i_ in range(64):
                col = i_
                kb = i_ % 2
                P.custom_dma('pool', lambda e, kb=kb, col=col: e.indirect_dma_start(
                    out=KTp[kb][:, :, :].rearrange("p g k -> p (g k)"), out_offset=None, in_=pskT, in_offset=bass.IndirectOffsetOnAxis(ap=idxt[:, col:col + 1], axis=0)),
                    reads=['idxt'], writes=[('KTp', kb)])
                P.custom_dma('pool', lambda e, kb=kb, col=col: e.indirect_dma_start(
                    out=Vraw[kb][:, :], out_offset=None, in_=D['psv'], in_offset=bass.IndirectOffsetOnAxis(ap=idxn[:, col:col + 1], axis=0)),
                    reads=['idxn'], writes=[('Vraw', kb)])
                P.op('act', lambda e, kb=kb: e.activation(Vp[kb][:, :, 0:64], Vraw[kb][:, :].rearrange("p (g d) -> p g d", g=4), AF.Copy),
                     reads=[('Vraw', kb), 'vinit'], writes=[('Vp', kb)])
                P.op('dve', lambda e, kb=kb, i_=i_: e.tensor_scalar(indpg[kb][:, :], t0f[:, :], float(2 * i_), None, op0=ALU.is_equal), reads=['t0f'], writes=[('indpg', kb)])
                for g in range(4):
                    qm, qmk = qq[g]
                    b = self.psrot((0, 1, 2, 3))
                    pst = self.ps[b]
                    P.op('pe', lambda e, pst=pst, kb=kb, g=g, qm=qm: e.matmul(pst[:, 0:32], KTp[kb][:, g, :], qm[:, 0:32], start=True, stop=False),
                         reads=[('KTp', kb), qmk], writes=[('ps', b)], inc=False)
                    P.op('pe', lambda e, pst=pst, kb=kb, g=g: e.matmul(pst[:, 0:32], indpg[kb][:, :], negMT[:, g, :, :].rearrange("p n t -> p (n t)"), start=False, stop=True),
                         reads=[('indpg', kb), ('negMT', g)], writes=[('ps', b)])
                    pob = 4 + g
                    cur = (unitA(pst, b, g, 0 if i_ == 63 else 4), Vp[kb][:, g, 0:66], [('Vp', kb)], self.ps[pob], pob, i_ == 0)
                    if pend[0] is not None:
                        unitB(*pend[0], False)
                    pend[0] = cur
            unitB(*pend[0], False)
            for g in range(4):
                gp = g // 2
                qm, qmk = qq[g]
                kk = g % 2
                P.op('act', lambda e, kk=kk, gp=gp, s=s: e.activation(ktn[kk][:, 0:8], self.KTs[:, gp, 2080 + 8 * s:2088 + 8 * s], AF.Copy),
                     reads=[('KT', 0, gp)], writes=[('ktn', kk)])
                b = self.psrot((0, 1, 2, 3))
                pst = self.ps[b]
                P.op('pe', lambda e, pst=pst, kk=kk, qm=qm: e.matmul(pst[:, 0:32], ktn[kk][:, :], qm[:, 0:32], start=True, stop=True),
                     reads=[('ktn', kk), qmk], writes=[('ps', b)])
                pob = 4 + g
                unit(pst, b, g, 1, self.Vs_s[:, s, g, 0:66], [('Vsm', 0, s)], self.ps[pob], pob, False, True)
                epilogue(self.ps[pob], pob, g, 1, False)
            for hh in range(2):
                P.dma('pool', wkT[hh * 64:(hh + 1) * 64, :, :], D['winkT'][s], writes=['wkT'])
            for j_ in range(4):
                P.dma('pool', wv[:, j_, :, 0:64], D['winv'][s][128 * j_:128 * j_ + 128, :].rearrange("p (g d) -> p g d", g=4), reads=['vinit'], writes=['wv'])
            for g in range(4):
                gp = g // 2
                qm, qmk = qq[g]
                pob = 4 + g
                for j in range(4):
                    b = self.psrot((0, 1, 2, 3))
                    pst = self.ps[b]
                    P.op('pe', lambda e, pst=pst, g=g, j=j, qm=qm: e.matmul(pst[:, 0:32], wkT[:, g, 128 * j:128 * j + 128], qm[:, 0:32], start=True, stop=True),
                         reads=['wkT', qmk], writes=[('ps', b)])
                    unit(pst, b, g, (2, 3, 4, 0)[j], wv[:, j, g, 0:66], ['wv'], self.ps[pob], pob, j == 0, False)
                kk = g % 2
                P.op('act', lambda e, kk=kk, gp=gp, s=s: e.activation(ktn[kk][:, 0:8], self.KTw[:, gp, 2080 + 8 * s:2088 + 8 * s], AF.Copy),
                     reads=[('KT', 1, gp)], writes=[('ktn', kk)])
                b = self.psrot((0, 1, 2, 3))
                pst = self.ps[b]
                P.op('pe', lambda e, pst=pst, kk=kk, qm=qm: e.matmul(pst[:, 0:32], ktn[kk][:, :], qm[:, 0:32], start=True, stop=True),
                     reads=[('ktn', kk), qmk], writes=[('ps', b)])
                unit(pst, b, g, 1, self.Vw_s[:, s, g, 0:66], [('Vsm', 1, s)], self.ps[pob], pob, False, True)
                epilogue(self.ps[pob], pob, g, 2, False)
            for g in range(4):
                P.op('act', lambda e, g=g: e.activation(opad[0:32, 0, 0:64], osum[0:32, g, :], AF.Copy), reads=[('osum', g)], writes=['opad'])
                P.op('act', lambda e, g=g: e.activation(opad[0:32, 1, 64:128], osum[0:32, g, :], AF.Copy), reads=[('osum', g)], writes=['opad'])
                for pr in range(2):
                    b = self.psrot((0, 1, 2, 3))
                    pst = self.ps[b]
                    for hh in range(2):
                        P.op('pe', lambda e, pst=pst, hh=hh, pr=pr: e.matmul(pst[:, 0:8], opad[:, hh, :], seln[:, 2 * pr + hh, :], start=(hh == 0), stop=(hh == 1)),
                             reads=['opad', 'seln'], writes=[('ps', b)], inc=(hh == 1))
                    c = 8 + 2 * g + pr
                    P.op('act', lambda e, pst=pst, c=c, s=s: e.activation(self.mixT[:, c, 1056 + 8 * s:1064 + 8 * s], pst[:, 0:8], AF.Copy), reads=[('ps', b)], writes=[('aT', c)])

    def build(self, n_pool):
        nc, P = self.nc, self.P
        D = self.D
        st = self.st
        self.din('xT_main', [128, 16, NT])
        self.din('xT_pre', [128, 16, NPRE])
        self.din('gains', [128, 4, 16])
        self.din('flag', [128, 1])
        self.din('w_ff1', [2, 2048, 8192])
        self.din('w_ff2', [2, 8192, 2048])
        self.din('w_in_o3', [2048, 6144])
        self.din('w_out_o', [2048, 2048])
        self.din('scw', [128, 3, 16])
        self.din('stsc', [128, 16, 4, 2])
        self.din('w_in_p', [2048, 4656])
        self.din('rgp', [128, 8, 8])
        self.din('rg_wa', [128, 8, 128])
        self.din('rg_wx', [128, 8, 128])
        self.din('kg', [128, 4])
        self.din('sth', [128, 8, 4])
        self.din('stconv', [128, 8, 4, 3])
        self.din('wink', [4, 512, 256])
        for nm, shp in (('ident', [128, 128]), ('jmat', [128, 128]), ('mw', [128, 256]), ('tabaug', [33, 16]), ('e1h', [33, DL]),
                        ('selmul', [1056, 32]), ('seladd', [1056, 32]), ('bvalid', [32, 1]), ('j32p', [128, 128]), ('cw1k', [128, 64, 64]), ('cw1v', [128, 64, 64]),
                        ('cw2k', [64, 2, 128]), ('cw2v', [64, 64]), ('cpe_in', [128, 1, 2, 256]), ('w_out_e', [2048, 2048]),
                        ('pck', [n_pool * 128, 256]), ('pcv', [n_pool * 128, 256]), ('psv', [n_pool * 128, 256]), ('pskT', [n_pool * 64, 512]),
                        ('winkT', [4, 64, 4, 512]), ('rep', [128, 128]), ('nsel', [128, 4]), ('seln', [128, 4, 8]), ('selsum', [128, 128]),
                        ('t0f', [128, 128]), ('iopc', [128, 2]), ('selmul_s', [8, 128]), ('seladd_s', [8, 128])):
            self.din(nm, shp)
        self.din('pt', [1, 256], I32)
        self.sbd = nc.dram_tensor('sbd', [6, 128, 128], F32, kind='Internal').ap()
        self.vd = nc.dram_tensor('vd', [16, DL], F32, kind='Internal').ap()
        self.din('winv', [4, 512, 256])
        self.dout('yT_out', [128, 16, NT])
        self.dout('scp_out', [128, 16, 2])
        self.dout('scs_out', [128, 16, 4, 2])
        for nm in ('kc_o', 'vc_o', 'vs_o', 'vw_o'):
            self.dout(nm, [1024, 256])
        for nm in ('kc_s_o', 'vc_s_o', 'vs_s_o', 'vw_s_o'):
            self.dout(nm, [32, 256])
        self.dout('ksT_o', [128, 2, NT])
        self.dout('kwT_o', [128, 2, NT])
        self.dout('rgo_h_o', [128, 8, 5])
        self.dout('wk_copy_o', [4, 504, 256])
        self.dout('wv_copy_o', [4, 504, 256])
        self.dout('rgo_conv_o', [128, 8, 5, 3])
        self.ps = [st.enter_context(nc.psum_tensor("ps%d" % i, [128, 512], F32)) for i in range(8)]
        self.wt = [self.sb("wt%d" % i, [128, 16, 256], BF16) for i in range(2)]
        self.hT = self.sb("hT", [128, 16, NT + 128], BF16)
        self.aT = self.sb("aT", [128, 16, NT], BF16)
        self.gT = self.aT
        self.mixT = self.aT
        self.sq = self.sb("sq", [128, 4, 512], BF16)
        self.rstd = self.sb("rstd", [128, 512], F32)
        self.tmpf = self.sb("tmpf", [128, 512], F32)
        self.ones_bf = self.sb("ones_bf", [128, 128], BF16)
        self.bones = self.sb("bones", [128, 128], BF16)
        self.epst = self.sb("epst", [128, 1], F32)
        self.onec = self.sb("onec", [128, 1], F32)
        self.gains = self.sb("gains", [128, 4, 16], F32)
        self.flag = self.sb("flag", [128, 1], F32)
        self.scw = self.sb("scw", [128, 3, 16], F32)
        self.stsc = self.sb("stsc", [128, 16, 4, 2], F32)
        self.xs = [self.sb("xs%d" % i, [128, 512], F32) for i in range(2)]
        self.xsi = 0
        self.stgi = 0

        P.op('pool', lambda e: e.memset(self.ones_bf[:], 1.0), writes=['ones'])
        P.op('pool', lambda e: e.memset(self.hT[:, :, NT:NT + 128], 0.0), writes=['hTpad'])
        P.op('pool', lambda e: e.memset(self.bones[:], 0.0), writes=['bones'])
        P.op('pool', lambda e: e.memset(self.bones[0:64, 0:64], 1.0), writes=['bones'])
        P.op('pool', lambda e: e.memset(self.bones[64:128, 64:128], 1.0), writes=['bones'])
        P.op('pool', lambda e: e.memset(self.epst[:], EPS), writes=['epst'])
        P.op('pool', lambda e: e.memset(self.onec[:], 1.0), writes=['onec'])
        P.dma('sp', self.gains[:], D['gains'], writes=['gains'])
        P.dma('sp', self.flag[:], D['flag'], writes=['flag'])
        P.dma('sp', self.scw[:], D['scw'], writes=['scw'])
        P.dma('sp', self.stsc[:], D['stsc'], writes=['stsc'])

        st0 = ExitStack()
        L0 = lambda name, shape, dt: self.sb(name, shape, dt, st0)
        self.KTs = L0("KTs", [128, 2, 1024 + NT], BF16)
        self.KTw = L0("KTw", [128, 2, 1024 + NT], BF16)
        self.Vs = L0("Vs", [128, 16, 4, 66], BF16)
        self.Vw = L0("Vw", [128, 16, 4, 66], BF16)
        self.Vs_s = L0("Vs_s", [128, 4, 4, 66], BF16)
        self.Vw_s = L0("Vw_s", [128, 4, 4, 66], BF16)
        self.rgscr = L0("rgscr", [128, 4, NT], F32)
        self.qT = self.rgscr[:].rearrange("p a t -> p (a t)").bitcast(BF16).rearrange("p (c t) -> p c t", c=8)
        self.gates = L0("gates", [128, 13, 48], F32)
        self.sqh = L0("sqh", [128, 512], BF16)
        self.rstd2 = self.rstd
        self.tmpq = self.tmpf
        self.rgp = L0("rgp", [128, 8, 8], F32)
        self.rgwa = L0("rgwa", [128, 8, 128], BF16)
        self.rgwx = L0("rgwx", [128, 8, 128], BF16)
        self.kg = L0("kg", [128, 4], F32)
        self.sth = L0("sth", [128, 8, 4], F32)
        self.stconv = L0("stconv", [128, 8, 4, 3], F32)
        self.clam = L0("clam", [128, 8], F32)
        self.convh = L0("convh", [128, 8, 3], F32)
        self.hst = L0("hst", [128, 8], F32)
        self.rgo_h = L0("rgo_h", [128, 8, 5], F32)
        self.rgo_conv = L0("rgo_conv", [128, 8, 5, 3], F32)
        self.rxc = self.rgscr[:, 0, :]
        self.rga = self.rgscr[:, 1, :]
        self.rgi = self.rgscr[:, 2, :]
        self.rtm = self.rgscr[:, 3, :]
        self.kcbT = L0("kcbT", [128, 2, 128], BF16)
        self.vcb = L0("vcb", [128, 4, 100], BF16)
        self.hidT = L0("hidT", [64, 2, 32, 4], BF16)
        self.cpe = L0("cpe", [64, 2, 4], F32)
        self.bvalid = L0("bvalid", [32, 1], F32)
        st0a = ExitStack()
        LA = lambda name, shape, dt: self.sb(name, shape, dt, st0a)
        self.hTp = LA("hTp", [128, 16, 256], BF16)
        self.kcpg = LA("kcpg", [128, 8, 2, 256], BF16)
        self.stg = [LA("stg%d" % i, [128, 512], F32) for i in range(2)]
        self.gx = self.stg[0]
        self.gt = self.stg[1]
        self.rxb = LA("rxb", [128, NT], F32)
        self.rxcb = LA("rxcb", [128, NT], BF16)
        self.rhist0 = LA("rhist0", [128, 3 + 1024], F32)
        self.rhists = LA("rhists", [128, 4, 11], F32)
        self.cw1b = LA("cw1b", [128, 64, 64], BF16)
        self.pe_t = LA("pe_t", [128, 1, 2, 256], BF16)
        self.hx = LA("hx", [64, 32], F32)
        self.htt = LA("htt", [64, 32], F32)
        P.dma('pool', self.pe_t[:], D['cpe_in'], writes=['pe_t'])
        P.dma('sp', self.bvalid[:], D['bvalid'], writes=['bvalid'])
        self.cx = dict(cw1b=self.cw1b, hx=self.hx, htt=self.htt, hid=self.hidT)
        self.job(None, lambda: self.compress1(self.pe_t, 1, ['pe_t'], 'pe'))

        for nm, t in (('rgp', self.rgp), ('kg', self.kg), ('sth', self.sth), ('stconv', self.stconv)):
            P.dma('sp', t[:], D[nm], writes=['cst'])
        P.dma('pool', self.rgwa[:], D['rg_wa'], writes=['cst'])
        P.dma('pool', self.rgwx[:], D['rg_wx'], writes=['cst'])
        P.op('pool', lambda e: e.memset(self.convh[:], 0.0), writes=[('convh', r) for r in range(8)])
        P.op('pool', lambda e: e.memset(self.hst[:], 0.0), writes=[('hst', r) for r in range(8)])
        P.op('pool', lambda e: e.memset(self.Vs[:], 1.0), writes=[('V', 0, i) for i in range(16)])
        P.op('pool', lambda e: e.memset(self.Vw[:], 1.0), writes=[('V', 1, i) for i in range(16)])
        for Vt, hh in ((self.Vs, 0), (self.Vw, 1)):
            P.op('dve', lambda e, Vt=Vt: e.tensor_scalar(Vt[:, 0:8, :, 64:65], Vt[:, 0:8, :, 64:65], self.flag[:, 0:1], None, op0=ALU.mult), reads=['flag'], writes=[('V', hh, i) for i in range(16)])
        P.op('pool', lambda e: e.memset(self.Vs_s[:], 1.0), writes=[('Vsm', 0, i) for i in range(4)])
        P.op('pool', lambda e: e.memset(self.Vw_s[:], 1.0), writes=[('Vsm', 1, i) for i in range(4)])
        P.op('act', lambda e: e.activation(self.clam[:], self.rgp[:, :, 7], AF.Exp, scale=-1.0), reads=['cst'], writes=['clam'])
        P.op('act', lambda e: e.activation(self.clam[:], self.clam[:], AF.Ln, bias=self.onec[:, 0:1], scale=1.0), reads=['clam', 'onec'], writes=['clam'])
        P.op('dve', lambda e: e.tensor_scalar(self.clam[:], self.clam[:], -8.0, None, op0=ALU.mult), reads=['clam'], writes=['clam'])

        if os.environ.get('K_NOCOPY') is None:
            P.dma('sp', D['wk_copy_o'], D['wink'][:, 8:512, :], writes=['wkc'])
            P.dma('sp', D['wv_copy_o'], D['winv'][:, 8:512, :], writes=['wvc'])
        self.l0_inproj()
        self.run_jobs()
        P.dma('sp', D['rgo_h_o'], self.rgo_h[:], reads=['rgo_h'], writes=['rgo_h_o'])
        P.dma('sp', D['rgo_conv_o'], self.rgo_conv[:], reads=['rgo_conv'], writes=['rgo_conv_o'])
        P.barrier()
        st0a.close()
        st0b = ExitStack()
        LB = lambda name, shape, dt: self.sb(name, shape, dt, st0b)
        self.cw2k = LB("cw2k", [64, 2, 128], BF16)
        self.cw2v = LB("cw2v", [64, 64], BF16)
        self.vtf = LB("vtf", [64, 32], F32)
        P.dma('pool', self.cw2k[:], D['cw2k'], writes=['cw2'])
        P.dma('pool', self.cw2v[:], D['cw2v'], writes=['cw2'])
        self.attn_setup(LB)
        P.op('pool', lambda e: e.memset(self.vcb[:], 0.0), writes=['vcb'])
        P.op('pool', lambda e: e.memset(self.kcbT[:], 0.0), writes=['kcbT'])
        self.cx = dict(cw2k=self.cw2k, cw2v=self.cw2v, vtf=self.vtf, identf=self.identf, hid=self.hidT)
        self.compress2(32, self.kcbT, self.vcb)
        for g in range(4):
            P.op('dve', lambda e, g=g: e.tensor_scalar(self.vcb[0:32, g, 0:64], self.vcb[0:32, g, 0:64], self.bvalid[:, 0:1], None, op0=ALU.mult), reads=['bvalid'], writes=['vcb'])
            P.op('dve', lambda e, g=g: e.tensor_copy(self.vcb[0:32, g, 64:65], self.bvalid[:, 0:1]), reads=['bvalid'], writes=['vcb'])
            P.op('dve', lambda e, g=g: e.tensor_copy(self.vcb[0:32, g, 66:98], self.identb[0:32, 0:32]), reads=['identb'], writes=['vcb'])
        P.op('pool', lambda e: e.memset(self.mixT[:, 8:16, 1056:1088], 0.0), writes=[('aT', c_) for c_ in range(8, 16)])
        if os.environ.get('K_NOATT') is None:
            self.attn_prompt_tile(0, 0, 32, 992)
            for j in range(8):
                self.attn_prompt_tile(1 + j, 32 + 128 * j, 128, 1024 + 128 * j)
        self.save_sample_bias()
        P.barrier()
        st0b.close()
        st0c = ExitStack()
        LC = lambda name, shape, dt: self.sb(name, shape, dt, st0c)
        if os.environ.get('K_NOSAMPLE') is None:
            self.attn_sample(LC, n_pool)
        P.barrier()
        st0c.close()
        st0.close()

        self.yT = self.sb("yT", [128, 16, NT], F32)
        self.tb = self.sb("tb", [128, NT], F32)
        self.tc = self.sb("tc", [128, NT], F32)
        self.tv = self.sb("tv", [128, NT], F32)
        self.ty = self.sb("ty", [128, NT], F32)
        self.tvs = self.sb("tvs", [128, 4, 10], F32)
        self.scp = self.sb("scp", [128, 16, 2], F32)
        self.scs = self.sb("scs", [128, 16, 4, 2], F32)
        for c in range(16):
            P.dma('sp', self.yT[:, c, :], D['xT_main'][:, c, :], writes=[('yT', c)])
        ysrc = lambda c, t0, n: self.yT[:, c, t0:t0 + n]
        hdst = lambda c, t0, n: self.hT[:, c, t0:t0 + n]
        ykey = lambda c: ('yT', c)
        hkey = lambda c: ('hT', c)
        Woe = D['w_out_e']
        mparts = [(self.mixT, lambda k: ('aT', k), t0, n, None) for (t0, n) in TT]
        for cg in range(8):
            def epi_o(m, tag, t0, n, ps, pkey, cg=cg):
                c = cg * 2 + m
                P.op('dve', lambda e, ps=ps, c=c, t0=t0, n=n: e.tensor_tensor(
                    self.yT[:, c, t0:t0 + n], ps, self.yT[:, c, t0:t0 + n], op=ALU.add),
                    reads=[pkey], writes=[('yT', c)])
            self.dense_fm(self.wview(Woe, 0, 16, cg * 256, 256), 16, 2, mparts, epi_o)
        self.rmsnorm(ysrc, ykey, self.gains[:, 2, :], hdst, hkey, TT)
        self.ffn(0)
        self.rmsnorm(ysrc, ykey, self.gains[:, 1, :], hdst, hkey, TT)
        self.mixer1()
        self.rmsnorm(ysrc, ykey, self.gains[:, 3, :], hdst, hkey, TT)
        self.ffn(1)
        self.run_jobs()
        for c in range(16):
            P.dma('sp', D['yT_out'][:, c, :], self.yT[:, c, :], reads=[('yT', c)], writes=[('yo', c)])
        P.dma('sp', D['scp_out'], self.scp[:], reads=['scp'], writes=['scpo'])
        P.dma('sp', D['scs_out'], self.scs[:], reads=['scs'], writes=['scso'])
        P.finish()
        P.emit()
        self.st.close()
        return nc


_CACHE = {}


def _fm(a, nchunk):
    T = a.shape[0]
    return np.ascontiguousarray(a.reshape(T, nchunk, 128).transpose(2, 1, 0))


def _vec_pm(v, nchunk):
    return np.ascontiguousarray(v.reshape(nchunk, 128).T)


def _tm(a):
    return np.ascontiguousarray(a.transpose(2, 1, 0).reshape(a.shape[2], -1))


def _q_perm():
    cols = []
    for i in range(8):
        hA = (i % 4) + 8 * (i // 4)
        hB = hA + 4
        cols += list(range(2048 + hA * 64, 2048 + hA * 64 + 64)) + list(range(2048 + hB * 64, 2048 + hB * 64 + 64))
    return cols


def prepare_inputs(inp):
    f = lambda k: np.asarray(inp[k])
    x_prompt = f('x_prompt')
    x_sample = f('x_sample')
    sh = {}
    sh['gains'] = np.ascontiguousarray(np.stack([_vec_pm(f('norm_mix')[0], 16), _vec_pm(f('norm_mix')[1], 16),
                                                 _vec_pm(f('norm_ffn')[0], 16), _vec_pm(f('norm_ffn')[1], 16)], axis=1))
    sh['w_ff1'] = f('w_ff1')
    sh['w_ff2'] = f('w_ff2')
    wio = f('w_in_o')[0]
    sh['w_in_o3'] = np.ascontiguousarray(wio.reshape(2048, 3, 16, 128).transpose(0, 2, 1, 3).reshape(2048, 6144))
    sh['w_out_o'] = f('w_out_o')[0]
    sh['scw'] = np.ascontiguousarray(f('sc_w')[0].reshape(3, 16, 128).transpose(2, 0, 1))
    wie = f('w_in_e')[0]
    R = lambda a, b: list(range(a, b))
    perm = R(0, 2048) + _q_perm() + R(3584, 3840) + R(4096, 4352) + R(3072, 3584) + R(3840, 4096) + R(4352, 4608) + R(4608, 4656)
    sh['w_in_p'] = np.ascontiguousarray(wie[:, perm])
    rows = [f('rg_conv_w')[0][j] for j in range(4)] + [f('rg_conv_b')[0], f('rg_ba')[0], f('rg_bx')[0], f('rg_lambda')[0]]
    sh['rgp'] = np.ascontiguousarray(np.stack([_vec_pm(r_, 8) for r_ in rows], axis=2))
    sh['rg_wa'] = np.ascontiguousarray(f('rg_wa')[0].transpose(1, 0, 2))
    sh['rg_wx'] = np.ascontiguousarray(f('rg_wx')[0].transpose(1, 0, 2))
    kn = f('k_norm')[0]
    qn = f('q_norm')[0]
    sh['kg'] = np.ascontiguousarray(np.stack([np.tile(kn[0], 2), np.tile(kn[1], 2), np.tile(kn[2], 2), np.tile(qn, 2)], axis=1))
    sh['w_out_e'] = f('w_out_e')[0]
    sh['ident'] = np.eye(128, dtype=np.float32)
    sh['jmat'] = np.ascontiguousarray(np.eye(128, dtype=np.float32)[::-1])
    j32p = np.zeros((128, 128), np.float32)
    j32p[0:32, 0:32] = np.eye(32, dtype=np.float32)[::-1]
    sh['j32p'] = j32p
    pp = np.arange(128)[:, None]
    xx = np.arange(384, 640)[None, :]
    sh['mw'] = np.where(xx - pp > 512, np.float32(MASKNEG), np.float32(0.0)).astype(np.float32)
    sh['tabaug'] = np.concatenate([f('rel_bias'), np.full((1, 16), MASKNEG, np.float32)], axis=0).astype(np.float32)
    dist = np.arange(DL) - DOFF
    nn = np.maximum(dist, 0)
    nf = np.maximum(nn, 1).astype(np.float32)
    large = 16 + (np.log(nf / np.float32(16.0)).astype(np.float32) / np.float32(np.log(8.0)) * np.float32(16.0)).astype(np.int32)
    large = np.minimum(large, 31)
    buck = np.where(nn < 16, nn, large)
    e1h = np.zeros((33, DL), np.float32)
    e1h[buck[dist >= 0], np.nonzero(dist >= 0)[0]] = 1.0
    e1h[32, dist < 0] = 1.0
    sh['e1h'] = e1h
    w1k = f('cmp_w1_k')[0]; w1v = f('cmp_w1_v')[0]
    sh['cw1k'] = np.ascontiguousarray(np.concatenate([w1k, w1k], axis=0))
    sh['cw1v'] = np.ascontiguousarray(np.concatenate([w1v, w1v], axis=0))
    w2k = f('cmp_w2_k')[0]
    c2 = np.zeros((64, 2, 128), np.float32)
    c2[:, 0, 0:64] = w2k
    c2[:, 1, 64:128] = w2k
    sh['cw2k'] = c2
    sh['cw2v'] = f('cmp_w2_v')[0]
    pek = f('cmp_pe_k')[0].reshape(64, 256); pev = f('cmp_pe_v')[0].reshape(64, 256)
    pe2 = np.stack([pek, pev], axis=1)
    sh['cpe_in'] = np.ascontiguousarray(np.concatenate([pe2, pe2], axis=0)[:, None])
    npool = f('cache_cmp_k').shape[1]
    sh['pck'] = f('cache_cmp_k')[0].reshape(npool * 128, 256)
    sh['pcv'] = f('cache_cmp_v')[0].reshape(npool * 128, 256)
    sh['psv'] = f('cache_sel_v')[0].reshape(npool * 128, 256)
    sh['pskT'] = np.ascontiguousarray(f('cache_sel_k')[0].transpose(0, 3, 2, 1)).reshape(npool * 64, 512)
    rr = np.arange(128)
    rep = np.zeros((128, 128), np.float32)
    for c_ in range(32):
        rep[c_ % 8, c_] = 1.0
    sh['rep'] = rep
    sh['nsel'] = ((rr[:, None] // 8 == np.arange(4)[None, :]) & (rr[:, None] < 32)).astype(np.float32)
    sh['seln'] = ((rr[:, None, None] < 32) & (rr[:, None, None] // 8 == np.arange(4)[None, :, None]) & (rr[:, None, None] % 8 == np.arange(8)[None, None, :])).astype(np.float32)
    ssum = np.zeros((128, 128), np.float32)
    for r_ in range(32):
        ssum[r_, r_ % 8] = 1.0
    sh['selsum'] = ssum
    sh['iopc'] = np.stack([rr, rr % 64], axis=1).astype(np.float32)
    sh['t0f'] = (rr[:, None] - (rr[None, :] // 64)).astype(np.float32)
    sh['selmul_s'] = np.ones((8, 128), np.float32)
    sa = np.zeros((8, 128), np.float32)
    sa[:, 0] = 1.0e4
    sa[:, 127] = 1.0e4
    sh['seladd_s'] = sa
    maps = []
    for c in range(8):
        b, half = c // 2, c % 2
        p0 = 1024 * half
        m = dict(sh)
        b0 = 0 if half == 1 else 16
        m['pt'] = np.ascontiguousarray(f('page_table')[4 * c:4 * c + 4].reshape(1, 256).astype(np.int32))
        m['winkT'] = np.ascontiguousarray(f('cache_win_k')[0, 4 * c:4 * c + 4].transpose(0, 3, 2, 1))
        qp = np.concatenate([np.arange(992, 1024), np.arange(1024, 2048)])
        cur = (qp // 64)[:, None]
        blk = np.arange(32)[None, :]
        bad = (blk > cur) | (blk < b0)
        forced = ((blk == cur) | (blk == cur - 1) | (blk == b0)) & ~bad
        m['selmul'] = np.where(bad, 0.0, 1.0).astype(np.float32)
        m['seladd'] = np.where(bad, -1.0e30, np.where(forced, 1.0e4, 0.0)).astype(np.float32)
        m['bvalid'] = (np.arange(32) >= b0).astype(np.float32)[:, None]
        own = x_prompt[b, p0:p0 + 1024]
        pre = x_prompt[b, 0:1024] if half == 1 else np.zeros((1024, 2048), np.float32)
        xs = x_sample[4 * c:4 * c + 4].reshape(32, 2048)
        main = np.concatenate([pre[992:1024], own, xs], axis=0)
        m['xT_main'] = _fm(main, 16)
        m['xT_pre'] = _fm(pre, 16)
        m['flag'] = np.full((128, 1), float(half), np.float32)
        st = f('state_sconv')[0, 4 * c:4 * c + 4]
        m['stsc'] = np.ascontiguousarray(st.reshape(4, 2, 16, 128).transpose(3, 2, 0, 1))
        m['sth'] = np.ascontiguousarray(f('state_rglru_h')[0, 4 * c:4 * c + 4].reshape(4, 8, 128).transpose(2, 1, 0))
        m['stconv'] = np.ascontiguousarray(f('state_rglru_conv')[0, 4 * c:4 * c + 4].reshape(4, 3, 8, 128).transpose(3, 2, 0, 1))
        m['wink'] = np.ascontiguousarray(f('cache_win_k')[0, 4 * c:4 * c + 4].reshape(4, 512, 256))
        m['winv'] = np.ascontiguousarray(f('cache_win_v')[0, 4 * c:4 * c + 4].reshape(4, 512, 256))
        maps.append(m)
    return maps


def _kT_to_rows(a):
    T = a.shape[2]
    return np.ascontiguousarray(a.reshape(2, 64, 2, T).transpose(3, 2, 0, 1).reshape(T, 4, 64))


def kernel(**inputs):
    n_pool = int(np.asarray(inputs['cache_cmp_k']).shape[1])
    if 'nc' not in _CACHE:
        b = Builder()
        _CACHE['nc'] = b.build(n_pool)
        _CACHE['in_names'] = b.in_names
    nc = _CACHE['nc']
    maps = prepare_inputs(inputs)
    maps = [{k: m[k] for k in _CACHE['in_names']} for m in maps]
    res = run_bass_kernel_spmd(nc, maps[:NCORES], core_ids=list(range(NCORES)))
    R = res.results
    z = lambda *s: np.zeros(s, np.float32)
    y_p = z(4, 2048, 2048); y_s = z(32, 8, 2048)
    ck_p = z(1, 4, 2048, 4, 64); cv_p = z(1, 4, 2048, 4, 64); sk_p = z(1, 4, 2048, 4, 64); sv_p = z(1, 4, 2048, 4, 64)
    ck_s = z(1, 32, 8, 4, 64); cv_s = z(1, 32, 8, 4, 64); sk_s = z(1, 32, 8, 4, 64); sv_s = z(1, 32, 8, 4, 64)
    wk_p = z(1, 4, 512, 4, 64); wv_p = z(1, 4, 512, 4, 64); wk_s = z(1, 32, 512, 4, 64); wv_s = z(1, 32, 512, 4, 64)
    h_p = z(1, 4, 1024); h_s = z(1, 32, 1024); cv1_p = z(1, 4, 3, 1024); cv1_s = z(1, 32, 3, 1024)
    sconv_p = z(1, 4, 2, 2048); sconv_s = z(1, 32, 2, 2048)
    for c in range(NCORES):
        b, half = c // 2, c % 2
        p0 = 1024 * half
        r = R[c]
        yt = _tm(r['yT_out'])
        y_p[b, p0:p0 + 1024] = yt[32:1056]
        y_s[4 * c:4 * c + 4] = yt[1056:1088].reshape(4, 8, 2048)
        ck_p[0, b, p0:p0 + 1024] = r['kc_o'].reshape(1024, 4, 64)
        cv_p[0, b, p0:p0 + 1024] = r['vc_o'].reshape(1024, 4, 64)
        sv_p[0, b, p0:p0 + 1024] = r['vs_o'].reshape(1024, 4, 64)
        ks_rows = _kT_to_rows(r['ksT_o'])
        kw_rows = _kT_to_rows(r['kwT_o'])
        sk_p[0, b, p0:p0 + 1024] = ks_rows[32:1056]
        ck_s[0, 4 * c:4 * c + 4] = r['kc_s_o'].reshape(4, 8, 4, 64)
        cv_s[0, 4 * c:4 * c + 4] = r['vc_s_o'].reshape(4, 8, 4, 64)
        sv_s[0, 4 * c:4 * c + 4] = r['vs_s_o'].reshape(4, 8, 4, 64)
        sk_s[0, 4 * c:4 * c + 4] = ks_rows[1056:1088].reshape(4, 8, 4, 64)
        wk_s[0, 4 * c:4 * c + 4, 0:504] = r['wk_copy_o'].reshape(4, 504, 4, 64)
        wv_s[0, 4 * c:4 * c + 4, 0:504] = r['wv_copy_o'].reshape(4, 504, 4, 64)
        wk_s[0, 4 * c:4 * c + 4, 504:512] = kw_rows[1056:1088].reshape(4, 8, 4, 64)
        wv_s[0, 4 * c:4 * c + 4, 504:512] = r['vw_s_o'].reshape(4, 8, 4, 64)
        rh = r['rgo_h_o']
        rc = r['rgo_conv_o']
        h_s[0, 4 * c:4 * c + 4] = rh[:, :, 1:5].transpose(2, 1, 0).reshape(4, 1024)
        cv1_s[0, 4 * c:4 * c + 4] = rc[:, :, 1:5, :].transpose(2, 3, 1, 0).reshape(4, 3, 1024)
        sconv_s[0, 4 * c:4 * c + 4] = r['scs_out'].transpose(2, 3, 1, 0).reshape(4, 2, 2048)
        if half == 1:
            wk_p[0, b] = kw_rows[32 + 512:32 + 1024]
            wv_p[0, b] = r['vw_o'][512:1024].reshape(512, 4, 64)
            h_p[0, b] = rh[:, :, 0].T.reshape(1024)
            cv1_p[0, b] = rc[:, :, 0, :].transpose(2, 1, 0).reshape(3, 1024)
            sconv_p[0, b] = _tm(r['scp_out'])
    return (y_p, y_s, ck_p, ck_s, cv_p, cv_s, sk_p, sk_s, sv_p, sv_s, wk_p, wk_s, wv_p, wv_s,
            h_p, h_s, cv1_p, cv1_s, sconv_p, sconv_s)
```

```python
import os
import numpy as np
from contextlib import ExitStack
import concourse.bass as bass
import concourse.mybir as mybir
from concourse.bass_utils import run_bass_kernel_spmd

F32 = mybir.dt.float32
BF16 = mybir.dt.bfloat16
I32 = mybir.dt.int32
ALU = mybir.AluOpType
AF = mybir.ActivationFunctionType

SAME_ENGINE_SYNC = True
EPOCH = 30000
NDMASEM = 48

NT = 1088
NPRE = 1024
TT = [(0, 512), (512, 512), (1024, 64)]
TTP = [(0, 512), (512, 512)]
EPS = 1e-6
MASKNEG = -30000.0
SCALE = 0.125
DOFF = 2304
DL = 10752
DEBUG_STAGE = 3
NCORES = 8


class Prog:
    ENGS = ('pe', 'dve', 'act', 'pool', 'sp')

    def __init__(self, nc, stack):
        self.nc = nc
        self.stack = stack
        self.nsem = 0
        self.ops = {e: [] for e in self.ENGS}
        self.csem = {e: self._newsem('c_' + e) for e in self.ENGS}
        self.ccnt = {e: 0 for e in self.ENGS}
        self.seen = {e: {} for e in self.ENGS}
        self.lastw = {}
        self.readers = {}
        self.dsem = [self._newsem('d%d' % i) for i in range(NDMASEM)]
        self.dval = [0] * NDMASEM
        self.di = 0
        self.dq = {'sp': (0, 32), 'pool': (32, 16), 'act': (0, 32)}
        self.dqi = {'sp': 0, 'pool': 0, 'act': 0}
        self.nwaits = 0
        self.allsems = []

    def _newsem(self, name):
        self.nsem += 1
        s = self.stack.enter_context(self.nc.semaphore(name + '_%d' % self.nsem))
        return s

    def _deps(self, reads, writes):
        deps = []
        for k in list(reads) + list(writes):
            t = self.lastw.get(k)
            if t is not None:
                deps.append(t)
        for k in writes:
            deps.extend(self.readers.get(k, ()))
        for k in reads:
            if isinstance(k, tuple) and k and k[0] == 'ps':
                deps.extend(self.readers.get(k, ()))
        return deps

    def _needed(self, eng, deps):
        out = {}
        own = self.csem[eng]
        for sem, val in deps:
            if sem is own:
                if eng == 'pe' or not SAME_ENGINE_SYNC:
                    continue
            key = id(sem)
            if self.seen[eng].get(key, 0) >= val:
                continue
            if key not in out or out[key][1] < val:
                out[key] = (sem, val)
        for key, (sem, val) in out.items():
            self.seen[eng][key] = val
        self.nwaits += len(out)
        return list(out.values())

    def _commit(self, tok, reads, writes):
        for k in writes:
            self.lastw[k] = tok
            self.readers[k] = []
        for k in reads:
            if k in writes:
                continue
            self.readers.setdefault(k, []).append(tok)

    def op(self, eng, fn, reads=(), writes=(), inc=True, mode=None):
        if eng == 'pe':
            m_ = mode or 'full'
            if m_ != getattr(self, 'pemode', 'full'):
                self.ops['pe'].append(([], lambda e: e.drain(), None))
            self.pemode = m_
        deps = self._deps(reads, writes)
        waits = self._needed(eng, deps)
        if inc:
            self.ccnt[eng] += 1
            tok = (self.csem[eng], self.ccnt[eng])
            incinfo = (self.csem[eng], 1)
        else:
            tok = (self.csem[eng], self.ccnt[eng] + 1)
            incinfo = None
        self.ops[eng].append((waits, fn, incinfo))
        self._commit(tok, reads, writes)
        if inc and self.ccnt[eng] >= EPOCH:
            self.csem[eng] = self._newsem('c_' + eng)
            self.ccnt[eng] = 0
        return tok

    def dma(self, q, out, in_, reads=(), writes=(), **kw):
        return self.custom_dma(q, lambda e: e.dma_start(out=out, in_=in_, **kw), reads, writes)

    def custom_dma(self, q, fn, reads=(), writes=()):
        base, cnt = self.dq[q]
        i = base + self.dqi[q] % cnt
        self.dqi[q] += 1
        self.di += 1
        sem = self.dsem[i]
        prev = self.dval[i]
        val = prev + 16
        self.dval[i] = val
        deps = self._deps(reads, writes)
        if prev > 0:
            deps.append((sem, prev))
        waits = self._needed(q, deps)
        tok = (sem, val)
        self.ops[q].append((waits, fn, (sem, 16)))
        self._commit(tok, reads, writes)
        return tok

    def barrier(self):
        for eng in self.ENGS:
            deps = [(self.dsem[i], self.dval[i]) for i in range(NDMASEM) if self.dval[i] > 0]
            for e in self.ENGS:
                if e != eng and self.ccnt[e] > 0:
                    deps.append((self.csem[e], self.ccnt[e]))
            waits = self._needed(eng, deps)
            if waits:
                self.ops[eng].append((waits, None, None))
        self.lastw.clear()
        self.readers.clear()

    def finish(self):
        deps = [(self.dsem[i], self.dval[i]) for i in range(NDMASEM) if self.dval[i] > 0]
        for e in self.ENGS:
            if e != 'sp' and self.ccnt[e] > 0:
                deps.append((self.csem[e], self.ccnt[e]))
        waits = self._needed('sp', deps)
        self.ops['sp'].append((waits, None, None))

    def emit(self):
        nc = self.nc
        engmap = {'pe': 'tensor', 'dve': 'vector', 'act': 'scalar', 'pool': 'gpsimd', 'sp': 'sync'}
        with nc.Block() as block:
            for e in self.ENGS:
                lst = self.ops[e]

                def body(eng, lst=lst):
                    for waits, fn, incinfo in lst:
                        for sem, val in waits:
                            eng.wait_ge(sem, val)
                        if fn is None:
                            continue
                        ins = fn(eng)
                        if incinfo is not None:
                            ins.then_inc(incinfo[0], incinfo[1])
                getattr(block, engmap[e])(body)


class Builder:
    def __init__(self):
        self.nc = bass.Bass("TRN2", target_bir_lowering=False)
        self.st = ExitStack()
        self.P = Prog(self.nc, self.st)
        self.D = {}
        self.jobs = []
        self.wi = 0
        self.psi = {}
        self.in_names = []
        self.out_names = []

    def din(self, name, shape, dt=F32):
        self.D[name] = self.nc.dram_tensor(name, list(shape), dt, kind="ExternalInput").ap()
        self.in_names.append(name)
        return self.D[name]

    def dout(self, name, shape, dt=F32):
        self.D[name] = self.nc.dram_tensor(name, list(shape), dt, kind="ExternalOutput").ap()
        self.out_names.append(name)
        return self.D[name]

    def sb(self, name, shape, dt, st=None):
        return (st or self.st).enter_context(self.nc.sbuf_tensor("s_" + name, list(shape), dt))

    def psrot(self, banks):
        i = self.psi.get(banks, 0)
        self.psi[banks] = i + 1
        return banks[i % len(banks)]

    def job(self, wsrc, fn):
        self.jobs.append((wsrc, fn))

    def run_jobs(self):
        P = self.P
        jobs = self.jobs
        if os.environ.get('K_STOP') is not None and not getattr(self, '_stopped', False):
            jobs = jobs[:int(os.environ['K_STOP'])]
            self._stopped = True
        widx = [i for i, j in enumerate(jobs) if j[0] is not None]
        loaded = {}
        nl = 0
        for i, (wsrc, fn) in enumerate(jobs):
            ahead = [k for k in widx if k >= i][:len(self.wt)]
            if ahead:
                while nl < len(widx) and widx[nl] <= ahead[-1]:
                    k = widx[nl]
                    src = jobs[k][0]
                    bi = self.wi % len(self.wt)
                    self.wi += 1
                    t = self.wt[bi]
                    kc, n = src.shape[1], src.shape[2]
                    if getattr(self, 'wstage', None) is not None:
                        stg_ = self.wstage
                        P.dma('sp', stg_[:, 0:kc, 0:n], src, writes=['wstage'])
                        P.op('pool', lambda e, t=t, stg_=stg_, kc=kc, n=n: e.tensor_copy(t[:, 0:kc, 0:n], stg_[:, 0:kc, 0:n]), reads=['wstage'], writes=[('wt', bi)])
                    else:
                        P.dma('pool', t[:, 0:kc, 0:n], src, writes=[('wt', bi)])
                    loaded[k] = (t, ('wt', bi))
                    nl += 1
            if wsrc is None:
                fn()
            else:
                t, key = loaded.pop(i)
                fn(t, key)
        self.jobs = []

    def wview(self, W, k0, kc, c0, n):
        return W[k0:k0 + kc * 128, c0:c0 + n].rearrange("(kc p) n -> p kc n", p=128)

    def dense_fm(self, wsrc, kc, nm, xparts, epi, banks=(0, 1, 2, 3)):
        P = self.P

        def fn(wt, wkey):
            for m in range(nm):
                for (xt, xkeyf, t0, n, tag) in xparts:
                    b = self.psrot(banks)
                    pst = self.ps[b]
                    pkey = ('ps', b)
                    for k in range(kc):
                        P.op('pe', lambda e, pst=pst, wt=wt, xt=xt, k=k, m=m, t0=t0, n=n: e.matmul(
                            pst[:, 0:n], wt[:, k, m * 128:(m + 1) * 128], xt[:, k, t0:t0 + n],
                            start=(k == 0), stop=(k == kc - 1)),
                            reads=[wkey, xkeyf(k)], writes=[pkey], inc=(k == kc - 1))
                    epi(m, tag, t0, n, pst[:, 0:n], pkey)
        self.job(wsrc, fn)

    def dense_tm(self, wsrc, kc, ncols, xparts, epi, banks=(0, 1, 2, 3)):
        P = self.P

        def fn(wt, wkey):
            for (xt, xkeyf, t0, n, tag) in xparts:
                b = self.psrot(banks)
                pst = self.ps[b]
                pkey = ('ps', b)
                for k in range(kc):
                    P.op('pe', lambda e, pst=pst, wt=wt, xt=xt, k=k, t0=t0, n=n: e.matmul(
                        pst[:, 0:ncols], xt[:, k, t0:t0 + 128], wt[:, k, 0:ncols],
                        start=(k == 0), stop=(k == kc - 1)),
                        reads=[wkey, xkeyf(k)], writes=[pkey], inc=(k == kc - 1))
                epi(tag, t0, n, pst[0:n, 0:ncols], pkey)
        self.job(wsrc, fn)

    def rmsnorm(self, srcf, skey, gain, dstf, dkey, tiles, fetch=None):
        P = self.P

        def getsrc(c, t0, n):
            if fetch is None:
                return srcf(c, t0, n), skey(c)
            i = self.xsi % 2
            self.xsi += 1
            P.dma('sp', self.xs[i][:, 0:n], fetch(c, t0, n), writes=[('xs', i)])
            return self.xs[i][:, 0:n], ('xs', i)

        def fn():
            for (t0, n) in tiles:
                b = self.psrot((4, 5))
                pst = self.ps[b]
                for c in range(16):
                    src, sk = getsrc(c, t0, n)
                    P.op('act', lambda e, c=c, n=n, src=src: e.activation(self.sq[:, c % 4, 0:n], src, AF.Square),
                         reads=[sk], writes=[('sq', c % 4)])
                    P.op('pe', lambda e, c=c, n=n, pst=pst: e.matmul(pst[:, 0:n], self.ones_bf[:, :], self.sq[:, c % 4, 0:n],
                                                                     start=(c == 0), stop=(c == 15)),
                         reads=[('sq', c % 4), 'ones'], writes=[('ps', b)], inc=True)
                P.op('act', lambda e, n=n, pst=pst: e.activation(self.rstd[:, 0:n], pst[:, 0:n], AF.Sqrt, bias=self.epst[:, 0:1], scale=1.0 / 2048.0),
                     reads=[('ps', b), 'epst'], writes=['rstd'])
                P.op('dve', lambda e, n=n: e.reciprocal(self.rstd[:, 0:n], self.rstd[:, 0:n]), reads=['rstd'], writes=['rstd'])
                for c in range(16):
                    src, sk = getsrc(c, t0, n)
                    P.op('dve', lambda e, c=c, t0=t0, n=n, src=src: e.scalar_tensor_tensor(
                        dstf(c, t0, n), src, gain[:, c:c + 1], self.rstd[:, 0:n], op0=ALU.mult, op1=ALU.mult),
                        reads=[sk, 'rstd', 'gains'], writes=[dkey(c)])
        self.job(None, fn)

    def ffn(self, li):
        P = self.P
        W1 = self.D['w_ff1'][li]
        W2 = self.D['w_ff2'][li]
        xparts = [(self.hT, lambda k: ('hT', k), t0, n, None) for (t0, n) in TT]
        aparts = [(self.aT, lambda k: ('aT', k), t0, n, None) for (t0, n) in TT]
        for q in range(4):
            for cg in range(8):
                def epi1(m, tag, t0, n, ps, pkey, cg=cg):
                    c = cg * 2 + m
                    P.op('act', lambda e, ps=ps, n=n: e.activation(self.tmpf[:, 0:n], ps, AF.Square), reads=[pkey], writes=['tmpf'])
                    P.op('dve', lambda e, ps=ps, c=c, t0=t0, n=n: e.scalar_tensor_tensor(
                        self.aT[:, c, t0:t0 + n], ps, 0.0, self.tmpf[:, 0:n], op0=ALU.is_gt, op1=ALU.mult),
                        reads=[pkey, 'tmpf'], writes=[('aT', c)])
                self.dense_fm(self.wview(W1, 0, 16, q * 2048 + cg * 256, 256), 16, 2, xparts, epi1)
            for cg in range(8):
                def epi2(m, tag, t0, n, ps, pkey, cg=cg):
                    c = cg * 2 + m
                    P.op('dve', lambda e, ps=ps, c=c, t0=t0, n=n: e.tensor_tensor(
                        self.yT[:, c, t0:t0 + n], ps, self.yT[:, c, t0:t0 + n], op=ALU.add),
                        reads=[pkey], writes=[('yT', c)])
                self.dense_fm(self.wview(W2, q * 2048, 16, cg * 256, 256), 16, 2, aparts, epi2)

    def mixer1(self):
        P = self.P
        W = self.D['w_in_o3']
        xparts = [(self.hT, lambda k: ('hT', k), t0, n, None) for (t0, n) in TT]
        for c in range(16):
            for m_ in range(3):
                def epi(m, tag, t0, n, ps, pkey, c=c, m_=m_):
                    if m_ == 0:
                        P.op('act', lambda e, ps=ps, t0=t0, n=n: e.activation(self.tb[:, t0:t0 + n], ps, AF.Copy), reads=[pkey], writes=['tb'])
                    elif m_ == 1:
                        P.op('act', lambda e, ps=ps, t0=t0, n=n: e.activation(self.tc[:, t0:t0 + n], ps, AF.Copy), reads=[pkey], writes=['tc'])
                    else:
                        P.op('dve', lambda e, ps=ps, t0=t0, n=n: e.tensor_tensor(self.tv[:, t0:t0 + n], ps, self.tc[:, t0:t0 + n], op=ALU.mult),
                             reads=[pkey, 'tc'], writes=['tv'])
                        if t0 + n == NT:
                            self.conv1(c)
                self.dense_fm(self.wview(W, 0, 16, c * 384 + m_ * 128, 128), 16, 1, xparts, epi)
        Wo = self.D['w_out_o']
        gparts = [(self.gT, lambda k: ('aT', k), t0, n, None) for (t0, n) in TT]
        for cg in range(8):
            def epi2(m, tag, t0, n, ps, pkey, cg=cg):
                c = cg * 2 + m
                P.op('dve', lambda e, ps=ps, c=c, t0=t0, n=n: e.tensor_tensor(
                    self.yT[:, c, t0:t0 + n], ps, self.yT[:, c, t0:t0 + n], op=ALU.add),
                    reads=[pkey], writes=[('yT', c)])
            self.dense_fm(self.wview(Wo, 0, 16, cg * 256, 256), 16, 2, gparts, epi2)

    def conv1(self, c):
        P = self.P
        w = self.scw
        P.op('dve', lambda e: e.tensor_scalar(self.tv[:, 0:32], self.tv[:, 0:32], self.flag[:, 0:1], None, op0=ALU.mult),
             reads=['tv', 'flag'], writes=['tv'])
        P.op('dve', lambda e: e.tensor_scalar(self.ty[:, 32:1056], self.tv[:, 30:1054], w[:, 0, c:c + 1], None, op0=ALU.mult),
             reads=['tv', 'scw'], writes=['ty'])
        P.op('dve', lambda e: e.scalar_tensor_tensor(self.ty[:, 32:1056], self.tv[:, 31:1055], w[:, 1, c:c + 1], self.ty[:, 32:1056], op0=ALU.mult, op1=ALU.add),
             reads=['tv', 'scw'], writes=['ty'])
        P.op('dve', lambda e: e.scalar_tensor_tensor(self.ty[:, 32:1056], self.tv[:, 32:1056], w[:, 2, c:c + 1], self.ty[:, 32:1056], op0=ALU.mult, op1=ALU.add),
             reads=['tv', 'scw'], writes=['ty'])
        P.op('dve', lambda e: e.tensor_copy(self.tvs[:, :, 0:2], self.stsc[:, c, :, :]), reads=['stsc'], writes=['tvs'])
        P.op('dve', lambda e: e.tensor_copy(self.tvs[:, :, 2:10], self.tv[:, 1056:1088].rearrange("p (s t) -> p s t", s=4)), reads=['tv'], writes=['tvs'])
        ys = self.ty[:, 1056:1088].rearrange("p (s t) -> p s t", s=4)
        P.op('dve', lambda e: e.tensor_scalar(ys, self.tvs[:, :, 0:8], w[:, 0, c:c + 1], None, op0=ALU.mult), reads=['tvs', 'scw'], writes=['ty'])
        P.op('dve', lambda e: e.scalar_tensor_tensor(ys, self.tvs[:, :, 1:9], w[:, 1, c:c + 1], ys, op0=ALU.mult, op1=ALU.add), reads=['tvs', 'scw'], writes=['ty'])
        P.op('dve', lambda e: e.scalar_tensor_tensor(ys, self.tvs[:, :, 2:10], w[:, 2, c:c + 1], ys, op0=ALU.mult, op1=ALU.add), reads=['tvs', 'scw'], writes=['ty'])
        P.op('pool', lambda e: e.memset(self.ty[:, 0:32], 0.0), writes=['ty'])
        P.op('dve', lambda e: e.tensor_tensor(self.gT[:, c, :], self.tb[:, :], self.ty[:, :], op=ALU.mult), reads=['tb', 'ty'], writes=[('aT', c)])
        P.op('act', lambda e: e.activation(self.scp[:, c, :], self.tv[:, 1054:1056], AF.Copy), reads=['tv'], writes=['scp'])
        P.op('act', lambda e: e.activation(self.scs[:, c, :, :], self.tvs[:, :, 8:10], AF.Copy), reads=['tvs'], writes=['scs'])

    def stage(self):
        i = self.stgi % 2
        self.stgi += 1
        return self.stg[i], ('stg', i)

    def headnorm(self, src, skey, n, gain, outs):
        P = self.P
        P.op('act', lambda e: e.activation(self.sqh[:, 0:n], src, AF.Square), reads=[skey], writes=['sqh'])
        b = self.psrot((4, 5))
        pst = self.ps[b]
        P.op('pe', lambda e: e.matmul(pst[:, 0:n], self.bones[:, :], self.sqh[:, 0:n], start=True, stop=True),
             reads=['sqh', 'bones'], writes=[('ps', b)])
        P.op('act', lambda e: e.activation(self.rstd2[:, 0:n], pst[:, 0:n], AF.Sqrt, bias=self.epst[:, 0:1], scale=1.0 / 64.0),
             reads=[('ps', b), 'epst'], writes=['rstd'])
        P.op('dve', lambda e: e.reciprocal(self.rstd2[:, 0:n], self.rstd2[:, 0:n]), reads=['rstd'], writes=['rstd'])
        for dst, dkeys in outs:
            P.op('dve', lambda e, dst=dst: e.scalar_tensor_tensor(dst, src, gain, self.rstd2[:, 0:n], op0=ALU.mult, op1=ALU.mult),
                 reads=[skey, 'rstd', 'cst'], writes=dkeys)

    def gelu_mul(self, ps, pkey, n, dst, dkey):
        P = self.P
        x = self.gx[:, 0:n]
        t = self.gt[:, 0:n]
        P.op('act', lambda e: e.activation(x, ps, AF.Copy), reads=[pkey], writes=[('stg', 0)])
        P.op('dve', lambda e: e.tensor_tensor(t, x, x, op=ALU.mult), reads=[('stg', 0)], writes=[('stg', 1)])
        P.op('dve', lambda e: e.tensor_scalar(t, t, 0.044715, 1.0, op0=ALU.mult, op1=ALU.add), reads=[('stg', 1)], writes=[('stg', 1)])
        P.op('dve', lambda e: e.tensor_tensor(t, t, x, op=ALU.mult), reads=[('stg', 1), ('stg', 0)], writes=[('stg', 1)])
        P.op('act', lambda e: e.activation(t, t, AF.Sigmoid, scale=1.5957691216057308), reads=[('stg', 1)], writes=[('stg', 1)])
        P.op('dve', lambda e: e.tensor_tensor(x, x, t, op=ALU.mult), reads=[('stg', 0), ('stg', 1)], writes=[('stg', 0)])
        P.op('dve', lambda e: e.tensor_tensor(dst, x, dst, op=ALU.mult), reads=[('stg', 0)], writes=[dkey])

    def rg_head(self, r, W, segs, mask_flag):
        P = self.P
        prm = self.rgp
        xx = self.rxb
        xc = self.rxc
        for sg in segs:
            c0, L = sg['c0'], sg['L']
            hb = self.rhist0 if sg['hslot'] == 0 else self.rhists[:, sg['hslot'] - 1, :]
            P.op('dve', lambda e, sg=sg, hb=hb: e.tensor_copy(hb[:, 0:3], sg['hist']), reads=sg['hist_keys'], writes=[('rhist', sg['hslot'])])
            P.op('act', lambda e, hb=hb, c0=c0, L=L: e.activation(hb[:, 3:3 + L], xx[:, c0:c0 + L], AF.Copy), reads=['rxb'], writes=[('rhist', sg['hslot'])])
            P.op('dve', lambda e, hb=hb, c0=c0, L=L: e.tensor_scalar(xc[:, c0:c0 + L], hb[:, 0:L], prm[:, r, 0:1], prm[:, r, 4:5], op0=ALU.mult, op1=ALU.add),
                 reads=[('rhist', sg['hslot']), 'cst'], writes=['rxc'])
            for j in range(1, 4):
                P.op('dve', lambda e, hb=hb, c0=c0, L=L, j=j: e.scalar_tensor_tensor(xc[:, c0:c0 + L], hb[:, j:j + L], prm[:, r, j:j + 1], xc[:, c0:c0 + L], op0=ALU.mult, op1=ALU.add),
                     reads=[('rhist', sg['hslot']), 'cst'], writes=['rxc'])
            if sg.get('hist_out') is not None:
                P.op('act', lambda e, sg=sg, hb=hb, L=L: e.activation(sg['hist_out'], hb[:, L:L + 3], AF.Copy), reads=[('rhist', sg['hslot'])], writes=sg['hist_out_keys'])
        lo = min(sg['c0'] for sg in segs)
        hi = max(sg['c0'] + sg['L'] for sg in segs)
        P.op('dve', lambda e: e.tensor_copy(self.rxcb[:, lo:hi], xc[:, lo:hi]), reads=['rxc'], writes=['rxcb'])
        ga = self.rga
        gi = self.rgi
        for (t0, n) in [(t, min(512, hi - t)) for t in range(lo, hi, 512)]:
            for which, wmat, bcol, dstt in ((0, self.rgwa, 5, ga), (1, self.rgwx, 6, gi)):
                b = self.psrot((4, 5))
                pst = self.ps[b]
                P.op('pe', lambda e, pst=pst, wmat=wmat, t0=t0, n=n: e.matmul(pst[:, 0:n], wmat[:, r, :], self.rxcb[:, t0:t0 + n], start=True, stop=True),
                     reads=['rxcb', 'cst'], writes=[('ps', b)])
                P.op('act', lambda e, pst=pst, dstt=dstt, bcol=bcol, t0=t0, n=n: e.activation(dstt[:, t0:t0 + n], pst[:, 0:n], AF.Sigmoid, bias=prm[:, r, bcol:bcol + 1], scale=1.0),
                     reads=[('ps', b), 'cst'], writes=['rga' if which == 0 else 'rgi'])
        P.op('act', lambda e: e.activation(ga[:, lo:hi], ga[:, lo:hi], AF.Exp, scale=self.clam[:, r:r + 1]), reads=['rga', 'clam'], writes=['rga'])
        tm = self.rtm
        P.op('dve', lambda e: e.tensor_tensor(tm[:, lo:hi], ga[:, lo:hi], ga[:, lo:hi], op=ALU.mult), reads=['rga'], writes=['rtm'])
        P.op('dve', lambda e: e.tensor_scalar(tm[:, lo:hi], tm[:, lo:hi], -1.0, 1.0, op0=ALU.mult, op1=ALU.add), reads=['rtm'], writes=['rtm'])
        P.op('dve', lambda e: e.tensor_scalar(tm[:, lo:hi], tm[:, lo:hi], 0.0, None, op0=ALU.max), reads=['rtm'], writes=['rtm'])
        P.op('act', lambda e: e.activation(tm[:, lo:hi], tm[:, lo:hi], AF.Sqrt), reads=['rtm'], writes=['rtm'])
        P.op('dve', lambda e: e.tensor_tensor(gi[:, lo:hi], gi[:, lo:hi], xc[:, lo:hi], op=ALU.mult), reads=['rgi', 'rxc'], writes=['rgi'])
        P.op('dve', lambda e: e.tensor_tensor(gi[:, lo:hi], gi[:, lo:hi], tm[:, lo:hi], op=ALU.mult), reads=['rgi', 'rtm'], writes=['rgi'])
        if mask_flag:
            P.op('dve', lambda e: e.tensor_scalar(gi[:, lo:hi], gi[:, lo:hi], self.flag[:, 0:1], None, op0=ALU.mult), reads=['rgi', 'flag'], writes=['rgi'])
        for sg in segs:
            c0, L = sg['c0'], sg['L']
            P.op('dve', lambda e, sg=sg, c0=c0, L=L: e.tensor_tensor_scan(tm[:, c0:c0 + L], ga[:, c0:c0 + L], gi[:, c0:c0 + L], sg['init'], ALU.mult, ALU.add),
                 reads=['rga', 'rgi'] + sg['init_keys'], writes=['rtm'])
            if sg.get('state_out') is not None:
                P.op('act', lambda e, sg=sg, c0=c0, L=L: e.activation(sg['state_out'], tm[:, c0 + L - 1:c0 + L], AF.Copy), reads=['rtm'], writes=sg['state_out_keys'])
            for (dst, dkeys, s0, sl) in sg.get('h_out', []):
                P.op('act', lambda e, dst=dst, c0=c0, s0=s0, sl=sl: e.activation(dst, tm[:, c0 + s0:c0 + s0 + sl], AF.Copy), reads=['rtm'], writes=dkeys)

    def l0_inproj(self):
        P = self.P
        W = self.D['w_in_p']
        D = self.D
        NPP = 4
        for pas in range(NPP + 1):
            ismain = (pas == NPP)
            if not ismain:
                p0 = pas * 256
                self.rmsnorm(None, None, self.gains[:, 0, :], lambda c, t0, n: self.hTp[:, c, 0:n], lambda c: ('hTp', c), [(0, 256)],
                             fetch=lambda c, t0, n, p0=p0: D['xT_pre'][:, c, p0:p0 + n])
                xt, xkf = self.hTp, (lambda k: ('hTp', k))
                fm_tiles = [(0, 256)]
                tm_tiles = [(0, 128, ('ctx', pas * 2)), (128, 128, ('ctx', pas * 2 + 1))]
                W_ = 256
            else:
                self.rmsnorm(None, None, self.gains[:, 0, :], lambda c, t0, n: self.hT[:, c, t0:t0 + n], lambda c: ('hT', c), TT,
                             fetch=lambda c, t0, n: D['xT_main'][:, c, t0:t0 + n])
                xt, xkf = self.hT, (lambda k: ('hT', k))
                fm_tiles = list(TT)
                tm_tiles = [(32 + 128 * j, 128, ('ctx', 8 + j)) for j in range(8)] + ([] if os.environ.get('K_NOSMP') else [(1056 + 8 * s, 8, ('smp', s)) for s in range(4)])
                W_ = NT
            fmparts = [(xt, xkf, t0, n, None) for (t0, n) in fm_tiles]
            kcol0 = 0 if ismain else 0
            kbase = 1024 if ismain else pas * 256

            for r in range(8):
                def epi_rx(m, tag, t0, n, ps, pkey, r=r, ismain=ismain, pas=pas, W_=W_):
                    P.op('act', lambda e: e.activation(self.rxb[:, t0:t0 + n], ps, AF.Copy), reads=[pkey], writes=['rxb'])
                    if t0 + n == W_:
                        if not ismain:
                            sg = dict(c0=0, L=256, hslot=0, hist=self.convh[:, r, :], hist_keys=[('convh', r)],
                                      hist_out=self.convh[:, r, :], hist_out_keys=[('convh', r)],
                                      init=self.hst[:, r:r + 1], init_keys=[('hst', r)],
                                      state_out=self.hst[:, r:r + 1], state_out_keys=[('hst', r)])
                            if pas == NPP - 1:
                                sg['h_out'] = [(self.mixT[:, r, 0:32], [('aT', r)], 224, 32)]
                            if os.environ.get('K_NORG') is None:
                                self.rg_head(r, 256, [sg], True)
                        else:
                            segs = [dict(c0=32, L=1024, hslot=0, hist=self.convh[:, r, :], hist_keys=[('convh', r)],
                                         hist_out=self.rgo_conv[:, r, 0, :], hist_out_keys=['rgo_conv'],
                                         init=self.hst[:, r:r + 1], init_keys=[('hst', r)],
                                         state_out=self.rgo_h[:, r, 0:1], state_out_keys=['rgo_h'],
                                         h_out=[(self.mixT[:, r, 32:1056], [('aT', r)], 0, 1024)])]
                            for s_ in range(4):
                                segs.append(dict(c0=1056 + 8 * s_, L=8, hslot=1 + s_, hist=self.stconv[:, r, s_, :], hist_keys=['cst'],
                                                 hist_out=self.rgo_conv[:, r, 1 + s_, :], hist_out_keys=['rgo_conv'],
                                                 init=self.sth[:, r, s_:s_ + 1], init_keys=['cst'],
                                                 state_out=self.rgo_h[:, r, 1 + s_:2 + s_], state_out_keys=['rgo_h'],
                                                 h_out=[(self.mixT[:, r, 1056 + 8 * s_:1064 + 8 * s_], [('aT', r)], 0, 8)]))
                            if os.environ.get('K_NORG') is None:
                                self.rg_head(r, NT, segs, False)
                self.dense_fm(self.wview(W, 0, 16, r * 128, 128), 16, 1, fmparts, epi_rx)

            for which, col0, KT, gcol, oname in ((0, 3072, self.KTs, 1, 'ksT_o'), (1, 3328, self.KTw, 2, 'kwT_o')):
                def epi_k(m, tag, t0, n, ps, pkey, KT=KT, gcol=gcol, oname=oname, ismain=ismain, kbase=kbase, which=which):
                    tq = self.tmpq[:, 0:n]
                    P.op('act', lambda e: e.activation(tq, ps, AF.Copy), reads=[pkey], writes=['tmpf'])
                    outs = [(KT[:, m, kbase + t0:kbase + t0 + n], [('KT', which, m)])]
                    if ismain:
                        stg, skey = self.stage()
                        outs.append((stg[:, 0:n], [skey]))
                    self.headnorm(tq, 'tmpf', n, self.kg[:, gcol:gcol + 1], outs)
                    if ismain:
                        P.dma('sp', D[oname][:, m, t0:t0 + n], stg[:, 0:n], reads=[skey], writes=[(oname, m, t0)])
                self.dense_fm(self.wview(W, 0, 16, col0, 256), 16, 2, fmparts, epi_k)

            tmparts = [(xt, xkf, t0, n, tag) for (t0, n, tag) in tm_tiles]
            for half_ in range(2):
                def epi_kcvc(tag, t0, n, ps, pkey, half_=half_, ismain=ismain):
                    kind, idx = tag
                    if kind == 'ctx':
                        P.op('act', lambda e: e.activation(self.kcpg[:, idx % 8, half_, :], ps, AF.Copy), reads=[pkey], writes=[('kcpg', idx % 8, half_)])
                    if ismain:
                        stg, skey = self.stage()
                        P.op('dve', lambda e: e.tensor_copy(stg[0:n, 0:256], ps), reads=[pkey], writes=[skey])
                        nm = ('kc_o', 'vc_o')[half_] if kind == 'ctx' else ('kc_s_o', 'vc_s_o')[half_]
                        r0 = (idx - 8) * 128 if kind == 'ctx' else idx * 8
                        if os.environ.get('K_NODMA') is None:
                            P.dma('sp', D[nm][r0:r0 + n, :], stg[0:n, 0:256], reads=[skey], writes=[(nm, r0)])
                self.dense_tm(self.wview(W, 0, 16, 3584 + 256 * half_, 256), 16, 256, tmparts, epi_kcvc)
            if pas >= NPP - 1:
                kk = [('kcpg', i_, hf_) for i_ in range(8) for hf_ in range(2)]
                self.job(None, lambda kk=kk, pg0=(8 if ismain else 0): self.compress1(self.kcpg, 8, kk, 'data', pg0=pg0))
            for half_ in range(2):
                def epi_v(tag, t0, n, ps, pkey, half_=half_, ismain=ismain):
                    kind, idx = tag
                    Vt = (self.Vs, self.Vw)[half_]
                    if kind == 'ctx':
                        dst = Vt[0:n, idx, :, 0:64]
                        dk = [('V', half_, idx)]
                    else:
                        dst = (self.Vs_s, self.Vw_s)[half_][0:n, idx, :, 0:64]
                        dk = [('Vsm', half_, idx)]
                    P.op('act', lambda e: e.activation(dst, ps.rearrange("p (g d) -> p g d", g=4), AF.Copy), reads=[pkey], writes=dk)
                    if ismain:
                        stg, skey = self.stage()
                        P.op('dve', lambda e: e.tensor_copy(stg[0:n, 0:256], ps), reads=[pkey], writes=[skey])
                        nm = ('vs_o', 'vw_o')[half_] if kind == 'ctx' else ('vs_s_o', 'vw_s_o')[half_]
                        r0 = (idx - 8) * 128 if kind == 'ctx' else idx * 8
                        P.dma('sp', D[nm][r0:r0 + n, :], stg[0:n, 0:256], reads=[skey], writes=[(nm, r0)])
                self.dense_tm(self.wview(W, 0, 16, 4096 + 256 * half_, 256), 16, 256, tmparts, epi_v)

            if ismain:
                for r in range(8):
                    def epi_rg(m, tag, t0, n, ps, pkey, r=r):
                        self.gelu_mul(ps, pkey, n, self.mixT[:, r, t0:t0 + n], ('aT', r))
                    self.dense_fm(self.wview(W, 0, 16, 1024 + r * 128, 128), 16, 1, fmparts, epi_rg)
                for i in range(8):
                    def epi_q(m, tag, t0, n, ps, pkey, i=i):
                        tq = self.tmpq[:, 0:n]
                        P.op('act', lambda e: e.activation(tq, ps, AF.Copy), reads=[pkey], writes=['tmpf'])
                        self.headnorm(tq, 'tmpf', n, self.kg[:, 3:4], [(self.qT[:, i, t0:t0 + n], [('qT', i), 'rga', 'rgi', 'rtm', 'rxc'])])
                    self.dense_fm(self.wview(W, 0, 16, 2048 + i * 128, 128), 16, 1, fmparts, epi_q)
                gl_tiles = [(0, 32, 0)] + [(32 + 128 * j, 128, 1 + j) for j in range(8)] + [(1056 + 8 * s, 8, 9 + s) for s in range(4)]
                glparts = [(xt, xkf, t0, n, gi_) for (t0, n, gi_) in gl_tiles]

                def epi_gl(tag, t0, n, ps, pkey):
                    P.op('act', lambda e: e.activation(self.gates[0:n, tag, :], ps, AF.Sigmoid), reads=[pkey], writes=[('gates', tag)])
                self.dense_tm(self.wview(W, 0, 16, 4608, 48), 16, 48, glparts, epi_gl)

    def compress1(self, X, npg, xkeys, kind, pg0=0, kvs=(0, 1), load_w=True):
        P = self.P
        cx = self.cx
        cw1b_, hx_, htt_, hid_ = cx['cw1b'], cx['hx'], cx['htt'], cx['hid']
        for kv in kvs:
            w1 = cw1b_
            if load_w:
                P.dma('pool', w1[:], self.D['cw1k' if kv == 0 else 'cw1v'], writes=['cw1'])
            for half in range(2):
                if kind == 'pe' and half == 1:
                    continue
                b = self.psrot((4, 5))
                pst = self.ps[b]
                lo, hi = half * 64, half * 64 + 64
                Xv = X[lo:hi, 0:npg, kv, :].rearrange("p n (g d) -> p n g d", g=4)
                for d in range(64):
                    P.op('pe', lambda e, pst=pst, w1=w1, Xv=Xv, d=d, lo=lo, hi=hi: e.matmul(
                        pst[0:64, 0:npg * 4], w1[lo:hi, d, :], Xv[:, :, :, d], start=(d == 0), stop=(d == 63)),
                        reads=list(xkeys) + ['cw1'], writes=[('ps', b)], inc=(d == 63), mode='k64m64')
                psv = pst[0:64, 0:npg * 4]
                if kind == 'pe':
                    P.op('act', lambda e, psv=psv, kv=kv: e.activation(self.cpe[:, kv, :], psv, AF.Copy), reads=[('ps', b)], writes=['cpe'])
                    continue
                hx = hx_[:, 0:npg * 4]
                hxv = hx.rearrange("p (n g) -> p n g", g=4)
                ht = htt_[:, 0:npg * 4]
                P.op('act', lambda e, psv=psv, hx=hx: e.activation(hx, psv, AF.Copy), reads=[('ps', b)], writes=['hx'])
                for g in range(4):
                    P.op('dve', lambda e, hxv=hxv, g=g, kv=kv: e.tensor_scalar(hxv[:, :, g], hxv[:, :, g], self.cpe[:, kv, g:g + 1], None, op0=ALU.add),
                         reads=['hx', 'cpe'], writes=['hx'])
                P.op('dve', lambda e, hx=hx, ht=ht: e.tensor_tensor(ht, hx, hx, op=ALU.mult), reads=['hx'], writes=['htt'])
                P.op('dve', lambda e, ht=ht: e.tensor_scalar(ht, ht, 0.044715, 1.0, op0=ALU.mult, op1=ALU.add), reads=['htt'], writes=['htt'])
                P.op('dve', lambda e, hx=hx, ht=ht: e.tensor_tensor(ht, ht, hx, op=ALU.mult), reads=['htt', 'hx'], writes=['htt'])
                P.op('act', lambda e, ht=ht: e.activation(ht, ht, AF.Sigmoid, scale=1.5957691216057308), reads=['htt'], writes=['htt'])
                dst = hid_[:, kv, 2 * pg0 + half:2 * (pg0 + npg):2, :]
                P.op('dve', lambda e, hx=hx, ht=ht, dst=dst: e.tensor_tensor(dst, hx.rearrange("p (n g) -> p n g", g=4), ht.rearrange("p (n g) -> p n g", g=4), op=ALU.mult),
                     reads=['hx', 'htt'], writes=['hidT'])

    def compress2(self, nblk, kcbT, vcb):
        P = self.P
        cx = self.cx
        cw2k_, cw2v_, vtf_, identf_, hid_ = cx['cw2k'], cx['cw2v'], cx['vtf'], cx['identf'], cx['hid']
        for gp in range(2):
            b = self.psrot((4, 5))
            pst = self.ps[b]
            for lh in range(2):
                P.op('pe', lambda e, pst=pst, lh=lh, gp=gp: e.matmul(pst[:, 0:nblk], cw2k_[:, lh, :], hid_[:, 0, 0:nblk, 2 * gp + lh],
                                                                     start=(lh == 0), stop=(lh == 1)),
                     reads=['hidT', 'cw2'], writes=[('ps', b)], inc=(lh == 1), mode='k64m128')
            tq = self.tmpq[:, 0:nblk]
            P.op('act', lambda e, pst=pst, tq=tq: e.activation(tq, pst[:, 0:nblk], AF.Copy), reads=[('ps', b)], writes=['tmpf'])
            self.headnorm(tq, 'tmpf', nblk, self.kg[:, 0:1], [(kcbT[:, gp, 0:nblk], ['kcbT'])])
        for g in range(4):
            b = self.psrot((4, 5))
            pst = self.ps[b]
            P.op('pe', lambda e, pst=pst, g=g: e.matmul(pst[0:64, 0:nblk], cw2v_[:, :], hid_[:, 1, 0:nblk, g], start=True, stop=True),
                 reads=['hidT', 'cw2'], writes=[('ps', b)], mode='k64m64')
            vt = vtf_[:, 0:nblk]
            P.op('act', lambda e, pst=pst, vt=vt: e.activation(vt, pst[0:64, 0:nblk], AF.Copy), reads=[('ps', b)], writes=['vtf'])
            b2 = self.psrot((4, 5))
            pst2 = self.ps[b2]
            P.op('pe', lambda e, pst2=pst2, vt=vt: e.transpose(pst2[0:nblk, 0:64], vt, identf_[0:64, 0:64]), reads=['vtf', 'identf'], writes=[('ps', b2)], mode='k64m64t')
            P.op('act', lambda e, pst2=pst2, g=g: e.activation(vcb[0:nblk, g, 0:64], pst2[0:nblk, 0:64], AF.Copy), reads=[('ps', b2)], writes=['vcb'])

    def load_qm(self, h, qc0, N):
        P = self.P
        g = h // 4
        hp = g % 2
        qi, _ = self.qchunk(h)
        lo, hi = hp * 64, hp * 64 + 64
        k = hp * 2 + (self.qmi[hp] % 2)
        self.qmi[hp] += 1
        qm = self.qm[k]
        P.op('act', lambda e: e.activation(qm[lo:hi, 0:N], self.qT[lo:hi, qi, qc0:qc0 + N], AF.Copy), reads=[('qT', qi)], writes=[('qm', k)])
        return qm, ('qm', k)

    def qchunk(self, h):
        if (h % 8) < 4:
            return (h % 4) + 4 * (h // 8), 0
        return ((h - 4) % 4) + 4 * ((h - 4) // 8), 1

    def softmax_unit(self, pst, b, nk, N, h, x0, branch):
        P = self.P
        i = self.pti % 2
        self.pti += 1
        pT = self.pT[i]
        pk = ('pT', i)
        if branch == 'win' and x0 >= 384:
            tt = self.ptmp[i]
            P.op('dve', lambda e: e.scalar_tensor_tensor(tt[0:nk, 0:N], pst[0:nk, 0:N], SCALE, self.mw[0:nk, x0 - 384:x0 - 384 + N], op0=ALU.mult, op1=ALU.add),
                 reads=[('ps', b), 'mw'], writes=[('ptmp', i)])
            P.op('act', lambda e: e.activation(pT[0:nk, 0:N], tt[0:nk, 0:N], AF.Exp, bias=self.chd[0:nk, h:h + 1], scale=1.0),
                 reads=[('ptmp', i), 'chd'], writes=[pk])
        elif x0 >= 256:
            P.op('act', lambda e: e.activation(pT[0:nk, 0:N], pst[0:nk, 0:N], AF.Exp, bias=self.chd[0:nk, h:h + 1], scale=SCALE),
                 reads=[('ps', b), 'chd'], writes=[pk])
        else:
            tt = self.ptmp[i]
            P.op('dve', lambda e: e.scalar_tensor_tensor(tt[0:nk, 0:N], pst[0:nk, 0:N], SCALE, self.G[0:nk, h, x0:x0 + N], op0=ALU.mult, op1=ALU.add),
                 reads=[('ps', b), 'G'], writes=[('ptmp', i)])
            P.op('act', lambda e: e.activation(pT[0:nk, 0:N], tt[0:nk, 0:N], AF.Exp), reads=[('ptmp', i)], writes=[pk])
        return pT, pk

    def branch_epilogue(self, po, pob, N, h, gate_ap, first):
        P = self.P
        rd = self.rden[0:N, 0:1]
        P.op('dve', lambda e: e.tensor_scalar(rd, po[0:N, 64:65], 1e-30, None, op0=ALU.max), reads=[('ps', pob)], writes=['rden'])
        P.op('dve', lambda e: e.reciprocal(rd, rd), reads=['rden'], writes=['rden'])
        sg = self.rden[0:N, 1:2]
        P.op('dve', lambda e: e.tensor_tensor(sg, rd, gate_ap, op=ALU.mult), reads=['rden', 'gates'], writes=['rden2'])
        om = self.omix[0:N, h, :]
        if first:
            P.op('dve', lambda e: e.tensor_scalar(om, po[0:N, 0:64], sg, None, op0=ALU.mult), reads=[('ps', pob), 'rden2'], writes=[('omix', h)])
        else:
            P.op('dve', lambda e: e.scalar_tensor_tensor(om, po[0:N, 0:64], sg, om, op0=ALU.mult, op1=ALU.add), reads=[('ps', pob), 'rden2'], writes=[('omix', h)])
        return rd

    def attn_prompt_tile(self, ti, qc0, N, qpos0):
        P = self.P
        D = self.D
        P.dma('sp', self.selmul[0:N, :], D['selmul'][qc0:qc0 + N, :], writes=['selmul'])
        P.dma('sp', self.seladd[0:N, :], D['seladd'][qc0:qc0 + N, :], writes=['seladd'])
        for hq in range(4):
            for hh in range(4):
                h = hq * 4 + hh
                src = bass.AP(self.vd.tensor, h * DL + DOFF + qpos0 - 63 - 64 * 31, [[64, 32], [1, N]])
                P.dma('sp', self.bcrev[0:32, hh, 0:N], src, reads=['vd'], writes=[('bcrev', hh)])
            b = self.psrot((4, 5))
            pst = self.ps[b]
            for hh in range(4):
                P.op('pe', lambda e, pst=pst, hh=hh: e.matmul(pst[:, hh * 128:hh * 128 + N], self.j32p[:, :], self.bcrev[:, hh, 0:N], start=True, stop=True),
                     reads=[('bcrev', hh), 'jmat'], writes=[('ps', b)], inc=True)
            for hh in range(4):
                P.op('act', lambda e, pst=pst, hh=hh: e.activation(self.bc[:, hh, 0:N], pst[0:32, hh * 128:hh * 128 + N], AF.Copy),
                     reads=[('ps', b)], writes=['bc'])
            for hh in range(4):
                h = hq * 4 + hh
                g = h // 4
                gp, hp = g // 2, g % 2
                qi, qh = self.qchunk(h)
                assert qh == hp
                lo, hi = hp * 64, hp * 64 + 64
                b = self.psrot((0, 1, 2, 3))
                pst = self.ps[b]
                qm, qmk = self.load_qm(h, qc0, N)
                P.op('pe', lambda e, pst=pst, gp=gp, qm=qm: e.matmul(pst[:, 0:N], self.kcbT[:, gp, 0:128], qm[:, 0:N], start=True, stop=True),
                     reads=['kcbT', qmk], writes=[('ps', b)])
                i = self.pti % 2
                self.pti += 1
                tt, pT = self.ptmp[i], self.pT[i]
                P.op('dve', lambda e, pst=pst, tt=tt, hh=hh: e.scalar_tensor_tensor(tt[0:32, 0:N], pst[0:32, 0:N], SCALE, self.bc[:, hh, 0:N], op0=ALU.mult, op1=ALU.add),
                     reads=[('ps', b), 'bc'], writes=[('ptmp', i)])
                P.op('act', lambda e, tt=tt, pT=pT: e.activation(pT[0:32, 0:N], tt[0:32, 0:N], AF.Exp), reads=[('ptmp', i)], writes=[('pT', i)])
                pob = 6 + (h % 2)
                po = self.ps[pob]
                P.op('pe', lambda e, po=po, pT=pT, g=g: e.matmul(po[:, 0:98], pT[:, 0:128], self.vcb[:, g, 0:98], start=True, stop=True),
                     reads=[('pT', i), 'vcb'], writes=[('ps', pob)])
                rd = self.branch_epilogue(po, pob, N, h, self.gates[0:N, ti, h:h + 1], True)
                im = self.imp[0:N, g, :]
                if h % 4 == 0:
                    P.op('dve', lambda e, po=po, rd=rd, im=im: e.tensor_scalar(im, po[0:N, 66:98], rd, None, op0=ALU.mult), reads=[('ps', pob), 'rden'], writes=[('imp', g)])
                else:
                    P.op('dve', lambda e, po=po, rd=rd, im=im: e.scalar_tensor_tensor(im, po[0:N, 66:98], rd, im, op0=ALU.mult, op1=ALU.add), reads=[('ps', pob), 'rden'], writes=[('imp', g)])
        for g in range(4):
            im = self.imp[0:N, g, :]
            P.op('dve', lambda e, im=im: e.tensor_tensor(im, im, self.selmul[0:N, :], op=ALU.mult), reads=['selmul'], writes=[('imp', g)])
            P.op('dve', lambda e, im=im: e.tensor_tensor(im, im, self.seladd[0:N, :], op=ALU.add), reads=['seladd'], writes=[('imp', g)])
            P.op('dve', lambda e, im=im: e.max(self.mx8[0:N, :], im), reads=[('imp', g)], writes=['mx8'])
            P.op('dve', lambda e, im=im: e.match_replace(self.impw[0:N, :], self.mx8[0:N, :], im, -3.0e38), reads=[('imp', g), 'mx8'], writes=['impw'])
            P.op('dve', lambda e: e.max(self.mx8[0:N, :], self.impw[0:N, :]), reads=['impw'], writes=['mx8'])
            P.op('dve', lambda e, im=im: e.tensor_scalar(self.impw[0:N, :], im, self.mx8[0:N, 7:8], None, op0=ALU.is_ge), reads=[('imp', g), 'mx8'], writes=['impw'])
            P.op('dve', lambda e, g=g: e.tensor_scalar(self.negM[0:N, g, :], self.impw[0:N, :], -1.0, -MASKNEG, op0=ALU.add, op1=ALU.mult), reads=['impw'], writes=[('negM', g)])
        for br, KT, Vt, gofs in (('sel', self.KTs, self.Vs, 16), ('win', self.KTw, self.Vw, 32)):
            for h in range(16):
                g = h // 4
                gp, hp = g // 2, g % 2
                qi, _ = self.qchunk(h)
                lo, hi = hp * 64, hp * 64 + 64
                ktmax = (qpos0 + N - 1) // 128
                kts = []
                for kt in range(ktmax + 1):
                    x0 = qpos0 - 128 * kt
                    if br == 'win' and x0 >= 640:
                        continue
                    kts.append((kt, x0))
                pob = 6 + (h % 2)
                po = self.ps[pob]
                qm, qmk = self.load_qm(h, qc0, N)
                pend = None
                for idx, (kt, x0) in enumerate(kts):
                    kcol0 = 128 * kt if kt < 8 else 1024 + 32 + 128 * (kt - 8)
                    b = self.psrot((0, 1, 2, 3))
                    pst = self.ps[b]
                    P.op('pe', lambda e, pst=pst, gp=gp, kcol0=kcol0, KT=KT, br=br, qm=qm: e.matmul(
                        pst[:, 0:N], KT[:, gp, kcol0:kcol0 + 128], qm[:, 0:N], start=True, stop=(br != 'sel')),
                        reads=[('KT', 0 if br == 'sel' else 1, gp), qmk], writes=[('ps', b)], inc=(br != 'sel'))
                    if br == 'sel':
                        if idx == 0 and h % 4 == 0:
                            P.op('dve', lambda e, g=g: e.tensor_copy(self.mexp[0:N, :, :], self.negM[0:N, g, :].unsqueeze(2).to_broadcast([N, 32, 64])),
                                 reads=[('negM', g)], writes=['mexp'])
                        mk = self.mexp[:, 2 * kt:2 * kt + 2, :].rearrange("p a d -> p (a d)")
                        P.op('pe', lambda e, pst=pst, mk=mk: e.matmul(pst[:, 0:N], mk, self.identb[:, 0:N], start=False, stop=True),
                             reads=['mexp', 'identb'], writes=[('ps', b)])
                    pT, pk = self.softmax_unit(pst, b, 128, N, h, x0, br)
                    cur = (idx, kt, pT, pk)
                    if pend is not None:
                        i2, kt2, pT2, pk2 = pend
                        P.op('pe', lambda e, po=po, pT2=pT2, kt2=kt2, g=g, Vt=Vt, i2=i2, nk=len(kts): e.matmul(
                            po[:, 0:66], pT2[:, 0:128], Vt[:, kt2, g, 0:66], start=(i2 == 0), stop=(i2 == nk - 1)),
                            reads=[pk2, ('V', 0 if br == 'sel' else 1, kt2)], writes=[('ps', pob)], inc=(i2 == len(kts) - 1))
                    pend = cur
                i2, kt2, pT2, pk2 = pend
                P.op('pe', lambda e, po=po, pT2=pT2, kt2=kt2, g=g, Vt=Vt, i2=i2, nk=len(kts): e.matmul(
                    po[:, 0:66], pT2[:, 0:128], Vt[:, kt2, g, 0:66], start=(i2 == 0), stop=(i2 == nk - 1)),
                    reads=[pk2, ('V', 0 if br == 'sel' else 1, kt2)], writes=[('ps', pob)], inc=True)
                self.branch_epilogue(po, pob, N, h, self.gates[0:N, ti, gofs + h:gofs + h + 1], False)
        for c in range(8):
            b = self.psrot((4, 5))
            pst = self.ps[b]
            src = self.omix[:, 2 * c:2 * c + 2, :].rearrange("p a d -> p (a d)")
            P.op('pe', lambda e, pst=pst, src=src: e.transpose(pst[:, 0:128], src, self.identf[:, :]), reads=[('omix', 2 * c), ('omix', 2 * c + 1), 'identf'], writes=[('ps', b)])
            P.op('act', lambda e, pst=pst, c=c: e.activation(self.mixT[:, 8 + c, qc0:qc0 + N], pst[:, 0:N], AF.Copy), reads=[('ps', b)], writes=[('aT', 8 + c)])

    def attn_setup(self, A):
        P, D = self.P, self.D
        self.identf = A("identf", [128, 128], F32)
        self.identb = A("identb", [128, 128], BF16)
        self.jmat = A("jmat", [128, 128], F32)
        self.G = A("G", [128, 16, 256], F32)
        self.chd = A("chd", [128, 16], F32)
        self.mw = A("mw", [128, 256], F32)
        self.tab = A("tab", [33, 16], F32)
        self.e1 = [A("e1_%d" % i, [33, 512], F32) for i in range(1)]
        self.vds = [A("vds%d" % i, [16, 512], F32) for i in range(1)]
        self.grev = [A("grev%d" % i, [128, 256], F32) for i in range(1)]
        self.bcrev = A("bcrev", [128, 4, 128], F32)
        self.j32p = A("j32p", [128, 128], F32)
        self.qm = [A("qm%d" % i, [128, 128], BF16) for i in range(4)]
        self.qmi = [0, 0]
        self.bc = A("bc", [32, 4, 128], F32)
        self.mexp = A("mexp", [128, 32, 64], BF16)
        self.selmul = A("selmul", [128, 32], F32)
        self.seladd = A("seladd", [128, 32], F32)
        self.pT = [A("pT%d" % i, [128, 128], BF16) for i in range(2)]
        self.ptmp = [A("ptmp%d" % i, [128, 128], F32) for i in range(2)]
        self.pti = 0
        self.rden = A("rden", [128, 2], F32)
        self.omix = A("omix", [128, 16, 64], F32)
        self.imp = A("imp", [128, 4, 32], F32)
        self.impw = A("impw", [128, 32], F32)
        self.mx8 = A("mx8", [128, 8], F32)
        self.negM = A("negM", [128, 4, 32], BF16)
        P.dma('sp', self.identf[:], D['ident'], writes=['identf'])
        P.dma('pool', self.identb[:], D['ident'], writes=['identb'])
        P.dma('sp', self.jmat[:], D['jmat'], writes=['jmat'])
        P.dma('sp', self.j32p[:], D['j32p'], writes=['jmat'])
        P.op('pool', lambda e: e.memset(self.bcrev[:], 0.0), writes=[('bcrev', i_) for i_ in range(4)])
        for i_ in range(4):
            P.op('pool', lambda e, i_=i_: e.memset(self.qm[i_][:], 0.0), writes=[('qm', i_)])
        for i_ in range(2):
            P.op('pool', lambda e, i_=i_: e.memset(self.pT[i_][:], 0.0), writes=[('pT', i_)])
        P.op('pool', lambda e: e.memset(self.mexp[:], 0.0), writes=['mexp'])
        P.op('pool', lambda e: e.memset(self.omix[:], 0.0), writes=[('omix', i_) for i_ in range(16)])
        P.dma('sp', self.mw[:], D['mw'], writes=['mw'])
        P.dma('sp', self.tab[:], D['tabaug'], writes=['tab'])
        for ch in range(DL // 512):
            i = 0
            P.dma('sp', self.e1[i][:], D['e1h'][:, ch * 512:(ch + 1) * 512], writes=[('e1', i)])
            b = self.psrot((4, 5))
            pst = self.ps[b]
            P.op('pe', lambda e, pst=pst, i=i: e.matmul(pst[0:16, 0:512], self.tab[:, :], self.e1[i][:, :], start=True, stop=True),
                 reads=['tab', ('e1', i)], writes=[('ps', b)], mode='k33m16')
            P.op('act', lambda e, pst=pst, i=i: e.activation(self.vds[i][:, :], pst[0:16, 0:512], AF.Copy), reads=[('ps', b)], writes=[('vds', i)])
            P.dma('sp', self.vd[:, ch * 512:(ch + 1) * 512], self.vds[i][:, :], reads=[('vds', i)], writes=['vd'])
        for h in range(16):
            i = 0
            src = bass.AP(self.vd.tensor, h * DL + DOFF - 127, [[1, 128], [1, 256]])
            P.dma('sp', self.grev[i][:, :], src, reads=['vd'], writes=[('grev', i)])
            b = self.psrot((4, 5))
            pst = self.ps[b]
            P.op('pe', lambda e, pst=pst, i=i: e.matmul(pst[:, 0:256], self.jmat[:, :], self.grev[i][:, :], start=True, stop=True),
                 reads=['jmat', ('grev', i)], writes=[('ps', b)])
            P.op('act', lambda e, pst=pst, h=h: e.activation(self.G[:, h, :], pst[:, 0:256], AF.Copy), reads=[('ps', b)], writes=['G'])
        P.op('dve', lambda e: e.tensor_copy(self.chd[:, :], self.G[:, :, 255]), reads=['G'], writes=['chd'])

    def save_sample_bias(self):
        P = self.P
        t0 = self.ptmp[0][:, :].rearrange("p (h t) -> p h t", h=16)
        t1 = self.ptmp[1][:, :].rearrange("p (h t) -> p h t", h=16)
        sbd = self.sbd

        def put(k):
            P.dma('sp', sbd[k], self.ptmp[0][:, :], reads=[('ptmp', 0)], writes=[('sbd', k)])
        P.op('dve', lambda e: e.tensor_copy(t0, self.G[:, :, 128:136]), reads=['G'], writes=[('ptmp', 0)])
        put(0)
        P.op('pool', lambda e: e.memset(self.ptmp[0][:, :], MASKNEG), writes=[('ptmp', 0)])
        P.op('dve', lambda e: e.tensor_copy(t0[0:8], self.G[0:8, :, 0:8]), reads=['G'], writes=[('ptmp', 0)])
        put(1)
        for k, c0 in ((2, 128), (3, 0)):
            P.op('dve', lambda e, c0=c0: e.tensor_copy(t0, self.mw[:, c0:c0 + 8].unsqueeze(1).to_broadcast([128, 16, 8])), reads=['mw'], writes=[('ptmp', 0)])
            P.op('dve', lambda e: e.tensor_tensor(t0, t0, self.chd[:, :].unsqueeze(2).to_broadcast([128, 16, 8]), op=ALU.add), reads=['chd'], writes=[('ptmp', 0)])
            put(k)
        P.op('dve', lambda e: e.tensor_copy(t0, self.chd[:, :].unsqueeze(2).to_broadcast([128, 16, 8])), reads=['chd'], writes=[('ptmp', 0)])
        put(4)
        src = bass.AP(self.vd.tensor, DOFF + 8192 - 63 - 64 * 127, [[64, 128], [DL, 16], [1, 8]])
        P.dma('sp', t1, src, reads=['vd'], writes=[('ptmp', 1)])
        b = self.psrot((4, 5))
        pst = self.ps[b]
        P.op('pe', lambda e: e.matmul(pst[:, 0:128], self.jmat[:, :], self.ptmp[1][:, :], start=True, stop=True), reads=['jmat', ('ptmp', 1)], writes=[('ps', b)])
        P.op('act', lambda e: e.activation(self.ptmp[0][:, :], pst[:, 0:128], AF.Copy), reads=[('ps', b)], writes=[('ptmp', 0)])
        put(5)

    def attn_sample(self, S, n_pool):
        P, D = self.P, self.D
        NPGS = 4
        identf2 = S("identf2", [128, 128], F32)
        identb2 = S("identb2", [128, 128], BF16)
        sbias = S("sbias", [128, 6, 128], F32)
        rep = S("rep", [128, 128], F32)
        nsel = S("nsel", [128, 4], F32)
        seln = S("seln", [128, 4, 8], BF16)
        selsum = S("selsum", [128, 128], BF16)
        t0f = S("t0f", [128, 128], F32)
        selas = S("selas", [128, 128], F32)
        cw1b2 = S("cw1b2", [128, 64, 64], BF16)
        cw2k2 = S("cw2k2", [64, 2, 128], BF16)
        cw2v2 = S("cw2v2", [64, 64], BF16)
        hx2 = S("hx2", [64, 32], F32)
        htt2 = S("htt2", [64, 32], F32)
        vtf2 = S("vtf2", [64, 128], F32)
        Xs = S("Xs", [128, NPGS, 2, 256], BF16)
        hidS = S("hidS", [64, 2, 128, 4], BF16)
        self.cx = dict(cw1b=cw1b2, hx=hx2, htt=htt2, hid=hidS, cw2k=cw2k2, cw2v=cw2v2, vtf=vtf2, identf=identf2)
        vcbS = S("vcbS", [128, 4, 196], BF16)
        KTp = [S("KTp%d" % i, [128, 4, 128], BF16) for i in range(2)]
        Vp = [S("Vp%d" % i, [128, 4, 66], BF16) for i in range(2)]
        Vraw = [S("Vraw%d" % i, [128, 256], BF16) for i in range(2)]
        wkT = S("wkT", [128, 4, 512], BF16)
        wv = S("wv", [128, 4, 4, 66], BF16)
        indpg = [S("indpg%d" % i, [128, 128], BF16) for i in range(2)]
        negMs = S("negMs", [128, 4, 128], F32)
        negMT = S("negMT", [128, 4, 4, 8], BF16)
        qms = [S("qms%d" % i, [128, 32], BF16) for i in range(4)]
        ktn = [S("ktn%d" % i, [128, 128], BF16) for i in range(2)]
        pTs = [S("pTs%d" % i, [128, 128], BF16) for i in range(2)]
        ptm = [S("ptm%d" % i, [128, 32], F32) for i in range(2)]
        osum = S("osum", [128, 4, 64], F32)
        opad = S("opad", [128, 2, 128], BF16)
        grow = S("grow", [128, 12], F32)
        gtmp = S("gtmp", [128, 3, 4, 4], F32)
        gts = S("gts", [128, 48], F32)
        rdn = S("rdn", [128, 2], F32)
        imps = S("imps", [128, 128], F32)
        impw = S("impw2", [128, 128], F32)
        mx8 = S("mx8b", [128, 8], F32)
        pnb = S("pnb", [128, 128], BF16)
        ptb = S("ptb", [128, 64], I32)
        ptf = S("ptf", [128, 64], F32)
        iop = S("iop", [128, 2], F32)
        idxn = S("idxn", [128, 64], I32)
        idxt = S("idxt", [128, 64], I32)

        P.dma('sp', identf2[:], D['ident'], writes=['identf'])
        P.dma('pool', identb2[:], D['ident'], writes=['identb'])
        P.dma('sp', sbias[:], self.sbd.rearrange("k p c -> p k c"), reads=[('sbd', k) for k in range(6)], writes=['sbias'])
        P.dma('sp', rep[:], D['rep'], writes=['rep'])
        P.dma('sp', nsel[:], D['nsel'], writes=['nsel'])
        P.dma('pool', seln[:], D['seln'], writes=['seln'])
        P.dma('pool', selsum[:], D['selsum'], writes=['selsum'])
        P.dma('sp', t0f[:], D['t0f'], writes=['t0f'])
        P.op('pool', lambda e: e.memset(selas[:], 0.0), writes=['selas'])
        P.dma('sp', selas[0:8, :], D['seladd_s'], writes=['selas'])
        P.dma('pool', cw2k2[:], D['cw2k'], writes=['cw2'])
        P.dma('pool', cw2v2[:], D['cw2v'], writes=['cw2'])
        for t_ in (Vp[0], Vp[1], wv):
            P.op('pool', lambda e, t_=t_: e.memset(t_[:], 1.0), writes=['vinit'])
        for i_, t_ in enumerate(qms):
            P.op('pool', lambda e, t_=t_: e.memset(t_[:], 0.0), writes=[('qms', i_)])
        for i_, t_ in enumerate(ktn):
            P.op('pool', lambda e, t_=t_: e.memset(t_[:], 0.0), writes=[('ktn', i_)])
        for i_, t_ in enumerate(pTs):
            P.op('pool', lambda e, t_=t_: e.memset(t_[:], 0.0), writes=[('pTs', i_)])
        P.op('pool', lambda e: e.memset(opad[:], 0.0), writes=['opad'])
        P.op('pool', lambda e: e.memset(pnb[:], 0.0), writes=['pnb'])
        P.op('pool', lambda e: e.memset(negMs[:], 0.0), writes=['negMs'])
        P.op('pool', lambda e: e.memset(osum[:], 0.0), writes=[('osum', g_) for g_ in range(4)])
        P.op('pool', lambda e: e.memset(gts[:], 0.0), writes=['gts'])
        P.op('pool', lambda e: e.memset(imps[:], 0.0), writes=['imps'])
        P.op('pool', lambda e: e.memset(vcbS[:], 0.0), writes=['vcbS'])
        P.op('pool', lambda e: e.memset(vcbS[:, :, 64:65], 1.0), writes=['vcbS'])
        for g in range(4):
            P.op('dve', lambda e, g=g: e.tensor_copy(vcbS[:, g, 66:194], identb2[:, :]), reads=['identb'], writes=['vcbS'])
        P.dma('sp', iop[:], D['iopc'], writes=['iop'])
        pck = D['pck']
        pcv = D['pcv']
        pskT = D['pskT']
        psv = D['psv'].rearrange("n (g d) -> n g d", g=4)
        sb_ = lambda k, g: sbias[:, k, :].rearrange("p (h t) -> p h t", h=16)[:, 4 * g:4 * g + 4, :].rearrange("p h t -> p (h t)")
        pti = [0]
        qmi = [0]

        def unitA(pst, b, g, bk):
            i = pti[0] % 2
            pti[0] += 1
            P.op('dve', lambda e: e.scalar_tensor_tensor(ptm[i][:, :], pst[:, 0:32], SCALE, sb_(bk, g), op0=ALU.mult, op1=ALU.add),
                 reads=[('ps', b), 'sbias'], writes=[('ptm', i)])
            P.op('act', lambda e: e.activation(pTs[i][:, 0:32], ptm[i][:, :], AF.Exp), reads=[('ptm', i)], writes=[('pTs', i)])
            return i

        def unitB(i, lhsV, vkeys, po, pob, first, last, ncols=66):
            P.op('pe', lambda e: e.matmul(po[:, 0:ncols], pTs[i][:, 0:128], lhsV, start=first, stop=last),
                 reads=[('pTs', i)] + vkeys, writes=[('ps', pob)], inc=last)

        def unit(pst, b, g, bk, lhsV, vkeys, po, pob, first, last, ncols=66):
            i = unitA(pst, b, g, bk)
            unitB(i, lhsV, vkeys, po, pob, first, last, ncols)

        def epilogue(po, pob, g, br, first):
            rd = rdn[0:32, 0:1]
            sg = rdn[0:32, 1:2]
            P.op('dve', lambda e: e.tensor_scalar(rd, po[0:32, 64:65], 1e-30, None, op0=ALU.max), reads=[('ps', pob)], writes=['rdn'])
            P.op('dve', lambda e: e.reciprocal(rd, rd), reads=['rdn'], writes=['rdn'])
            P.op('dve', lambda e: e.tensor_tensor(sg, rd, grow[0:32, br * 4 + g:br * 4 + g + 1], op=ALU.mult), reads=['rdn', 'grow'], writes=['rdn2'])
            om = osum[0:32, g, :]
            if first:
                P.op('dve', lambda e: e.tensor_scalar(om, po[0:32, 0:64], sg, None, op0=ALU.mult), reads=[('ps', pob), 'rdn2'], writes=[('osum', g)])
            else:
                P.op('dve', lambda e: e.scalar_tensor_tensor(om, po[0:32, 0:64], sg, om, op0=ALU.mult, op1=ALU.add), reads=[('ps', pob), 'rdn2'], writes=[('osum', g)])
            return rd

        def load_q(s, g):
            hp = g % 2
            lo, hi = hp * 64, hp * 64 + 64
            k = hp * 2 + g // 2
            c0 = 4 * (g // 2)
            P.op('act', lambda e: e.activation(qms[k][lo:hi, :].rearrange("p (n t) -> p n t", n=4), self.qT[lo:hi, c0:c0 + 4, 1056 + 8 * s:1064 + 8 * s], AF.Copy),
                 reads=[('qT', c0 + n_) for n_ in range(4)], writes=[('qms', k)])
            return qms[k], ('qms', k)

        for s in range(4):
            P.dma('sp', ptb[:], D['pt'][:, 64 * s:64 * s + 64].partition_broadcast(128), reads=['idxn', 'idxt'], writes=['ptb'])
            P.op('dve', lambda e: e.tensor_copy(ptf[:], ptb[:]), reads=['ptb'], writes=['ptf'])
            P.op('dve', lambda e: e.tensor_scalar(ptf[:], ptf[:], 128.0, iop[:, 0:1], op0=ALU.mult, op1=ALU.add), reads=['ptf', 'iop'], writes=['ptf'])
            P.op('dve', lambda e: e.tensor_copy(idxn[:], ptf[:]), reads=['ptf'], writes=['idxn'])
            P.op('dve', lambda e: e.tensor_copy(ptf[:], ptb[:]), reads=['ptb', 'idxn'], writes=['ptf'])
            P.op('dve', lambda e: e.tensor_scalar(ptf[:], ptf[:], 64.0, iop[:, 1:2], op0=ALU.mult, op1=ALU.add), reads=['ptf', 'iop'], writes=['ptf'])
            P.op('dve', lambda e: e.tensor_copy(idxt[:], ptf[:]), reads=['ptf'], writes=['idxt'])
            P.op('act', lambda e, s=s: e.activation(gts[0:8, :], self.gates[0:8, 9 + s, :], AF.Copy), reads=[('gates', 9 + s)], writes=['gts'])
            b = self.psrot((0, 1, 2, 3))
            pst = self.ps[b]
            P.op('pe', lambda e, pst=pst: e.matmul(pst[:, 0:48], rep[:, :], gts[:, :], start=True, stop=True), reads=['rep', 'gts'], writes=[('ps', b)])
            P.op('dve', lambda e, pst=pst: e.tensor_tensor(gtmp[:, :, :, :], pst[:, 0:48].rearrange("p (b g n) -> p b g n", b=3, g=4),
                                                           nsel[:, :].unsqueeze(1).unsqueeze(1).to_broadcast([128, 3, 4, 4]), op=ALU.mult),
                 reads=[('ps', b), 'nsel'], writes=['gtmp'])
            P.op('dve', lambda e: e.tensor_reduce(grow[:, :], gtmp[:, :, :, :].rearrange("p b g n -> p (b g) n"), mybir.AxisListType.X, ALU.add),
                 reads=['gtmp'], writes=['grow'])
            for kv, src in ((0, pck), (1, pcv)):
                P.dma('pool', cw1b2[:], D['cw1k' if kv == 0 else 'cw1v'], writes=['cw1'])
                for st_ in range(64 // NPGS):
                    for j in range(NPGS):
                        col = st_ * NPGS + j
                        P.custom_dma('pool', lambda e, j=j, kv=kv, src=src, col=col: e.indirect_dma_start(
                            out=Xs[:, j, kv, :], out_offset=None, in_=src, in_offset=bass.IndirectOffsetOnAxis(ap=idxn[:, col:col + 1], axis=0)),
                            reads=['idxn'], writes=[('Xs', j, kv)])
                    self.compress1(Xs, NPGS, [('Xs', j, kv) for j in range(NPGS)], 'data', pg0=st_ * NPGS, kvs=(kv,), load_w=False)
            self.compress2(128, self.kcbT, vcbS)
            for g in range(4):
                gp = g // 2
                qm, qmk = load_q(s, g)
                b = self.psrot((0, 1, 2, 3))
                pst = self.ps[b]
                P.op('pe', lambda e, pst=pst, gp=gp, qm=qm: e.matmul(pst[:, 0:32], self.kcbT[:, gp, 0:128], qm[:, 0:32], start=True, stop=True),
                     reads=['kcbT', qmk], writes=[('ps', b)])
                pob = 4 + g
                po = self.ps[pob]
                unit(pst, b, g, 5, vcbS[:, g, 0:194], ['vcbS'], po, pob, True, True, ncols=194)
                rd = epilogue(po, pob, g, 0, True)
                P.op('dve', lambda e, po=po, rd=rd: e.tensor_scalar(pnb[0:32, :], po[0:32, 66:194], rd, None, op0=ALU.mult), reads=[('ps', pob), 'rdn'], writes=['pnb'])
                b2 = self.psrot((0, 1, 2, 3))
                ps2 = self.ps[b2]
                P.op('pe', lambda e, ps2=ps2: e.matmul(ps2[:, 0:128], selsum[:, :], pnb[:, :], start=True, stop=True), reads=['selsum', 'pnb'], writes=[('ps', b2)])
                P.op('dve', lambda e, ps2=ps2: e.tensor_tensor(imps[0:8, :], ps2[0:8, 0:128], selas[0:8, :], op=ALU.add), reads=[('ps', b2), 'selas'], writes=['imps'])
                P.op('dve', lambda e: e.max(mx8[0:8, :], imps[0:8, :]), reads=['imps'], writes=['mx8'])
                P.op('dve', lambda e: e.match_replace(impw[0:8, :], mx8[0:8, :], imps[0:8, :], -3.0e38), reads=['imps', 'mx8'], writes=['impw'])
                P.op('dve', lambda e: e.max(mx8[0:8, :], impw[0:8, :]), reads=['impw'], writes=['mx8'])
                P.op('dve', lambda e: e.tensor_scalar(impw[0:8, :], imps[0:8, :], mx8[0:8, 6:7], None, op0=ALU.is_ge), reads=['imps', 'mx8'], writes=['impw'])
                P.op('dve', lambda e, g=g: e.tensor_scalar(negMs[0:8, g, :], impw[0:8, :], -1.0, -MASKNEG, op0=ALU.add, op1=ALU.mult), reads=['impw'], writes=['negMs'])
                b3 = self.psrot((0, 1, 2, 3))
                ps3 = self.ps[b3]
                P.op('pe', lambda e, ps3=ps3, g=g: e.transpose(ps3[:, 0:128], negMs[:, g, :], identf2[:, :]), reads=['negMs', 'identf'], writes=[('ps', b3)])
                P.op('act', lambda e, ps3=ps3: e.activation(ptm[0][:, 0:8], ps3[:, 0:8], AF.Copy), reads=[('ps', b3)], writes=[('ptm', 0)])
                P.op('dve', lambda e, g=g: e.tensor_copy(negMT[:, g, :, :], ptm[0][:, 0:8].unsqueeze(1).to_broadcast([128, 4, 8])), reads=[('ptm', 0)], writes=[('negMT', g)])
            qq = [load_q(s, g) for g in range(4)]
            pend = [None]
            for i_ in range(64):
                col = i_
                kb = i_ % 2
                P.custom_dma('pool', lambda e, kb=kb, col=col: e.indirect_dma_start(
                    out=KTp[kb][:, :, :].rearrange("p g k -> p (g k)"), out_offset=None, in_=pskT, in_offset=bass.IndirectOffsetOnAxis(ap=idxt[:, col:col + 1], axis=0)),
                    reads=['idxt'], writes=[('KTp', kb)])
                P.custom_dma('pool', lambda e, kb=kb, col=col: e.indirect_dma_start(
                    out=Vraw[kb][:, :], out_offset=None, in_=D['psv'], in_offset=bass.IndirectOffsetOnAxis(ap=idxn[:, col:col + 1], axis=0)),
                    reads=['idxn'], writes=[('Vraw', kb)])
                P.op('act', lambda e, kb=kb: e.activation(Vp[kb][:, :, 0:64], Vraw[kb][:, :].rearrange("p (g d) -> p g d", g=4), AF.Copy),
                     reads=[('Vraw', kb), 'vinit'], writes=[('Vp', kb)])
                P.op('dve', lambda e, kb=kb, i_=i_: e.tensor_scalar(indpg[kb][:, :], t0f[:, :], float(2 * i_), None, op0=ALU.is_equal), reads=['t0f'], writes=[('indpg', kb)])
                for g in range(4):
                    qm, qmk = qq[g]
                    b = self.psrot((0, 1, 2, 3))
                    pst = self.ps[b]
                    P.op('pe', lambda e, pst=pst, kb=kb, g=g, qm=qm: e.matmul(pst[:, 0:32], KTp[kb][:, g, :], qm[:, 0:32], start=True, stop=False),
                         reads=[('KTp', kb), qmk], writes=[('ps', b)], inc=False)
                    P.op('pe', lambda e, pst=pst, kb=kb, g=g: e.matmul(pst[:, 0:32], indpg[kb][:, :], negMT[:, g, :, :].rearrange("p n t -> p (n t)"), start=False, stop=True),
                         reads=[('indpg', kb), ('negMT', g)], writes=[('ps', b)])
                    pob = 4 + g
                    cur = (unitA(pst, b, g, 0 if i_ == 63 else 4), Vp[kb][:, g, 0:66], [('Vp', kb)], self.ps[pob], pob, i_ == 0)
                    if pend[0] is not None:
                        unitB(*pend[0], False)
                    pend[0] = cur
            unitB(*pend[0], False)
            for g in range(4):
                gp = g // 2
                qm, qmk = qq[g]
                kk = g % 2
                P.op('act', lambda e, kk=kk, gp=gp, s=s: e.activation(ktn[kk][:, 0:8], self.KTs[:, gp, 2080 + 8 * s:2088 + 8 * s], AF.Copy),
                     reads=[('KT', 0, gp)], writes=[('ktn', kk)])
                b = self.psrot((0, 1, 2, 3))
                pst = self.ps[b]
                P.op('pe', lambda e, pst=pst, kk=kk, qm=qm: e.matmul(pst[:, 0:32], ktn[kk][:, :], qm[:, 0:32], start=True, stop=True),
                     reads=[('ktn', kk), qmk], writes=[('ps', b)])
                pob = 4 + g
                unit(pst, b, g, 1, self.Vs_s[:, s, g, 0:66], [('Vsm', 0, s)], self.ps[pob], pob, False, True)
                epilogue(self.ps[pob], pob, g, 1, False)
            for hh in range(2):
                P.dma('pool', wkT[hh * 64:(hh + 1) * 64, :, :], D['winkT'][s], writes=['wkT'])
            for j_ in range(4):
                P.dma('pool', wv[:, j_, :, 0:64], D['winv'][s][128 * j_:128 * j_ + 128, :].rearrange("p (g d) -> p g d", g=4), reads=['vinit'], writes=['wv'])
            for g in range(4):
                gp = g // 2
                qm, qmk = qq[g]
                pob = 4 + g
                for j in range(4):
                    b = self.psrot((0, 1, 2, 3))
                    pst = self.ps[b]
                    P.op('pe', lambda e, pst=pst, g=g, j=j, qm=qm: e.matmul(pst[:, 0:32], wkT[:, g, 128 * j:128 * j + 128], qm[:, 0:32], start=True, stop=True),
                         reads=['wkT', qmk], writes=[('ps', b)])
                    unit(pst, b, g, (2, 3, 4, 0)[j], wv[:, j, g, 0:66], ['wv'], self.ps[pob], pob, j == 0, False)
                kk = g % 2
                P.op('act', lambda e, kk=kk, gp=gp, s=s: e.activation(ktn[kk][:, 0:8], self.KTw[:, gp, 2080 + 8 * s:2088 + 8 * s], AF.Copy),
                     reads=[('KT', 1, gp)], writes=[('ktn', kk)])
                b = self.psrot((0, 1, 2, 3))
                pst = self.ps[b]
                P.op('pe', lambda e, pst=pst, kk=kk, qm=qm: e.matmul(pst[:, 0:32], ktn[kk][:, :], qm[:, 0:32], start=True, stop=True),
                     reads=[('ktn', kk), qmk], writes=[('ps', b)])
                unit(pst, b, g, 1, self.Vw_s[:, s, g, 0:66], [('Vsm', 1, s)], self.ps[pob], pob, False, True)
                epilogue(self.ps[pob], pob, g, 2, False)
            for g in range(4):
                P.op('act', lambda e, g=g: e.activation(opad[0:32, 0, 0:64], osum[0:32, g, :], AF.Copy), reads=[('osum', g)], writes=['opad'])
                P.op('act', lambda e, g=g: e.activation(opad[0:32, 1, 64:128], osum[0:32, g, :], AF.Copy), reads=[('osum', g)], writes=['opad'])
                for pr in range(2):
                    b = self.psrot((0, 1, 2, 3))
                    pst = self.ps[b]
                    for hh in range(2):
                        P.op('pe', lambda e, pst=pst, hh=hh, pr=pr: e.matmul(pst[:, 0:8], opad[:, hh, :], seln[:, 2 * pr + hh, :], start=(hh == 0), stop=(hh == 1)),
                             reads=['opad', 'seln'], writes=[('ps', b)], inc=(hh == 1))
                    c = 8 + 2 * g + pr
                    P.op('act', lambda e, pst=pst, c=c, s=s: e.activation(self.mixT[:, c, 1056 + 8 * s:1064 + 8 * s], pst[:, 0:8], AF.Copy), reads=[('ps', b)], writes=[('aT', c)])

    def build(self, n_pool):
        nc, P = self.nc, self.P
        D = self.D
        st = self.st
        self.din('xT_main', [128, 16, NT])
        self.din('xT_pre', [128, 16, NPRE])
        self.din('gains', [128, 4, 16])
        self.din('flag', [128, 1])
        self.din('w_ff1', [2, 2048, 8192])
        self.din('w_ff2', [2, 8192, 2048])
        self.din('w_in_o3', [2048, 6144])
        self.din('w_out_o', [2048, 2048])
        self.din('scw', [128, 3, 16])
        self.din('stsc', [128, 16, 4, 2])
        self.din('w_in_p', [2048, 4656])
        self.din('rgp', [128, 8, 8])
        self.din('rg_wa', [128, 8, 128])
        self.din('rg_wx', [128, 8, 128])
        self.din('kg', [128, 4])
        self.din('sth', [128, 8, 4])
        self.din('stconv', [128, 8, 4, 3])
        self.din('wink', [4, 512, 256])
        for nm, shp in (('ident', [128, 128]), ('jmat', [128, 128]), ('mw', [128, 256]), ('tabaug', [33, 16]), ('e1h', [33, DL]),
                        ('selmul', [1056, 32]), ('seladd', [1056, 32]), ('bvalid', [32, 1]), ('j32p', [128, 128]), ('cw1k', [128, 64, 64]), ('cw1v', [128, 64, 64]),
                        ('cw2k', [64, 2, 128]), ('cw2v', [64, 64]), ('cpe_in', [128, 1, 2, 256]), ('w_out_e', [2048, 2048]),
                        ('pck', [n_pool * 128, 256]), ('pcv', [n_pool * 128, 256]), ('psv', [n_pool * 128, 256]), ('pskT', [n_pool * 64, 512]),
                        ('winkT', [4, 64, 4, 512]), ('rep', [128, 128]), ('nsel', [128, 4]), ('seln', [128, 4, 8]), ('selsum', [128, 128]),
                        ('t0f', [128, 128]), ('iopc', [128, 2]), ('selmul_s', [8, 128]), ('seladd_s', [8, 128])):
            self.din(nm, shp)
        self.din('pt', [1, 256], I32)
        self.sbd = nc.dram_tensor('sbd', [6, 128, 128], F32, kind='Internal').ap()
        self.vd = nc.dram_tensor('vd', [16, DL], F32, kind='Internal').ap()
        self.din('winv', [4, 512, 256])
        self.dout('yT_out', [128, 16, NT])
        self.dout('scp_out', [128, 16, 2])
        self.dout('scs_out', [128, 16, 4, 2])
        for nm in ('kc_o', 'vc_o', 'vs_o', 'vw_o'):
            self.dout(nm, [1024, 256])
        for nm in ('kc_s_o', 'vc_s_o', 'vs_s_o', 'vw_s_o'):
            self.dout(nm, [32, 256])
        self.dout('ksT_o', [128, 2, NT])
        self.dout('kwT_o', [128, 2, NT])
        self.dout('rgo_h_o', [128, 8, 5])
        self.dout('wk_copy_o', [4, 504, 256])
        self.dout('wv_copy_o', [4, 504, 256])
        self.dout('rgo_conv_o', [128, 8, 5, 3])
        self.ps = [st.enter_context(nc.psum_tensor("ps%d" % i, [128, 512], F32)) for i in range(8)]
        self.wt = [self.sb("wt%d" % i, [128, 16, 256], BF16) for i in range(2)]
        self.hT = self.sb("hT", [128, 16, NT + 128], BF16)
        self.aT = self.sb("aT", [128, 16, NT], BF16)
        self.gT = self.aT
        self.mixT = self.aT
        self.sq = self.sb("sq", [128, 4, 512], BF16)
        self.rstd = self.sb("rstd", [128, 512], F32)
        self.tmpf = self.sb("tmpf", [128, 512], F32)
        self.ones_bf = self.sb("ones_bf", [128, 128], BF16)
        self.bones = self.sb("bones", [128, 128], BF16)
        self.epst = self.sb("epst", [128, 1], F32)
        self.onec = self.sb("onec", [128, 1], F32)
        self.gains = self.sb("gains", [128, 4, 16], F32)
        self.flag = self.sb("flag", [128, 1], F32)
        self.scw = self.sb("scw", [128, 3, 16], F32)
        self.stsc = self.sb("stsc", [128, 16, 4, 2], F32)
        self.xs = [self.sb("xs%d" % i, [128, 512], F32) for i in range(2)]
        self.xsi = 0
        self.stgi = 0

        P.op('pool', lambda e: e.memset(self.ones_bf[:], 1.0), writes=['ones'])
        P.op('pool', lambda e: e.memset(self.hT[:, :, NT:NT + 128], 0.0), writes=['hTpad'])
        P.op('pool', lambda e: e.memset(self.bones[:], 0.0), writes=['bones'])
        P.op('pool', lambda e: e.memset(self.bones[0:64, 0:64], 1.0), writes=['bones'])
        P.op('pool', lambda e: e.memset(self.bones[64:128, 64:128], 1.0), writes=['bones'])
        P.op('pool', lambda e: e.memset(self.epst[:], EPS), writes=['epst'])
        P.op('pool', lambda e: e.memset(self.onec[:], 1.0), writes=['onec'])
        P.dma('sp', self.gains[:], D['gains'], writes=['gains'])
        P.dma('sp', self.flag[:], D['flag'], writes=['flag'])
        P.dma('sp', self.scw[:], D['scw'], writes=['scw'])
        P.dma('sp', self.stsc[:], D['stsc'], writes=['stsc'])

        st0 = ExitStack()
        L0 = lambda name, shape, dt: self.sb(name, shape, dt, st0)
        self.KTs = L0("KTs", [128, 2, 1024 + NT], BF16)
        self.KTw = L0("KTw", [128, 2, 1024 + NT], BF16)
        self.Vs = L0("Vs", [128, 16, 4, 66], BF16)
        self.Vw = L0("Vw", [128, 16, 4, 66], BF16)
        self.Vs_s = L0("Vs_s", [128, 4, 4, 66], BF16)
        self.Vw_s = L0("Vw_s", [128, 4, 4, 66], BF16)
        self.rgscr = L0("rgscr", [128, 4, NT], F32)
        self.qT = self.rgscr[:].rearrange("p a t -> p (a t)").bitcast(BF16).rearrange("p (c t) -> p c t", c=8)
        self.gates = L0("gates", [128, 13, 48], F32)
        self.sqh = L0("sqh", [128, 512], BF16)
        self.rstd2 = self.rstd
        self.tmpq = self.tmpf
        self.rgp = L0("rgp", [128, 8, 8], F32)
        self.rgwa = L0("rgwa", [128, 8, 128], BF16)
        self.rgwx = L0("rgwx", [128, 8, 128], BF16)
        self.kg = L0("kg", [128, 4], F32)
        self.sth = L0("sth", [128, 8, 4], F32)
        self.stconv = L0("stconv", [128, 8, 4, 3], F32)
        self.clam = L0("clam", [128, 8], F32)
        self.convh = L0("convh", [128, 8, 3], F32)
        self.hst = L0("hst", [128, 8], F32)
        self.rgo_h = L0("rgo_h", [128, 8, 5], F32)
        self.rgo_conv = L0("rgo_conv", [128, 8, 5, 3], F32)
        self.rxc = self.rgscr[:, 0, :]
        self.rga = self.rgscr[:, 1, :]
        self.rgi = self.rgscr[:, 2, :]
        self.rtm = self.rgscr[:, 3, :]
        self.kcbT = L0("kcbT", [128, 2, 128], BF16)
        self.vcb = L0("vcb", [128, 4, 100], BF16)
        self.hidT = L0("hidT", [64, 2, 32, 4], BF16)
        self.cpe = L0("cpe", [64, 2, 4], F32)
        self.bvalid = L0("bvalid", [32, 1], F32)
        st0a = ExitStack()
        LA = lambda name, shape, dt: self.sb(name, shape, dt, st0a)
        self.hTp = LA("hTp", [128, 16, 256], BF16)
        self.kcpg = LA("kcpg", [128, 8, 2, 256], BF16)
        self.stg = [LA("stg%d" % i, [128, 512], F32) for i in range(2)]
        self.gx = self.stg[0]
        self.gt = self.stg[1]
        self.rxb = LA("rxb", [128, NT], F32)
        self.rxcb = LA("rxcb", [128, NT], BF16)
        self.rhist0 = LA("rhist0", [128, 3 + 1024], F32)
        self.rhists = LA("rhists", [128, 4, 11], F32)
        self.cw1b = LA("cw1b", [128, 64, 64], BF16)
        self.pe_t = LA("pe_t", [128, 1, 2, 256], BF16)
        self.hx = LA("hx", [64, 32], F32)
        self.htt = LA("htt", [64, 32], F32)
        P.dma('pool', self.pe_t[:], D['cpe_in'], writes=['pe_t'])
        P.dma('sp', self.bvalid[:], D['bvalid'], writes=['bvalid'])
        self.cx = dict(cw1b=self.cw1b, hx=self.hx, htt=self.htt, hid=self.hidT)
        self.job(None, lambda: self.compress1(self.pe_t, 1, ['pe_t'], 'pe'))

        for nm, t in (('rgp', self.rgp), ('kg', self.kg), ('sth', self.sth), ('stconv', self.stconv)):
            P.dma('sp', t[:], D[nm], writes=['cst'])
        P.dma('pool', self.rgwa[:], D['rg_wa'], writes=['cst'])
        P.dma('pool', self.rgwx[:], D['rg_wx'], writes=['cst'])
        P.op('pool', lambda e: e.memset(self.convh[:], 0.0), writes=[('convh', r) for r in range(8)])
        P.op('pool', lambda e: e.memset(self.hst[:], 0.0), writes=[('hst', r) for r in range(8)])
        P.op('pool', lambda e: e.memset(self.Vs[:], 1.0), writes=[('V', 0, i) for i in range(16)])
        P.op('pool', lambda e: e.memset(self.Vw[:], 1.0), writes=[('V', 1, i) for i in range(16)])
        for Vt, hh in ((self.Vs, 0), (self.Vw, 1)):
            P.op('dve', lambda e, Vt=Vt: e.tensor_scalar(Vt[:, 0:8, :, 64:65], Vt[:, 0:8, :, 64:65], self.flag[:, 0:1], None, op0=ALU.mult), reads=['flag'], writes=[('V', hh, i) for i in range(16)])
        P.op('pool', lambda e: e.memset(self.Vs_s[:], 1.0), writes=[('Vsm', 0, i) for i in range(4)])
        P.op('pool', lambda e: e.memset(self.Vw_s[:], 1.0), writes=[('Vsm', 1, i) for i in range(4)])
        P.op('act', lambda e: e.activation(self.clam[:], self.rgp[:, :, 7], AF.Exp, scale=-1.0), reads=['cst'], writes=['clam'])
        P.op('act', lambda e: e.activation(self.clam[:], self.clam[:], AF.Ln, bias=self.onec[:, 0:1], scale=1.0), reads=['clam', 'onec'], writes=['clam'])
        P.op('dve', lambda e: e.tensor_scalar(self.clam[:], self.clam[:], -8.0, None, op0=ALU.mult), reads=['clam'], writes=['clam'])

        if os.environ.get('K_NOCOPY') is None:
            P.dma('sp', D['wk_copy_o'], D['wink'][:, 8:512, :], writes=['wkc'])
            P.dma('sp', D['wv_copy_o'], D['winv'][:, 8:512, :], writes=['wvc'])
        self.l0_inproj()
        self.run_jobs()
        P.dma('sp', D['rgo_h_o'], self.rgo_h[:], reads=['rgo_h'], writes=['rgo_h_o'])
        P.dma('sp', D['rgo_conv_o'], self.rgo_conv[:], reads=['rgo_conv'], writes=['rgo_conv_o'])
        P.barrier()
        st0a.close()
        st0b = ExitStack()
        LB = lambda name, shape, dt: self.sb(name, shape, dt, st0b)
        self.cw2k = LB("cw2k", [64, 2, 128], BF16)
        self.cw2v = LB("cw2v", [64, 64], BF16)
        self.vtf = LB("vtf", [64, 32], F32)
        P.dma('pool', self.cw2k[:], D['cw2k'], writes=['cw2'])
        P.dma('pool', self.cw2v[:], D['cw2v'], writes=['cw2'])
        self.attn_setup(LB)
        P.op('pool', lambda e: e.memset(self.vcb[:], 0.0), writes=['vcb'])
        P.op('pool', lambda e: e.memset(self.kcbT[:], 0.0), writes=['kcbT'])
        self.cx = dict(cw2k=self.cw2k, cw2v=self.cw2v, vtf=self.vtf, identf=self.identf, hid=self.hidT)
        self.compress2(32, self.kcbT, self.vcb)
        for g in range(4):
            P.op('dve', lambda e, g=g: e.tensor_scalar(self.vcb[0:32, g, 0:64], self.vcb[0:32, g, 0:64], self.bvalid[:, 0:1], None, op0=ALU.mult), reads=['bvalid'], writes=['vcb'])
            P.op('dve', lambda e, g=g: e.tensor_copy(self.vcb[0:32, g, 64:65], self.bvalid[:, 0:1]), reads=['bvalid'], writes=['vcb'])
            P.op('dve', lambda e, g=g: e.tensor_copy(self.vcb[0:32, g, 66:98], self.identb[0:32, 0:32]), reads=['identb'], writes=['vcb'])
        P.op('pool', lambda e: e.memset(self.mixT[:, 8:16, 1056:1088], 0.0), writes=[('aT', c_) for c_ in range(8, 16)])
        if os.environ.get('K_NOATT') is None:
            self.attn_prompt_tile(0, 0, 32, 992)
            for j in range(8):
                self.attn_prompt_tile(1 + j, 32 + 128 * j, 128, 1024 + 128 * j)
        self.save_sample_bias()
        P.barrier()
        st0b.close()
        st0c = ExitStack()
        LC = lambda name, shape, dt: self.sb(name, shape, dt, st0c)
        if os.environ.get('K_NOSAMPLE') is None:
            self.attn_sample(LC, n_pool)
        P.barrier()
        st0c.close()
        st0.close()

        self.yT = self.sb("yT", [128, 16, NT], F32)
        self.wt = self.wt + [self.sb("wt%d" % i, [128, 16, 256], BF16) for i in (2, 3)]
        self.tb = self.sb("tb", [128, NT], F32)
        self.tc = self.sb("tc", [128, NT], F32)
        self.tv = self.sb("tv", [128, NT], F32)
        self.ty = self.sb("ty", [128, NT], F32)
        self.tvs = self.sb("tvs", [128, 4, 10], F32)
        self.scp = self.sb("scp", [128, 16, 2], F32)
        self.scs = self.sb("scs", [128, 16, 4, 2], F32)
        for c in range(16):
            P.dma('sp', self.yT[:, c, :], D['xT_main'][:, c, :], writes=[('yT', c)])
        ysrc = lambda c, t0, n: self.yT[:, c, t0:t0 + n]
        hdst = lambda c, t0, n: self.hT[:, c, t0:t0 + n]
        ykey = lambda c: ('yT', c)
        hkey = lambda c: ('hT', c)
        Woe = D['w_out_e']
        mparts = [(self.mixT, lambda k: ('aT', k), t0, n, None) for (t0, n) in TT]
        for cg in range(8):
            def epi_o(m, tag, t0, n, ps, pkey, cg=cg):
                c = cg * 2 + m
                P.op('dve', lambda e, ps=ps, c=c, t0=t0, n=n: e.tensor_tensor(
                    self.yT[:, c, t0:t0 + n], ps, self.yT[:, c, t0:t0 + n], op=ALU.add),
                    reads=[pkey], writes=[('yT', c)])
            self.dense_fm(self.wview(Woe, 0, 16, cg * 256, 256), 16, 2, mparts, epi_o)
        self.rmsnorm(ysrc, ykey, self.gains[:, 2, :], hdst, hkey, TT)
        self.ffn(0)
        self.rmsnorm(ysrc, ykey, self.gains[:, 1, :], hdst, hkey, TT)
        self.mixer1()
        self.rmsnorm(ysrc, ykey, self.gains[:, 3, :], hdst, hkey, TT)
        self.ffn(1)
        self.run_jobs()
        for c in range(16):
            P.dma('sp', D['yT_out'][:, c, :], self.yT[:, c, :], reads=[('yT', c)], writes=[('yo', c)])
        P.dma('sp', D['scp_out'], self.scp[:], reads=['scp'], writes=['scpo'])
        P.dma('sp', D['scs_out'], self.scs[:], reads=['scs'], writes=['scso'])
        P.finish()
        P.emit()
        self.st.close()
        return nc


_CACHE = {}


def _fm(a, nchunk):
    T = a.shape[0]
    return np.ascontiguousarray(a.reshape(T, nchunk, 128).transpose(2, 1, 0))


def _vec_pm(v, nchunk):
    return np.ascontiguousarray(v.reshape(nchunk, 128).T)


def _tm(a):
    return np.ascontiguousarray(a.transpose(2, 1, 0).reshape(a.shape[2], -1))


def _q_perm():
    cols = []
    for i in range(8):
        hA = (i % 4) + 8 * (i // 4)
        hB = hA + 4
        cols += list(range(2048 + hA * 64, 2048 + hA * 64 + 64)) + list(range(2048 + hB * 64, 2048 + hB * 64 + 64))
    return cols


def prepare_inputs(inp):
    f = lambda k: np.asarray(inp[k])
    x_prompt = f('x_prompt')
    x_sample = f('x_sample')
    sh = {}
    sh['gains'] = np.ascontiguousarray(np.stack([_vec_pm(f('norm_mix')[0], 16), _vec_pm(f('norm_mix')[1], 16),
                                                 _vec_pm(f('norm_ffn')[0], 16), _vec_pm(f('norm_ffn')[1], 16)], axis=1))
    sh['w_ff1'] = f('w_ff1')
    sh['w_ff2'] = f('w_ff2')
    wio = f('w_in_o')[0]
    sh['w_in_o3'] = np.ascontiguousarray(wio.reshape(2048, 3, 16, 128).transpose(0, 2, 1, 3).reshape(2048, 6144))
    sh['w_out_o'] = f('w_out_o')[0]
    sh['scw'] = np.ascontiguousarray(f('sc_w')[0].reshape(3, 16, 128).transpose(2, 0, 1))
    wie = f('w_in_e')[0]
    R = lambda a, b: list(range(a, b))
    perm = R(0, 2048) + _q_perm() + R(3584, 3840) + R(4096, 4352) + R(3072, 3584) + R(3840, 4096) + R(4352, 4608) + R(4608, 4656)
    sh['w_in_p'] = np.ascontiguousarray(wie[:, perm])
    rows = [f('rg_conv_w')[0][j] for j in range(4)] + [f('rg_conv_b')[0], f('rg_ba')[0], f('rg_bx')[0], f('rg_lambda')[0]]
    sh['rgp'] = np.ascontiguousarray(np.stack([_vec_pm(r_, 8) for r_ in rows], axis=2))
    sh['rg_wa'] = np.ascontiguousarray(f('rg_wa')[0].transpose(1, 0, 2))
    sh['rg_wx'] = np.ascontiguousarray(f('rg_wx')[0].transpose(1, 0, 2))
    kn = f('k_norm')[0]
    qn = f('q_norm')[0]
    sh['kg'] = np.ascontiguousarray(np.stack([np.tile(kn[0], 2), np.tile(kn[1], 2), np.tile(kn[2], 2), np.tile(qn, 2)], axis=1))
    sh['w_out_e'] = f('w_out_e')[0]
    sh['ident'] = np.eye(128, dtype=np.float32)
    sh['jmat'] = np.ascontiguousarray(np.eye(128, dtype=np.float32)[::-1])
    j32p = np.zeros((128, 128), np.float32)
    j32p[0:32, 0:32] = np.eye(32, dtype=np.float32)[::-1]
    sh['j32p'] = j32p
    pp = np.arange(128)[:, None]
    xx = np.arange(384, 640)[None, :]
    sh['mw'] = np.where(xx - pp > 512, np.float32(MASKNEG), np.float32(0.0)).astype(np.float32)
    sh['tabaug'] = np.concatenate([f('rel_bias'), np.full((1, 16), MASKNEG, np.float32)], axis=0).astype(np.float32)
    dist = np.arange(DL) - DOFF
    nn = np.maximum(dist, 0)
    nf = np.maximum(nn, 1).astype(np.float32)
    large = 16 + (np.log(nf / np.float32(16.0)).astype(np.float32) / np.float32(np.log(8.0)) * np.float32(16.0)).astype(np.int32)
    large = np.minimum(large, 31)
    buck = np.where(nn < 16, nn, large)
    e1h = np.zeros((33, DL), np.float32)
    e1h[buck[dist >= 0], np.nonzero(dist >= 0)[0]] = 1.0
    e1h[32, dist < 0] = 1.0
    sh['e1h'] = e1h
    w1k = f('cmp_w1_k')[0]; w1v = f('cmp_w1_v')[0]
    sh['cw1k'] = np.ascontiguousarray(np.concatenate([w1k, w1k], axis=0))
    sh['cw1v'] = np.ascontiguousarray(np.concatenate([w1v, w1v], axis=0))
    w2k = f('cmp_w2_k')[0]
    c2 = np.zeros((64, 2, 128), np.float32)
    c2[:, 0, 0:64] = w2k
    c2[:, 1, 64:128] = w2k
    sh['cw2k'] = c2
    sh['cw2v'] = f('cmp_w2_v')[0]
    pek = f('cmp_pe_k')[0].reshape(64, 256); pev = f('cmp_pe_v')[0].reshape(64, 256)
    pe2 = np.stack([pek, pev], axis=1)
    sh['cpe_in'] = np.ascontiguousarray(np.concatenate([pe2, pe2], axis=0)[:, None])
    npool = f('cache_cmp_k').shape[1]
    sh['pck'] = f('cache_cmp_k')[0].reshape(npool * 128, 256)
    sh['pcv'] = f('cache_cmp_v')[0].reshape(npool * 128, 256)
    sh['psv'] = f('cache_sel_v')[0].reshape(npool * 128, 256)
    sh['pskT'] = np.ascontiguousarray(f('cache_sel_k')[0].transpose(0, 3, 2, 1)).reshape(npool * 64, 512)
    rr = np.arange(128)
    rep = np.zeros((128, 128), np.float32)
    for c_ in range(32):
        rep[c_ % 8, c_] = 1.0
    sh['rep'] = rep
    sh['nsel'] = ((rr[:, None] // 8 == np.arange(4)[None, :]) & (rr[:, None] < 32)).astype(np.float32)
    sh['seln'] = ((rr[:, None, None] < 32) & (rr[:, None, None] // 8 == np.arange(4)[None, :, None]) & (rr[:, None, None] % 8 == np.arange(8)[None, None, :])).astype(np.float32)
    ssum = np.zeros((128, 128), np.float32)
    for r_ in range(32):
        ssum[r_, r_ % 8] = 1.0
    sh['selsum'] = ssum
    sh['iopc'] = np.stack([rr, rr % 64], axis=1).astype(np.float32)
    sh['t0f'] = (rr[:, None] - (rr[None, :] // 64)).astype(np.float32)
    sh['selmul_s'] = np.ones((8, 128), np.float32)
    sa = np.zeros((8, 128), np.float32)
    sa[:, 0] = 1.0e4
    sa[:, 127] = 1.0e4
    sh['seladd_s'] = sa
    maps = []
    for c in range(8):
        b, half = c // 2, c % 2
        p0 = 1024 * half
        m = dict(sh)
        b0 = 0 if half == 1 else 16
        m['pt'] = np.ascontiguousarray(f('page_table')[4 * c:4 * c + 4].reshape(1, 256).astype(np.int32))
        m['winkT'] = np.ascontiguousarray(f('cache_win_k')[0, 4 * c:4 * c + 4].transpose(0, 3, 2, 1))
        qp = np.concatenate([np.arange(992, 1024), np.arange(1024, 2048)])
        cur = (qp // 64)[:, None]
        blk = np.arange(32)[None, :]
        bad = (blk > cur) | (blk < b0)
        forced = ((blk == cur) | (blk == cur - 1) | (blk == b0)) & ~bad
        m['selmul'] = np.where(bad, 0.0, 1.0).astype(np.float32)
        m['seladd'] = np.where(bad, -1.0e30, np.where(forced, 1.0e4, 0.0)).astype(np.float32)
        m['bvalid'] = (np.arange(32) >= b0).astype(np.float32)[:, None]
        own = x_prompt[b, p0:p0 + 1024]
        pre = x_prompt[b, 0:1024] if half == 1 else np.zeros((1024, 2048), np.float32)
        xs = x_sample[4 * c:4 * c + 4].reshape(32, 2048)
        main = np.concatenate([pre[992:1024], own, xs], axis=0)
        m['xT_main'] = _fm(main, 16)
        m['xT_pre'] = _fm(pre, 16)
        m['flag'] = np.full((128, 1), float(half), np.float32)
        st = f('state_sconv')[0, 4 * c:4 * c + 4]
        m['stsc'] = np.ascontiguousarray(st.reshape(4, 2, 16, 128).transpose(3, 2, 0, 1))
        m['sth'] = np.ascontiguousarray(f('state_rglru_h')[0, 4 * c:4 * c + 4].reshape(4, 8, 128).transpose(2, 1, 0))
        m['stconv'] = np.ascontiguousarray(f('state_rglru_conv')[0, 4 * c:4 * c + 4].reshape(4, 3, 8, 128).transpose(3, 2, 0, 1))
        m['wink'] = np.ascontiguousarray(f('cache_win_k')[0, 4 * c:4 * c + 4].reshape(4, 512, 256))
        m['winv'] = np.ascontiguousarray(f('cache_win_v')[0, 4 * c:4 * c + 4].reshape(4, 512, 256))
        maps.append(m)
    return maps


def _kT_to_rows(a):
    T = a.shape[2]
    return np.ascontiguousarray(a.reshape(2, 64, 2, T).transpose(3, 2, 0, 1).reshape(T, 4, 64))


def kernel(**inputs):
    n_pool = int(np.asarray(inputs['cache_cmp_k']).shape[1])
    if 'nc' not in _CACHE:
        b = Builder()
        _CACHE['nc'] = b.build(n_pool)
        _CACHE['in_names'] = b.in_names
    nc = _CACHE['nc']
    maps = prepare_inputs(inputs)
    maps = [{k: m[k] for k in _CACHE['in_names']} for m in maps]
    res = run_bass_kernel_spmd(nc, maps[:NCORES], core_ids=list(range(NCORES)))
    R = res.results
    z = lambda *s: np.zeros(s, np.float32)
    y_p = z(4, 2048, 2048); y_s = z(32, 8, 2048)
    ck_p = z(1, 4, 2048, 4, 64); cv_p = z(1, 4, 2048, 4, 64); sk_p = z(1, 4, 2048, 4, 64); sv_p = z(1, 4, 2048, 4, 64)
    ck_s = z(1, 32, 8, 4, 64); cv_s = z(1, 32, 8, 4, 64); sk_s = z(1, 32, 8, 4, 64); sv_s = z(1, 32, 8, 4, 64)
    wk_p = z(1, 4, 512, 4, 64); wv_p = z(1, 4, 512, 4, 64); wk_s = z(1, 32, 512, 4, 64); wv_s = z(1, 32, 512, 4, 64)
    h_p = z(1, 4, 1024); h_s = z(1, 32, 1024); cv1_p = z(1, 4, 3, 1024); cv1_s = z(1, 32, 3, 1024)
    sconv_p = z(1, 4, 2, 2048); sconv_s = z(1, 32, 2, 2048)
    for c in range(NCORES):
        b, half = c // 2, c % 2
        p0 = 1024 * half
        r = R[c]
        yt = _tm(r['yT_out'])
        y_p[b, p0:p0 + 1024] = yt[32:1056]
        y_s[4 * c:4 * c + 4] = yt[1056:1088].reshape(4, 8, 2048)
        ck_p[0, b, p0:p0 + 1024] = r['kc_o'].reshape(1024, 4, 64)
        cv_p[0, b, p0:p0 + 1024] = r['vc_o'].reshape(1024, 4, 64)
        sv_p[0, b, p0:p0 + 1024] = r['vs_o'].reshape(1024, 4, 64)
        ks_rows = _kT_to_rows(r['ksT_o'])
        kw_rows = _kT_to_rows(r['kwT_o'])
        sk_p[0, b, p0:p0 + 1024] = ks_rows[32:1056]
        ck_s[0, 4 * c:4 * c + 4] = r['kc_s_o'].reshape(4, 8, 4, 64)
        cv_s[0, 4 * c:4 * c + 4] = r['vc_s_o'].reshape(4, 8, 4, 64)
        sv_s[0, 4 * c:4 * c + 4] = r['vs_s_o'].reshape(4, 8, 4, 64)
        sk_s[0, 4 * c:4 * c + 4] = ks_rows[1056:1088].reshape(4, 8, 4, 64)
        wk_s[0, 4 * c:4 * c + 4, 0:504] = r['wk_copy_o'].reshape(4, 504, 4, 64)
        wv_s[0, 4 * c:4 * c + 4, 0:504] = r['wv_copy_o'].reshape(4, 504, 4, 64)
        wk_s[0, 4 * c:4 * c + 4, 504:512] = kw_rows[1056:1088].reshape(4, 8, 4, 64)
        wv_s[0, 4 * c:4 * c + 4, 504:512] = r['vw_s_o'].reshape(4, 8, 4, 64)
        rh = r['rgo_h_o']
        rc = r['rgo_conv_o']
        h_s[0, 4 * c:4 * c + 4] = rh[:, :, 1:5].transpose(2, 1, 0).reshape(4, 1024)
        cv1_s[0, 4 * c:4 * c + 4] = rc[:, :, 1:5, :].transpose(2, 3, 1, 0).reshape(4, 3, 1024)
        sconv_s[0, 4 * c:4 * c + 4] = r['scs_out'].transpose(2, 3, 1, 0).reshape(4, 2, 2048)
        if half == 1:
            wk_p[0, b] = kw_rows[32 + 512:32 + 1024]
            wv_p[0, b] = r['vw_o'][512:1024].reshape(512, 4, 64)
            h_p[0, b] = rh[:, :, 0].T.reshape(1024)
            cv1_p[0, b] = rc[:, :, 0, :].transpose(2, 1, 0).reshape(3, 1024)
            sconv_p[0, b] = _tm(r['scp_out'])
    return (y_p, y_s, ck_p, ck_s, cv_p, cv_s, sk_p, sk_s, sv_p, sv_s, wk_p, wk_s, wv_p, wv_s,
            h_p, h_s, cv1_p, cv1_s, sconv_p, sconv_s)
```
